# Optimizing a Trainium2 kernel written in Bass

```python
import jax, jax.numpy as jnp
from jax import lax
import numpy as np

D_MODEL = 1024
BATCH = 8
SEQ = 2048
DEPTH = 4
DEC_BATCH = 128
DEC_SEQ = 1
PAST_LEN = 16384
PAGE_SIZE = 128

HEAD_DIM = 64
C_RWKV = 3 * D_MODEL // 8
C_HGRN = 3 * D_MODEL // 8
C_LRU = D_MODEL - C_RWKV - C_HGRN
D_MIX = C_RWKV + C_HGRN + C_LRU
RWKV_HEADS = C_RWKV // HEAD_DIM
HGRN_HEADS = C_HGRN // HEAD_DIM
HGRN_KDIM = 64
HGRN_FDIM = HGRN_HEADS * HGRN_KDIM
HGRN_CHUNK = 64
LRU_BLOCKS = 4
LRU_BLOCK = C_LRU // LRU_BLOCKS
CONV_WIDTH = 4
LRU_C = 8.0
W_LORA = 64
A_LORA = 64
G_LORA = 128
RWKV_COLS = 3 * C_RWKV + W_LORA + A_LORA + G_LORA
HGRN_COLS = 2 * HGRN_FDIM + 2 * C_HGRN
LRU_COLS = 2 * C_LRU
C_IN = RWKV_COLS + HGRN_COLS + LRU_COLS
D_FF = 4 * D_MODEL
NORM_EPS = 1e-6
GN_EPS = 64e-5

kernel_name = 'hybrid_rwkv7_hgrn2_rglru_decode_step'

F32 = jnp.float32


def _rmsnorm(x, g):
    xf = x.astype(F32)
    return (xf * lax.rsqrt(jnp.mean(xf * xf, -1, keepdims=True) + NORM_EPS) * g).astype(x.dtype)


def _rwkv7(pm, S0, w0, w_up, a0, a_up, g_up, k_k, k_a, r_k, ln_w, ln_b):
    B, T, _ = pm.shape
    H, N, c = RWKV_HEADS, HEAD_DIM, C_RWKV
    r, k, v, xw, xa, xg = jnp.split(pm, [c, 2 * c, 3 * c, 3 * c + W_LORA, 3 * c + W_LORA + A_LORA], axis=-1)
    w_log = -jax.nn.softplus(-(w0 + jnp.tanh(xw) @ w_up)) - 0.5
    decay = jnp.exp(-jnp.exp(w_log))
    a = jax.nn.sigmoid(a0 + xa @ a_up)
    g = jax.nn.sigmoid(xg) @ g_up
    hd = lambda t: t.reshape(B, T, H, N)
    kk = hd(k * k_k)
    kk = kk * lax.rsqrt(jnp.maximum(jnp.sum(kk * kk, -1, keepdims=True), 1e-24))
    k = k * (1.0 + (a - 1.0) * k_a)
    r_h, k_h, v_h, a_h, w_h = hd(r), hd(k), hd(v), hd(a), hd(decay)

    def step(S, inp):
        rt, wt, kt, vt, kkt, at = inp
        sa = jnp.einsum('bhvk,bhk->bhv', S, -kkt)
        S = S * wt[:, :, None, :] + sa[..., None] * (kkt * at)[:, :, None, :] + vt[..., None] * kt[:, :, None, :]
        return S, jnp.einsum('bhvk,bhk->bhv', S, rt)

    tm = lambda t: jnp.moveaxis(t, 1, 0)
    S_T, out = lax.scan(step, S0, (tm(r_h), tm(w_h), tm(k_h), tm(v_h), tm(kk), tm(a_h)))
    out = jnp.moveaxis(out, 0, 1)
    mean = jnp.mean(out, -1, keepdims=True)
    var = jnp.mean(jnp.square(out - mean), -1, keepdims=True)
    gn = ((out - mean) * lax.rsqrt(var + GN_EPS)).reshape(B, T, c) * ln_w + ln_b
    bonus = (jnp.sum(r_h * k_h * r_k, -1, keepdims=True) * v_h).reshape(B, T, c)
    return (gn + bonus) * g, S_T


def _hgrn2(ph, S0, lb, norm_g):
    B, T, _ = ph.shape
    H, K, V = HGRN_HEADS, HGRN_KDIM, HEAD_DIM
    q, fpre, iv, g = jnp.split(ph, [HGRN_FDIM, 2 * HGRN_FDIM, 2 * HGRN_FDIM + C_HGRN], axis=-1)
    f = lb + (1.0 - lb) * jax.nn.sigmoid(fpre)
    logf = jnp.log(f)
    kin = (1.0 - lb) * jax.nn.sigmoid(-fpre)
    C = HGRN_CHUNK if T % HGRN_CHUNK == 0 else T
    nC = T // C
    ch = lambda t, d: jnp.moveaxis(t.reshape(B, nC, C, H, d), 1, 0)
    causal = jnp.tril(jnp.ones((C, C), bool))[None, :, :, None, None]

    def chunk_step(S, inp):
        qc, kc, vc, lc = inp
        b = jnp.cumsum(lc, axis=1)
        diff = jnp.where(causal, b[:, :, None] - b[:, None, :], 0.0)
        dmat = jnp.where(causal, jnp.exp(diff), 0.0)
        att = jnp.einsum('bthk,bshk,btshk->bhts', qc, kc, dmat)
        o = jnp.einsum('bhts,bshv->bthv', att, vc) + jnp.einsum('bthk,bhkv->bthv', qc * jnp.exp(b), S)
        b_last = b[:, -1]
        S = jnp.exp(b_last)[..., None] * S + jnp.einsum('bshk,bshv->bhkv', kc * jnp.exp(b_last[:, None] - b), vc)
        return S, o

    S_T, o = lax.scan(chunk_step, S0, (ch(q, K), ch(kin, K), ch(iv, V), ch(logf, K)))
    o = jnp.moveaxis(o, 0, 1).reshape(B, T, H, V)
    o = o * lax.rsqrt(jnp.mean(o * o, -1, keepdims=True) + NORM_EPS)
    return o.reshape(B, T, C_HGRN) * norm_g * jax.nn.silu(g), S_T


def _combine(c1, c2):
    a1, b1 = c1
    a2, b2 = c2
    return a1 * a2, a2 * b1 + b2


def _rglru(pl, h0, conv0, conv_w, conv_b, wa, ba, wx, bx, lam):
    B, T, _ = pl.shape
    xb, gate = jnp.split(pl, [C_LRU], axis=-1)
    xpad = jnp.concatenate([conv0, xb], axis=1)
    xc = conv_b + sum(xpad[:, j:j + T] * conv_w[j] for j in range(CONV_WIDTH))
    new_conv = xpad[:, T:]
    xh = xc.reshape(B, T, LRU_BLOCKS, LRU_BLOCK)
    r = jax.nn.sigmoid(jnp.einsum('btnc,ncd->btnd', xh, wa).reshape(B, T, C_LRU) + ba)
    i = jax.nn.sigmoid(jnp.einsum('btnc,ncd->btnd', xh, wx).reshape(B, T, C_LRU) + bx)
    log_a = -LRU_C * r * jax.nn.softplus(-lam)
    a = jnp.exp(log_a)
    u = xc * i * jnp.sqrt(jnp.maximum(-jnp.expm1(2.0 * log_a), 1e-12))
    u = u.at[:, 0].add(a[:, 0] * h0)
    _, h = lax.associative_scan(_combine, (a, u), axis=1)
    return h * jax.nn.gelu(gate), h[:, -1], new_conv


def _trunk(x, wkv, shift, hgrn, lru, conv, prm):
    dt = x.dtype
    s = jax.nn.softmax(prm['hgrn_lb'].astype(F32), axis=0)
    lb_all = jnp.cumsum(s, axis=0) - s[0:1]
    n_wkv, n_shift, n_hgrn, n_lru, n_conv = [], [], [], [], []
    for l in range(DEPTH):
        xn = _rmsnorm(x, prm['norm1_g'][l])
        w_in = prm['w_in'][l]
        P = (xn @ w_in).astype(F32)
        pr, ph, pl = jnp.split(P, [RWKV_COLS, RWKV_COLS + HGRN_COLS], axis=-1)
        p0 = (shift[l].astype(dt) @ w_in[:, :RWKV_COLS]).astype(F32)
        pr_prev = jnp.concatenate([p0[:, None], pr[:, :-1]], axis=1)
        pm = pr + prm['mu_shift'][l] * (pr_prev - pr)
        o_r, S_r = _rwkv7(pm, wkv[l].astype(F32), prm['rwkv_w0'][l], prm['rwkv_w_up'][l], prm['rwkv_a0'][l],
                          prm['rwkv_a_up'][l], prm['rwkv_g_up'][l], prm['rwkv_k_k'][l], prm['rwkv_k_a'][l],
                          prm['rwkv_r_k'][l], prm['rwkv_ln_w'][l], prm['rwkv_ln_b'][l])
        o_h, S_h = _hgrn2(ph, hgrn[l].astype(F32), lb_all[l], prm['hgrn_norm_g'][l])
        o_l, h_l, c_l = _rglru(pl, lru[l].astype(F32), conv[l].astype(F32), prm['lru_conv_w'][l],
                               prm['lru_conv_b'][l], prm['lru_wa'][l], prm['lru_ba'][l], prm['lru_wx'][l],
                               prm['lru_bx'][l], prm['lru_lambda'][l])
        mix = jnp.concatenate([o_r, o_h, o_l], axis=-1).astype(dt)
        x = x + mix @ prm['w_out'][l]
        xn2 = _rmsnorm(x, prm['norm2_g'][l])
        x = x + jnp.square(jax.nn.relu(xn2 @ prm['mlp_w1'][l])) @ prm['mlp_w2'][l]
        n_wkv.append(S_r.astype(dt)); n_shift.append(xn[:, -1]); n_hgrn.append(S_h.astype(dt))
        n_lru.append(h_l.astype(dt)); n_conv.append(c_l.astype(dt))
    y = _rmsnorm(x, prm['final_g'])
    return (y, jnp.stack(n_wkv), jnp.stack(n_shift), jnp.stack(n_hgrn), jnp.stack(n_lru), jnp.stack(n_conv))


def setup_inputs(seed: int = 0) -> dict:
    key = jax.random.key(seed)
    ks = jax.random.split(key, 40)
    nrm = lambda i, shape, sc: jax.random.normal(ks[i], shape, F32) * sc
    u = jax.random.uniform(ks[30], (DEPTH, C_LRU), F32, 0.9, 0.999) ** (1.0 / LRU_C)
    return {
        'x_prompt': nrm(0, (BATCH, SEQ, D_MODEL), 1.0),
        'x_sample': nrm(1, (DEC_BATCH, DEC_SEQ, D_MODEL), 1.0),
        'state_wkv': nrm(2, (DEPTH, DEC_BATCH, RWKV_HEADS, HEAD_DIM, HEAD_DIM), 0.3),
        'state_shift': nrm(3, (DEPTH, DEC_BATCH, D_MODEL), 1.0),
        'state_hgrn': nrm(4, (DEPTH, DEC_BATCH, HGRN_HEADS, HGRN_KDIM, HEAD_DIM), 0.3),
        'state_lru': nrm(5, (DEPTH, DEC_BATCH, C_LRU), 0.5),
        'state_conv': nrm(6, (DEPTH, DEC_BATCH, CONV_WIDTH - 1, C_LRU), 1.0),
        'norm1_g': 1.0 + nrm(7, (DEPTH, D_MODEL), 0.02),
        'w_in': nrm(8, (DEPTH, D_MODEL, C_IN), D_MODEL ** -0.5),
        'mu_shift': jax.random.uniform(ks[9], (DEPTH, RWKV_COLS), F32),
        'rwkv_w0': jax.random.uniform(ks[10], (DEPTH, C_RWKV), F32, -6.0, -1.0),
        'rwkv_w_up': nrm(11, (DEPTH, W_LORA, C_RWKV), 0.5 * W_LORA ** -0.5),
        'rwkv_a0': nrm(12, (DEPTH, C_RWKV), 0.1),
        'rwkv_a_up': nrm(13, (DEPTH, A_LORA, C_RWKV), 0.5 * A_LORA ** -0.5),
        'rwkv_g_up': nrm(14, (DEPTH, G_LORA, C_RWKV), G_LORA ** -0.5),
        'rwkv_k_k': 0.85 + nrm(15, (DEPTH, C_RWKV), 0.02),
        'rwkv_k_a': 1.0 + nrm(16, (DEPTH, C_RWKV), 0.02),
        'rwkv_r_k': nrm(17, (DEPTH, RWKV_HEADS, HEAD_DIM), 0.1),
        'rwkv_ln_w': 1.0 + nrm(18, (DEPTH, C_RWKV), 0.02),
        'rwkv_ln_b': nrm(19, (DEPTH, C_RWKV), 0.01),
        'hgrn_lb': nrm(20, (DEPTH, HGRN_FDIM), 0.1),
        'hgrn_norm_g': 1.0 + nrm(21, (DEPTH, C_HGRN), 0.02),
        'lru_conv_w': nrm(22, (DEPTH, CONV_WIDTH, C_LRU), CONV_WIDTH ** -0.5),
        'lru_conv_b': nrm(23, (DEPTH, C_LRU), 0.01),
        'lru_wa': nrm(24, (DEPTH, LRU_BLOCKS, LRU_BLOCK, LRU_BLOCK), LRU_BLOCK ** -0.5),
        'lru_ba': nrm(25, (DEPTH, C_LRU), 0.01),
        'lru_wx': nrm(26, (DEPTH, LRU_BLOCKS, LRU_BLOCK, LRU_BLOCK), LRU_BLOCK ** -0.5),
        'lru_bx': nrm(27, (DEPTH, C_LRU), 0.01),
        'lru_lambda': jnp.log(u) - jnp.log1p(-u),
        'w_out': nrm(28, (DEPTH, D_MIX, D_MODEL), D_MIX ** -0.5),
        'norm2_g': 1.0 + nrm(29, (DEPTH, D_MODEL), 0.02),
        'mlp_w1': nrm(31, (DEPTH, D_MODEL, D_FF), D_MODEL ** -0.5),
        'mlp_w2': nrm(32, (DEPTH, D_FF, D_MODEL), D_FF ** -0.5),
        'final_g': 1.0 + nrm(33, (D_MODEL,), 0.02),
    }


def reference(x_prompt, x_sample, state_wkv, state_shift, state_hgrn, state_lru, state_conv,
              norm1_g, w_in, mu_shift, rwkv_w0, rwkv_w_up, rwkv_a0, rwkv_a_up, rwkv_g_up, rwkv_k_k,
              rwkv_k_a, rwkv_r_k, rwkv_ln_w, rwkv_ln_b, hgrn_lb, hgrn_norm_g, lru_conv_w, lru_conv_b,
              lru_wa, lru_ba, lru_wx, lru_bx, lru_lambda, w_out, norm2_g, mlp_w1, mlp_w2, final_g):
    prm = dict(norm1_g=norm1_g, w_in=w_in, mu_shift=mu_shift, rwkv_w0=rwkv_w0, rwkv_w_up=rwkv_w_up,
               rwkv_a0=rwkv_a0, rwkv_a_up=rwkv_a_up, rwkv_g_up=rwkv_g_up, rwkv_k_k=rwkv_k_k,
               rwkv_k_a=rwkv_k_a, rwkv_r_k=rwkv_r_k, rwkv_ln_w=rwkv_ln_w, rwkv_ln_b=rwkv_ln_b,
               hgrn_lb=hgrn_lb, hgrn_norm_g=hgrn_norm_g, lru_conv_w=lru_conv_w, lru_conv_b=lru_conv_b,
               lru_wa=lru_wa, lru_ba=lru_ba, lru_wx=lru_wx, lru_bx=lru_bx, lru_lambda=lru_lambda,
               w_out=w_out, norm2_g=norm2_g, mlp_w1=mlp_w1, mlp_w2=mlp_w2, final_g=final_g)
    Bp, dt = x_prompt.shape[0], x_prompt.dtype
    z_wkv = jnp.zeros((DEPTH, Bp, RWKV_HEADS, HEAD_DIM, HEAD_DIM), dt)
    z_shift = jnp.zeros((DEPTH, Bp, D_MODEL), dt)
    z_hgrn = jnp.zeros((DEPTH, Bp, HGRN_HEADS, HGRN_KDIM, HEAD_DIM), dt)
    z_lru = jnp.zeros((DEPTH, Bp, C_LRU), dt)
    z_conv = jnp.zeros((DEPTH, Bp, CONV_WIDTH - 1, C_LRU), dt)
    y_prompt, p_wkv, p_shift, p_hgrn, p_lru, p_conv = _trunk(x_prompt, z_wkv, z_shift, z_hgrn, z_lru, z_conv, prm)
    y_sample, s_wkv, s_shift, s_hgrn, s_lru, s_conv = _trunk(x_sample, state_wkv, state_shift, state_hgrn,
                                                             state_lru, state_conv, prm)
    return (y_prompt, y_sample, p_wkv, p_shift, p_hgrn, p_lru, p_conv, s_wkv, s_shift, s_hgrn, s_lru, s_conv)
```

```python
import contextlib
import numpy as np
import concourse.bass as bass
import concourse.mybir as mybir
from concourse.bass_utils import run_bass_kernel_spmd

F32 = mybir.dt.float32
BF16 = mybir.dt.bfloat16
AF = mybir.ActivationFunctionType
ALU = mybir.AluOpType

NCORES = 8
D = 1024
KD = 8
SEQ = 2048
DEPTH = 4
TT = 256
NPASS = 2
TPP = SEQ // NPASS
NTILE = TPP // TT
CH = 64
NCH = TT // CH
MID = CH // 2 - 1
C_IN = 3456
NCT_IN = 27
DFF = 4096
FG = 512
NFG = DFF // FG
NORM_EPS = 1e-6
GN_EPS = 64e-5
C0 = float(np.exp(-0.5))

ENGS = ['pe', 'act', 'dve', 'pool', 'sp']


class Sched:
    def __init__(self, nc):
        self.nc = nc
        self.ops = []
        self.per_eng = {e: [] for e in ENGS}
        self.last_w = {}
        self.readers = {}
        self.synced = {e: {} for e in ENGS}
        self.dma_cum = {}
        self.pe_mode = None
        self.pe_last = None

    @staticmethod
    def _keys(aps):
        ks = []
        for a in aps:
            if a is None or isinstance(a, (int, float)):
                continue
            if isinstance(a, (str, tuple)):
                ks.append(a)
                continue
            t = getattr(a, 'tensor', None)
            if t is None or type(t).__name__ != 'SBTensorHandle' or len(t.shape) < 3 or a.dtype != t.dtype:
                ks.append(a.name)
                continue
            shp = [int(v) for v in t.shape]
            row = int(np.prod(shp[1:]))
            bs = int(np.prod(shp[2:]))
            lo = int(a.offset) % row
            hi = lo
            for step, cnt in list(a.ap)[1:]:
                if step > 0:
                    hi += (int(cnt) - 1) * int(step)
            for b in range(lo // bs, min(hi // bs, shp[1] - 1) + 1):
                ks.append((a.name, b))
        return ks

    maxops = None

    def add(self, eng, method, r=(), w=(), dma=None, **kw):
        if self.maxops is not None and len(self.ops) >= self.maxops:
            return None
        idx = len(self.ops)
        r = list(r)
        w = list(w)
        for kn, v in kw.items():
            if hasattr(v, 'name') and hasattr(v, 'ap'):
                (w if kn in ('out', 'ap', 'accum_out') else r).append(v)
        rk = self._keys(r)
        wk = self._keys(w)
        deps = set()
        for k in rk:
            if k in self.last_w:
                deps.add(self.last_w[k])
        for k in wk:
            if k in self.last_w:
                deps.add(self.last_w[k])
            deps.update(self.readers.get(k, {}).values())
        waits = []
        if eng == 'pe' and method in ('matmul', 'transpose'):
            st_ap = kw['lhsT'] if method == 'matmul' else kw['in_']
            shp = list(st_ap.shape)
            rkk = 32 if shp[0] <= 32 else (64 if shp[0] <= 64 else 128)
            mfree = int(np.prod(shp[1:]))
            rm = 32 if mfree <= 32 else (64 if mfree <= 64 else 128)
            mode = (rkk, rm, method, str(st_ap.dtype))
            if getattr(self, 'pe_mode', None) is not None and mode != self.pe_mode and self.pe_last is not None:
                p = self.ops[self.pe_last]
                p['inc'] = True
                waits.append(('eng', self.pe_last))
            self.pe_mode = mode
        for d in sorted(deps):
            p = self.ops[d]
            if p['dma'] is not None:
                key = ('dma', p['dma'])
                if self.synced[eng].get(key, 0) >= p['cum']:
                    continue
                self.synced[eng][key] = p['cum']
                waits.append(('dma', p['dma'], p['cum']))
            else:
                if p['eng'] == 'pe' and eng == 'pe':
                    continue
                if self.synced[eng].get(p['eng'], -1) >= p['seq']:
                    continue
                self.synced[eng][p['eng']] = p['seq']
                p['inc'] = True
                waits.append(('eng', d))
        o = dict(eng=eng, method=method, kw=kw, waits=waits, dma=dma, seq=len(self.per_eng[eng]),
                 inc=False, cum=None, tick=None)
        if dma is not None:
            self.dma_cum[dma] = self.dma_cum.get(dma, 0) + 16
            o['cum'] = self.dma_cum[dma]
        self.ops.append(o)
        self.per_eng[eng].append(idx)
        if eng == 'pe':
            self.pe_last = idx
        for k in wk:
            self.last_w[k] = idx
            self.readers[k] = {}
        rtag = ('dma', dma) if dma is not None else eng
        for k in rk:
            self.readers.setdefault(k, {})[rtag] = idx
        return idx

    def barrier(self):
        last_real = {}
        for f in ENGS:
            j = len(self.per_eng[f]) - 1
            while j >= 0 and (self.ops[self.per_eng[f][j]]['dma'] is not None
                              or self.ops[self.per_eng[f][j]]['method'] is None):
                j -= 1
            if j >= 0:
                last_real[f] = self.per_eng[f][j]
        cums = dict(self.dma_cum)
        for e in ENGS:
            waits = []
            for f, qi in last_real.items():
                if f == e:
                    continue
                q = self.ops[qi]
                if self.synced[e].get(f, -1) < q['seq']:
                    self.synced[e][f] = q['seq']
                    q['inc'] = True
                    waits.append(('eng', qi))
            for key, cum in cums.items():
                k2 = ('dma', key)
                if self.synced[e].get(k2, 0) < cum:
                    self.synced[e][k2] = cum
                    waits.append(('dma', key, cum))
            o = dict(eng=e, method=None, kw={}, waits=waits, dma=None, seq=len(self.per_eng[e]),
                     inc=False, cum=None, tick=None)
            self.per_eng[e].append(len(self.ops))
            self.ops.append(o)

    def emit(self, es):
        nc = self.nc
        ROT = 20000
        eng_sems = {e: [] for e in ENGS}
        for e in ENGS:
            n = 0
            for i in self.per_eng[e]:
                o = self.ops[i]
                if o['inc']:
                    n += 1
                    o['tick'] = n
            nsem = (n + ROT - 1) // ROT
            for j in range(max(nsem, 1)):
                eng_sems[e].append(es.enter_context(nc.semaphore(f"s_{e}_{j}")))
        dma_sems = {}
        for key in self.dma_cum:
            dma_sems[key] = es.enter_context(nc.semaphore(f"d_{len(dma_sems)}"))
        objs = {'pe': nc.tensor, 'act': nc.scalar, 'dve': nc.vector, 'pool': nc.gpsimd, 'sp': nc.sync}

        def tick_ref(e, tick):
            j = (tick - 1) // ROT
            return eng_sems[e][j], tick - j * ROT

        def run(e, eng):
            for i in self.per_eng[e]:
                o = self.ops[i]
                for wt in o['waits']:
                    if wt[0] == 'dma':
                        eng.wait_ge(dma_sems[wt[1]], wt[2])
                    else:
                        p = self.ops[wt[1]]
                        s, v = tick_ref(p['eng'], p['tick'])
                        eng.wait_ge(s, v)
                if o['method'] is None:
                    continue
                ins = getattr(eng, o['method'])(**o['kw'])
                if o['dma'] is not None:
                    ins.then_inc(dma_sems[o['dma']], 16)
                elif o['inc']:
                    s, v = tick_ref(e, o['tick'])
                    ins.then_inc(s, 1)

        with nc.Block() as block:
            @block.tensor
            def _(eng):
                run('pe', eng)

            @block.scalar
            def _(eng):
                run('act', eng)

            @block.vector
            def _(eng):
                run('dve', eng)

            @block.gpsimd
            def _(eng):
                run('pool', eng)

            @block.sync
            def _(eng):
                run('sp', eng)


def _fm(v, nt):
    return np.ascontiguousarray(np.asarray(v, np.float32).reshape(nt, 128).T)


class CP:
    def __init__(self):
        self.cols = {}
        self.n = 0

    def alloc(self, name, w):
        self.cols[name] = (self.n, w)
        self.n += w

    def sl(self, name):
        a, w = self.cols[name]
        return slice(a, a + w)


def make_cp():
    cp = CP()
    for l in range(DEPTH):
        for nm, w in [('g1', 8), ('g2', 8), ('mu', 11), ('w0', 3), ('a0', 3), ('kk', 3), ('ka', 3), ('rk', 3),
                      ('lnw', 3), ('lnb', 3), ('hng', 3), ('cw', 8), ('cb', 2), ('ba', 2), ('bx', 2), ('lam', 2)]:
            cp.alloc(f"{nm}{l}", w)
    cp.alloc('hlb', 12)
    cp.alloc('fg', 8)
    cp.alloc('ident', 128)
    cp.alloc('msu', 64)
    cp.alloc('msl', 64)
    cp.alloc('miu', 64)
    cp.alloc('msun', 64)
    cp.alloc('msln', 64)
    cp.alloc('id2', 64)
    return cp


def pack_consts(inp, cp):
    A = np.zeros((128, cp.n), np.float32)
    for l in range(DEPTH):
        A[:, cp.sl(f"g1{l}")] = _fm(inp['norm1_g'][l], 8)
        A[:, cp.sl(f"g2{l}")] = _fm(inp['norm2_g'][l], 8)
        A[:, cp.sl(f"mu{l}")] = _fm(inp['mu_shift'][l], 11)
        A[:, cp.sl(f"w0{l}")] = _fm(inp['rwkv_w0'][l], 3)
        A[:, cp.sl(f"a0{l}")] = _fm(inp['rwkv_a0'][l], 3)
        A[:, cp.sl(f"kk{l}")] = _fm(inp['rwkv_k_k'][l], 3)
        A[:, cp.sl(f"ka{l}")] = _fm(inp['rwkv_k_a'][l], 3)
        A[:, cp.sl(f"rk{l}")] = _fm(inp['rwkv_r_k'][l].reshape(-1), 3)
        A[:, cp.sl(f"lnw{l}")] = _fm(inp['rwkv_ln_w'][l], 3)
        A[:, cp.sl(f"lnb{l}")] = _fm(inp['rwkv_ln_b'][l], 3)
        A[:, cp.sl(f"hng{l}")] = _fm(inp['hgrn_norm_g'][l], 3)
        cw = np.stack([_fm(inp['lru_conv_w'][l, j], 2) for j in range(4)], axis=1)
        A[:, cp.sl(f"cw{l}")] = cw.reshape(128, 8)
        A[:, cp.sl(f"cb{l}")] = _fm(inp['lru_conv_b'][l], 2)
        A[:, cp.sl(f"ba{l}")] = _fm(inp['lru_ba'][l], 2)
        A[:, cp.sl(f"bx{l}")] = _fm(inp['lru_bx'][l], 2)
        A[:, cp.sl(f"lam{l}")] = _fm(inp['lru_lambda'][l], 2)
    hl = np.stack([_fm(inp['hgrn_lb'][l], 3) for l in range(DEPTH)], axis=2)
    A[:, cp.sl('hlb')] = hl.reshape(128, 12)
    A[:, cp.sl('fg')] = _fm(inp['final_g'], 8)
    A[:, cp.sl('ident')] = np.eye(128, dtype=np.float32)
    r = np.arange(64)
    for hb in (0, 64):
        A[hb:hb + 64, cp.sl('msu')] = (r[:, None] < r[None, :]).astype(np.float32)
        A[hb:hb + 64, cp.sl('msl')] = (r[:, None] > r[None, :]).astype(np.float32)
        A[hb:hb + 64, cp.sl('miu')] = (r[:, None] <= r[None, :]).astype(np.float32)
        A[hb:hb + 64, cp.sl('msun')] = -(r[:, None] < r[None, :]).astype(np.float32)
        A[hb:hb + 64, cp.sl('msln')] = -(r[:, None] > r[None, :]).astype(np.float32)
        A[hb:hb + 64, cp.sl('id2')] = np.eye(64, dtype=np.float32)
    return A


def pack_small_mats(inp):
    W = np.zeros((128, DEPTH, 384 * 2 + 256), np.float32)
    for l in range(DEPTH):
        W[:64, l, 0:384] = inp['rwkv_w_up'][l]
        W[64:, l, 0:384] = inp['rwkv_a_up'][l]
        W[:, l, 384:768] = inp['rwkv_g_up'][l]
        wa = np.asarray(inp['lru_wa'][l]).reshape(2, 2, 64, 64)
        wx = np.asarray(inp['lru_wx'][l]).reshape(2, 2, 64, 64)
        for t in range(2):
            W[:, l, 768 + t * 64: 768 + (t + 1) * 64] = wa[t].reshape(128, 64)
            W[:, l, 896 + t * 64: 896 + (t + 1) * 64] = wx[t].reshape(128, 64)
    return W


def bc(ap, shape):
    return ap.broadcast_to(list(shape))


class Cfg:
    def __init__(self, n, ch):
        self.n = n
        self.ch = ch
        self.nch = n // ch
        self.mid = max(ch // 2 - 1, 0)
        lv = 0
        while (1 << lv) < ch:
            lv += 1
        self.lv = lv


class Builder:
    def __init__(self, nlayers=DEPTH, npass=NPASS, do_mlp=True, do_smp=True):
        self.do_smp = do_smp
        self.nlayers = nlayers
        self.npass = npass
        self.do_mlp = do_mlp
        self.cp = make_cp()
        self.uid = 0

    def sb(self, name, shape, dt=F32):
        nb = int(np.prod(shape[1:])) * (2 if dt == BF16 else 4)
        nb = (nb + 31) // 32 * 32
        t = self.nc.alloc_sbuf_tensor_at(name, list(shape), dt, offset=self.sb_off)
        self.sb_off += nb
        assert self.sb_off <= 229376 - 256, (name, self.sb_off)
        return t

    def bank(self):
        b = self.banks[self.bank_i % 8]
        self.bank_i += 1
        return b

    def build(self):
        nc = bass.Bass("TRN2", target_bir_lowering=False)
        self.nc = nc
        S = Sched(nc)
        import os as _os
        if _os.environ.get("K_MAXOPS"):
            S.maxops = int(_os.environ["K_MAXOPS"])
        self.S = S
        cp = self.cp
        dram_in = lambda n, shp: nc.dram_tensor(n, list(shp), F32, kind="ExternalInput").ap()
        dram_out = lambda n, shp: nc.dram_tensor(n, list(shp), F32, kind="ExternalOutput").ap()
        self.xp = dram_in("x_prompt", [SEQ, D])
        self.cpk = dram_in("cpack", [128, cp.n])
        self.smat = dram_in("smat", [128, DEPTH, 1024])
        self.w_in = dram_in("w_in", [self.nlayers, D, C_IN])
        self.w_out = dram_in("w_out", [self.nlayers, D, D])
        self.w1 = dram_in("mlp_w1", [self.nlayers if self.do_mlp else 1, D, DFF])
        self.w2 = dram_in("mlp_w2", [self.nlayers if self.do_mlp else 1, DFF, D])
        self.y_p = dram_out("y_prompt", [SEQ, D])
        self.o_pshift = dram_out("p_shift", [DEPTH, D])
        self.o_pwkv = dram_out("p_wkv", [DEPTH, 6, 64, 64])
        self.o_phgrn = dram_out("p_hgrn", [DEPTH, 6, 64, 64])
        self.o_plru = dram_out("p_lru", [DEPTH, 256])
        self.o_pconv = dram_out("p_conv", [DEPTH, 3, 256])
        self.xs_d = dram_in("x_sample", [16, D])
        self.st_wkv = dram_in("state_wkv", [DEPTH, 16, 6, 64, 64])
        self.st_shift = dram_in("state_shift", [DEPTH, 16, D])
        self.st_hgrn = dram_in("state_hgrn", [DEPTH, 16, 6, 64, 64])
        self.st_lru = dram_in("state_lru", [DEPTH, 16, 256])
        self.st_conv = dram_in("state_conv", [DEPTH, 16, 3, 256])
        self.y_s = dram_out("y_sample", [16, D])
        self.o_swkv = dram_out("s_wkv", [DEPTH, 16, 6, 64, 64])
        self.o_sshift = dram_out("s_shift", [DEPTH, 16, D])
        self.o_shgrn = dram_out("s_hgrn", [DEPTH, 16, 6, 64, 64])
        self.o_slru = dram_out("s_lru", [DEPTH, 16, 256])
        self.o_sconv = dram_out("s_conv", [DEPTH, 16, 3, 256])

        with contextlib.ExitStack() as es:
            self.es = es
            self.sb_off = 0x4000 + 512
            sb = self.sb
            self.banks = [es.enter_context(nc.psum_tensor(f"bank{i}", [128, 512], F32)) for i in range(8)]
            self.bank_i = 0
            self.C = sb("cpk", [128, cp.n])
            S.add('sp', 'dma_start', dma='c0', out=self.C[:], in_=self.cpk)
            self.ident = self.C[:, cp.sl('ident')]
            self.ident_bf = sb("ident_bf", [128, 128], BF16)
            S.add('dve', 'tensor_copy', out=self.ident_bf[:], in_=self.ident)
            self.ones_bf = sb("ones_bf", [128, 128], BF16)
            S.add('dve', 'memset', ap=self.ones_bf[:], constant=1.0)
            self.bmean_bf = sb("bmean_bf", [128, 128], BF16)
            S.add('dve', 'memset', ap=self.bmean_bf[:], constant=0.0)
            S.add('dve', 'memset', ap=self.bmean_bf[0:64, 0:64], constant=1.0 / 64)
            S.add('dve', 'memset', ap=self.bmean_bf[64:128, 64:128], constant=1.0 / 64)
            self.bones_bf = sb("bones_bf", [128, 128], BF16)
            S.add('dve', 'memset', ap=self.bones_bf[:], constant=0.0)
            S.add('dve', 'memset', ap=self.bones_bf[0:64, 0:64], constant=1.0)
            S.add('dve', 'memset', ap=self.bones_bf[64:128, 64:128], constant=1.0)
            self.rmask = sb("rmask", [128, TT])
            S.add('dve', 'memset', ap=self.rmask[:], constant=1.0)
            S.add('dve', 'memset', ap=self.rmask[:].rearrange("p (c t) -> p c t", t=CH)[:, :, 0:1], constant=0.0)
            self.epsc = {}
            for v in (NORM_EPS, GN_EPS, 1e-12, 1.0):
                t = sb(self.nm("eps"), [128, 1])
                S.add('pool', 'memset', ap=t[:], constant=float(v))
                self.epsc[v] = t
            self.lb = sb("lb", [128, 3, DEPTH])
            self.oml = sb("oml", [128, 3, DEPTH])
            self.noml = sb("noml", [128, 3, DEPTH])
            self.hgrn_lb_setup()
            self.c8 = sb("c8", [128, DEPTH, 2])
            self.c16 = sb("c16", [128, DEPTH, 2])
            self.lru_setup()
            self.x = [sb(f"x{t}", [128, KD, TT]) for t in range(NTILE)]
            self.WAin = sb("WAin", [128, KD, C_IN], BF16)
            self.WAout = sb("WAout", [128, KD, D], BF16)
            self.SM = sb("SM", [128, 1024], BF16)
            self.shl = sb("shl", [128, DEPTH, KD])
            S.add('pool', 'memset', ap=self.shl[:], constant=0.0)
            self.Tst = [sb(f"Tst{l}", [128, 3, 64]) for l in range(DEPTH)]
            self.Sst = [sb(f"Sst{l}", [128, 3, 64]) for l in range(DEPTH)]
            self.hst = [sb(f"hst{l}", [128, 2, 1]) for l in range(DEPTH)]
            self.cvh = [sb(f"cvh{l}", [128, 2, 3]) for l in range(DEPTH)]
            self.shh = [sb(f"shh{l}", [128, 11, 1]) for l in range(DEPTH)]
            for l in range(DEPTH):
                for t in (self.Tst[l], self.Sst[l], self.hst[l], self.cvh[l], self.shh[l]):
                    S.add('pool', 'memset', ap=t[:], constant=0.0)
            self.xs = sb("xs", [128, KD, 16])
            self.work_base = self.sb_off
            self.pcfg = Cfg(TT, CH)
            self.scfg = Cfg(16, 1)
            self.smp = False

            self.load_weights_a(0)
            for ps in range(self.npass):
                self.alloc_io()
                self.load_x(ps)
                if self.do_smp and ps == self.npass - 1:
                    self.load_xs()
                S.barrier()
                for l in range(self.nlayers):
                    self.alloc_a()
                    for tt in range(NTILE):
                        last = (ps == self.npass - 1 and tt == NTILE - 1)
                        self.phase_a_tile(l, tt, self.pcfg, last)
                    if self.do_smp and ps == self.npass - 1:
                        self.sample_tile(l)
                    S.barrier()
                    nl, nps = (l + 1, ps) if l + 1 < self.nlayers else (0, ps + 1)
                    nxt = self.weight_loads_a(nl) if nps < self.npass else []
                    if self.do_mlp:
                        self.alloc_b()
                        self.cur_ps = ps
                        self.phase_b(l, nxt)
                        S.barrier()
                    else:
                        for t in nxt:
                            t()
                self.alloc_io()
                self.store_y(ps)
                if self.do_smp and ps == self.npass - 1:
                    self.store_ys()
                S.barrier()
            self.store_states()
            S.barrier()
            S.emit(es)
        return nc

    def mk(self, label):
        import os
        if os.environ.get('K_MARK'):
            print('MARK', label, len(self.S.ops))

    def interleave(self, chain_fns):
        S = self.S
        recs = []
        for fn in chain_fns:
            lst = []
            S.add = (lambda *a, _l=lst, **k: _l.append((a, k)))
            try:
                fn()
            finally:
                del S.add
            recs.append(lst)
        L = max(len(r) for r in recs)
        for w in range(L + len(recs) - 1):
            for j, r in enumerate(recs):
                si = w - j
                if 0 <= si < len(r):
                    a, k = r[si]
                    S.add(*a, **k)

    def wavefront(self, steps, nj):
        for w in range(len(steps) + nj - 1):
            for si in range(len(steps)):
                j = w - si
                if 0 <= j < nj:
                    steps[si](j)

    def nm(self, s):
        self.uid += 1
        return f"{s}_{self.uid}"

    def hgrn_lb_setup(self):
        S, cp = self.S, self.cp
        sb = self.sb
        e = sb("hl_e", [128, 3, DEPTH])
        ssum = sb("hl_s", [128, 3, 1])
        hl = self.C[:, cp.sl('hlb')].rearrange("p (j l) -> p j l", l=DEPTH)
        S.add('act', 'activation', out=e[:], in_=hl, func=AF.Exp)
        S.add('dve', 'tensor_reduce', out=ssum[:], in_=e[:], axis=mybir.AxisListType.X, op=ALU.add)
        S.add('dve', 'reciprocal', out=ssum[:], in_=ssum[:])
        S.add('dve', 'tensor_tensor', out=e[:], in0=e[:], in1=bc(ssum[:], [128, 3, DEPTH]), op=ALU.mult)
        S.add('dve', 'memset', ap=self.lb[:, :, 0:1], constant=0.0)
        for l in range(1, DEPTH):
            S.add('dve', 'tensor_tensor', out=self.lb[:, :, l:l + 1], in0=self.lb[:, :, l - 1:l], in1=e[:, :, l:l + 1], op=ALU.add)
        S.add('dve', 'tensor_scalar', out=self.oml[:], in0=self.lb[:], scalar1=-1.0, scalar2=1.0, op0=ALU.mult, op1=ALU.add)
        S.add('dve', 'tensor_scalar', out=self.noml[:], in0=self.lb[:], scalar1=-1.0, scalar2=None, op0=ALU.add)

    def lru_setup(self):
        S, cp = self.S, self.cp
        t = self.sb("lru_t", [128, DEPTH, 2])
        for l in range(DEPTH):
            S.add('act', 'activation', out=t[:, l, :], in_=self.C[:, cp.sl(f"lam{l}")], func=AF.Exp, scale=-1.0)
        S.add('act', 'activation', out=t[:], in_=t[:], func=AF.Ln, bias=self.epsc[1.0][:])
        S.add('dve', 'tensor_scalar', out=self.c8[:], in0=t[:], scalar1=-8.0, scalar2=None, op0=ALU.mult)
        S.add('dve', 'tensor_scalar', out=self.c16[:], in0=t[:], scalar1=-16.0, scalar2=None, op0=ALU.mult)

    def weight_loads_a(self, l):
        S = self.S
        src = self.w_in[l].rearrange("(k p) c -> p k c", p=128)
        q = C_IN // 4
        th = []
        for i in range(4):
            th.append(lambda i=i: S.add('pool', 'dma_start', dma=f'wain{i}', out=self.WAin[:, :, i * q:(i + 1) * q],
                                        in_=src[:, :, i * q:(i + 1) * q]))
        th.insert(1, lambda: S.add('pool', 'dma_start', dma='sm', out=self.SM[:], in_=self.smat[:, l, :]))
        th.append(lambda: S.add('pool', 'dma_start', dma='waout', out=self.WAout[:], in_=self.w_out[l].rearrange("(k p) c -> p k c", p=128)))
        return th

    def load_weights_a(self, l):
        for t in self.weight_loads_a(l):
            t()

    def alloc_a(self):
        self.sb_off = self.work_base
        sb = self.sb
        u = self.nm("a")
        N = TT
        self.xn = sb(f"xn{u}", [128, KD, N], BF16)
        self.nln = sb(f"nln{u}", [128, N])
        self.nrs = sb(f"nrs{u}", [128, N])
        self.Pr = sb(f"Pr{u}", [128, 11, N + 1])
        self.ds = [sb(f"ds{i}{u}", [128, N]) for i in range(2)]
        self.twxa = sb(f"twxa{u}", [128, N], BF16)
        self.sgg = sb(f"sgg{u}", [128, N], BF16)
        self.A = [sb(f"A{i}{u}", [128, 3, N]) for i in range(6)]
        self._offB0 = self.sb_off
        self.B = [sb(f"B{i}{u}", [128, 3, N], BF16) for i in range(6)]
        self.O = sb(f"O{u}", [128, 3, N])
        self.eM = sb(f"eM{u}", [128, 3, 16])
        self.eC = sb(f"eC{u}", [128, 3, 16])
        self.eCM = sb(f"eCM{u}", [128, 3, 16])
        self.cmid = sb(f"cmid{u}", [128, 3, 16])
        self.AkT = sb(f"AkT{u}", [128, 3, 64], BF16)
        self.TM2 = sb(f"TM2{u}", [128, 6, 64], BF16)
        self.T0f = sb(f"T0f{u}", [128, 3, 64])
        self.T0b = sb(f"T0b{u}", [128, 3, 64], BF16)
        self.mix = sb(f"mix{u}", [128, KD, N], BF16)
        self.nsq = self.mix
        self.xs32 = sb(f"xs32{u}", [128, KD, 16])
        self.P0 = sb(f"P0{u}", [128, 11, 16])
        self.shfm = sb(f"shfm{u}", [128, KD, 16], BF16)
        self._offR2 = self.sb_off
        _stm = sb(f"stm{u}", [16, D])
        self.stm = [_stm, _stm]
        self.Ts = [sb(f"Ts{i}{u}", [128, 3, 64]) for i in range(2)]
        self.Ss = [sb(f"Ss{i}{u}", [128, 3, 64]) for i in range(2)]
        _wkT = sb(f"wkT{u}", [64, 3, 128])
        self.wkT = [_wkT, _wkT]
        self.swi = [sb(f"swi{i}{u}", [64, 6, 64]) for i in range(2)]
        self.CV = sb(f"CV{u}", [128, 6, 16])
        self.H0 = sb(f"H0{u}", [128, 2, 16])
        self.cvtm = sb(f"cvtm{u}", [16, 768])
        self.cvo = sb(f"cvo{u}", [16, 256])
        self.lrtm = sb(f"lrtm{u}", [16, 256])
        self.lro = sb(f"lro{u}", [16, 256])
        self.XBs = sb(f"XBs{u}", [128, 2, 16])
        end_off = self.sb_off
        lim = 0
        self.R = {}
        for nmx, shp in [('ZY0', [128, 2, 3, 64]), ('ZY1', [128, 2, 3, 64]), ('NT', [128, 3, 64]), ('AbT', [128, 3, 64]),
                         ('AkT', [128, 3, 64]), ('PP0', [128, 3, 64]), ('PP1', [128, 3, 64]), ('TM2', [128, 6, 64]),
                         ('TMK', [128, 3, 64]), ('Xb', [128, 3, 64]), ('Ub', [128, 3, 64])]:
            self.R[nmx] = sb(f"R{nmx}{u}", shp)
        self.sb_off = max(self.sb_off, end_off)
        end2 = self.sb_off
        self.sb_off = self._offR2
        self.R2 = dict(self.R)
        for nmx, shp in [('ZY0', [128, 2, 3, 64]), ('ZY1', [128, 2, 3, 64]), ('NT', [128, 3, 64]), ('AbT', [128, 3, 64]),
                         ('AkT', [128, 3, 64]), ('PP0', [128, 3, 64]), ('PP1', [128, 3, 64]), ('TM2', [128, 6, 64]),
                         ('TMK', [128, 3, 64])]:
            self.R2[nmx] = sb(f"R2{nmx}{u}", shp)
        self.HB2 = {'TM2': sb(f"HB2TM2{u}", [128, 6, 64], BF16), 'AkT': sb(f"HB2AkT{u}", [128, 3, 64], BF16)}
        assert self.sb_off <= self._offR2 + 4096 + 4 * 768 + 1536 + 2 * 1536, self.sb_off - self._offR2
        self.sb_off = end2
        if not hasattr(self, '_pa'):
            self._pa = True
            print("phase A work bytes", self.sb_off - self.work_base, "limit", 0x4000 + 212000 - self.work_base)

    def alloc_io(self):
        self.sb_off = self.work_base
        sb = self.sb
        u = self.nm("io")
        N = TT
        self.nsq = sb(f"nsq{u}", [128, KD, N], BF16)
        self.nln = sb(f"nln{u}", [128, N])
        self.nrs = sb(f"nrs{u}", [128, N])
        self.xtm = [sb(f"xtm{i}{u}", [128, D]) for i in range(2)]
        self.yfm = sb(f"yfm{u}", [128, KD, N])
        self.xs32 = sb(f"xs32{u}", [128, KD, 16])
        self.stm = [sb(f"stm{i}{u}", [16, D]) for i in range(2)]

    def load_x(self, ps):
        S = self.S
        for tb in range(TPP // 128):
            buf = self.xtm[tb % 2]
            t0 = ps * TPP + tb * 128
            S.add('sp', 'dma_start', dma=f'xtm{tb % 2}', out=buf[:], in_=self.xp[t0:t0 + 128, :])
            xt = self.x[(tb * 128) // TT]
            tl = (tb * 128) % TT
            for kh in range(2):
                bk = self.bank()
                for kk in range(4):
                    k = kh * 4 + kk
                    S.add('pe', 'transpose', out=bk[:, kk * 128:(kk + 1) * 128],
                          in_=buf[:, k * 128:(k + 1) * 128], identity=self.ident)
                src = bk[:].rearrange("p (k t) -> p k t", k=4)
                dst = xt[:, kh * 4:(kh + 1) * 4, tl:tl + 128]
                if kh == 0:
                    S.add('act', 'activation', out=dst, in_=src, func=AF.Copy)
                else:
                    S.add('dve', 'tensor_copy', out=dst, in_=src)

    def rmsnorm(self, xt, g, out, N, want_last=None):
        S = self.S
        S.add('act', 'activation', out=self.nsq[:, :, :N], in_=xt[:, :, :N], func=AF.Square)
        bk = self.bank()
        for k in range(KD):
            S.add('pe', 'matmul', out=bk[:, :N], lhsT=self.ones_bf[:], rhs=self.nsq[:, k, :N], start=(k == 0), stop=(k == KD - 1))
        S.add('act', 'activation', out=self.nln[:, :N], in_=bk[:, :N], func=AF.Ln, scale=1.0 / D, bias=self.epsc[NORM_EPS][:])
        S.add('act', 'activation', out=self.nrs[:, :N], in_=self.nln[:, :N], func=AF.Exp, scale=-0.5)
        for k in range(KD):
            S.add('dve', 'scalar_tensor_tensor', out=out[:, k, :N], in0=xt[:, k, :N], scalar=g[:, k:k + 1], in1=self.nrs[:, :N],
                  op0=ALU.mult, op1=ALU.mult)
        if want_last is not None:
            S.add('dve', 'tensor_tensor', out=want_last, in0=xt[:, :, N - 1], in1=g, op=ALU.mult)
            S.add('dve', 'tensor_scalar', out=want_last, in0=want_last, scalar1=self.nrs[:, N - 1:N], scalar2=None, op0=ALU.mult)

    def project(self, cols, N, evac, rhs=None):
        S = self.S
        for i in range(0, len(cols), 2):
            pair = cols[i:i + 2]
            bk = self.bank()
            for idx, c in enumerate(pair):
                for k in range(KD):
                    S.add('pe', 'matmul', out=bk[:, idx * 256: idx * 256 + N], lhsT=self.WAin[:, k, c * 128:(c + 1) * 128],
                          rhs=(self.xn if rhs is None else rhs)[:, k, :N], start=(k == 0), stop=(k == KD - 1))
            evac(bk, pair)

    def phase_a_tile(self, l, tt, cfg, last):
        S, cp = self.S, self.cp
        N = cfg.n
        xt = self.x[tt]
        self.rmsnorm(xt, self.C[:, cp.sl(f"g1{l}")], self.xn, N, want_last=self.shl[:, l, :] if last else None)
        import os
        mixs = os.environ.get("K_MIX", "rhl")
        if mixs != "rhl":
            S.add('dve', 'memset', ap=self.mix[:], constant=0.0)
        if 'r' in mixs:
            self.rwkv(l, cfg)
        if 'h' in mixs:
            self.hgrn(l, cfg)
        if 'l' in mixs:
            self.lru(l, cfg, last)
        for i in range(0, KD, 2):
            bk = self.bank()
            for idx in range(2):
                c = i + idx
                for k in range(KD):
                    S.add('pe', 'matmul', out=bk[:, idx * 256: idx * 256 + N], lhsT=self.WAout[:, k, c * 128:(c + 1) * 128],
                          rhs=self.mix[:, k, :N], start=(k == 0), stop=(k == KD - 1))
            S.add('dve', 'tensor_tensor', out=xt[:, i:i + 2, :N], in0=xt[:, i:i + 2, :N],
                  in1=bk[:].rearrange("p (c t) -> p c t", c=2)[:, :, :N], op=ALU.add)

    def tm2fm(self, src, nblk, dst, eng='act'):
        S = self.S
        bk = self.bank()
        for b in range(nblk):
            S.add('pe', 'transpose', out=bk[:, b * 16:(b + 1) * 16], in_=src[:, b * 128:(b + 1) * 128], identity=self.ident[0:16, 0:16])
        srcv = bk[:, 0:nblk * 16].rearrange("p (b t) -> p b t", t=16)
        if eng == 'act':
            S.add('act', 'activation', out=dst, in_=srcv, func=AF.Copy)
        else:
            S.add('dve', 'tensor_copy', out=dst, in_=srcv)

    def fm2tm(self, src, nblk, dst):
        S = self.S
        for b0 in range(0, nblk, 4):
            nb = min(4, nblk - b0)
            bk = self.bank()
            for b in range(nb):
                S.add('pe', 'transpose', out=bk[0:16, b * 128:(b + 1) * 128], in_=src(b0 + b), identity=self.ident)
            S.add('act', 'activation', out=dst[:, b0 * 128:(b0 + nb) * 128], in_=bk[0:16, 0:nb * 128], func=AF.Copy)

    def load_xs(self):
        S = self.S
        S.add('sp', 'dma_start', dma='stm0', out=self.stm[0][:], in_=self.xs_d)
        self.tm2fm(self.stm[0], KD, self.xs[:])

    def store_ys(self):
        S, cp = self.S, self.cp
        self.rmsnorm(self.xs, self.C[:, cp.sl('fg')], self.xs32, 16)
        self.fm2tm(lambda b: self.xs32[:, b, :], KD, self.stm[0])
        S.add('sp', 'dma_start', dma='stm0', out=self.y_s, in_=self.stm[0][:])

    def sample_tile(self, l):
        S, cp = self.S, self.cp
        cfg = self.scfg
        N = 16
        S.barrier()
        self.smp = True
        self.rmsnorm(self.xs, self.C[:, cp.sl(f"g1{l}")], self.xs32, N)
        S.add('act', 'activation', out=self.xn[:, :, :N], in_=self.xs32[:], func=AF.Copy)
        self.fm2tm(lambda b: self.xs32[:, b, :], KD, self.stm[1])
        S.add('sp', 'dma_start', dma='stm0o', out=self.o_sshift[l], in_=self.stm[1][:])
        S.add('sp', 'dma_start', dma='stm0', out=self.stm[0][:], in_=self.st_shift[l])
        self.tm2fm(self.stm[0], KD, self.shfm[:])
        S.add('sp', 'dma_start', dma='cvtm', out=self.cvtm[:], in_=self.st_conv[l].rearrange("b j c -> b (j c)"))
        self.tm2fm(self.cvtm, 6, self.CV[:], eng='dve')
        S.add('sp', 'dma_start', dma='lrtm', out=self.lrtm[:], in_=self.st_lru[l])
        self.tm2fm(self.lrtm, 2, self.H0[:], eng='dve')
        self.rwkv(l, cfg)
        self.hgrn(l, cfg)
        self.lru(l, cfg, False)
        S.add('sp', 'dma_start', dma='cvo0', out=self.o_sconv[l][:, 0:2, :].rearrange("b j c -> b (j c)"), in_=self.cvtm[:, 256:768])
        self.fm2tm(lambda b: self.XBs[:, b, :], 2, self.cvo)
        S.add('sp', 'dma_start', dma='cvo1', out=self.o_sconv[l][:, 2, :], in_=self.cvo[:])
        self.fm2tm(lambda b: self.H0[:, b, :], 2, self.lro)
        S.add('sp', 'dma_start', dma='lro', out=self.o_slru[l], in_=self.lro[:])
        for i in range(0, KD, 2):
            bk = self.bank()
            for idx in range(2):
                c = i + idx
                for k in range(KD):
                    S.add('pe', 'matmul', out=bk[:, idx * 256: idx * 256 + N], lhsT=self.WAout[:, k, c * 128:(c + 1) * 128],
                          rhs=self.mix[:, k, :N], start=(k == 0), stop=(k == KD - 1))
            S.add('dve', 'tensor_tensor', out=self.xs[:, i:i + 2, :N], in0=self.xs[:, i:i + 2, :N],
                  in1=bk[:].rearrange("p (c t) -> p c t", c=2)[:, :, :N], op=ALU.add)
        self.smp = False

    def rwkv(self, l, cfg):
        S, cp = self.S, self.cp
        N, CHL, NCHK = cfg.n, cfg.ch, cfg.nch
        Pr = self.Pr
        C = self.C
        sg, a_, kap, t1, E, bonus = self.A
        rt, kt, bt, kkt, vbf, rkbf = self.B
        cs = lambda nm: C[:, cp.sl(f"{nm}{l}")]
        smp = self.smp
        if not smp:
            S.add('act', 'activation', out=Pr[:, 0:11, 0:1], in_=self.shh[l][:], func=AF.Copy)

        def evac(bk, pair):
            S.add('act', 'activation', out=Pr[:, pair[0]:pair[0] + len(pair), 1:N + 1],
                  in_=bk[:].rearrange("p (c t) -> p c t", c=2)[:, 0:len(pair), :N], func=AF.Copy)
        self.project(list(range(11)), N, evac)
        if smp:
            def evac0(bk, pair):
                S.add('act', 'activation', out=self.P0[:, pair[0]:pair[0] + len(pair), :N],
                      in_=bk[:].rearrange("p (c t) -> p c t", c=2)[:, 0:len(pair), :N], func=AF.Copy)
            self.project(list(range(11)), N, evac0, rhs=self.shfm)
        else:
            S.add('act', 'activation', out=self.shh[l][:], in_=Pr[:, 0:11, N:N + 1], func=AF.Copy)
        self.mk('r_after_proj')
        mu = cs('mu')
        for c in range(11):
            d = self.ds[c % 2]
            S.add('pool', 'tensor_tensor', out=d[:, :N], in0=(self.P0[:, c, :N] if smp else Pr[:, c, 0:N]), in1=Pr[:, c, 1:N + 1], op=ALU.subtract)
            S.add('dve', 'scalar_tensor_tensor', out=Pr[:, c, 1:N + 1], in0=d[:, :N], scalar=mu[:, c:c + 1], in1=Pr[:, c, 1:N + 1],
                  op0=ALU.mult, op1=ALU.add)
        r = Pr[:, 0:3, 1:N + 1]
        k = Pr[:, 3:6, 1:N + 1]
        v = Pr[:, 6:9, 1:N + 1]
        self.mk('r_after_pm')
        S.add('act', 'activation', out=self.twxa[0:64, :N], in_=Pr[0:64, 9, 1:N + 1], func=AF.Tanh)
        S.add('act', 'activation', out=self.twxa[64:128, :N], in_=Pr[64:128, 9, 1:N + 1], func=AF.Copy)
        S.add('act', 'activation', out=self.sgg[:, :N], in_=Pr[:, 10, 1:N + 1], func=AF.Sigmoid)
        c3 = lambda t, j: t[:, j, :N].rearrange("p (c t) -> p c t", t=CHL)
        rj = lambda j: Pr[:, j, 1:N + 1]
        kj = lambda j: Pr[:, 3 + j, 1:N + 1]
        vj = lambda j: Pr[:, 6 + j, 1:N + 1]
        st = {}

        def s_lora(j):
            st[('bk', j)] = self.bank()
            st[('bk2', j)] = self.bank()
            S.add('pe', 'matmul', out=st[('bk', j)][:, 0:N], lhsT=self.SM[0:64, j * 128:(j + 1) * 128], rhs=self.twxa[0:64, :N], start=True, stop=True)
            S.add('pe', 'matmul', out=st[('bk2', j)][:, 0:N], lhsT=self.SM[64:128, j * 128:(j + 1) * 128], rhs=self.twxa[64:128, :N], start=True, stop=True)

        def s_bones1(j):
            st[('b3', j)] = self.bank()
            S.add('pe', 'matmul', out=st[('b3', j)][:, 0:N], lhsT=self.bones_bf[:], rhs=rkbf[:, j, :N], start=True, stop=True)

        def s_bones2(j):
            st[('b4', j)] = self.bank()
            S.add('pe', 'matmul', out=st[('b4', j)][:, 0:N], lhsT=self.bones_bf[:], rhs=rkbf[:, j, :N], start=True, stop=True)

        def s_scan(j):
            if CHL > 1:
                S.add('dve', 'tensor_tensor_scan', out=t1[:, j, :N], data0=self.rmask[:, :N], data1=sg[:, j, :N], initial=0.0, op0=ALU.mult, op1=ALU.add)
            else:
                S.add('dve', 'tensor_copy', out=t1[:, j, :N], in_=sg[:, j, :N])

        steps = [
            s_lora,
            lambda j: S.add('act', 'activation', out=sg[:, j, :N], in_=st[('bk', j)][:, 0:N], func=AF.Sigmoid, bias=cs('w0')[:, j:j + 1]),
            lambda j: S.add('act', 'activation', out=a_[:, j, :N], in_=st[('bk2', j)][:, 0:N], func=AF.Sigmoid, bias=cs('a0')[:, j:j + 1]),
            lambda j: S.add('dve', 'tensor_scalar', out=kap[:, j, :N], in0=kj(j), scalar1=cs('kk')[:, j:j + 1], scalar2=None, op0=ALU.mult),
            lambda j: S.add('act', 'activation', out=rkbf[:, j, :N], in_=kap[:, j, :N], func=AF.Square),
            s_bones1,
            lambda j: S.add('act', 'activation', out=t1[:, j, :N], in_=st[('b3', j)][:, 0:N], func=AF.Ln, bias=self.epsc[1e-12][:]),
            lambda j: S.add('act', 'activation', out=t1[:, j, :N], in_=t1[:, j, :N], func=AF.Exp, scale=-0.5),
            lambda j: S.add('dve', 'tensor_tensor', out=kap[:, j, :N], in0=kap[:, j, :N], in1=t1[:, j, :N], op=ALU.mult),
            lambda j: S.add('dve', 'tensor_scalar', out=t1[:, j, :N], in0=a_[:, j, :N], scalar1=-1.0, scalar2=cs('ka')[:, j:j + 1], op0=ALU.add, op1=ALU.mult),
            lambda j: S.add('dve', 'scalar_tensor_tensor', out=kj(j), in0=t1[:, j, :N], scalar=1.0, in1=kj(j), op0=ALU.add, op1=ALU.mult),
            lambda j: S.add('dve', 'tensor_tensor', out=a_[:, j, :N], in0=a_[:, j, :N], in1=kap[:, j, :N], op=ALU.mult),
            lambda j: S.add('dve', 'scalar_tensor_tensor', out=rkbf[:, j, :N], in0=rj(j), scalar=cs('rk')[:, j:j + 1], in1=kj(j), op0=ALU.mult, op1=ALU.mult),
            s_bones2,
            lambda j: S.add('dve', 'tensor_tensor', out=bonus[:, j, :N], in0=vj(j), in1=st[('b4', j)][:, 0:N], op=ALU.mult),
            s_scan,
            lambda j: S.add('act', 'activation', out=self.eM[:, j, :NCHK], in_=c3(t1, j)[:, :, cfg.mid], func=AF.Exp, scale=-C0),
            lambda j: S.add('act', 'activation', out=self.eC[:, j, :NCHK], in_=c3(t1, j)[:, :, CHL - 1], func=AF.Exp, scale=-C0),
            lambda j: S.add('act', 'activation', out=self.cmid[:, j, :NCHK], in_=c3(t1, j)[:, :, cfg.mid], func=AF.Copy),
            lambda j: S.add('dve', 'tensor_tensor', out=c3(t1, j), in0=c3(t1, j), in1=bc(self.cmid[:, j, :NCHK].unsqueeze(2), [128, NCHK, CHL]), op=ALU.subtract),
            lambda j: S.add('act', 'activation', out=E[:, j, :N], in_=t1[:, j, :N], func=AF.Exp, scale=-C0),
            lambda j: S.add('dve', 'tensor_tensor', out=rj(j), in0=rj(j), in1=E[:, j, :N], op=ALU.mult),
            lambda j: S.add('act', 'activation', out=self.eCM[:, j, :NCHK], in_=c3(E, j)[:, :, CHL - 1], func=AF.Copy),
            lambda j: S.add('act', 'activation', out=E[:, j, :N], in_=t1[:, j, :N], func=AF.Exp, scale=C0),
            lambda j: S.add('dve', 'tensor_tensor', out=a_[:, j, :N], in0=a_[:, j, :N], in1=E[:, j, :N], op=ALU.mult),
            lambda j: S.add('dve', 'tensor_tensor', out=kj(j), in0=kj(j), in1=E[:, j, :N], op=ALU.mult),
            lambda j: S.add('dve', 'tensor_tensor', out=t1[:, j, :N], in0=t1[:, j, :N], in1=sg[:, j, :N], op=ALU.subtract),
            lambda j: S.add('act', 'activation', out=E[:, j, :N], in_=t1[:, j, :N], func=AF.Exp, scale=-C0),
            lambda j: S.add('dve', 'tensor_tensor', out=kap[:, j, :N], in0=kap[:, j, :N], in1=E[:, j, :N], op=ALU.mult),
        ]
        self.wavefront(steps, 3)
        self.mk('r_pre_chunk')
        T = self.Tst[l]
        PS = [slice(0, 128)] if CHL == 64 else [slice(0, CHL), slice(64, 64 + CHL)]
        msun = C[:, cp.sl('msun')]
        msln = C[:, cp.sl('msln')]
        msu = C[:, cp.sl('msu')]
        miu = C[:, cp.sl('miu')]
        id2 = C[:, cp.sl('id2')]
        npz = lambda ps: ps.stop - ps.start
        m3 = lambda m, ps: bc(m[ps, 0:CHL].unsqueeze(1), [npz(ps), 3, CHL])
        hv = lambda bk, ps: bk[ps, 0:192].rearrange("p (j t) -> p j t", j=3)[:, :, 0:CHL]
        sv = lambda t, ps: t[ps, :, 0:CHL]
        hs = lambda hh: slice(hh * 64, hh * 64 + 64)
        ph = lambda hh: slice(hh * 64, hh * 64 + CHL)
        HEADS = [(j, hh) for hh in range(2) for j in range(3)]
        def chunk_gen(ch, R):
            T = self.Tst[l]
            ts = slice(ch * CHL, (ch + 1) * CHL)
            if smp:
                T = self.Ts[ch % 2]
                swi = self.swi[ch % 2]
                if ch == 0:
                    S.add('sp', 'dma_start', dma='swi0', out=swi[:], in_=self.st_wkv[l, 0].rearrange("h v k -> v h k"))
                if ch + 1 < NCHK:
                    S.add('sp', 'dma_start', dma=f'swi{(ch + 1) % 2}', out=self.swi[(ch + 1) % 2][:],
                          in_=self.st_wkv[l, ch + 1].rearrange("h v k -> v h k"))
                bki = self.bank()
                for j in range(3):
                    S.add('pe', 'transpose', out=bki[:, j * 64:(j + 1) * 64], in_=swi[:, 2 * j:2 * j + 2, :], identity=self.ident[0:64, 0:64])
                S.add('act', 'activation', out=T[:], in_=bki[:, 0:192].rearrange("p (j v) -> p j v", j=3), func=AF.Copy)
            bkA, bkB = self.bank(), self.bank()
            for j in range(3):
                for hh in range(2):
                    S.add('pe', 'matmul', out=bkA[ph(hh), j * 64:(j + 1) * 64], lhsT=Pr[:, 6 + j, 1 + ch * CHL:1 + (ch + 1) * CHL], rhs=self.ident[:, hs(hh)], start=True, stop=True)
                    S.add('pe', 'matmul', out=bkA[ph(hh), 192 + j * 64:192 + (j + 1) * 64], lhsT=a_[:, j, ts], rhs=self.ident[:, hs(hh)], start=True, stop=True)
                    S.add('pe', 'matmul', out=bkB[ph(hh), j * 64:(j + 1) * 64], lhsT=Pr[:, 3 + j, 1 + ch * CHL:1 + (ch + 1) * CHL], rhs=self.ident[:, hs(hh)], start=True, stop=True)
            for ps in PS:
                S.add('act', 'activation', out=R['TM2'][ps], in_=bkA[ps, 0:384].rearrange("p (q f) -> p q f", f=64), func=AF.Copy)
                S.add('dve', 'tensor_copy', out=R['TMK'][ps], in_=bkB[ps, 0:192].rearrange("p (q f) -> p q f", f=64))
            Vtm = lambda j, hh: R['TM2'][ph(hh), j, :]
            Btm = lambda j, hh: R['TM2'][ph(hh), 3 + j, :]
            Ktm = lambda j, hh: R['TMK'][ph(hh), j, :]
            ts1 = slice(1 + ch * CHL, 1 + (ch + 1) * CHL)
            FMSRC = {'rt': lambda p, j: Pr[p, j, ts1], 'kkt': lambda p, j: Pr[p, 3 + j, ts1], 'v': lambda p, j: Pr[p, 6 + j, ts1],
                     'bt': lambda p, j: a_[p, j, ts], 'kt': lambda p, j: kap[p, j, ts]}
            fm = lambda t, j, hh: FMSRC[t](hs(hh), j)
            cc = lambda t, j, hh: t[ph(hh), j, 0:CHL]
            pc = lambda bk, j, hh: bk[ph(hh), j * 64:j * 64 + CHL]
            self.mk('r_after_tm')
            yield 0
            ONE = (CHL == 1)
            if ONE:
                pab, pak = self.bank(), self.bank()
            else:
                pz, py, pn, pab, pak = [self.bank() for _ in range(5)]
            for (j, hh) in HEADS:
                if not ONE:
                    S.add('pe', 'matmul', out=pc(pz, j, hh), lhsT=fm('bt', j, hh), rhs=fm('kt', j, hh), start=True, stop=True)
                    S.add('pe', 'matmul', out=pc(py, j, hh), lhsT=fm('kt', j, hh), rhs=fm('bt', j, hh), start=True, stop=True)
                    S.add('pe', 'matmul', out=pc(pn, j, hh), lhsT=fm('kkt', j, hh), rhs=fm('kt', j, hh), start=True, stop=True)
                S.add('pe', 'matmul', out=pc(pab, j, hh), lhsT=fm('bt', j, hh), rhs=fm('rt', j, hh), start=True, stop=True)
                S.add('pe', 'matmul', out=pc(pak, j, hh), lhsT=fm('kkt', j, hh), rhs=fm('rt', j, hh), start=True, stop=True)
            zy = R['ZY0']
            Zt = lambda z: z[:, 0, :, :]
            Yt = lambda z: z[:, 1, :, :]
            P = R['PP0']
            for ps in PS:
                if not ONE:
                    S.add('dve', 'tensor_tensor', out=sv(Zt(zy), ps), in0=hv(pz, ps), in1=m3(msun, ps), op=ALU.mult)
                    S.add('dve', 'tensor_tensor', out=sv(Yt(zy), ps), in0=hv(py, ps), in1=m3(msln, ps), op=ALU.mult)
                    S.add('dve', 'tensor_tensor', out=sv(R['NT'], ps), in0=hv(pn, ps), in1=m3(msu, ps), op=ALU.mult)
                S.add('dve', 'tensor_tensor', out=sv(R['AbT'], ps), in0=hv(pab, ps), in1=m3(miu, ps), op=ALU.mult)
                S.add('dve', 'tensor_tensor', out=sv(R['AkT'], ps), in0=hv(pak, ps), in1=m3(miu, ps), op=ALU.mult)
                if not ONE:
                    S.add('dve', 'tensor_tensor', out=sv(P, ps), in0=sv(Zt(zy), ps), in1=m3(id2, ps), op=ALU.add)
            self.mk('r_after_masks')
            yield 0
            cur = 0
            for lv in range(1, cfg.lv):
                zn = R['ZY%d' % (1 - cur)]
                zo = R['ZY%d' % cur]
                pyb = self.bank()
                for (j, hh) in HEADS:
                    S.add('pe', 'matmul', out=pc(pyb, j, hh), lhsT=cc(Zt(zo), j, hh), rhs=cc(Yt(zo), j, hh), start=True, stop=True)
                for ps in PS:
                    S.add('act', 'activation', out=sv(Yt(zn), ps), in_=hv(pyb, ps), func=AF.Copy)
                if lv < cfg.lv - 1:
                    pzb = self.bank()
                    for (j, hh) in HEADS:
                        S.add('pe', 'matmul', out=pc(pzb, j, hh), lhsT=cc(Yt(zo), j, hh), rhs=cc(Zt(zo), j, hh), start=True, stop=True)
                    for ps in PS:
                        S.add('act', 'activation', out=sv(Zt(zn), ps), in_=hv(pzb, ps), func=AF.Copy)
                yield 0
                ppb = self.bank()
                Pn = R['PP1'] if P is R['PP0'] else R['PP0']
                for (j, hh) in HEADS:
                    S.add('pe', 'matmul', out=pc(ppb, j, hh), lhsT=cc(Yt(zn), j, hh), rhs=cc(P, j, hh), start=True, stop=True)
                for ps in PS:
                    S.add('dve', 'tensor_tensor', out=sv(Pn, ps), in0=hv(ppb, ps), in1=sv(P, ps), op=ALU.add)
                P = Pn
                cur = 1 - cur
                yield 0
            self.mk('r_after_inv')
            yield 'SD'
            S.add('dve', 'tensor_tensor', out=self.T0f[:], in0=T[:], in1=bc(self.eM[:, :, ch:ch + 1], [128, 3, 64]), op=ALU.mult)
            T0h = lambda j, hh: self.T0f[hs(hh), j, :]
            pv = lambda bk, j, hh: bk[ph(hh), j * 64:(j + 1) * 64]
            px = self.bank()
            for (j, hh) in HEADS:
                if not ONE:
                    S.add('pe', 'matmul', out=pv(px, j, hh), lhsT=cc(R['NT'], j, hh), rhs=Vtm(j, hh), start=True, stop=False)
                S.add('pe', 'matmul', out=pv(px, j, hh), lhsT=fm('kt', j, hh), rhs=T0h(j, hh), start=ONE, stop=True)
            if ONE:
                for ps in PS:
                    S.add('act', 'activation', out=R['Ub'][ps], in_=px[ps, 0:192].rearrange("p (j v) -> p j v", j=3), func=AF.Copy, scale=-1.0)
            else:
                for ps in PS:
                    S.add('act', 'activation', out=R['Xb'][ps], in_=px[ps, 0:192].rearrange("p (j v) -> p j v", j=3), func=AF.Copy)
                self.mk('r_after_x')
                yield 1
                pu = self.bank()
                for (j, hh) in HEADS:
                    S.add('pe', 'matmul', out=pv(pu, j, hh), lhsT=cc(P, j, hh), rhs=R['Xb'][ph(hh), j, :], start=True, stop=True)
                for ps in PS:
                    S.add('act', 'activation', out=R['Ub'][ps], in_=pu[ps, 0:192].rearrange("p (j v) -> p j v", j=3), func=AF.Copy, scale=-1.0)
            self.mk('r_after_u')
            yield 1
            po = self.bank()
            if ONE:
                po2 = self.bank()
                for (j, hh) in HEADS:
                    S.add('pe', 'matmul', out=po[hs(hh), j * CHL:(j + 1) * CHL], lhsT=T0h(j, hh), rhs=fm('rt', j, hh), start=True, stop=True)
                for (j, hh) in HEADS:
                    o_ap = po2[hs(hh), j * CHL:(j + 1) * CHL]
                    S.add('pe', 'matmul', out=o_ap, lhsT=R['Ub'][ph(hh), j, :], rhs=cc(R['AbT'], j, hh), start=True, stop=False)
                    S.add('pe', 'matmul', out=o_ap, lhsT=Vtm(j, hh), rhs=cc(R['AkT'], j, hh), start=False, stop=True)
                S.add('act', 'activation', out=self.O[:, :, ts], in_=po[:, 0:3 * CHL].rearrange("p (j t) -> p j t", j=3), func=AF.Copy)
                S.add('dve', 'tensor_tensor', out=self.O[:, :, ts], in0=self.O[:, :, ts], in1=po2[:, 0:3 * CHL].rearrange("p (j t) -> p j t", j=3), op=ALU.add)
            else:
                for (j, hh) in HEADS:
                    o_ap = po[hs(hh), j * CHL:(j + 1) * CHL]
                    S.add('pe', 'matmul', out=o_ap, lhsT=T0h(j, hh), rhs=fm('rt', j, hh), start=True, stop=False)
                    S.add('pe', 'matmul', out=o_ap, lhsT=R['Ub'][ph(hh), j, :], rhs=cc(R['AbT'], j, hh), start=False, stop=False)
                    S.add('pe', 'matmul', out=o_ap, lhsT=Vtm(j, hh), rhs=cc(R['AkT'], j, hh), start=False, stop=True)
                S.add('act', 'activation', out=self.O[:, :, ts], in_=po[:, 0:3 * CHL].rearrange("p (j t) -> p j t", j=3), func=AF.Copy)
            self.mk('r_after_o')
            yield 1
            pt = self.bank()
            for (j, hh) in HEADS:
                t_ap = pt[hs(hh), j * 64:(j + 1) * 64]
                S.add('pe', 'matmul', out=t_ap, lhsT=Btm(j, hh), rhs=R['Ub'][ph(hh), j, :], start=True, stop=False)
                S.add('pe', 'matmul', out=t_ap, lhsT=Ktm(j, hh), rhs=Vtm(j, hh), start=False, stop=True)
            S.add('dve', 'tensor_tensor', out=self.T0f[:], in0=self.T0f[:], in1=pt[:, 0:192].rearrange("p (j v) -> p j v", j=3), op=ALU.add)
            S.add('dve', 'tensor_tensor', out=T[:], in0=self.T0f[:], in1=bc(self.eCM[:, :, ch:ch + 1], [128, 3, 64]), op=ALU.mult)
            if smp:
                wk = self.wkT[ch % 2]
                bko = self.bank()
                for j in range(3):
                    S.add('pe', 'transpose', out=bko[0:64, j * 128:(j + 1) * 128], in_=T[:, j, :], identity=self.ident)
                S.add('act', 'activation', out=wk[:], in_=bko[0:64, 0:384].rearrange("p (j f) -> p j f", j=3), func=AF.Copy)
                S.add('sp', 'dma_start', dma='swo0', out=self.o_swkv[l, ch].rearrange("h v k -> v h k"),
                      in_=wk[:].rearrange("p j (hh k) -> p (j hh) k", hh=2))

        if CHL == 64 and not smp:
            gens = [chunk_gen(ch, self.R if ch % 2 == 0 else self.R2) for ch in range(NCHK)]
            while next(gens[0]) != 'SD':
                pass
            for ch in range(NCHK):
                cur = gens[ch]
                nxt = gens[ch + 1] if ch + 1 < NCHK else None
                cur_done, nxt_done = False, nxt is None
                while not (cur_done and nxt_done):
                    if not nxt_done:
                        if next(nxt) == 'SD':
                            nxt_done = True
                    if not cur_done:
                        try:
                            next(cur)
                        except StopIteration:
                            cur_done = True
        else:
            for ch in range(NCHK):
                for _ in chunk_gen(ch, self.R):
                    pass
        self.mk('r_post')
        O = self.O
        pst = {}

        def p_mm(key, rhs_fn, lhsT):
            def f(j):
                pst[(key, j)] = self.bank()
                S.add('pe', 'matmul', out=pst[(key, j)][:, 0:N], lhsT=lhsT(j), rhs=rhs_fn(j), start=True, stop=True)
            return f
        psteps = [
            lambda j: S.add('act', 'activation', out=rkbf[:, j, :N], in_=O[:, j, :N], func=AF.Copy),
            p_mm('m', lambda j: rkbf[:, j, :N], lambda j: self.bmean_bf[:]),
            lambda j: S.add('dve', 'tensor_tensor', out=O[:, j, :N], in0=O[:, j, :N], in1=pst[('m', j)][:, 0:N], op=ALU.subtract),
            lambda j: S.add('act', 'activation', out=rkbf[:, j, :N], in_=O[:, j, :N], func=AF.Square),
            p_mm('v', lambda j: rkbf[:, j, :N], lambda j: self.bmean_bf[:]),
            lambda j: S.add('act', 'activation', out=t1[:, j, :N], in_=pst[('v', j)][:, 0:N], func=AF.Ln, bias=self.epsc[GN_EPS][:]),
            lambda j: S.add('act', 'activation', out=t1[:, j, :N], in_=t1[:, j, :N], func=AF.Exp, scale=-0.5),
            lambda j: S.add('dve', 'tensor_tensor', out=O[:, j, :N], in0=O[:, j, :N], in1=t1[:, j, :N], op=ALU.mult),
            lambda j: S.add('dve', 'tensor_scalar', out=O[:, j, :N], in0=O[:, j, :N], scalar1=cs('lnw')[:, j:j + 1], scalar2=cs('lnb')[:, j:j + 1],
                            op0=ALU.mult, op1=ALU.add),
            lambda j: S.add('dve', 'tensor_tensor', out=O[:, j, :N], in0=O[:, j, :N], in1=bonus[:, j, :N], op=ALU.add),
            p_mm('g', lambda j: self.sgg[:, :N], lambda j: self.SM[:, 384 + j * 128:384 + (j + 1) * 128]),
            lambda j: S.add('dve', 'tensor_tensor', out=self.mix[:, j, :N], in0=O[:, j, :N], in1=pst[('g', j)][:, 0:N], op=ALU.mult),
        ]
        self.wavefront(psteps, 3)

    def hgrn(self, l, cfg):
        S, cp = self.S, self.cp
        N, CHL, NCHK = cfg.n, cfg.ch, cfg.nch
        Pr = self.Pr
        C = self.C
        q32, lf, kin, cum, E, sil = self.A
        qt, kkt, vbf, tmpb = self.B[0], self.B[1], self.B[2], self.B[3]
        oml = lambda j: self.oml[:, j, l:l + 1]
        noml = lambda j: self.noml[:, j, l:l + 1]
        lbj = lambda j: self.lb[:, j, l:l + 1]
        self.mk('h_start')
        def evac(bk, pair):
            v2 = bk[:].rearrange("p (c t) -> p c t", c=2)
            for idx, c in enumerate(pair):
                g_, j = (c - 11) // 3, (c - 11) % 3
                src = v2[:, idx, :N]
                if g_ == 0:
                    S.add('act', 'activation', out=q32[:, j, :N], in_=src, func=AF.Copy)
                elif g_ == 1:
                    S.add('act', 'activation', out=kin[:, j, :N], in_=src, func=AF.Sigmoid)
                elif g_ == 2:
                    S.add('act', 'activation', out=vbf[:, j, :N], in_=src, func=AF.Copy)
                else:
                    S.add('act', 'activation', out=sil[:, j, :N], in_=src, func=AF.Silu)
        self.project(list(range(11, 23)), N, evac)
        self.mk('h_after_proj')
        c3 = lambda t, j: t[:, j, :N].rearrange("p (c t) -> p c t", t=CHL)

        def h_scan(j):
            if CHL > 1:
                S.add('dve', 'tensor_tensor_scan', out=cum[:, j, :N], data0=self.rmask[:, :N], data1=lf[:, j, :N], initial=0.0, op0=ALU.mult, op1=ALU.add)
            else:
                S.add('dve', 'tensor_copy', out=cum[:, j, :N], in_=lf[:, j, :N])
        hsteps = [
            lambda j: S.add('act', 'activation', out=lf[:, j, :N], in_=kin[:, j, :N], func=AF.Ln, scale=oml(j), bias=lbj(j)),
            lambda j: S.add('dve', 'tensor_scalar', out=kin[:, j, :N], in0=kin[:, j, :N], scalar1=-1.0, scalar2=noml(j), op0=ALU.add, op1=ALU.mult),
            h_scan,
            lambda j: S.add('act', 'activation', out=self.eM[:, j, :NCHK], in_=c3(cum, j)[:, :, cfg.mid], func=AF.Exp),
            lambda j: S.add('act', 'activation', out=self.cmid[:, j, :NCHK], in_=c3(cum, j)[:, :, cfg.mid], func=AF.Copy),
            lambda j: S.add('dve', 'tensor_tensor', out=c3(cum, j), in0=c3(cum, j), in1=bc(self.cmid[:, j, :NCHK].unsqueeze(2), [128, NCHK, CHL]), op=ALU.subtract),
            lambda j: S.add('act', 'activation', out=E[:, j, :N], in_=cum[:, j, :N], func=AF.Exp),
            lambda j: S.add('dve', 'tensor_tensor', out=qt[:, j, :N], in0=q32[:, j, :N], in1=E[:, j, :N], op=ALU.mult),
            lambda j: S.add('act', 'activation', out=self.eCM[:, j, :NCHK], in_=c3(E, j)[:, :, CHL - 1], func=AF.Copy),
            lambda j: S.add('act', 'activation', out=E[:, j, :N], in_=cum[:, j, :N], func=AF.Exp, scale=-1.0),
            lambda j: S.add('dve', 'tensor_tensor', out=kkt[:, j, :N], in0=kin[:, j, :N], in1=E[:, j, :N], op=ALU.mult),
        ]
        self.wavefront(hsteps, 3)
        self.mk('h_pre_chunk')
        Sst = self.Sst[l]
        smp = self.smp
        PS = [slice(0, 128)] if CHL == 64 else [slice(0, CHL), slice(64, 64 + CHL)]
        miu = C[:, cp.sl('miu')]
        npz = lambda ps: ps.stop - ps.start
        m3 = lambda m, ps: bc(m[ps, 0:CHL].unsqueeze(1), [npz(ps), 3, CHL])
        hv = lambda bk, ps: bk[ps, 0:192].rearrange("p (j t) -> p j t", j=3)[:, :, 0:CHL]
        sv = lambda t, ps: t[ps, :, 0:CHL]
        hs = lambda hh: slice(hh * 64, hh * 64 + 64)
        ph = lambda hh: slice(hh * 64, hh * 64 + CHL)
        HEADS = [(j, hh) for hh in range(2) for j in range(3)]
        def hchunk_gen(ch, HB):
            Sst = self.Sst[l]
            ts = slice(ch * CHL, (ch + 1) * CHL)
            if smp:
                Sst = self.Ss[ch % 2]
                for cn in ([0, 1] if ch == 0 else [ch + 1]):
                    if cn < NCHK:
                        for hh in range(2):
                            S.add('sp', 'dma_start', dma=f'shi{cn % 2}', out=self.Ss[cn % 2][hh * 64:(hh + 1) * 64, :, :],
                                  in_=self.st_hgrn[l, cn].rearrange("(j hh) k v -> hh k j v", hh=2)[hh])
            bkA = self.bank()
            for j in range(3):
                for hh in range(2):
                    S.add('pe', 'matmul', out=bkA[ph(hh), j * 64:(j + 1) * 64], lhsT=vbf[:, j, ts], rhs=self.ident_bf[:, hs(hh)], start=True, stop=True)
                    S.add('pe', 'matmul', out=bkA[ph(hh), 192 + j * 64:192 + (j + 1) * 64], lhsT=kkt[:, j, ts], rhs=self.ident_bf[:, hs(hh)], start=True, stop=True)
            for ps in PS:
                S.add('act', 'activation', out=HB['TM2'][ps], in_=bkA[ps, 0:384].rearrange("p (q f) -> p q f", f=64), func=AF.Copy)
            Vtm = lambda j, hh: HB['TM2'][ph(hh), j, :]
            Ktm = lambda j, hh: HB['TM2'][ph(hh), 3 + j, :]
            fm = lambda t, j, hh: t[hs(hh), j, ts]
            cc = lambda t, j, hh: t[ph(hh), j, 0:CHL]
            pc = lambda bk, j, hh: bk[ph(hh), j * 64:j * 64 + CHL]
            pa = self.bank()
            for (j, hh) in HEADS:
                S.add('pe', 'matmul', out=pc(pa, j, hh), lhsT=fm(kkt, j, hh), rhs=fm(qt, j, hh), start=True, stop=True)
            for ps in PS:
                S.add('dve', 'tensor_tensor', out=sv(HB['AkT'], ps), in0=hv(pa, ps), in1=m3(miu, ps), op=ALU.mult)
            yield 'SD'
            S.add('dve', 'tensor_tensor', out=self.T0f[:], in0=Sst[:], in1=bc(self.eM[:, :, ch:ch + 1], [128, 3, 64]), op=ALU.mult)
            S.add('act', 'activation', out=self.T0b[:], in_=self.T0f[:], func=AF.Copy)
            T0h = lambda j, hh: self.T0b[hs(hh), j, :]
            po = self.bank()
            if CHL == 1:
                po2 = self.bank()
                for (j, hh) in HEADS:
                    S.add('pe', 'matmul', out=po[hs(hh), j * CHL:(j + 1) * CHL], lhsT=T0h(j, hh), rhs=fm(qt, j, hh), start=True, stop=True)
                for (j, hh) in HEADS:
                    S.add('pe', 'matmul', out=po2[hs(hh), j * CHL:(j + 1) * CHL], lhsT=Vtm(j, hh), rhs=cc(HB['AkT'], j, hh), start=True, stop=True)
                S.add('act', 'activation', out=self.O[:, :, ts], in_=po[:, 0:3 * CHL].rearrange("p (j t) -> p j t", j=3), func=AF.Copy)
                S.add('dve', 'tensor_tensor', out=self.O[:, :, ts], in0=self.O[:, :, ts], in1=po2[:, 0:3 * CHL].rearrange("p (j t) -> p j t", j=3), op=ALU.add)
            else:
                for (j, hh) in HEADS:
                    o_ap = po[hs(hh), j * CHL:(j + 1) * CHL]
                    S.add('pe', 'matmul', out=o_ap, lhsT=T0h(j, hh), rhs=fm(qt, j, hh), start=True, stop=False)
                    S.add('pe', 'matmul', out=o_ap, lhsT=Vtm(j, hh), rhs=cc(HB['AkT'], j, hh), start=False, stop=True)
                S.add('act', 'activation', out=self.O[:, :, ts], in_=po[:, 0:3 * CHL].rearrange("p (j t) -> p j t", j=3), func=AF.Copy)
            pt = self.bank()
            for (j, hh) in HEADS:
                t_ap = pt[hs(hh), j * 64:(j + 1) * 64]
                S.add('pe', 'matmul', out=t_ap, lhsT=Ktm(j, hh), rhs=Vtm(j, hh), start=True, stop=True)
            S.add('dve', 'tensor_tensor', out=self.T0f[:], in0=self.T0f[:], in1=pt[:, 0:192].rearrange("p (j v) -> p j v", j=3), op=ALU.add)
            S.add('dve', 'tensor_tensor', out=Sst[:], in0=self.T0f[:], in1=bc(self.eCM[:, :, ch:ch + 1], [128, 3, 64]), op=ALU.mult)
            if smp:
                for hh in range(2):
                    S.add('sp', 'dma_start', dma=f'sho{ch % 2}', out=self.o_shgrn[l, ch].rearrange("(j hh) k v -> hh k j v", hh=2)[hh],
                          in_=Sst[hh * 64:(hh + 1) * 64, :, :])

        GRP = 2 if (CHL == 64 and not smp) else 1
        HB0 = {'TM2': self.TM2, 'AkT': self.AkT}
        for c0 in range(0, NCHK, GRP):
            gens = [hchunk_gen(c0 + i, HB0 if i == 0 else self.HB2) for i in range(min(GRP, NCHK - c0))]
            for gq in gens:
                next(gq)
            for gq in gens:
                for _ in gq:
                    pass
        self.mk('h_post')
        O = self.O
        hng = C[:, cp.sl(f"hng{l}")]
        hst_ = {}

        def hp_mm(j):
            hst_[j] = self.bank()
            S.add('pe', 'matmul', out=hst_[j][:, 0:N], lhsT=self.bmean_bf[:], rhs=tmpb[:, j, :N], start=True, stop=True)
        hpsteps = [
            lambda j: S.add('act', 'activation', out=tmpb[:, j, :N], in_=O[:, j, :N], func=AF.Square),
            hp_mm,
            lambda j: S.add('act', 'activation', out=E[:, j, :N], in_=hst_[j][:, 0:N], func=AF.Ln, bias=self.epsc[NORM_EPS][:]),
            lambda j: S.add('act', 'activation', out=E[:, j, :N], in_=E[:, j, :N], func=AF.Exp, scale=-0.5),
            lambda j: S.add('dve', 'scalar_tensor_tensor', out=O[:, j, :N], in0=O[:, j, :N], scalar=hng[:, j:j + 1], in1=E[:, j, :N], op0=ALU.mult, op1=ALU.mult),
            lambda j: S.add('dve', 'tensor_tensor', out=self.mix[:, 3 + j, :N], in0=O[:, j, :N], in1=sil[:, j, :N], op=ALU.mult),
        ]
        self.wavefront(hpsteps, 3)

    def lru(self, l, cfg, last):
        S, cp = self.S, self.cp
        N = cfg.n
        C = self.C
        xb = self.Pr
        XB = self.A[0]
        xbv = XB[:].rearrange("p j n -> p (j n)")[:, 0:2 * (N + 3)].rearrange("p (c n) -> p c n", c=2)
        xc, rr, ii, aa, uu, gt = self.A[1], self.A[2], self.A[3], self.A[4], self.A[5], self.O
        xcb = self.B[0]
        cw = C[:, cp.sl(f"cw{l}")].rearrange("p (j c) -> p j c", c=2)
        smp = self.smp
        if not smp:
            S.add('act', 'activation', out=xbv[:, :, 0:3], in_=self.cvh[l][:], func=AF.Copy)

        def evac(bk, pair):
            v2 = bk[:].rearrange("p (c t) -> p c t", c=2)
            if pair[0] == 23:
                S.add('act', 'activation', out=xbv[:, :, 3:N + 3], in_=v2[:, :, :N], func=AF.Copy)
            else:
                S.add('act', 'activation', out=gt[:, 0:2, :N], in_=v2[:, :, :N], func=AF.Copy)
        self.project([23, 24, 25, 26], N, evac)
        if not smp:
            S.add('act', 'activation', out=self.cvh[l][:], in_=xbv[:, :, N:N + 3], func=AF.Copy)
        else:
            S.add('act', 'activation', out=self.XBs[:], in_=xbv[:, :, 3:N + 3], func=AF.Copy)
        for c in range(2):
            S.add('dve', 'tensor_scalar', out=xc[:, c, :N], in0=xbv[:, c, 3:N + 3], scalar1=cw[:, 3, c:c + 1], scalar2=C[:, cp.sl(f"cb{l}")][:, c:c + 1],
                  op0=ALU.mult, op1=ALU.add)
            for jj in range(3):
                prevj = self.CV[:, jj * 2 + c, :N] if smp else xbv[:, c, jj:jj + N]
                S.add('dve', 'scalar_tensor_tensor', out=xc[:, c, :N], in0=prevj, scalar=cw[:, jj, c:c + 1], in1=xc[:, c, :N],
                      op0=ALU.mult, op1=ALU.add)
        S.add('act', 'activation', out=xcb[:, 0:2, :N], in_=xc[:, 0:2, :N], func=AF.Copy)

        def lru_chain(c):
            bk = self.bank()
            for nn in range(2):
                pr_ = slice(nn * 64, nn * 64 + 64)
                S.add('pe', 'matmul', out=bk[pr_, 0:N], lhsT=self.SM[pr_, 768 + c * 64:768 + (c + 1) * 64], rhs=xcb[pr_, c, :N], start=True, stop=True)
                S.add('pe', 'matmul', out=bk[pr_, 256:256 + N], lhsT=self.SM[pr_, 896 + c * 64:896 + (c + 1) * 64], rhs=xcb[pr_, c, :N], start=True, stop=True)
            S.add('act', 'activation', out=rr[:, c, :N], in_=bk[:, 0:N], func=AF.Sigmoid, bias=C[:, cp.sl(f"ba{l}")][:, c:c + 1])
            S.add('act', 'activation', out=ii[:, c, :N], in_=bk[:, 256:256 + N], func=AF.Sigmoid, bias=C[:, cp.sl(f"bx{l}")][:, c:c + 1])
            S.add('act', 'activation', out=aa[:, c, :N], in_=rr[:, c, :N], func=AF.Exp, scale=self.c8[:, l, c:c + 1])
            S.add('act', 'activation', out=rr[:, c, :N], in_=rr[:, c, :N], func=AF.Exp, scale=self.c16[:, l, c:c + 1])
            S.add('dve', 'tensor_scalar', out=rr[:, c, :N], in0=rr[:, c, :N], scalar1=-1.0, scalar2=1.0, op0=ALU.mult, op1=ALU.add)
            S.add('dve', 'tensor_scalar', out=rr[:, c, :N], in0=rr[:, c, :N], scalar1=1e-12, scalar2=None, op0=ALU.max)
            S.add('act', 'activation', out=rr[:, c, :N], in_=rr[:, c, :N], func=AF.Sqrt)
            S.add('dve', 'tensor_tensor', out=uu[:, c, :N], in0=xc[:, c, :N], in1=ii[:, c, :N], op=ALU.mult)
            S.add('dve', 'tensor_tensor', out=uu[:, c, :N], in0=uu[:, c, :N], in1=rr[:, c, :N], op=ALU.mult)
            if smp:
                S.add('dve', 'tensor_tensor', out=ii[:, c, :N], in0=aa[:, c, :N], in1=self.H0[:, c, :N], op=ALU.mult)
                S.add('dve', 'tensor_tensor', out=ii[:, c, :N], in0=ii[:, c, :N], in1=uu[:, c, :N], op=ALU.add)
                S.add('act', 'activation', out=self.H0[:, c, :N], in_=ii[:, c, :N], func=AF.Copy)
            else:
                S.add('dve', 'tensor_tensor_scan', out=ii[:, c, :N], data0=aa[:, c, :N], data1=uu[:, c, :N], initial=self.hst[l][:, c, :], op0=ALU.mult, op1=ALU.add)
                S.add('act', 'activation', out=self.hst[l][:, c, :], in_=ii[:, c, N - 1:N], func=AF.Copy)
            S.add('act', 'activation', out=uu[:, c, :N], in_=gt[:, c, :N], func=AF.Square)
            S.add('dve', 'tensor_scalar', out=uu[:, c, :N], in0=uu[:, c, :N], scalar1=0.044715, scalar2=1.0, op0=ALU.mult, op1=ALU.add)
            S.add('dve', 'tensor_tensor', out=uu[:, c, :N], in0=uu[:, c, :N], in1=gt[:, c, :N], op=ALU.mult)
            S.add('act', 'activation', out=uu[:, c, :N], in_=uu[:, c, :N], func=AF.Sigmoid, scale=1.5957691216057308)
            S.add('dve', 'tensor_tensor', out=uu[:, c, :N], in0=uu[:, c, :N], in1=gt[:, c, :N], op=ALU.mult)
            S.add('dve', 'tensor_tensor', out=self.mix[:, 6 + c, :N], in0=uu[:, c, :N], in1=ii[:, c, :N], op=ALU.mult)
        self.interleave([lambda c=c: lru_chain(c) for c in range(2)])

    def alloc_b(self):
        self.sb_off = self.work_base
        sb = self.sb
        u = self.nm("b")
        self.xn2 = sb(f"xn2{u}", [128, KD, TPP], BF16)
        self.nsq = sb(f"nsq{u}", [128, KD, TT], BF16)
        self.nln = sb(f"nln{u}", [128, TT])
        self.nrs = sb(f"nrs{u}", [128, TT])
        self.w1g = [sb(f"w1g{i}{u}", [128, KD, FG], BF16) for i in range(2)]
        self.w2g = [sb(f"w2g{i}{u}", [128, FG // 128, D], BF16) for i in range(2)]
        self.hb = [sb(f"hb{i}{u}", [128, FG // 128, 512], BF16) for i in range(2)]
        self.xn2s = sb(f"xn2s{u}", [128, KD, 16], BF16)
        self.hbs = sb(f"hbs{u}", [128, FG // 128, 16], BF16)
        self.rls = sb(f"rls{u}", [128, FG // 128, 16])
        self.rl = [sb(f"rl{i}{u}", [128, 512]) for i in range(2)]

    def phase_b(self, l, nxt_loads=()):
        S, cp = self.S, self.cp
        g2 = self.C[:, cp.sl(f"g2{l}")]
        def load_group(g):
            s = g % 2
            S.add('pool', 'dma_start', dma=f'w1g{s}', out=self.w1g[s][:], in_=self.w1[l][:, g * FG:(g + 1) * FG].rearrange("(k p) c -> p k c", p=128))
            S.add('pool', 'dma_start', dma=f'w2g{s}', out=self.w2g[s][:], in_=self.w2[l][g * FG:(g + 1) * FG, :].rearrange("(k p) c -> p k c", p=128))
        nxt_loads = list(nxt_loads)
        load_group(0)
        load_group(1)
        if nxt_loads:
            nxt_loads.pop(0)()
        for tt in range(NTILE):
            self.rmsnorm_into(self.x[tt], g2, tt)
        nmt = TPP // 512
        do_s = self.do_smp and self.cur_ps == self.npass - 1
        if do_s:
            self.rmsnorm(self.xs, g2, self.xn2s, 16)

        def sample_group(g):
            s_ = g % 2
            bk = self.bank()
            for c in range(FG // 128):
                for k in range(KD):
                    S.add('pe', 'matmul', out=bk[:, c * 16:(c + 1) * 16], lhsT=self.w1g[s_][:, k, c * 128:(c + 1) * 128], rhs=self.xn2s[:, k, :],
                          start=(k == 0), stop=(k == KD - 1))
            S.add('act', 'activation', out=self.rls[:], in_=bk[:, 0:(FG // 128) * 16].rearrange("p (c t) -> p c t", t=16), func=AF.Relu)
            S.add('pool', 'tensor_tensor', out=self.hbs[:], in0=self.rls[:], in1=self.rls[:], op=ALU.mult)
            bk = self.bank()
            for c in range(KD):
                for k in range(FG // 128):
                    S.add('pe', 'matmul', out=bk[:, c * 16:(c + 1) * 16], lhsT=self.w2g[s_][:, k, c * 128:(c + 1) * 128], rhs=self.hbs[:, k, :],
                          start=(k == 0), stop=(k == FG // 128 - 1))
            S.add('dve', 'tensor_tensor', out=self.xs[:], in0=self.xs[:], in1=bk[:, 0:KD * 16].rearrange("p (c t) -> p c t", t=16), op=ALU.add)

        def stage1(u, g, mt):
            s_ = g % 2
            hb = self.hb[u % 2]
            for c in range(FG // 128):
                bk = self.bank()
                for k in range(KD):
                    S.add('pe', 'matmul', out=bk[:], lhsT=self.w1g[s_][:, k, c * 128:(c + 1) * 128], rhs=self.xn2[:, k, mt * 512:(mt + 1) * 512],
                          start=(k == 0), stop=(k == KD - 1))
                rl = self.rl[c % 2]
                S.add('act', 'activation', out=rl[:], in_=bk[:], func=AF.Relu)
                S.add('pool', 'tensor_tensor', out=hb[:, c, :], in0=rl[:], in1=rl[:], op=ALU.mult)

        def stage2(u, g, mt):
            s_ = g % 2
            hb = self.hb[u % 2]
            for c in range(KD):
                bk = self.bank()
                for k in range(FG // 128):
                    S.add('pe', 'matmul', out=bk[:], lhsT=self.w2g[s_][:, k, c * 128:(c + 1) * 128], rhs=hb[:, k, :],
                          start=(k == 0), stop=(k == FG // 128 - 1))
                for hh in range(512 // TT):
                    xt = self.x[mt * (512 // TT) + hh]
                    S.add('dve', 'tensor_tensor', out=xt[:, c, :], in0=xt[:, c, :], in1=bk[:, hh * TT:(hh + 1) * TT], op=ALU.add)

        units = [(g, mt) for g in range(NFG) for mt in range(nmt)]
        stage1(0, *units[0])
        for u, (g, mt) in enumerate(units):
            if u + 1 < len(units):
                stage1(u + 1, *units[u + 1])
            stage2(u, g, mt)
            if mt == nmt - 1:
                if do_s:
                    sample_group(g)
                if g + 2 < NFG:
                    load_group(g + 2)
                if nxt_loads:
                    nxt_loads.pop(0)()
        for t in nxt_loads:
            t()

    def rmsnorm_into(self, xt, g, tt):
        S = self.S
        N = TT
        S.add('act', 'activation', out=self.nsq[:], in_=xt[:], func=AF.Square)
        bk = self.bank()
        for k in range(KD):
            S.add('pe', 'matmul', out=bk[:, :N], lhsT=self.ones_bf[:], rhs=self.nsq[:, k, :], start=(k == 0), stop=(k == KD - 1))
        S.add('act', 'activation', out=self.nln[:], in_=bk[:, :N], func=AF.Ln, scale=1.0 / D, bias=self.epsc[NORM_EPS][:])
        S.add('act', 'activation', out=self.nrs[:], in_=self.nln[:], func=AF.Exp, scale=-0.5)
        for k in range(KD):
            S.add('dve', 'scalar_tensor_tensor', out=self.xn2[:, k, tt * TT:(tt + 1) * TT], in0=xt[:, k, :], scalar=g[:, k:k + 1], in1=self.nrs[:],
                  op0=ALU.mult, op1=ALU.mult)

    def store_y(self, ps):
        S, cp = self.S, self.cp
        g = self.C[:, cp.sl('fg')]
        for tt in range(NTILE):
            self.rmsnorm(self.x[tt], g, self.yfm, TT)
            for tb in range(TT // 128):
                ob = self.xtm[tb % 2]
                for kh in range(2):
                    bk = self.bank()
                    for kk in range(4):
                        k = kh * 4 + kk
                        S.add('pe', 'transpose', out=bk[:, kk * 128:(kk + 1) * 128],
                              in_=self.yfm[:, k, tb * 128:(tb + 1) * 128], identity=self.ident)
                    S.add('act', 'activation', out=ob[:, kh * 512:(kh + 1) * 512], in_=bk[:], func=AF.Copy)
                t0 = ps * TPP + tt * TT + tb * 128
                S.add('sp', 'dma_start', dma=f'ytm{tb % 2}', out=self.y_p[t0:t0 + 128, :], in_=ob[:])

    def store_states(self):
        S, nc = self.S, self.nc
        self.sb_off = self.work_base
        wk = self.sb("wkvT", [64, DEPTH, 3, 128])
        with nc.allow_non_contiguous_dma(reason="small strided state outputs"):
            S.add('sp', 'dma_start', dma='o_small', allow_slow_non_contiguous=True, out=self.o_pshift.rearrange("l (k p) -> p l k", p=128), in_=self.shl[:])
            for l in range(DEPTH):
                bk = self.bank()
                for j in range(3):
                    S.add('pe', 'transpose', out=bk[0:64, j * 128:(j + 1) * 128], in_=self.Tst[l][:, j, :], identity=self.ident)
                S.add('act', 'activation', out=wk[:, l, :, :], in_=bk[0:64, 0:384].rearrange("p (j f) -> p j f", j=3), func=AF.Copy)
                S.add('sp', 'dma_start', dma=f'o_wkv{l}', allow_slow_non_contiguous=True, out=self.o_pwkv[l].rearrange("h v k -> v h k"),
                      in_=wk[:, l, :, :].rearrange("p j (hh k) -> p (j hh) k", hh=2))
                for hh in range(2):
                    S.add('sp', 'dma_start', dma='o_small', allow_slow_non_contiguous=True, out=self.o_phgrn[l].rearrange("(j hh) k v -> hh k j v", hh=2)[hh],
                          in_=self.Sst[l][hh * 64:(hh + 1) * 64, :, :])
                S.add('sp', 'dma_start', dma='o_small', allow_slow_non_contiguous=True, out=self.o_plru[l].rearrange("(c p) -> p c", p=128), in_=self.hst[l][:, :, 0])
                for c in range(2):
                    S.add('sp', 'dma_start', dma='o_small', allow_slow_non_contiguous=True, out=self.o_pconv[l][:, c * 128:(c + 1) * 128].rearrange("j p -> p j"), in_=self.cvh[l][:, c, :])


def kernel(**inputs):
    inp = {k: np.asarray(v) for k, v in inputs.items()}
    import os
    b = Builder(nlayers=int(os.environ.get("K_NL", DEPTH)), npass=int(os.environ.get("K_NP", NPASS)),
                do_mlp=os.environ.get("K_MLP", "1") == "1", do_smp=os.environ.get("K_SMP", "1") == "1")
    nc = b.build()
    cpk = pack_consts(inp, b.cp)
    smat = pack_small_mats(inp)
    in_maps = []
    for c in range(NCORES):
        in_maps.append({
            "x_prompt": np.ascontiguousarray(inp['x_prompt'][c]),
            "x_sample": np.ascontiguousarray(inp['x_sample'][c * 16:(c + 1) * 16, 0]),
            "state_wkv": np.ascontiguousarray(inp['state_wkv'][:, c * 16:(c + 1) * 16]),
            "state_shift": np.ascontiguousarray(inp['state_shift'][:, c * 16:(c + 1) * 16]),
            "state_hgrn": np.ascontiguousarray(inp['state_hgrn'][:, c * 16:(c + 1) * 16]),
            "state_lru": np.ascontiguousarray(inp['state_lru'][:, c * 16:(c + 1) * 16]),
            "state_conv": np.ascontiguousarray(inp['state_conv'][:, c * 16:(c + 1) * 16]),
            "cpack": cpk, "smat": smat,
            "w_in": inp['w_in'][:b.nlayers], "w_out": inp['w_out'][:b.nlayers],
            "mlp_w1": inp['mlp_w1'][:(b.nlayers if b.do_mlp else 1)], "mlp_w2": inp['mlp_w2'][:(b.nlayers if b.do_mlp else 1)],
        })
    ncr = int(os.environ.get("K_CORES", NCORES))
    res = run_bass_kernel_spmd(nc, in_maps[:ncr], core_ids=list(range(ncr)))
    R = list(res.results) + [res.results[0]] * (NCORES - ncr)
    print("total ops", len(b.S.ops))
    y_prompt = np.stack([R[c]["y_prompt"] for c in range(NCORES)], 0)
    p_shift = np.stack([R[c]["p_shift"] for c in range(NCORES)], 1)
    p_wkv = np.stack([R[c]["p_wkv"] for c in range(NCORES)], 1)
    p_hgrn = np.stack([R[c]["p_hgrn"] for c in range(NCORES)], 1)
    p_lru = np.stack([R[c]["p_lru"] for c in range(NCORES)], 1)
    p_conv = np.stack([R[c]["p_conv"] for c in range(NCORES)], 1)
    cat = lambda n, ax: np.concatenate([R[c][n] for c in range(NCORES)], ax)
    y_sample = cat("y_sample", 0)[:, None, :]
    return (y_prompt, y_sample, p_wkv, p_shift, p_hgrn, p_lru, p_conv,
            cat("s_wkv", 1), cat("s_shift", 1), cat("s_hgrn", 1), cat("s_lru", 1), cat("s_conv", 1))
```

```python
import contextlib
import numpy as np
import concourse.bass as bass
import concourse.mybir as mybir
from concourse.bass_utils import run_bass_kernel_spmd

F32 = mybir.dt.float32
BF16 = mybir.dt.bfloat16
AF = mybir.ActivationFunctionType
ALU = mybir.AluOpType

NCORES = 8
D = 1024
KD = 8
SEQ = 2048
DEPTH = 4
TT = 256
NPASS = 2
TPP = SEQ // NPASS
NTILE = TPP // TT
CH = 64
NCH = TT // CH
MID = CH // 2 - 1
C_IN = 3456
NCT_IN = 27
DFF = 4096
FG = 512
NFG = DFF // FG
NORM_EPS = 1e-6
GN_EPS = 64e-5
C0 = float(np.exp(-0.5))

ENGS = ['pe', 'act', 'dve', 'pool', 'sp']


class Sched:
    def __init__(self, nc):
        self.nc = nc
        self.ops = []
        self.per_eng = {e: [] for e in ENGS}
        self.last_w = {}
        self.readers = {}
        self.synced = {e: {} for e in ENGS}
        self.dma_cum = {}
        self.pe_mode = None
        self.pe_last = None

    @staticmethod
    def _keys(aps):
        ks = []
        for a in aps:
            if a is None or isinstance(a, (int, float)):
                continue
            if isinstance(a, (str, tuple)):
                ks.append(a)
                continue
            t = getattr(a, 'tensor', None)
            if t is None or type(t).__name__ != 'SBTensorHandle' or len(t.shape) < 3 or a.dtype != t.dtype:
                ks.append(a.name)
                continue
            shp = [int(v) for v in t.shape]
            row = int(np.prod(shp[1:]))
            bs = int(np.prod(shp[2:]))
            lo = int(a.offset) % row
            hi = lo
            for step, cnt in list(a.ap)[1:]:
                if step > 0:
                    hi += (int(cnt) - 1) * int(step)
            for b in range(lo // bs, min(hi // bs, shp[1] - 1) + 1):
                ks.append((a.name, b))
        return ks

    maxops = None

    def add(self, eng, method, r=(), w=(), dma=None, **kw):
        if self.maxops is not None and len(self.ops) >= self.maxops:
            return None
        idx = len(self.ops)
        r = list(r)
        w = list(w)
        for kn, v in kw.items():
            if hasattr(v, 'name') and hasattr(v, 'ap'):
                (w if kn in ('out', 'ap', 'accum_out') else r).append(v)
        rk = self._keys(r)
        wk = self._keys(w)
        deps = set()
        for k in rk:
            if k in self.last_w:
                deps.add(self.last_w[k])
        for k in wk:
            if k in self.last_w:
                deps.add(self.last_w[k])
            deps.update(self.readers.get(k, {}).values())
        waits = []
        if eng == 'pe' and method in ('matmul', 'transpose'):
            st_ap = kw['lhsT'] if method == 'matmul' else kw['in_']
            shp = list(st_ap.shape)
            rkk = 32 if shp[0] <= 32 else (64 if shp[0] <= 64 else 128)
            mfree = int(np.prod(shp[1:]))
            rm = 32 if mfree <= 32 else (64 if mfree <= 64 else 128)
            mode = (rkk, rm, method, str(st_ap.dtype))
            if getattr(self, 'pe_mode', None) is not None and mode != self.pe_mode and self.pe_last is not None:
                p = self.ops[self.pe_last]
                p['inc'] = True
                waits.append(('eng', self.pe_last))
            self.pe_mode = mode
        for d in sorted(deps):
            p = self.ops[d]
            if p['dma'] is not None:
                key = ('dma', p['dma'])
                if self.synced[eng].get(key, 0) >= p['cum']:
                    continue
                self.synced[eng][key] = p['cum']
                waits.append(('dma', p['dma'], p['cum']))
            else:
                if p['eng'] == 'pe' and eng == 'pe':
                    continue
                if self.synced[eng].get(p['eng'], -1) >= p['seq']:
                    continue
                self.synced[eng][p['eng']] = p['seq']
                p['inc'] = True
                waits.append(('eng', d))
        o = dict(eng=eng, method=method, kw=kw, waits=waits, dma=dma, seq=len(self.per_eng[eng]),
                 inc=False, cum=None, tick=None)
        if dma is not None:
            self.dma_cum[dma] = self.dma_cum.get(dma, 0) + 16
            o['cum'] = self.dma_cum[dma]
        self.ops.append(o)
        self.per_eng[eng].append(idx)
        if eng == 'pe':
            self.pe_last = idx
        for k in wk:
            self.last_w[k] = idx
            self.readers[k] = {}
        rtag = ('dma', dma) if dma is not None else eng
        for k in rk:
            self.readers.setdefault(k, {})[rtag] = idx
        return idx

    def barrier(self):
        last_real = {}
        for f in ENGS:
            j = len(self.per_eng[f]) - 1
            while j >= 0 and (self.ops[self.per_eng[f][j]]['dma'] is not None
                              or self.ops[self.per_eng[f][j]]['method'] is None):
                j -= 1
            if j >= 0:
                last_real[f] = self.per_eng[f][j]
        cums = dict(self.dma_cum)
        for e in ENGS:
            waits = []
            for f, qi in last_real.items():
                if f == e:
                    continue
                q = self.ops[qi]
                if self.synced[e].get(f, -1) < q['seq']:
                    self.synced[e][f] = q['seq']
                    q['inc'] = True
                    waits.append(('eng', qi))
            for key, cum in cums.items():
                k2 = ('dma', key)
                if self.synced[e].get(k2, 0) < cum:
                    self.synced[e][k2] = cum
                    waits.append(('dma', key, cum))
            o = dict(eng=e, method=None, kw={}, waits=waits, dma=None, seq=len(self.per_eng[e]),
                     inc=False, cum=None, tick=None)
            self.per_eng[e].append(len(self.ops))
            self.ops.append(o)

    def emit(self, es):
        nc = self.nc
        ROT = 20000
        eng_sems = {e: [] for e in ENGS}
        for e in ENGS:
            n = 0
            for i in self.per_eng[e]:
                o = self.ops[i]
                if o['inc']:
                    n += 1
                    o['tick'] = n
            nsem = (n + ROT - 1) // ROT
            for j in range(max(nsem, 1)):
                eng_sems[e].append(es.enter_context(nc.semaphore(f"s_{e}_{j}")))
        dma_sems = {}
        for key in self.dma_cum:
            dma_sems[key] = es.enter_context(nc.semaphore(f"d_{len(dma_sems)}"))
        objs = {'pe': nc.tensor, 'act': nc.scalar, 'dve': nc.vector, 'pool': nc.gpsimd, 'sp': nc.sync}

        def tick_ref(e, tick):
            j = (tick - 1) // ROT
            return eng_sems[e][j], tick - j * ROT

        def run(e, eng):
            for i in self.per_eng[e]:
                o = self.ops[i]
                for wt in o['waits']:
                    if wt[0] == 'dma':
                        eng.wait_ge(dma_sems[wt[1]], wt[2])
                    else:
                        p = self.ops[wt[1]]
                        s, v = tick_ref(p['eng'], p['tick'])
                        eng.wait_ge(s, v)
                if o['method'] is None:
                    continue
                ins = getattr(eng, o['method'])(**o['kw'])
                if o['dma'] is not None:
                    ins.then_inc(dma_sems[o['dma']], 16)
                elif o['inc']:
                    s, v = tick_ref(e, o['tick'])
                    ins.then_inc(s, 1)

        with nc.Block() as block:
            @block.tensor
            def _(eng):
                run('pe', eng)

            @block.scalar
            def _(eng):
                run('act', eng)

            @block.vector
            def _(eng):
                run('dve', eng)

            @block.gpsimd
            def _(eng):
                run('pool', eng)

            @block.sync
            def _(eng):
                run('sp', eng)


def _fm(v, nt):
    return np.ascontiguousarray(np.asarray(v, np.float32).reshape(nt, 128).T)


class CP:
    def __init__(self):
        self.cols = {}
        self.n = 0

    def alloc(self, name, w):
        self.cols[name] = (self.n, w)
        self.n += w

    def sl(self, name):
        a, w = self.cols[name]
        return slice(a, a + w)


def make_cp():
    cp = CP()
    for l in range(DEPTH):
        for nm, w in [('g1', 8), ('g2', 8), ('mu', 11), ('w0', 3), ('a0', 3), ('kk', 3), ('ka', 3), ('rk', 3),
                      ('lnw', 3), ('lnb', 3), ('hng', 3), ('cw', 8), ('cb', 2), ('ba', 2), ('bx', 2), ('lam', 2)]:
            cp.alloc(f"{nm}{l}", w)
    cp.alloc('hlb', 12)
    cp.alloc('fg', 8)
    cp.alloc('ident', 128)
    cp.alloc('msu', 64)
    cp.alloc('msl', 64)
    cp.alloc('miu', 64)
    cp.alloc('msun', 64)
    cp.alloc('msln', 64)
    cp.alloc('id2', 64)
    return cp


def pack_consts(inp, cp):
    A = np.zeros((128, cp.n), np.float32)
    for l in range(DEPTH):
        A[:, cp.sl(f"g1{l}")] = _fm(inp['norm1_g'][l], 8)
        A[:, cp.sl(f"g2{l}")] = _fm(inp['norm2_g'][l], 8)
        A[:, cp.sl(f"mu{l}")] = _fm(inp['mu_shift'][l], 11)
        A[:, cp.sl(f"w0{l}")] = _fm(inp['rwkv_w0'][l], 3)
        A[:, cp.sl(f"a0{l}")] = _fm(inp['rwkv_a0'][l], 3)
        A[:, cp.sl(f"kk{l}")] = _fm(inp['rwkv_k_k'][l], 3)
        A[:, cp.sl(f"ka{l}")] = _fm(inp['rwkv_k_a'][l], 3)
        A[:, cp.sl(f"rk{l}")] = _fm(inp['rwkv_r_k'][l].reshape(-1), 3)
        A[:, cp.sl(f"lnw{l}")] = _fm(inp['rwkv_ln_w'][l], 3)
        A[:, cp.sl(f"lnb{l}")] = _fm(inp['rwkv_ln_b'][l], 3)
        A[:, cp.sl(f"hng{l}")] = _fm(inp['hgrn_norm_g'][l], 3)
        cw = np.stack([_fm(inp['lru_conv_w'][l, j], 2) for j in range(4)], axis=1)
        A[:, cp.sl(f"cw{l}")] = cw.reshape(128, 8)
        A[:, cp.sl(f"cb{l}")] = _fm(inp['lru_conv_b'][l], 2)
        A[:, cp.sl(f"ba{l}")] = _fm(inp['lru_ba'][l], 2)
        A[:, cp.sl(f"bx{l}")] = _fm(inp['lru_bx'][l], 2)
        A[:, cp.sl(f"lam{l}")] = _fm(inp['lru_lambda'][l], 2)
    hl = np.stack([_fm(inp['hgrn_lb'][l], 3) for l in range(DEPTH)], axis=2)
    A[:, cp.sl('hlb')] = hl.reshape(128, 12)
    A[:, cp.sl('fg')] = _fm(inp['final_g'], 8)
    A[:, cp.sl('ident')] = np.eye(128, dtype=np.float32)
    r = np.arange(64)
    for hb in (0, 64):
        A[hb:hb + 64, cp.sl('msu')] = (r[:, None] < r[None, :]).astype(np.float32)
        A[hb:hb + 64, cp.sl('msl')] = (r[:, None] > r[None, :]).astype(np.float32)
        A[hb:hb + 64, cp.sl('miu')] = (r[:, None] <= r[None, :]).astype(np.float32)
        A[hb:hb + 64, cp.sl('msun')] = -(r[:, None] < r[None, :]).astype(np.float32)
        A[hb:hb + 64, cp.sl('msln')] = -(r[:, None] > r[None, :]).astype(np.float32)
        A[hb:hb + 64, cp.sl('id2')] = np.eye(64, dtype=np.float32)
    return A


def pack_small_mats(inp):
    W = np.zeros((128, DEPTH, 384 * 2 + 256), np.float32)
    for l in range(DEPTH):
        W[:64, l, 0:384] = inp['rwkv_w_up'][l]
        W[64:, l, 0:384] = inp['rwkv_a_up'][l]
        W[:, l, 384:768] = inp['rwkv_g_up'][l]
        wa = np.asarray(inp['lru_wa'][l]).reshape(2, 2, 64, 64)
        wx = np.asarray(inp['lru_wx'][l]).reshape(2, 2, 64, 64)
        for t in range(2):
            W[:, l, 768 + t * 64: 768 + (t + 1) * 64] = wa[t].reshape(128, 64)
            W[:, l, 896 + t * 64: 896 + (t + 1) * 64] = wx[t].reshape(128, 64)
    return W


def bc(ap, shape):
    return ap.broadcast_to(list(shape))


class Cfg:
    def __init__(self, n, ch):
        self.n = n
        self.ch = ch
        self.nch = n // ch
        self.mid = max(ch // 2 - 1, 0)
        lv = 0
        while (1 << lv) < ch:
            lv += 1
        self.lv = lv


class Builder:
    def __init__(self, nlayers=DEPTH, npass=NPASS, do_mlp=True, do_smp=True):
        self.do_smp = do_smp
        self.nlayers = nlayers
        self.npass = npass
        self.do_mlp = do_mlp
        self.cp = make_cp()
        self.uid = 0

    def sb(self, name, shape, dt=F32):
        nb = int(np.prod(shape[1:])) * (2 if dt == BF16 else 4)
        nb = (nb + 31) // 32 * 32
        t = self.nc.alloc_sbuf_tensor_at(name, list(shape), dt, offset=self.sb_off)
        self.sb_off += nb
        assert self.sb_off <= 229376 - 256, (name, self.sb_off)
        return t

    def bank(self):
        b = self.banks[self.bank_i % 8]
        self.bank_i += 1
        return b

    def build(self):
        nc = bass.Bass("TRN2", target_bir_lowering=False)
        self.nc = nc
        S = Sched(nc)
        import os as _os
        if _os.environ.get("K_MAXOPS"):
            S.maxops = int(_os.environ["K_MAXOPS"])
        self.S = S
        cp = self.cp
        dram_in = lambda n, shp: nc.dram_tensor(n, list(shp), F32, kind="ExternalInput").ap()
        dram_out = lambda n, shp: nc.dram_tensor(n, list(shp), F32, kind="ExternalOutput").ap()
        self.xp = dram_in("x_prompt", [SEQ, D])
        self.cpk = dram_in("cpack", [128, cp.n])
        self.smat = dram_in("smat", [128, DEPTH, 1024])
        self.w_in = dram_in("w_in", [self.nlayers, D, C_IN])
        self.w_out = dram_in("w_out", [self.nlayers, D, D])
        self.w1 = dram_in("mlp_w1", [self.nlayers if self.do_mlp else 1, D, DFF])
        self.w2 = dram_in("mlp_w2", [self.nlayers if self.do_mlp else 1, DFF, D])
        self.y_p = dram_out("y_prompt", [SEQ, D])
        self.o_pshift = dram_out("p_shift", [DEPTH, D])
        self.o_pwkv = dram_out("p_wkv", [DEPTH, 6, 64, 64])
        self.o_phgrn = dram_out("p_hgrn", [DEPTH, 6, 64, 64])
        self.o_plru = dram_out("p_lru", [DEPTH, 256])
        self.o_pconv = dram_out("p_conv", [DEPTH, 3, 256])
        self.xs_d = dram_in("x_sample", [16, D])
        self.st_wkv = dram_in("state_wkv", [DEPTH, 16, 6, 64, 64])
        self.st_shift = dram_in("state_shift", [DEPTH, 16, D])
        self.st_hgrn = dram_in("state_hgrn", [DEPTH, 16, 6, 64, 64])
        self.st_lru = dram_in("state_lru", [DEPTH, 16, 256])
        self.st_conv = dram_in("state_conv", [DEPTH, 16, 3, 256])
        self.y_s = dram_out("y_sample", [16, D])
        self.o_swkv = dram_out("s_wkv", [DEPTH, 16, 6, 64, 64])
        self.o_sshift = dram_out("s_shift", [DEPTH, 16, D])
        self.o_shgrn = dram_out("s_hgrn", [DEPTH, 16, 6, 64, 64])
        self.o_slru = dram_out("s_lru", [DEPTH, 16, 256])
        self.o_sconv = dram_out("s_conv", [DEPTH, 16, 3, 256])

        with contextlib.ExitStack() as es:
            self.es = es
            self.sb_off = 0x4000 + 512
            sb = self.sb
            self.banks = [es.enter_context(nc.psum_tensor(f"bank{i}", [128, 512], F32)) for i in range(8)]
            self.bank_i = 0
            self.C = sb("cpk", [128, cp.n])
            S.add('sp', 'dma_start', dma='c0', out=self.C[:], in_=self.cpk)
            self.ident = self.C[:, cp.sl('ident')]
            self.ident_bf = sb("ident_bf", [128, 128], BF16)
            S.add('dve', 'tensor_copy', out=self.ident_bf[:], in_=self.ident)
            self.ones_bf = sb("ones_bf", [128, 128], BF16)
            S.add('dve', 'memset', ap=self.ones_bf[:], constant=1.0)
            self.bmean_bf = sb("bmean_bf", [128, 128], BF16)
            S.add('dve', 'memset', ap=self.bmean_bf[:], constant=0.0)
            S.add('dve', 'memset', ap=self.bmean_bf[0:64, 0:64], constant=1.0 / 64)
            S.add('dve', 'memset', ap=self.bmean_bf[64:128, 64:128], constant=1.0 / 64)
            self.bones_bf = sb("bones_bf", [128, 128], BF16)
            S.add('dve', 'memset', ap=self.bones_bf[:], constant=0.0)
            S.add('dve', 'memset', ap=self.bones_bf[0:64, 0:64], constant=1.0)
            S.add('dve', 'memset', ap=self.bones_bf[64:128, 64:128], constant=1.0)
            self.rmask = sb("rmask", [128, TT])
            S.add('dve', 'memset', ap=self.rmask[:], constant=1.0)
            S.add('dve', 'memset', ap=self.rmask[:].rearrange("p (c t) -> p c t", t=CH)[:, :, 0:1], constant=0.0)
            self.epsc = {}
            for v in (NORM_EPS, GN_EPS, 1e-12, 1.0):
                t = sb(self.nm("eps"), [128, 1])
                S.add('pool', 'memset', ap=t[:], constant=float(v))
                self.epsc[v] = t
            self.lb = sb("lb", [128, 3, DEPTH])
            self.oml = sb("oml", [128, 3, DEPTH])
            self.noml = sb("noml", [128, 3, DEPTH])
            self.hgrn_lb_setup()
            self.c8 = sb("c8", [128, DEPTH, 2])
            self.c16 = sb("c16", [128, DEPTH, 2])
            self.lru_setup()
            self.x = [sb(f"x{t}", [128, KD, TT]) for t in range(NTILE)]
            self.WAin = sb("WAin", [128, KD, C_IN], BF16)
            self.WAout = sb("WAout", [128, KD, D], BF16)
            self.SM = sb("SM", [128, 1024], BF16)
            self.shl = sb("shl", [128, DEPTH, KD])
            S.add('pool', 'memset', ap=self.shl[:], constant=0.0)
            self.Tst = [sb(f"Tst{l}", [128, 3, 64]) for l in range(DEPTH)]
            self.Sst = [sb(f"Sst{l}", [128, 3, 64]) for l in range(DEPTH)]
            self.hst = [sb(f"hst{l}", [128, 2, 1]) for l in range(DEPTH)]
            self.cvh = [sb(f"cvh{l}", [128, 2, 3]) for l in range(DEPTH)]
            self.shh = [sb(f"shh{l}", [128, 11, 1]) for l in range(DEPTH)]
            for l in range(DEPTH):
                for t in (self.Tst[l], self.Sst[l], self.hst[l], self.cvh[l], self.shh[l]):
                    S.add('pool', 'memset', ap=t[:], constant=0.0)
            self.xs = sb("xs", [128, KD, 16])
            self.work_base = self.sb_off
            self.pcfg = Cfg(TT, CH)
            self.scfg = Cfg(16, 1)
            self.smp = False

            self.load_weights_a(0)
            for ps in range(self.npass):
                self.alloc_io()
                self.load_x(ps)
                if self.do_smp and ps == self.npass - 1:
                    self.load_xs()
                S.barrier()
                for l in range(self.nlayers):
                    self.alloc_a()
                    for tt in range(NTILE):
                        last = (ps == self.npass - 1 and tt == NTILE - 1)
                        self.phase_a_tile(l, tt, self.pcfg, last)
                    if self.do_smp and ps == self.npass - 1:
                        self.sample_tile(l)
                    S.barrier()
                    nl, nps = (l + 1, ps) if l + 1 < self.nlayers else (0, ps + 1)
                    nxt = self.weight_loads_a(nl) if nps < self.npass else []
                    if self.do_mlp:
                        self.alloc_b()
                        self.cur_ps = ps
                        self.phase_b(l, nxt)
                        S.barrier()
                    else:
                        for t in nxt:
                            t()
                self.alloc_io()
                self.store_y(ps)
                if self.do_smp and ps == self.npass - 1:
                    self.store_ys()
                S.barrier()
            self.store_states()
            S.barrier()
            S.emit(es)
        return nc

    def mk(self, label):
        import os
        if os.environ.get('K_MARK'):
            print('MARK', label, len(self.S.ops))

    def interleave(self, chain_fns):
        S = self.S
        recs = []
        for fn in chain_fns:
            lst = []
            S.add = (lambda *a, _l=lst, **k: _l.append((a, k)))
            try:
                fn()
            finally:
                del S.add
            recs.append(lst)
        L = max(len(r) for r in recs)
        for w in range(L + len(recs) - 1):
            for j, r in enumerate(recs):
                si = w - j
                if 0 <= si < len(r):
                    a, k = r[si]
                    S.add(*a, **k)

    def wavefront(self, steps, nj):
        for w in range(len(steps) + nj - 1):
            for si in range(len(steps)):
                j = w - si
                if 0 <= j < nj:
                    steps[si](j)

    def nm(self, s):
        self.uid += 1
        return f"{s}_{self.uid}"

    def hgrn_lb_setup(self):
        S, cp = self.S, self.cp
        sb = self.sb
        e = sb("hl_e", [128, 3, DEPTH])
        ssum = sb("hl_s", [128, 3, 1])
        hl = self.C[:, cp.sl('hlb')].rearrange("p (j l) -> p j l", l=DEPTH)
        S.add('act', 'activation', out=e[:], in_=hl, func=AF.Exp)
        S.add('dve', 'tensor_reduce', out=ssum[:], in_=e[:], axis=mybir.AxisListType.X, op=ALU.add)
        S.add('dve', 'reciprocal', out=ssum[:], in_=ssum[:])
        S.add('dve', 'tensor_tensor', out=e[:], in0=e[:], in1=bc(ssum[:], [128, 3, DEPTH]), op=ALU.mult)
        S.add('dve', 'memset', ap=self.lb[:, :, 0:1], constant=0.0)
        for l in range(1, DEPTH):
            S.add('dve', 'tensor_tensor', out=self.lb[:, :, l:l + 1], in0=self.lb[:, :, l - 1:l], in1=e[:, :, l:l + 1], op=ALU.add)
        S.add('dve', 'tensor_scalar', out=self.oml[:], in0=self.lb[:], scalar1=-1.0, scalar2=1.0, op0=ALU.mult, op1=ALU.add)
        S.add('dve', 'tensor_scalar', out=self.noml[:], in0=self.lb[:], scalar1=-1.0, scalar2=None, op0=ALU.add)

    def lru_setup(self):
        S, cp = self.S, self.cp
        t = self.sb("lru_t", [128, DEPTH, 2])
        for l in range(DEPTH):
            S.add('act', 'activation', out=t[:, l, :], in_=self.C[:, cp.sl(f"lam{l}")], func=AF.Exp, scale=-1.0)
        S.add('act', 'activation', out=t[:], in_=t[:], func=AF.Ln, bias=self.epsc[1.0][:])
        S.add('dve', 'tensor_scalar', out=self.c8[:], in0=t[:], scalar1=-8.0, scalar2=None, op0=ALU.mult)
        S.add('dve', 'tensor_scalar', out=self.c16[:], in0=t[:], scalar1=-16.0, scalar2=None, op0=ALU.mult)

    def weight_loads_a(self, l):
        S = self.S
        src = self.w_in[l].rearrange("(k p) c -> p k c", p=128)
        q = C_IN // 4
        th = []
        for i in range(4):
            th.append(lambda i=i: S.add('pool', 'dma_start', dma=f'wain{i}', out=self.WAin[:, :, i * q:(i + 1) * q],
                                        in_=src[:, :, i * q:(i + 1) * q]))
        th.insert(1, lambda: S.add('pool', 'dma_start', dma='sm', out=self.SM[:], in_=self.smat[:, l, :]))
        th.append(lambda: S.add('pool', 'dma_start', dma='waout', out=self.WAout[:], in_=self.w_out[l].rearrange("(k p) c -> p k c", p=128)))
        return th

    def load_weights_a(self, l):
        for t in self.weight_loads_a(l):
            t()

    def alloc_a(self):
        self.sb_off = self.work_base
        sb = self.sb
        u = self.nm("a")
        N = TT
        self.xn = sb(f"xn{u}", [128, KD, N], BF16)
        self.nln = sb(f"nln{u}", [128, N])
        self.nrs = sb(f"nrs{u}", [128, N])
        self.Pr = sb(f"Pr{u}", [128, 11, N + 1])
        self.ds = [sb(f"ds{i}{u}", [128, N]) for i in range(2)]
        self.twxa = sb(f"twxa{u}", [128, N], BF16)
        self.sgg = sb(f"sgg{u}", [128, N], BF16)
        self.A = [sb(f"A{i}{u}", [128, 3, N]) for i in range(6)]
        self._offB0 = self.sb_off
        self.B = [sb(f"B{i}{u}", [128, 3, N], BF16) for i in range(6)]
        self.O = sb(f"O{u}", [128, 3, N])
        self.eM = sb(f"eM{u}", [128, 3, 16])
        self.eC = sb(f"eC{u}", [128, 3, 16])
        self.eCM = sb(f"eCM{u}", [128, 3, 16])
        self.cmid = sb(f"cmid{u}", [128, 3, 16])
        self.AkT = sb(f"AkT{u}", [128, 3, 64], BF16)
        self.TM2 = sb(f"TM2{u}", [128, 6, 64], BF16)
        self.T0f = sb(f"T0f{u}", [128, 3, 64])
        self.T0b = sb(f"T0b{u}", [128, 3, 64], BF16)
        self.mix = sb(f"mix{u}", [128, KD, N], BF16)
        self.nsq = self.mix
        self.xs32 = sb(f"xs32{u}", [128, KD, 16])
        self.P0 = sb(f"P0{u}", [128, 11, 16])
        self.shfm = sb(f"shfm{u}", [128, KD, 16], BF16)
        self._offR2 = self.sb_off
        _stm = sb(f"stm{u}", [16, D])
        self.stm = [_stm, _stm]
        self.Ts = [sb(f"Ts{i}{u}", [128, 3, 64]) for i in range(2)]
        self.Ss = [sb(f"Ss{i}{u}", [128, 3, 64]) for i in range(2)]
        _wkT = sb(f"wkT{u}", [64, 3, 128])
        self.wkT = [_wkT, _wkT]
        self.swi = [sb(f"swi{i}{u}", [64, 6, 64]) for i in range(2)]
        self.CV = sb(f"CV{u}", [128, 6, 16])
        self.H0 = sb(f"H0{u}", [128, 2, 16])
        self.cvtm = sb(f"cvtm{u}", [16, 768])
        self.cvo = sb(f"cvo{u}", [16, 256])
        self.lrtm = sb(f"lrtm{u}", [16, 256])
        self.lro = sb(f"lro{u}", [16, 256])
        self.XBs = sb(f"XBs{u}", [128, 2, 16])
        end_off = self.sb_off
        lim = 0
        self.R = {}
        for nmx, shp in [('ZY0', [128, 2, 3, 64]), ('ZY1', [128, 2, 3, 64]), ('NT', [128, 3, 64]), ('AbT', [128, 3, 64]),
                         ('AkT', [128, 3, 64]), ('PP0', [128, 3, 64]), ('PP1', [128, 3, 64]), ('TM2', [128, 6, 64]),
                         ('TMK', [128, 3, 64]), ('Xb', [128, 3, 64]), ('Ub', [128, 3, 64])]:
            self.R[nmx] = sb(f"R{nmx}{u}", shp)
        self.sb_off = max(self.sb_off, end_off)
        end2 = self.sb_off
        self.sb_off = self._offR2
        self.R2 = dict(self.R)
        for nmx, shp in [('ZY0', [128, 2, 3, 64]), ('ZY1', [128, 2, 3, 64]), ('NT', [128, 3, 64]), ('AbT', [128, 3, 64]),
                         ('AkT', [128, 3, 64]), ('PP0', [128, 3, 64]), ('PP1', [128, 3, 64]), ('TM2', [128, 6, 64]),
                         ('TMK', [128, 3, 64])]:
            self.R2[nmx] = sb(f"R2{nmx}{u}", shp)
        self.HB2 = {'TM2': sb(f"HB2TM2{u}", [128, 6, 64], BF16), 'AkT': sb(f"HB2AkT{u}", [128, 3, 64], BF16)}
        assert self.sb_off <= self._offR2 + 4096 + 4 * 768 + 1536 + 2 * 1536, self.sb_off - self._offR2
        self.sb_off = end2
        if not hasattr(self, '_pa'):
            self._pa = True
            print("phase A work bytes", self.sb_off - self.work_base, "limit", 0x4000 + 212000 - self.work_base)

    def alloc_io(self):
        self.sb_off = self.work_base
        sb = self.sb
        u = self.nm("io")
        N = TT
        self.nsq = sb(f"nsq{u}", [128, KD, N], BF16)
        self.nln = sb(f"nln{u}", [128, N])
        self.nrs = sb(f"nrs{u}", [128, N])
        self.xtm = [sb(f"xtm{i}{u}", [128, D]) for i in range(2)]
        self.yfm = sb(f"yfm{u}", [128, KD, N])
        self.xs32 = sb(f"xs32{u}", [128, KD, 16])
        self.stm = [sb(f"stm{i}{u}", [16, D]) for i in range(2)]

    def load_x(self, ps):
        S = self.S
        for tb in range(TPP // 128):
            buf = self.xtm[tb % 2]
            t0 = ps * TPP + tb * 128
            S.add('sp', 'dma_start', dma=f'xtm{tb % 2}', out=buf[:], in_=self.xp[t0:t0 + 128, :])
            xt = self.x[(tb * 128) // TT]
            tl = (tb * 128) % TT
            for kh in range(2):
                bk = self.bank()
                for kk in range(4):
                    k = kh * 4 + kk
                    S.add('pe', 'transpose', out=bk[:, kk * 128:(kk + 1) * 128],
                          in_=buf[:, k * 128:(k + 1) * 128], identity=self.ident)
                src = bk[:].rearrange("p (k t) -> p k t", k=4)
                dst = xt[:, kh * 4:(kh + 1) * 4, tl:tl + 128]
                if kh == 0:
                    S.add('act', 'activation', out=dst, in_=src, func=AF.Copy)
                else:
                    S.add('dve', 'tensor_copy', out=dst, in_=src)

    def rmsnorm(self, xt, g, out, N, want_last=None):
        S = self.S
        S.add('act', 'activation', out=self.nsq[:, :, :N], in_=xt[:, :, :N], func=AF.Square)
        bk = self.bank()
        for k in range(KD):
            S.add('pe', 'matmul', out=bk[:, :N], lhsT=self.ones_bf[:], rhs=self.nsq[:, k, :N], start=(k == 0), stop=(k == KD - 1))
        S.add('act', 'activation', out=self.nln[:, :N], in_=bk[:, :N], func=AF.Ln, scale=1.0 / D, bias=self.epsc[NORM_EPS][:])
        S.add('act', 'activation', out=self.nrs[:, :N], in_=self.nln[:, :N], func=AF.Exp, scale=-0.5)
        for k in range(KD):
            S.add('dve', 'scalar_tensor_tensor', out=out[:, k, :N], in0=xt[:, k, :N], scalar=g[:, k:k + 1], in1=self.nrs[:, :N],
                  op0=ALU.mult, op1=ALU.mult)
        if want_last is not None:
            S.add('dve', 'tensor_tensor', out=want_last, in0=xt[:, :, N - 1], in1=g, op=ALU.mult)
            S.add('dve', 'tensor_scalar', out=want_last, in0=want_last, scalar1=self.nrs[:, N - 1:N], scalar2=None, op0=ALU.mult)

    def project(self, cols, N, evac, rhs=None):
        S = self.S
        for i in range(0, len(cols), 2):
            pair = cols[i:i + 2]
            bk = self.bank()
            for idx, c in enumerate(pair):
                for k in range(KD):
                    S.add('pe', 'matmul', out=bk[:, idx * 256: idx * 256 + N], lhsT=self.WAin[:, k, c * 128:(c + 1) * 128],
                          rhs=(self.xn if rhs is None else rhs)[:, k, :N], start=(k == 0), stop=(k == KD - 1))
            evac(bk, pair)

    def phase_a_tile(self, l, tt, cfg, last):
        S, cp = self.S, self.cp
        N = cfg.n
        xt = self.x[tt]
        self.rmsnorm(xt, self.C[:, cp.sl(f"g1{l}")], self.xn, N, want_last=self.shl[:, l, :] if last else None)
        import os
        mixs = os.environ.get("K_MIX", "rhl")
        if mixs != "rhl":
            S.add('dve', 'memset', ap=self.mix[:], constant=0.0)
        if 'r' in mixs:
            self.rwkv(l, cfg)
        if 'h' in mixs:
            self.hgrn(l, cfg)
        if 'l' in mixs:
            self.lru(l, cfg, last)
        for i in range(0, KD, 2):
            bk = self.bank()
            for idx in range(2):
                c = i + idx
                for k in range(KD):
                    S.add('pe', 'matmul', out=bk[:, idx * 256: idx * 256 + N], lhsT=self.WAout[:, k, c * 128:(c + 1) * 128],
                          rhs=self.mix[:, k, :N], start=(k == 0), stop=(k == KD - 1))
            S.add('dve', 'tensor_tensor', out=xt[:, i:i + 2, :N], in0=xt[:, i:i + 2, :N],
                  in1=bk[:].rearrange("p (c t) -> p c t", c=2)[:, :, :N], op=ALU.add)

    def tm2fm(self, src, nblk, dst, eng='act'):
        S = self.S
        bk = self.bank()
        for b in range(nblk):
            S.add('pe', 'transpose', out=bk[:, b * 16:(b + 1) * 16], in_=src[:, b * 128:(b + 1) * 128], identity=self.ident[0:16, 0:16])
        srcv = bk[:, 0:nblk * 16].rearrange("p (b t) -> p b t", t=16)
        if eng == 'act':
            S.add('act', 'activation', out=dst, in_=srcv, func=AF.Copy)
        else:
            S.add('dve', 'tensor_copy', out=dst, in_=srcv)

    def fm2tm(self, src, nblk, dst):
        S = self.S
        for b0 in range(0, nblk, 4):
            nb = min(4, nblk - b0)
            bk = self.bank()
            for b in range(nb):
                S.add('pe', 'transpose', out=bk[0:16, b * 128:(b + 1) * 128], in_=src(b0 + b), identity=self.ident)
            S.add('act', 'activation', out=dst[:, b0 * 128:(b0 + nb) * 128], in_=bk[0:16, 0:nb * 128], func=AF.Copy)

    def load_xs(self):
        S = self.S
        S.add('sp', 'dma_start', dma='stm0', out=self.stm[0][:], in_=self.xs_d)
        self.tm2fm(self.stm[0], KD, self.xs[:])

    def store_ys(self):
        S, cp = self.S, self.cp
        self.rmsnorm(self.xs, self.C[:, cp.sl('fg')], self.xs32, 16)
        self.fm2tm(lambda b: self.xs32[:, b, :], KD, self.stm[0])
        S.add('sp', 'dma_start', dma='stm0', out=self.y_s, in_=self.stm[0][:])

    def sample_tile(self, l):
        S, cp = self.S, self.cp
        cfg = self.scfg
        N = 16
        S.barrier()
        self.smp = True
        self.rmsnorm(self.xs, self.C[:, cp.sl(f"g1{l}")], self.xs32, N)
        S.add('act', 'activation', out=self.xn[:, :, :N], in_=self.xs32[:], func=AF.Copy)
        self.fm2tm(lambda b: self.xs32[:, b, :], KD, self.stm[1])
        S.add('sp', 'dma_start', dma='stm0o', out=self.o_sshift[l], in_=self.stm[1][:])
        S.add('sp', 'dma_start', dma='stm0', out=self.stm[0][:], in_=self.st_shift[l])
        self.tm2fm(self.stm[0], KD, self.shfm[:])
        S.add('sp', 'dma_start', dma='cvtm', out=self.cvtm[:], in_=self.st_conv[l].rearrange("b j c -> b (j c)"))
        self.tm2fm(self.cvtm, 6, self.CV[:], eng='dve')
        S.add('sp', 'dma_start', dma='lrtm', out=self.lrtm[:], in_=self.st_lru[l])
        self.tm2fm(self.lrtm, 2, self.H0[:], eng='dve')
        self.rwkv(l, cfg)
        self.hgrn(l, cfg)
        self.lru(l, cfg, False)
        S.add('sp', 'dma_start', dma='cvo0', out=self.o_sconv[l][:, 0:2, :].rearrange("b j c -> b (j c)"), in_=self.cvtm[:, 256:768])
        self.fm2tm(lambda b: self.XBs[:, b, :], 2, self.cvo)
        S.add('sp', 'dma_start', dma='cvo1', out=self.o_sconv[l][:, 2, :], in_=self.cvo[:])
        self.fm2tm(lambda b: self.H0[:, b, :], 2, self.lro)
        S.add('sp', 'dma_start', dma='lro', out=self.o_slru[l], in_=self.lro[:])
        for i in range(0, KD, 2):
            bk = self.bank()
            for idx in range(2):
                c = i + idx
                for k in range(KD):
                    S.add('pe', 'matmul', out=bk[:, idx * 256: idx * 256 + N], lhsT=self.WAout[:, k, c * 128:(c + 1) * 128],
                          rhs=self.mix[:, k, :N], start=(k == 0), stop=(k == KD - 1))
            S.add('dve', 'tensor_tensor', out=self.xs[:, i:i + 2, :N], in0=self.xs[:, i:i + 2, :N],
                  in1=bk[:].rearrange("p (c t) -> p c t", c=2)[:, :, :N], op=ALU.add)
        self.smp = False

    def rwkv(self, l, cfg):
        S, cp = self.S, self.cp
        N, CHL, NCHK = cfg.n, cfg.ch, cfg.nch
        Pr = self.Pr
        C = self.C
        sg, a_, kap, t1, E, bonus = self.A
        rt, kt, bt, kkt, vbf, rkbf = self.B
        cs = lambda nm: C[:, cp.sl(f"{nm}{l}")]
        smp = self.smp
        if not smp:
            S.add('act', 'activation', out=Pr[:, 0:11, 0:1], in_=self.shh[l][:], func=AF.Copy)

        def evac(bk, pair):
            S.add('act', 'activation', out=Pr[:, pair[0]:pair[0] + len(pair), 1:N + 1],
                  in_=bk[:].rearrange("p (c t) -> p c t", c=2)[:, 0:len(pair), :N], func=AF.Copy)
        self.project(list(range(11)), N, evac)
        if smp:
            def evac0(bk, pair):
                S.add('act', 'activation', out=self.P0[:, pair[0]:pair[0] + len(pair), :N],
                      in_=bk[:].rearrange("p (c t) -> p c t", c=2)[:, 0:len(pair), :N], func=AF.Copy)
            self.project(list(range(11)), N, evac0, rhs=self.shfm)
        else:
            S.add('act', 'activation', out=self.shh[l][:], in_=Pr[:, 0:11, N:N + 1], func=AF.Copy)
        self.mk('r_after_proj')
        mu = cs('mu')
        for c in range(11):
            d = self.ds[c % 2]
            S.add('pool', 'tensor_tensor', out=d[:, :N], in0=(self.P0[:, c, :N] if smp else Pr[:, c, 0:N]), in1=Pr[:, c, 1:N + 1], op=ALU.subtract)
            S.add('dve', 'scalar_tensor_tensor', out=Pr[:, c, 1:N + 1], in0=d[:, :N], scalar=mu[:, c:c + 1], in1=Pr[:, c, 1:N + 1],
                  op0=ALU.mult, op1=ALU.add)
        r = Pr[:, 0:3, 1:N + 1]
        k = Pr[:, 3:6, 1:N + 1]
        v = Pr[:, 6:9, 1:N + 1]
        self.mk('r_after_pm')
        S.add('act', 'activation', out=self.twxa[0:64, :N], in_=Pr[0:64, 9, 1:N + 1], func=AF.Tanh)
        S.add('act', 'activation', out=self.twxa[64:128, :N], in_=Pr[64:128, 9, 1:N + 1], func=AF.Copy)
        S.add('act', 'activation', out=self.sgg[:, :N], in_=Pr[:, 10, 1:N + 1], func=AF.Sigmoid)
        c3 = lambda t, j: t[:, j, :N].rearrange("p (c t) -> p c t", t=CHL)
        rj = lambda j: Pr[:, j, 1:N + 1]
        kj = lambda j: Pr[:, 3 + j, 1:N + 1]
        vj = lambda j: Pr[:, 6 + j, 1:N + 1]
        st = {}

        def s_lora(j):
            st[('bk', j)] = self.bank()
            st[('bk2', j)] = self.bank()
            S.add('pe', 'matmul', out=st[('bk', j)][:, 0:N], lhsT=self.SM[0:64, j * 128:(j + 1) * 128], rhs=self.twxa[0:64, :N], start=True, stop=True)
            S.add('pe', 'matmul', out=st[('bk2', j)][:, 0:N], lhsT=self.SM[64:128, j * 128:(j + 1) * 128], rhs=self.twxa[64:128, :N], start=True, stop=True)

        def s_bones1(j):
            st[('b3', j)] = self.bank()
            S.add('pe', 'matmul', out=st[('b3', j)][:, 0:N], lhsT=self.bones_bf[:], rhs=rkbf[:, j, :N], start=True, stop=True)

        def s_bones2(j):
            st[('b4', j)] = self.bank()
            S.add('pe', 'matmul', out=st[('b4', j)][:, 0:N], lhsT=self.bones_bf[:], rhs=rkbf[:, j, :N], start=True, stop=True)

        def s_scan(j):
            if CHL > 1:
                S.add('dve', 'tensor_tensor_scan', out=t1[:, j, :N], data0=self.rmask[:, :N], data1=sg[:, j, :N], initial=0.0, op0=ALU.mult, op1=ALU.add)
            else:
                S.add('dve', 'tensor_copy', out=t1[:, j, :N], in_=sg[:, j, :N])

        steps = [
            s_lora,
            lambda j: S.add('act', 'activation', out=sg[:, j, :N], in_=st[('bk', j)][:, 0:N], func=AF.Sigmoid, bias=cs('w0')[:, j:j + 1]),
            lambda j: S.add('act', 'activation', out=a_[:, j, :N], in_=st[('bk2', j)][:, 0:N], func=AF.Sigmoid, bias=cs('a0')[:, j:j + 1]),
            lambda j: S.add('dve', 'tensor_scalar', out=kap[:, j, :N], in0=kj(j), scalar1=cs('kk')[:, j:j + 1], scalar2=None, op0=ALU.mult),
            lambda j: S.add('act', 'activation', out=rkbf[:, j, :N], in_=kap[:, j, :N], func=AF.Square),
            s_bones1,
            lambda j: S.add('act', 'activation', out=t1[:, j, :N], in_=st[('b3', j)][:, 0:N], func=AF.Ln, bias=self.epsc[1e-12][:]),
            lambda j: S.add('act', 'activation', out=t1[:, j, :N], in_=t1[:, j, :N], func=AF.Exp, scale=-0.5),
            lambda j: S.add('dve', 'tensor_tensor', out=kap[:, j, :N], in0=kap[:, j, :N], in1=t1[:, j, :N], op=ALU.mult),
            lambda j: S.add('dve', 'tensor_scalar', out=t1[:, j, :N], in0=a_[:, j, :N], scalar1=-1.0, scalar2=cs('ka')[:, j:j + 1], op0=ALU.add, op1=ALU.mult),
            lambda j: S.add('dve', 'scalar_tensor_tensor', out=kj(j), in0=t1[:, j, :N], scalar=1.0, in1=kj(j), op0=ALU.add, op1=ALU.mult),
            lambda j: S.add('dve', 'tensor_tensor', out=a_[:, j, :N], in0=a_[:, j, :N], in1=kap[:, j, :N], op=ALU.mult),
            lambda j: S.add('dve', 'scalar_tensor_tensor', out=rkbf[:, j, :N], in0=rj(j), scalar=cs('rk')[:, j:j + 1], in1=kj(j), op0=ALU.mult, op1=ALU.mult),
            s_bones2,
            lambda j: S.add('dve', 'tensor_tensor', out=bonus[:, j, :N], in0=vj(j), in1=st[('b4', j)][:, 0:N], op=ALU.mult),
            s_scan,
            lambda j: S.add('act', 'activation', out=self.eM[:, j, :NCHK], in_=c3(t1, j)[:, :, cfg.mid], func=AF.Exp, scale=-C0),
            lambda j: S.add('act', 'activation', out=self.eC[:, j, :NCHK], in_=c3(t1, j)[:, :, CHL - 1], func=AF.Exp, scale=-C0),
            lambda j: S.add('act', 'activation', out=self.cmid[:, j, :NCHK], in_=c3(t1, j)[:, :, cfg.mid], func=AF.Copy),
            lambda j: S.add('dve', 'tensor_tensor', out=c3(t1, j), in0=c3(t1, j), in1=bc(self.cmid[:, j, :NCHK].unsqueeze(2), [128, NCHK, CHL]), op=ALU.subtract),
            lambda j: S.add('act', 'activation', out=E[:, j, :N], in_=t1[:, j, :N], func=AF.Exp, scale=-C0),
            lambda j: S.add('dve', 'tensor_tensor', out=rj(j), in0=rj(j), in1=E[:, j, :N], op=ALU.mult),
            lambda j: S.add('act', 'activation', out=self.eCM[:, j, :NCHK], in_=c3(E, j)[:, :, CHL - 1], func=AF.Copy),
            lambda j: S.add('act', 'activation', out=E[:, j, :N], in_=t1[:, j, :N], func=AF.Exp, scale=C0),
            lambda j: S.add('dve', 'tensor_tensor', out=a_[:, j, :N], in0=a_[:, j, :N], in1=E[:, j, :N], op=ALU.mult),
            lambda j: S.add('dve', 'tensor_tensor', out=kj(j), in0=kj(j), in1=E[:, j, :N], op=ALU.mult),
            lambda j: S.add('dve', 'tensor_tensor', out=t1[:, j, :N], in0=t1[:, j, :N], in1=sg[:, j, :N], op=ALU.subtract),
            lambda j: S.add('act', 'activation', out=E[:, j, :N], in_=t1[:, j, :N], func=AF.Exp, scale=-C0),
            lambda j: S.add('dve', 'tensor_tensor', out=kap[:, j, :N], in0=kap[:, j, :N], in1=E[:, j, :N], op=ALU.mult),
        ]
        self.wavefront(steps, 3)
        self.mk('r_pre_chunk')
        T = self.Tst[l]
        PS = [slice(0, 128)] if CHL == 64 else [slice(0, CHL), slice(64, 64 + CHL)]
        msun = C[:, cp.sl('msun')]
        msln = C[:, cp.sl('msln')]
        msu = C[:, cp.sl('msu')]
        miu = C[:, cp.sl('miu')]
        id2 = C[:, cp.sl('id2')]
        npz = lambda ps: ps.stop - ps.start
        m3 = lambda m, ps: bc(m[ps, 0:CHL].unsqueeze(1), [npz(ps), 3, CHL])
        hv = lambda bk, ps: bk[ps, 0:192].rearrange("p (j t) -> p j t", j=3)[:, :, 0:CHL]
        sv = lambda t, ps: t[ps, :, 0:CHL]
        hs = lambda hh: slice(hh * 64, hh * 64 + 64)
        ph = lambda hh: slice(hh * 64, hh * 64 + CHL)
        HEADS = [(j, hh) for hh in range(2) for j in range(3)]
        def chunk_gen(ch, R):
            T = self.Tst[l]
            ts = slice(ch * CHL, (ch + 1) * CHL)
            if smp:
                T = self.Ts[ch % 2]
                swi = self.swi[ch % 2]
                if ch == 0:
                    S.add('sp', 'dma_start', dma='swi0', out=swi[:], in_=self.st_wkv[l, 0].rearrange("h v k -> v h k"))
                if ch + 1 < NCHK:
                    S.add('sp', 'dma_start', dma=f'swi{(ch + 1) % 2}', out=self.swi[(ch + 1) % 2][:],
                          in_=self.st_wkv[l, ch + 1].rearrange("h v k -> v h k"))
                bki = self.bank()
                for j in range(3):
                    S.add('pe', 'transpose', out=bki[:, j * 64:(j + 1) * 64], in_=swi[:, 2 * j:2 * j + 2, :], identity=self.ident[0:64, 0:64])
                S.add('act', 'activation', out=T[:], in_=bki[:, 0:192].rearrange("p (j v) -> p j v", j=3), func=AF.Copy)
            bkA, bkB = self.bank(), self.bank()
            for j in range(3):
                for hh in range(2):
                    S.add('pe', 'matmul', out=bkA[ph(hh), j * 64:(j + 1) * 64], lhsT=Pr[:, 6 + j, 1 + ch * CHL:1 + (ch + 1) * CHL], rhs=self.ident[:, hs(hh)], start=True, stop=True)
                    S.add('pe', 'matmul', out=bkA[ph(hh), 192 + j * 64:192 + (j + 1) * 64], lhsT=a_[:, j, ts], rhs=self.ident[:, hs(hh)], start=True, stop=True)
                    S.add('pe', 'matmul', out=bkB[ph(hh), j * 64:(j + 1) * 64], lhsT=Pr[:, 3 + j, 1 + ch * CHL:1 + (ch + 1) * CHL], rhs=self.ident[:, hs(hh)], start=True, stop=True)
            for ps in PS:
                S.add('act', 'activation', out=R['TM2'][ps], in_=bkA[ps, 0:384].rearrange("p (q f) -> p q f", f=64), func=AF.Copy)
                S.add('dve', 'tensor_copy', out=R['TMK'][ps], in_=bkB[ps, 0:192].rearrange("p (q f) -> p q f", f=64))
            Vtm = lambda j, hh: R['TM2'][ph(hh), j, :]
            Btm = lambda j, hh: R['TM2'][ph(hh), 3 + j, :]
            Ktm = lambda j, hh: R['TMK'][ph(hh), j, :]
            ts1 = slice(1 + ch * CHL, 1 + (ch + 1) * CHL)
            FMSRC = {'rt': lambda p, j: Pr[p, j, ts1], 'kkt': lambda p, j: Pr[p, 3 + j, ts1], 'v': lambda p, j: Pr[p, 6 + j, ts1],
                     'bt': lambda p, j: a_[p, j, ts], 'kt': lambda p, j: kap[p, j, ts]}
            fm = lambda t, j, hh: FMSRC[t](hs(hh), j)
            cc = lambda t, j, hh: t[ph(hh), j, 0:CHL]
            pc = lambda bk, j, hh: bk[ph(hh), j * 64:j * 64 + CHL]
            self.mk('r_after_tm')
            yield 0
            ONE = (CHL == 1)
            if ONE:
                pab, pak = self.bank(), self.bank()
            else:
                pz, py, pn, pab, pak = [self.bank() for _ in range(5)]
            for (j, hh) in HEADS:
                if not ONE:
                    S.add('pe', 'matmul', out=pc(pz, j, hh), lhsT=fm('bt', j, hh), rhs=fm('kt', j, hh), start=True, stop=True)
                    S.add('pe', 'matmul', out=pc(py, j, hh), lhsT=fm('kt', j, hh), rhs=fm('bt', j, hh), start=True, stop=True)
                    S.add('pe', 'matmul', out=pc(pn, j, hh), lhsT=fm('kkt', j, hh), rhs=fm('kt', j, hh), start=True, stop=True)
                S.add('pe', 'matmul', out=pc(pab, j, hh), lhsT=fm('bt', j, hh), rhs=fm('rt', j, hh), start=True, stop=True)
                S.add('pe', 'matmul', out=pc(pak, j, hh), lhsT=fm('kkt', j, hh), rhs=fm('rt', j, hh), start=True, stop=True)
            zy = R['ZY0']
            Zt = lambda z: z[:, 0, :, :]
            Yt = lambda z: z[:, 1, :, :]
            P = R['PP0']
            for ps in PS:
                if not ONE:
                    S.add('dve', 'tensor_tensor', out=sv(Zt(zy), ps), in0=hv(pz, ps), in1=m3(msun, ps), op=ALU.mult)
                    S.add('dve', 'tensor_tensor', out=sv(Yt(zy), ps), in0=hv(py, ps), in1=m3(msln, ps), op=ALU.mult)
                    S.add('dve', 'tensor_tensor', out=sv(R['NT'], ps), in0=hv(pn, ps), in1=m3(msu, ps), op=ALU.mult)
                S.add('dve', 'tensor_tensor', out=sv(R['AbT'], ps), in0=hv(pab, ps), in1=m3(miu, ps), op=ALU.mult)
                S.add('dve', 'tensor_tensor', out=sv(R['AkT'], ps), in0=hv(pak, ps), in1=m3(miu, ps), op=ALU.mult)
                if not ONE:
                    S.add('dve', 'tensor_tensor', out=sv(P, ps), in0=sv(Zt(zy), ps), in1=m3(id2, ps), op=ALU.add)
            self.mk('r_after_masks')
            yield 0
            cur = 0
            for lv in range(1, cfg.lv):
                zn = R['ZY%d' % (1 - cur)]
                zo = R['ZY%d' % cur]
                pyb = self.bank()
                for (j, hh) in HEADS:
                    S.add('pe', 'matmul', out=pc(pyb, j, hh), lhsT=cc(Zt(zo), j, hh), rhs=cc(Yt(zo), j, hh), start=True, stop=True)
                for ps in PS:
                    S.add('act', 'activation', out=sv(Yt(zn), ps), in_=hv(pyb, ps), func=AF.Copy)
                if lv < cfg.lv - 1:
                    pzb = self.bank()
                    for (j, hh) in HEADS:
                        S.add('pe', 'matmul', out=pc(pzb, j, hh), lhsT=cc(Yt(zo), j, hh), rhs=cc(Zt(zo), j, hh), start=True, stop=True)
                    for ps in PS:
                        S.add('act', 'activation', out=sv(Zt(zn), ps), in_=hv(pzb, ps), func=AF.Copy)
                yield 0
                ppb = self.bank()
                Pn = R['PP1'] if P is R['PP0'] else R['PP0']
                for (j, hh) in HEADS:
                    S.add('pe', 'matmul', out=pc(ppb, j, hh), lhsT=cc(Yt(zn), j, hh), rhs=cc(P, j, hh), start=True, stop=True)
                for ps in PS:
                    S.add('dve', 'tensor_tensor', out=sv(Pn, ps), in0=hv(ppb, ps), in1=sv(P, ps), op=ALU.add)
                P = Pn
                cur = 1 - cur
                yield 0
            self.mk('r_after_inv')
            yield 'SD'
            S.add('dve', 'tensor_tensor', out=self.T0f[:], in0=T[:], in1=bc(self.eM[:, :, ch:ch + 1], [128, 3, 64]), op=ALU.mult)
            T0h = lambda j, hh: self.T0f[hs(hh), j, :]
            pv = lambda bk, j, hh: bk[ph(hh), j * 64:(j + 1) * 64]
            px = self.bank()
            for (j, hh) in HEADS:
                if not ONE:
                    S.add('pe', 'matmul', out=pv(px, j, hh), lhsT=cc(R['NT'], j, hh), rhs=Vtm(j, hh), start=True, stop=False)
                S.add('pe', 'matmul', out=pv(px, j, hh), lhsT=fm('kt', j, hh), rhs=T0h(j, hh), start=ONE, stop=True)
            if ONE:
                for ps in PS:
                    S.add('act', 'activation', out=R['Ub'][ps], in_=px[ps, 0:192].rearrange("p (j v) -> p j v", j=3), func=AF.Copy, scale=-1.0)
            else:
                for ps in PS:
                    S.add('act', 'activation', out=R['Xb'][ps], in_=px[ps, 0:192].rearrange("p (j v) -> p j v", j=3), func=AF.Copy)
                self.mk('r_after_x')
                pu = self.bank()
                for (j, hh) in HEADS:
                    S.add('pe', 'matmul', out=pv(pu, j, hh), lhsT=cc(P, j, hh), rhs=R['Xb'][ph(hh), j, :], start=True, stop=True)
                for ps in PS:
                    S.add('act', 'activation', out=R['Ub'][ps], in_=pu[ps, 0:192].rearrange("p (j v) -> p j v", j=3), func=AF.Copy, scale=-1.0)
            self.mk('r_after_u')
            po = self.bank()
            if ONE:
                po2 = self.bank()
                for (j, hh) in HEADS:
                    S.add('pe', 'matmul', out=po[hs(hh), j * CHL:(j + 1) * CHL], lhsT=T0h(j, hh), rhs=fm('rt', j, hh), start=True, stop=True)
                for (j, hh) in HEADS:
                    o_ap = po2[hs(hh), j * CHL:(j + 1) * CHL]
                    S.add('pe', 'matmul', out=o_ap, lhsT=R['Ub'][ph(hh), j, :], rhs=cc(R['AbT'], j, hh), start=True, stop=False)
                    S.add('pe', 'matmul', out=o_ap, lhsT=Vtm(j, hh), rhs=cc(R['AkT'], j, hh), start=False, stop=True)
                S.add('act', 'activation', out=self.O[:, :, ts], in_=po[:, 0:3 * CHL].rearrange("p (j t) -> p j t", j=3), func=AF.Copy)
                S.add('dve', 'tensor_tensor', out=self.O[:, :, ts], in0=self.O[:, :, ts], in1=po2[:, 0:3 * CHL].rearrange("p (j t) -> p j t", j=3), op=ALU.add)
            else:
                for (j, hh) in HEADS:
                    o_ap = po[hs(hh), j * CHL:(j + 1) * CHL]
                    S.add('pe', 'matmul', out=o_ap, lhsT=T0h(j, hh), rhs=fm('rt', j, hh), start=True, stop=False)
                    S.add('pe', 'matmul', out=o_ap, lhsT=R['Ub'][ph(hh), j, :], rhs=cc(R['AbT'], j, hh), start=False, stop=False)
                    S.add('pe', 'matmul', out=o_ap, lhsT=Vtm(j, hh), rhs=cc(R['AkT'], j, hh), start=False, stop=True)
                S.add('act', 'activation', out=self.O[:, :, ts], in_=po[:, 0:3 * CHL].rearrange("p (j t) -> p j t", j=3), func=AF.Copy)
            self.mk('r_after_o')
            pt = self.bank()
            for (j, hh) in HEADS:
                t_ap = pt[hs(hh), j * 64:(j + 1) * 64]
                S.add('pe', 'matmul', out=t_ap, lhsT=Btm(j, hh), rhs=R['Ub'][ph(hh), j, :], start=True, stop=False)
                S.add('pe', 'matmul', out=t_ap, lhsT=Ktm(j, hh), rhs=Vtm(j, hh), start=False, stop=True)
            S.add('dve', 'tensor_tensor', out=self.T0f[:], in0=self.T0f[:], in1=pt[:, 0:192].rearrange("p (j v) -> p j v", j=3), op=ALU.add)
            S.add('dve', 'tensor_tensor', out=T[:], in0=self.T0f[:], in1=bc(self.eCM[:, :, ch:ch + 1], [128, 3, 64]), op=ALU.mult)
            if smp:
                wk = self.wkT[ch % 2]
                bko = self.bank()
                for j in range(3):
                    S.add('pe', 'transpose', out=bko[0:64, j * 128:(j + 1) * 128], in_=T[:, j, :], identity=self.ident)
                S.add('act', 'activation', out=wk[:], in_=bko[0:64, 0:384].rearrange("p (j f) -> p j f", j=3), func=AF.Copy)
                S.add('sp', 'dma_start', dma='swo0', out=self.o_swkv[l, ch].rearrange("h v k -> v h k"),
                      in_=wk[:].rearrange("p j (hh k) -> p (j hh) k", hh=2))

        GRP = 2 if (CHL == 64 and not smp) else 1
        for c0 in range(0, NCHK, GRP):
            gens = [chunk_gen(c0 + i, self.R if i == 0 else self.R2) for i in range(min(GRP, NCHK - c0))]
            live = list(gens)
            while live:
                nxt = []
                for gq in live:
                    if next(gq) != 'SD':
                        nxt.append(gq)
                live = nxt
            for gq in gens:
                for _ in gq:
                    pass
        self.mk('r_post')
        O = self.O
        pst = {}

        def p_mm(key, rhs_fn, lhsT):
            def f(j):
                pst[(key, j)] = self.bank()
                S.add('pe', 'matmul', out=pst[(key, j)][:, 0:N], lhsT=lhsT(j), rhs=rhs_fn(j), start=True, stop=True)
            return f
        psteps = [
            lambda j: S.add('act', 'activation', out=rkbf[:, j, :N], in_=O[:, j, :N], func=AF.Copy),
            p_mm('m', lambda j: rkbf[:, j, :N], lambda j: self.bmean_bf[:]),
            lambda j: S.add('dve', 'tensor_tensor', out=O[:, j, :N], in0=O[:, j, :N], in1=pst[('m', j)][:, 0:N], op=ALU.subtract),
            lambda j: S.add('act', 'activation', out=rkbf[:, j, :N], in_=O[:, j, :N], func=AF.Square),
            p_mm('v', lambda j: rkbf[:, j, :N], lambda j: self.bmean_bf[:]),
            lambda j: S.add('act', 'activation', out=t1[:, j, :N], in_=pst[('v', j)][:, 0:N], func=AF.Ln, bias=self.epsc[GN_EPS][:]),
            lambda j: S.add('act', 'activation', out=t1[:, j, :N], in_=t1[:, j, :N], func=AF.Exp, scale=-0.5),
            lambda j: S.add('dve', 'tensor_tensor', out=O[:, j, :N], in0=O[:, j, :N], in1=t1[:, j, :N], op=ALU.mult),
            lambda j: S.add('dve', 'tensor_scalar', out=O[:, j, :N], in0=O[:, j, :N], scalar1=cs('lnw')[:, j:j + 1], scalar2=cs('lnb')[:, j:j + 1],
                            op0=ALU.mult, op1=ALU.add),
            lambda j: S.add('dve', 'tensor_tensor', out=O[:, j, :N], in0=O[:, j, :N], in1=bonus[:, j, :N], op=ALU.add),
            p_mm('g', lambda j: self.sgg[:, :N], lambda j: self.SM[:, 384 + j * 128:384 + (j + 1) * 128]),
            lambda j: S.add('dve', 'tensor_tensor', out=self.mix[:, j, :N], in0=O[:, j, :N], in1=pst[('g', j)][:, 0:N], op=ALU.mult),
        ]
        self.wavefront(psteps, 3)

    def hgrn(self, l, cfg):
        S, cp = self.S, self.cp
        N, CHL, NCHK = cfg.n, cfg.ch, cfg.nch
        Pr = self.Pr
        C = self.C
        q32, lf, kin, cum, E, sil = self.A
        qt, kkt, vbf, tmpb = self.B[0], self.B[1], self.B[2], self.B[3]
        oml = lambda j: self.oml[:, j, l:l + 1]
        noml = lambda j: self.noml[:, j, l:l + 1]
        lbj = lambda j: self.lb[:, j, l:l + 1]
        self.mk('h_start')
        def evac(bk, pair):
            v2 = bk[:].rearrange("p (c t) -> p c t", c=2)
            for idx, c in enumerate(pair):
                g_, j = (c - 11) // 3, (c - 11) % 3
                src = v2[:, idx, :N]
                if g_ == 0:
                    S.add('act', 'activation', out=q32[:, j, :N], in_=src, func=AF.Copy)
                elif g_ == 1:
                    S.add('act', 'activation', out=kin[:, j, :N], in_=src, func=AF.Sigmoid)
                elif g_ == 2:
                    S.add('act', 'activation', out=vbf[:, j, :N], in_=src, func=AF.Copy)
                else:
                    S.add('act', 'activation', out=sil[:, j, :N], in_=src, func=AF.Silu)
        self.project(list(range(11, 23)), N, evac)
        self.mk('h_after_proj')
        c3 = lambda t, j: t[:, j, :N].rearrange("p (c t) -> p c t", t=CHL)

        def h_scan(j):
            if CHL > 1:
                S.add('dve', 'tensor_tensor_scan', out=cum[:, j, :N], data0=self.rmask[:, :N], data1=lf[:, j, :N], initial=0.0, op0=ALU.mult, op1=ALU.add)
            else:
                S.add('dve', 'tensor_copy', out=cum[:, j, :N], in_=lf[:, j, :N])
        hsteps = [
            lambda j: S.add('act', 'activation', out=lf[:, j, :N], in_=kin[:, j, :N], func=AF.Ln, scale=oml(j), bias=lbj(j)),
            lambda j: S.add('dve', 'tensor_scalar', out=kin[:, j, :N], in0=kin[:, j, :N], scalar1=-1.0, scalar2=noml(j), op0=ALU.add, op1=ALU.mult),
            h_scan,
            lambda j: S.add('act', 'activation', out=self.eM[:, j, :NCHK], in_=c3(cum, j)[:, :, cfg.mid], func=AF.Exp),
            lambda j: S.add('act', 'activation', out=self.cmid[:, j, :NCHK], in_=c3(cum, j)[:, :, cfg.mid], func=AF.Copy),
            lambda j: S.add('dve', 'tensor_tensor', out=c3(cum, j), in0=c3(cum, j), in1=bc(self.cmid[:, j, :NCHK].unsqueeze(2), [128, NCHK, CHL]), op=ALU.subtract),
            lambda j: S.add('act', 'activation', out=E[:, j, :N], in_=cum[:, j, :N], func=AF.Exp),
            lambda j: S.add('dve', 'tensor_tensor', out=qt[:, j, :N], in0=q32[:, j, :N], in1=E[:, j, :N], op=ALU.mult),
            lambda j: S.add('act', 'activation', out=self.eCM[:, j, :NCHK], in_=c3(E, j)[:, :, CHL - 1], func=AF.Copy),
            lambda j: S.add('act', 'activation', out=E[:, j, :N], in_=cum[:, j, :N], func=AF.Exp, scale=-1.0),
            lambda j: S.add('dve', 'tensor_tensor', out=kkt[:, j, :N], in0=kin[:, j, :N], in1=E[:, j, :N], op=ALU.mult),
        ]
        self.wavefront(hsteps, 3)
        self.mk('h_pre_chunk')
        Sst = self.Sst[l]
        smp = self.smp
        PS = [slice(0, 128)] if CHL == 64 else [slice(0, CHL), slice(64, 64 + CHL)]
        miu = C[:, cp.sl('miu')]
        npz = lambda ps: ps.stop - ps.start
        m3 = lambda m, ps: bc(m[ps, 0:CHL].unsqueeze(1), [npz(ps), 3, CHL])
        hv = lambda bk, ps: bk[ps, 0:192].rearrange("p (j t) -> p j t", j=3)[:, :, 0:CHL]
        sv = lambda t, ps: t[ps, :, 0:CHL]
        hs = lambda hh: slice(hh * 64, hh * 64 + 64)
        ph = lambda hh: slice(hh * 64, hh * 64 + CHL)
        HEADS = [(j, hh) for hh in range(2) for j in range(3)]
        def hchunk_gen(ch, HB):
            Sst = self.Sst[l]
            ts = slice(ch * CHL, (ch + 1) * CHL)
            if smp:
                Sst = self.Ss[ch % 2]
                for cn in ([0, 1] if ch == 0 else [ch + 1]):
                    if cn < NCHK:
                        for hh in range(2):
                            S.add('sp', 'dma_start', dma=f'shi{cn % 2}', out=self.Ss[cn % 2][hh * 64:(hh + 1) * 64, :, :],
                                  in_=self.st_hgrn[l, cn].rearrange("(j hh) k v -> hh k j v", hh=2)[hh])
            bkA = self.bank()
            for j in range(3):
                for hh in range(2):
                    S.add('pe', 'matmul', out=bkA[ph(hh), j * 64:(j + 1) * 64], lhsT=vbf[:, j, ts], rhs=self.ident_bf[:, hs(hh)], start=True, stop=True)
                    S.add('pe', 'matmul', out=bkA[ph(hh), 192 + j * 64:192 + (j + 1) * 64], lhsT=kkt[:, j, ts], rhs=self.ident_bf[:, hs(hh)], start=True, stop=True)
            for ps in PS:
                S.add('act', 'activation', out=HB['TM2'][ps], in_=bkA[ps, 0:384].rearrange("p (q f) -> p q f", f=64), func=AF.Copy)
            Vtm = lambda j, hh: HB['TM2'][ph(hh), j, :]
            Ktm = lambda j, hh: HB['TM2'][ph(hh), 3 + j, :]
            fm = lambda t, j, hh: t[hs(hh), j, ts]
            cc = lambda t, j, hh: t[ph(hh), j, 0:CHL]
            pc = lambda bk, j, hh: bk[ph(hh), j * 64:j * 64 + CHL]
            pa = self.bank()
            for (j, hh) in HEADS:
                S.add('pe', 'matmul', out=pc(pa, j, hh), lhsT=fm(kkt, j, hh), rhs=fm(qt, j, hh), start=True, stop=True)
            for ps in PS:
                S.add('dve', 'tensor_tensor', out=sv(HB['AkT'], ps), in0=hv(pa, ps), in1=m3(miu, ps), op=ALU.mult)
            yield 'SD'
            S.add('dve', 'tensor_tensor', out=self.T0f[:], in0=Sst[:], in1=bc(self.eM[:, :, ch:ch + 1], [128, 3, 64]), op=ALU.mult)
            S.add('act', 'activation', out=self.T0b[:], in_=self.T0f[:], func=AF.Copy)
            T0h = lambda j, hh: self.T0b[hs(hh), j, :]
            po = self.bank()
            if CHL == 1:
                po2 = self.bank()
                for (j, hh) in HEADS:
                    S.add('pe', 'matmul', out=po[hs(hh), j * CHL:(j + 1) * CHL], lhsT=T0h(j, hh), rhs=fm(qt, j, hh), start=True, stop=True)
                for (j, hh) in HEADS:
                    S.add('pe', 'matmul', out=po2[hs(hh), j * CHL:(j + 1) * CHL], lhsT=Vtm(j, hh), rhs=cc(HB['AkT'], j, hh), start=True, stop=True)
                S.add('act', 'activation', out=self.O[:, :, ts], in_=po[:, 0:3 * CHL].rearrange("p (j t) -> p j t", j=3), func=AF.Copy)
                S.add('dve', 'tensor_tensor', out=self.O[:, :, ts], in0=self.O[:, :, ts], in1=po2[:, 0:3 * CHL].rearrange("p (j t) -> p j t", j=3), op=ALU.add)
            else:
                for (j, hh) in HEADS:
                    o_ap = po[hs(hh), j * CHL:(j + 1) * CHL]
                    S.add('pe', 'matmul', out=o_ap, lhsT=T0h(j, hh), rhs=fm(qt, j, hh), start=True, stop=False)
                    S.add('pe', 'matmul', out=o_ap, lhsT=Vtm(j, hh), rhs=cc(HB['AkT'], j, hh), start=False, stop=True)
                S.add('act', 'activation', out=self.O[:, :, ts], in_=po[:, 0:3 * CHL].rearrange("p (j t) -> p j t", j=3), func=AF.Copy)
            pt = self.bank()
            for (j, hh) in HEADS:
                t_ap = pt[hs(hh), j * 64:(j + 1) * 64]
                S.add('pe', 'matmul', out=t_ap, lhsT=Ktm(j, hh), rhs=Vtm(j, hh), start=True, stop=True)
            S.add('dve', 'tensor_tensor', out=self.T0f[:], in0=self.T0f[:], in1=pt[:, 0:192].rearrange("p (j v) -> p j v", j=3), op=ALU.add)
            S.add('dve', 'tensor_tensor', out=Sst[:], in0=self.T0f[:], in1=bc(self.eCM[:, :, ch:ch + 1], [128, 3, 64]), op=ALU.mult)
            if smp:
                for hh in range(2):
                    S.add('sp', 'dma_start', dma=f'sho{ch % 2}', out=self.o_shgrn[l, ch].rearrange("(j hh) k v -> hh k j v", hh=2)[hh],
                          in_=Sst[hh * 64:(hh + 1) * 64, :, :])

        GRP = 2 if (CHL == 64 and not smp) else 1
        HB0 = {'TM2': self.TM2, 'AkT': self.AkT}
        for c0 in range(0, NCHK, GRP):
            gens = [hchunk_gen(c0 + i, HB0 if i == 0 else self.HB2) for i in range(min(GRP, NCHK - c0))]
            for gq in gens:
                next(gq)
            for gq in gens:
                for _ in gq:
                    pass
        self.mk('h_post')
        O = self.O
        hng = C[:, cp.sl(f"hng{l}")]
        hst_ = {}

        def hp_mm(j):
            hst_[j] = self.bank()
            S.add('pe', 'matmul', out=hst_[j][:, 0:N], lhsT=self.bmean_bf[:], rhs=tmpb[:, j, :N], start=True, stop=True)
        hpsteps = [
            lambda j: S.add('act', 'activation', out=tmpb[:, j, :N], in_=O[:, j, :N], func=AF.Square),
            hp_mm,
            lambda j: S.add('act', 'activation', out=E[:, j, :N], in_=hst_[j][:, 0:N], func=AF.Ln, bias=self.epsc[NORM_EPS][:]),
            lambda j: S.add('act', 'activation', out=E[:, j, :N], in_=E[:, j, :N], func=AF.Exp, scale=-0.5),
            lambda j: S.add('dve', 'scalar_tensor_tensor', out=O[:, j, :N], in0=O[:, j, :N], scalar=hng[:, j:j + 1], in1=E[:, j, :N], op0=ALU.mult, op1=ALU.mult),
            lambda j: S.add('dve', 'tensor_tensor', out=self.mix[:, 3 + j, :N], in0=O[:, j, :N], in1=sil[:, j, :N], op=ALU.mult),
        ]
        self.wavefront(hpsteps, 3)

    def lru(self, l, cfg, last):
        S, cp = self.S, self.cp
        N = cfg.n
        C = self.C
        xb = self.Pr
        XB = self.A[0]
        xbv = XB[:].rearrange("p j n -> p (j n)")[:, 0:2 * (N + 3)].rearrange("p (c n) -> p c n", c=2)
        xc, rr, ii, aa, uu, gt = self.A[1], self.A[2], self.A[3], self.A[4], self.A[5], self.O
        xcb = self.B[0]
        cw = C[:, cp.sl(f"cw{l}")].rearrange("p (j c) -> p j c", c=2)
        smp = self.smp
        if not smp:
            S.add('act', 'activation', out=xbv[:, :, 0:3], in_=self.cvh[l][:], func=AF.Copy)

        def evac(bk, pair):
            v2 = bk[:].rearrange("p (c t) -> p c t", c=2)
            if pair[0] == 23:
                S.add('act', 'activation', out=xbv[:, :, 3:N + 3], in_=v2[:, :, :N], func=AF.Copy)
            else:
                S.add('act', 'activation', out=gt[:, 0:2, :N], in_=v2[:, :, :N], func=AF.Copy)
        self.project([23, 24, 25, 26], N, evac)
        if not smp:
            S.add('act', 'activation', out=self.cvh[l][:], in_=xbv[:, :, N:N + 3], func=AF.Copy)
        else:
            S.add('act', 'activation', out=self.XBs[:], in_=xbv[:, :, 3:N + 3], func=AF.Copy)
        for c in range(2):
            S.add('dve', 'tensor_scalar', out=xc[:, c, :N], in0=xbv[:, c, 3:N + 3], scalar1=cw[:, 3, c:c + 1], scalar2=C[:, cp.sl(f"cb{l}")][:, c:c + 1],
                  op0=ALU.mult, op1=ALU.add)
            for jj in range(3):
                prevj = self.CV[:, jj * 2 + c, :N] if smp else xbv[:, c, jj:jj + N]
                S.add('dve', 'scalar_tensor_tensor', out=xc[:, c, :N], in0=prevj, scalar=cw[:, jj, c:c + 1], in1=xc[:, c, :N],
                      op0=ALU.mult, op1=ALU.add)
        S.add('act', 'activation', out=xcb[:, 0:2, :N], in_=xc[:, 0:2, :N], func=AF.Copy)

        def lru_chain(c):
            bk = self.bank()
            for nn in range(2):
                pr_ = slice(nn * 64, nn * 64 + 64)
                S.add('pe', 'matmul', out=bk[pr_, 0:N], lhsT=self.SM[pr_, 768 + c * 64:768 + (c + 1) * 64], rhs=xcb[pr_, c, :N], start=True, stop=True)
                S.add('pe', 'matmul', out=bk[pr_, 256:256 + N], lhsT=self.SM[pr_, 896 + c * 64:896 + (c + 1) * 64], rhs=xcb[pr_, c, :N], start=True, stop=True)
            S.add('act', 'activation', out=rr[:, c, :N], in_=bk[:, 0:N], func=AF.Sigmoid, bias=C[:, cp.sl(f"ba{l}")][:, c:c + 1])
            S.add('act', 'activation', out=ii[:, c, :N], in_=bk[:, 256:256 + N], func=AF.Sigmoid, bias=C[:, cp.sl(f"bx{l}")][:, c:c + 1])
            S.add('act', 'activation', out=aa[:, c, :N], in_=rr[:, c, :N], func=AF.Exp, scale=self.c8[:, l, c:c + 1])
            S.add('act', 'activation', out=rr[:, c, :N], in_=rr[:, c, :N], func=AF.Exp, scale=self.c16[:, l, c:c + 1])
            S.add('dve', 'tensor_scalar', out=rr[:, c, :N], in0=rr[:, c, :N], scalar1=-1.0, scalar2=1.0, op0=ALU.mult, op1=ALU.add)
            S.add('dve', 'tensor_scalar', out=rr[:, c, :N], in0=rr[:, c, :N], scalar1=1e-12, scalar2=None, op0=ALU.max)
            S.add('act', 'activation', out=rr[:, c, :N], in_=rr[:, c, :N], func=AF.Sqrt)
            S.add('dve', 'tensor_tensor', out=uu[:, c, :N], in0=xc[:, c, :N], in1=ii[:, c, :N], op=ALU.mult)
            S.add('dve', 'tensor_tensor', out=uu[:, c, :N], in0=uu[:, c, :N], in1=rr[:, c, :N], op=ALU.mult)
            if smp:
                S.add('dve', 'tensor_tensor', out=ii[:, c, :N], in0=aa[:, c, :N], in1=self.H0[:, c, :N], op=ALU.mult)
                S.add('dve', 'tensor_tensor', out=ii[:, c, :N], in0=ii[:, c, :N], in1=uu[:, c, :N], op=ALU.add)
                S.add('act', 'activation', out=self.H0[:, c, :N], in_=ii[:, c, :N], func=AF.Copy)
            else:
                S.add('dve', 'tensor_tensor_scan', out=ii[:, c, :N], data0=aa[:, c, :N], data1=uu[:, c, :N], initial=self.hst[l][:, c, :], op0=ALU.mult, op1=ALU.add)
                S.add('act', 'activation', out=self.hst[l][:, c, :], in_=ii[:, c, N - 1:N], func=AF.Copy)
            S.add('act', 'activation', out=uu[:, c, :N], in_=gt[:, c, :N], func=AF.Square)
            S.add('dve', 'tensor_scalar', out=uu[:, c, :N], in0=uu[:, c, :N], scalar1=0.044715, scalar2=1.0, op0=ALU.mult, op1=ALU.add)
            S.add('dve', 'tensor_tensor', out=uu[:, c, :N], in0=uu[:, c, :N], in1=gt[:, c, :N], op=ALU.mult)
            S.add('act', 'activation', out=uu[:, c, :N], in_=uu[:, c, :N], func=AF.Sigmoid, scale=1.5957691216057308)
            S.add('dve', 'tensor_tensor', out=uu[:, c, :N], in0=uu[:, c, :N], in1=gt[:, c, :N], op=ALU.mult)
            S.add('dve', 'tensor_tensor', out=self.mix[:, 6 + c, :N], in0=uu[:, c, :N], in1=ii[:, c, :N], op=ALU.mult)
        self.interleave([lambda c=c: lru_chain(c) for c in range(2)])

    def alloc_b(self):
        self.sb_off = self.work_base
        sb = self.sb
        u = self.nm("b")
        self.xn2 = sb(f"xn2{u}", [128, KD, TPP], BF16)
        self.nsqs = [sb(f"nsq{i}{u}", [128, KD, TT], BF16) for i in range(NTILE)]
        self.nlns = [sb(f"nln{i}{u}", [128, TT]) for i in range(NTILE)]
        self.nrss = [sb(f"nrs{i}{u}", [128, TT]) for i in range(NTILE)]
        self.nsq, self.nln, self.nrs = self.nsqs[0], self.nlns[0], self.nrss[0]
        self.w1g = [sb(f"w1g{i}{u}", [128, KD, FG], BF16) for i in range(2)]
        self.w2g = [sb(f"w2g{i}{u}", [128, FG // 128, D], BF16) for i in range(2)]
        self.hb = [sb(f"hb{i}{u}", [128, FG // 128, 512], BF16) for i in range(2)]
        self.xn2s = sb(f"xn2s{u}", [128, KD, 16], BF16)
        self.hbs = sb(f"hbs{u}", [128, FG // 128, 16], BF16)
        self.rls = sb(f"rls{u}", [128, FG // 128, 16])
        self.rl = [sb(f"rl{i}{u}", [128, 512]) for i in range(2)]

    def phase_b(self, l, nxt_loads=()):
        S, cp = self.S, self.cp
        g2 = self.C[:, cp.sl(f"g2{l}")]
        def load_group(g):
            s = g % 2
            S.add('pool', 'dma_start', dma=f'w1g{s}', out=self.w1g[s][:], in_=self.w1[l][:, g * FG:(g + 1) * FG].rearrange("(k p) c -> p k c", p=128))
            S.add('pool', 'dma_start', dma=f'w2g{s}', out=self.w2g[s][:], in_=self.w2[l][g * FG:(g + 1) * FG, :].rearrange("(k p) c -> p k c", p=128))
        nxt_loads = list(nxt_loads)
        load_group(0)
        load_group(1)
        if nxt_loads:
            nxt_loads.pop(0)()
        self.interleave([lambda tt=tt: self.rmsnorm_into(self.x[tt], g2, tt) for tt in range(NTILE)])
        nmt = TPP // 512
        do_s = self.do_smp and self.cur_ps == self.npass - 1
        if do_s:
            self.rmsnorm(self.xs, g2, self.xn2s, 16)

        def sample_group(g):
            s_ = g % 2
            bk = self.bank()
            for c in range(FG // 128):
                for k in range(KD):
                    S.add('pe', 'matmul', out=bk[:, c * 16:(c + 1) * 16], lhsT=self.w1g[s_][:, k, c * 128:(c + 1) * 128], rhs=self.xn2s[:, k, :],
                          start=(k == 0), stop=(k == KD - 1))
            S.add('act', 'activation', out=self.rls[:], in_=bk[:, 0:(FG // 128) * 16].rearrange("p (c t) -> p c t", t=16), func=AF.Relu)
            S.add('pool', 'tensor_tensor', out=self.hbs[:], in0=self.rls[:], in1=self.rls[:], op=ALU.mult)
            bk = self.bank()
            for c in range(KD):
                for k in range(FG // 128):
                    S.add('pe', 'matmul', out=bk[:, c * 16:(c + 1) * 16], lhsT=self.w2g[s_][:, k, c * 128:(c + 1) * 128], rhs=self.hbs[:, k, :],
                          start=(k == 0), stop=(k == FG // 128 - 1))
            S.add('dve', 'tensor_tensor', out=self.xs[:], in0=self.xs[:], in1=bk[:, 0:KD * 16].rearrange("p (c t) -> p c t", t=16), op=ALU.add)

        def stage1(u, g, mt):
            s_ = g % 2
            hb = self.hb[u % 2]
            for c in range(FG // 128):
                bk = self.bank()
                for k in range(KD):
                    S.add('pe', 'matmul', out=bk[:], lhsT=self.w1g[s_][:, k, c * 128:(c + 1) * 128], rhs=self.xn2[:, k, mt * 512:(mt + 1) * 512],
                          start=(k == 0), stop=(k == KD - 1))
                rl = self.rl[c % 2]
                S.add('act', 'activation', out=rl[:], in_=bk[:], func=AF.Relu)
                S.add('pool', 'tensor_tensor', out=hb[:, c, :], in0=rl[:], in1=rl[:], op=ALU.mult)

        def stage2(u, g, mt):
            s_ = g % 2
            hb = self.hb[u % 2]
            for c in range(KD):
                bk = self.bank()
                for k in range(FG // 128):
                    S.add('pe', 'matmul', out=bk[:], lhsT=self.w2g[s_][:, k, c * 128:(c + 1) * 128], rhs=hb[:, k, :],
                          start=(k == 0), stop=(k == FG // 128 - 1))
                for hh in range(512 // TT):
                    xt = self.x[mt * (512 // TT) + hh]
                    S.add('dve', 'tensor_tensor', out=xt[:, c, :], in0=xt[:, c, :], in1=bk[:, hh * TT:(hh + 1) * TT], op=ALU.add)

        units = [(g, mt) for g in range(NFG) for mt in range(nmt)]
        stage1(0, *units[0])
        for u, (g, mt) in enumerate(units):
            if u + 1 < len(units):
                stage1(u + 1, *units[u + 1])
            stage2(u, g, mt)
            if mt == nmt - 1:
                if do_s:
                    sample_group(g)
                if g + 2 < NFG:
                    load_group(g + 2)
                if nxt_loads:
                    nxt_loads.pop(0)()
        for t in nxt_loads:
            t()

    def rmsnorm_into(self, xt, g, tt):
        S = self.S
        N = TT
        nsq, nln, nrs = self.nsqs[tt], self.nlns[tt], self.nrss[tt]
        S.add('act', 'activation', out=nsq[:], in_=xt[:], func=AF.Square)
        bk = self.bank()
        for k in range(KD):
            S.add('pe', 'matmul', out=bk[:, :N], lhsT=self.ones_bf[:], rhs=nsq[:, k, :], start=(k == 0), stop=(k == KD - 1))
        S.add('act', 'activation', out=nln[:], in_=bk[:, :N], func=AF.Ln, scale=1.0 / D, bias=self.epsc[NORM_EPS][:])
        S.add('act', 'activation', out=nrs[:], in_=nln[:], func=AF.Exp, scale=-0.5)
        for k in range(KD):
            S.add('dve', 'scalar_tensor_tensor', out=self.xn2[:, k, tt * TT:(tt + 1) * TT], in0=xt[:, k, :], scalar=g[:, k:k + 1], in1=nrs[:],
                  op0=ALU.mult, op1=ALU.mult)

    def store_y(self, ps):
        S, cp = self.S, self.cp
        g = self.C[:, cp.sl('fg')]
        for tt in range(NTILE):
            self.rmsnorm(self.x[tt], g, self.yfm, TT)
            for tb in range(TT // 128):
                ob = self.xtm[tb % 2]
                for kh in range(2):
                    bk = self.bank()
                    for kk in range(4):
                        k = kh * 4 + kk
                        S.add('pe', 'transpose', out=bk[:, kk * 128:(kk + 1) * 128],
                              in_=self.yfm[:, k, tb * 128:(tb + 1) * 128], identity=self.ident)
                    S.add('act', 'activation', out=ob[:, kh * 512:(kh + 1) * 512], in_=bk[:], func=AF.Copy)
                t0 = ps * TPP + tt * TT + tb * 128
                S.add('sp', 'dma_start', dma=f'ytm{tb % 2}', out=self.y_p[t0:t0 + 128, :], in_=ob[:])

    def store_states(self):
        S, nc = self.S, self.nc
        self.sb_off = self.work_base
        wk = self.sb("wkvT", [64, DEPTH, 3, 128])
        with nc.allow_non_contiguous_dma(reason="small strided state outputs"):
            S.add('sp', 'dma_start', dma='o_small', allow_slow_non_contiguous=True, out=self.o_pshift.rearrange("l (k p) -> p l k", p=128), in_=self.shl[:])
            for l in range(DEPTH):
                bk = self.bank()
                for j in range(3):
                    S.add('pe', 'transpose', out=bk[0:64, j * 128:(j + 1) * 128], in_=self.Tst[l][:, j, :], identity=self.ident)
                S.add('act', 'activation', out=wk[:, l, :, :], in_=bk[0:64, 0:384].rearrange("p (j f) -> p j f", j=3), func=AF.Copy)
                S.add('sp', 'dma_start', dma=f'o_wkv{l}', allow_slow_non_contiguous=True, out=self.o_pwkv[l].rearrange("h v k -> v h k"),
                      in_=wk[:, l, :, :].rearrange("p j (hh k) -> p (j hh) k", hh=2))
                for hh in range(2):
                    S.add('sp', 'dma_start', dma='o_small', allow_slow_non_contiguous=True, out=self.o_phgrn[l].rearrange("(j hh) k v -> hh k j v", hh=2)[hh],
                          in_=self.Sst[l][hh * 64:(hh + 1) * 64, :, :])
                S.add('sp', 'dma_start', dma='o_small', allow_slow_non_contiguous=True, out=self.o_plru[l].rearrange("(c p) -> p c", p=128), in_=self.hst[l][:, :, 0])
                for c in range(2):
                    S.add('sp', 'dma_start', dma='o_small', allow_slow_non_contiguous=True, out=self.o_pconv[l][:, c * 128:(c + 1) * 128].rearrange("j p -> p j"), in_=self.cvh[l][:, c, :])


def kernel(**inputs):
    inp = {k: np.asarray(v) for k, v in inputs.items()}
    import os
    b = Builder(nlayers=int(os.environ.get("K_NL", DEPTH)), npass=int(os.environ.get("K_NP", NPASS)),
                do_mlp=os.environ.get("K_MLP", "1") == "1", do_smp=os.environ.get("K_SMP", "1") == "1")
    nc = b.build()
    cpk = pack_consts(inp, b.cp)
    smat = pack_small_mats(inp)
    in_maps = []
    for c in range(NCORES):
        in_maps.append({
            "x_prompt": np.ascontiguousarray(inp['x_prompt'][c]),
            "x_sample": np.ascontiguousarray(inp['x_sample'][c * 16:(c + 1) * 16, 0]),
            "state_wkv": np.ascontiguousarray(inp['state_wkv'][:, c * 16:(c + 1) * 16]),
            "state_shift": np.ascontiguousarray(inp['state_shift'][:, c * 16:(c + 1) * 16]),
            "state_hgrn": np.ascontiguousarray(inp['state_hgrn'][:, c * 16:(c + 1) * 16]),
            "state_lru": np.ascontiguousarray(inp['state_lru'][:, c * 16:(c + 1) * 16]),
            "state_conv": np.ascontiguousarray(inp['state_conv'][:, c * 16:(c + 1) * 16]),
            "cpack": cpk, "smat": smat,
            "w_in": inp['w_in'][:b.nlayers], "w_out": inp['w_out'][:b.nlayers],
            "mlp_w1": inp['mlp_w1'][:(b.nlayers if b.do_mlp else 1)], "mlp_w2": inp['mlp_w2'][:(b.nlayers if b.do_mlp else 1)],
        })
    ncr = int(os.environ.get("K_CORES", NCORES))
    res = run_bass_kernel_spmd(nc, in_maps[:ncr], core_ids=list(range(ncr)))
    R = list(res.results) + [res.results[0]] * (NCORES - ncr)
    print("total ops", len(b.S.ops))
    y_prompt = np.stack([R[c]["y_prompt"] for c in range(NCORES)], 0)
    p_shift = np.stack([R[c]["p_shift"] for c in range(NCORES)], 1)
    p_wkv = np.stack([R[c]["p_wkv"] for c in range(NCORES)], 1)
    p_hgrn = np.stack([R[c]["p_hgrn"] for c in range(NCORES)], 1)
    p_lru = np.stack([R[c]["p_lru"] for c in range(NCORES)], 1)
    p_conv = np.stack([R[c]["p_conv"] for c in range(NCORES)], 1)
    cat = lambda n, ax: np.concatenate([R[c][n] for c in range(NCORES)], ax)
    y_sample = cat("y_sample", 0)[:, None, :]
    return (y_prompt, y_sample, p_wkv, p_shift, p_hgrn, p_lru, p_conv,
            cat("s_wkv", 1), cat("s_shift", 1), cat("s_hgrn", 1), cat("s_lru", 1), cat("s_conv", 1))
```

```python
import contextlib
import numpy as np
import concourse.bass as bass
import concourse.mybir as mybir
from concourse.bass_utils import run_bass_kernel_spmd

F32 = mybir.dt.float32
BF16 = mybir.dt.bfloat16
AF = mybir.ActivationFunctionType
ALU = mybir.AluOpType

NCORES = 8
D = 1024
KD = 8
SEQ = 2048
DEPTH = 4
TT = 256
NPASS = 2
TPP = SEQ // NPASS
NTILE = TPP // TT
CH = 64
NCH = TT // CH
MID = CH // 2 - 1
C_IN = 3456
NCT_IN = 27
DFF = 4096
FG = 512
NFG = DFF // FG
NORM_EPS = 1e-6
GN_EPS = 64e-5
C0 = float(np.exp(-0.5))

ENGS = ['pe', 'act', 'dve', 'pool', 'sp']


class Sched:
    def __init__(self, nc):
        self.nc = nc
        self.ops = []
        self.per_eng = {e: [] for e in ENGS}
        self.last_w = {}
        self.readers = {}
        self.synced = {e: {} for e in ENGS}
        self.dma_cum = {}
        self.pe_mode = None
        self.pe_last = None

    @staticmethod
    def _keys(aps):
        ks = []
        for a in aps:
            if a is None or isinstance(a, (int, float)):
                continue
            if isinstance(a, (str, tuple)):
                ks.append(a)
                continue
            t = getattr(a, 'tensor', None)
            if t is None or type(t).__name__ != 'SBTensorHandle' or len(t.shape) < 3 or a.dtype != t.dtype:
                ks.append(a.name)
                continue
            shp = [int(v) for v in t.shape]
            row = int(np.prod(shp[1:]))
            bs = int(np.prod(shp[2:]))
            lo = int(a.offset) % row
            hi = lo
            for step, cnt in list(a.ap)[1:]:
                if step > 0:
                    hi += (int(cnt) - 1) * int(step)
            for b in range(lo // bs, min(hi // bs, shp[1] - 1) + 1):
                ks.append((a.name, b))
        return ks

    maxops = None

    def add(self, eng, method, r=(), w=(), dma=None, **kw):
        if self.maxops is not None and len(self.ops) >= self.maxops:
            return None
        idx = len(self.ops)
        r = list(r)
        w = list(w)
        for kn, v in kw.items():
            if hasattr(v, 'name') and hasattr(v, 'ap'):
                (w if kn in ('out', 'ap', 'accum_out') else r).append(v)
        rk = self._keys(r)
        wk = self._keys(w)
        deps = set()
        for k in rk:
            if k in self.last_w:
                deps.add(self.last_w[k])
        for k in wk:
            if k in self.last_w:
                deps.add(self.last_w[k])
            deps.update(self.readers.get(k, {}).values())
        waits = []
        if eng == 'pe' and method in ('matmul', 'transpose'):
            st_ap = kw['lhsT'] if method == 'matmul' else kw['in_']
            shp = list(st_ap.shape)
            rkk = 32 if shp[0] <= 32 else (64 if shp[0] <= 64 else 128)
            mfree = int(np.prod(shp[1:]))
            rm = 32 if mfree <= 32 else (64 if mfree <= 64 else 128)
            mode = (rkk, rm, method, str(st_ap.dtype))
            if getattr(self, 'pe_mode', None) is not None and mode != self.pe_mode and self.pe_last is not None:
                p = self.ops[self.pe_last]
                p['inc'] = True
                waits.append(('eng', self.pe_last))
            self.pe_mode = mode
        for d in sorted(deps):
            p = self.ops[d]
            if p['dma'] is not None:
                key = ('dma', p['dma'])
                if self.synced[eng].get(key, 0) >= p['cum']:
                    continue
                self.synced[eng][key] = p['cum']
                waits.append(('dma', p['dma'], p['cum']))
            else:
                if p['eng'] == 'pe' and eng == 'pe':
                    continue
                if self.synced[eng].get(p['eng'], -1) >= p['seq']:
                    continue
                self.synced[eng][p['eng']] = p['seq']
                p['inc'] = True
                waits.append(('eng', d))
        o = dict(eng=eng, method=method, kw=kw, waits=waits, dma=dma, seq=len(self.per_eng[eng]),
                 inc=False, cum=None, tick=None)
        if dma is not None:
            self.dma_cum[dma] = self.dma_cum.get(dma, 0) + 16
            o['cum'] = self.dma_cum[dma]
        self.ops.append(o)
        self.per_eng[eng].append(idx)
        if eng == 'pe':
            self.pe_last = idx
        for k in wk:
            self.last_w[k] = idx
            self.readers[k] = {}
        rtag = ('dma', dma) if dma is not None else eng
        for k in rk:
            self.readers.setdefault(k, {})[rtag] = idx
        return idx

    def barrier(self):
        last_real = {}
        for f in ENGS:
            j = len(self.per_eng[f]) - 1
            while j >= 0 and (self.ops[self.per_eng[f][j]]['dma'] is not None
                              or self.ops[self.per_eng[f][j]]['method'] is None):
                j -= 1
            if j >= 0:
                last_real[f] = self.per_eng[f][j]
        cums = dict(self.dma_cum)
        for e in ENGS:
            waits = []
            for f, qi in last_real.items():
                if f == e:
                    continue
                q = self.ops[qi]
                if self.synced[e].get(f, -1) < q['seq']:
                    self.synced[e][f] = q['seq']
                    q['inc'] = True
                    waits.append(('eng', qi))
            for key, cum in cums.items():
                k2 = ('dma', key)
                if self.synced[e].get(k2, 0) < cum:
                    self.synced[e][k2] = cum
                    waits.append(('dma', key, cum))
            o = dict(eng=e, method=None, kw={}, waits=waits, dma=None, seq=len(self.per_eng[e]),
                     inc=False, cum=None, tick=None)
            self.per_eng[e].append(len(self.ops))
            self.ops.append(o)

    def emit(self, es):
        nc = self.nc
        ROT = 20000
        eng_sems = {e: [] for e in ENGS}
        for e in ENGS:
            n = 0
            for i in self.per_eng[e]:
                o = self.ops[i]
                if o['inc']:
                    n += 1
                    o['tick'] = n
            nsem = (n + ROT - 1) // ROT
            for j in range(max(nsem, 1)):
                eng_sems[e].append(es.enter_context(nc.semaphore(f"s_{e}_{j}")))
        dma_sems = {}
        for key in self.dma_cum:
            dma_sems[key] = es.enter_context(nc.semaphore(f"d_{len(dma_sems)}"))
        objs = {'pe': nc.tensor, 'act': nc.scalar, 'dve': nc.vector, 'pool': nc.gpsimd, 'sp': nc.sync}

        def tick_ref(e, tick):
            j = (tick - 1) // ROT
            return eng_sems[e][j], tick - j * ROT

        def run(e, eng):
            for i in self.per_eng[e]:
                o = self.ops[i]
                for wt in o['waits']:
                    if wt[0] == 'dma':
                        eng.wait_ge(dma_sems[wt[1]], wt[2])
                    else:
                        p = self.ops[wt[1]]
                        s, v = tick_ref(p['eng'], p['tick'])
                        eng.wait_ge(s, v)
                if o['method'] is None:
                    continue
                ins = getattr(eng, o['method'])(**o['kw'])
                if o['dma'] is not None:
                    ins.then_inc(dma_sems[o['dma']], 16)
                elif o['inc']:
                    s, v = tick_ref(e, o['tick'])
                    ins.then_inc(s, 1)

        with nc.Block() as block:
            @block.tensor
            def _(eng):
                run('pe', eng)

            @block.scalar
            def _(eng):
                run('act', eng)

            @block.vector
            def _(eng):
                run('dve', eng)

            @block.gpsimd
            def _(eng):
                run('pool', eng)

            @block.sync
            def _(eng):
                run('sp', eng)


def _fm(v, nt):
    return np.ascontiguousarray(np.asarray(v, np.float32).reshape(nt, 128).T)


class CP:
    def __init__(self):
        self.cols = {}
        self.n = 0

    def alloc(self, name, w):
        self.cols[name] = (self.n, w)
        self.n += w

    def sl(self, name):
        a, w = self.cols[name]
        return slice(a, a + w)


def make_cp():
    cp = CP()
    for l in range(DEPTH):
        for nm, w in [('g1', 8), ('g2', 8), ('mu', 11), ('w0', 3), ('a0', 3), ('kk', 3), ('ka', 3), ('rk', 3),
                      ('lnw', 3), ('lnb', 3), ('hng', 3), ('cw', 8), ('cb', 2), ('ba', 2), ('bx', 2), ('lam', 2)]:
            cp.alloc(f"{nm}{l}", w)
    cp.alloc('hlb', 12)
    cp.alloc('fg', 8)
    cp.alloc('ident', 128)
    cp.alloc('msu', 64)
    cp.alloc('msl', 64)
    cp.alloc('miu', 64)
    cp.alloc('msun', 64)
    cp.alloc('msln', 64)
    cp.alloc('id2', 64)
    return cp


def pack_consts(inp, cp):
    A = np.zeros((128, cp.n), np.float32)
    for l in range(DEPTH):
        A[:, cp.sl(f"g1{l}")] = _fm(inp['norm1_g'][l], 8)
        A[:, cp.sl(f"g2{l}")] = _fm(inp['norm2_g'][l], 8)
        A[:, cp.sl(f"mu{l}")] = _fm(inp['mu_shift'][l], 11)
        A[:, cp.sl(f"w0{l}")] = _fm(inp['rwkv_w0'][l], 3)
        A[:, cp.sl(f"a0{l}")] = _fm(inp['rwkv_a0'][l], 3)
        A[:, cp.sl(f"kk{l}")] = _fm(inp['rwkv_k_k'][l], 3)
        A[:, cp.sl(f"ka{l}")] = _fm(inp['rwkv_k_a'][l], 3)
        A[:, cp.sl(f"rk{l}")] = _fm(inp['rwkv_r_k'][l].reshape(-1), 3)
        A[:, cp.sl(f"lnw{l}")] = _fm(inp['rwkv_ln_w'][l], 3)
        A[:, cp.sl(f"lnb{l}")] = _fm(inp['rwkv_ln_b'][l], 3)
        A[:, cp.sl(f"hng{l}")] = _fm(inp['hgrn_norm_g'][l], 3)
        cw = np.stack([_fm(inp['lru_conv_w'][l, j], 2) for j in range(4)], axis=1)
        A[:, cp.sl(f"cw{l}")] = cw.reshape(128, 8)
        A[:, cp.sl(f"cb{l}")] = _fm(inp['lru_conv_b'][l], 2)
        A[:, cp.sl(f"ba{l}")] = _fm(inp['lru_ba'][l], 2)
        A[:, cp.sl(f"bx{l}")] = _fm(inp['lru_bx'][l], 2)
        A[:, cp.sl(f"lam{l}")] = _fm(inp['lru_lambda'][l], 2)
    hl = np.stack([_fm(inp['hgrn_lb'][l], 3) for l in range(DEPTH)], axis=2)
    A[:, cp.sl('hlb')] = hl.reshape(128, 12)
    A[:, cp.sl('fg')] = _fm(inp['final_g'], 8)
    A[:, cp.sl('ident')] = np.eye(128, dtype=np.float32)
    r = np.arange(64)
    for hb in (0, 64):
        A[hb:hb + 64, cp.sl('msu')] = (r[:, None] < r[None, :]).astype(np.float32)
        A[hb:hb + 64, cp.sl('msl')] = (r[:, None] > r[None, :]).astype(np.float32)
        A[hb:hb + 64, cp.sl('miu')] = (r[:, None] <= r[None, :]).astype(np.float32)
        A[hb:hb + 64, cp.sl('msun')] = -(r[:, None] < r[None, :]).astype(np.float32)
        A[hb:hb + 64, cp.sl('msln')] = -(r[:, None] > r[None, :]).astype(np.float32)
        A[hb:hb + 64, cp.sl('id2')] = np.eye(64, dtype=np.float32)
    return A


def pack_small_mats(inp):
    W = np.zeros((128, DEPTH, 384 * 2 + 256), np.float32)
    for l in range(DEPTH):
        W[:64, l, 0:384] = inp['rwkv_w_up'][l]
        W[64:, l, 0:384] = inp['rwkv_a_up'][l]
        W[:, l, 384:768] = inp['rwkv_g_up'][l]
        wa = np.asarray(inp['lru_wa'][l]).reshape(2, 2, 64, 64)
        wx = np.asarray(inp['lru_wx'][l]).reshape(2, 2, 64, 64)
        for t in range(2):
            W[:, l, 768 + t * 64: 768 + (t + 1) * 64] = wa[t].reshape(128, 64)
            W[:, l, 896 + t * 64: 896 + (t + 1) * 64] = wx[t].reshape(128, 64)
    return W


def bc(ap, shape):
    return ap.broadcast_to(list(shape))


class Cfg:
    def __init__(self, n, ch):
        self.n = n
        self.ch = ch
        self.nch = n // ch
        self.mid = max(ch // 2 - 1, 0)
        lv = 0
        while (1 << lv) < ch:
            lv += 1
        self.lv = lv


class Builder:
    def __init__(self, nlayers=DEPTH, npass=NPASS, do_mlp=True, do_smp=True):
        self.do_smp = do_smp
        self.nlayers = nlayers
        self.npass = npass
        self.do_mlp = do_mlp
        self.cp = make_cp()
        self.uid = 0

    def sb(self, name, shape, dt=F32):
        nb = int(np.prod(shape[1:])) * (2 if dt == BF16 else 4)
        nb = (nb + 31) // 32 * 32
        t = self.nc.alloc_sbuf_tensor_at(name, list(shape), dt, offset=self.sb_off)
        self.sb_off += nb
        assert self.sb_off <= 229376 - 256, (name, self.sb_off)
        return t

    def bank(self):
        b = self.banks[self.bank_i % 8]
        self.bank_i += 1
        return b

    def build(self):
        nc = bass.Bass("TRN2", target_bir_lowering=False)
        self.nc = nc
        S = Sched(nc)
        import os as _os
        if _os.environ.get("K_MAXOPS"):
            S.maxops = int(_os.environ["K_MAXOPS"])
        self.S = S
        cp = self.cp
        dram_in = lambda n, shp: nc.dram_tensor(n, list(shp), F32, kind="ExternalInput").ap()
        dram_out = lambda n, shp: nc.dram_tensor(n, list(shp), F32, kind="ExternalOutput").ap()
        self.xp = dram_in("x_prompt", [SEQ, D])
        self.cpk = dram_in("cpack", [128, cp.n])
        self.smat = dram_in("smat", [128, DEPTH, 1024])
        self.w_in = dram_in("w_in", [self.nlayers, D, C_IN])
        self.w_out = dram_in("w_out", [self.nlayers, D, D])
        self.w1 = dram_in("mlp_w1", [self.nlayers if self.do_mlp else 1, D, DFF])
        self.w2 = dram_in("mlp_w2", [self.nlayers if self.do_mlp else 1, DFF, D])
        self.y_p = dram_out("y_prompt", [SEQ, D])
        self.o_pshift = dram_out("p_shift", [DEPTH, D])
        self.o_pwkv = dram_out("p_wkv", [DEPTH, 6, 64, 64])
        self.o_phgrn = dram_out("p_hgrn", [DEPTH, 6, 64, 64])
        self.o_plru = dram_out("p_lru", [DEPTH, 256])
        self.o_pconv = dram_out("p_conv", [DEPTH, 3, 256])
        self.xs_d = dram_in("x_sample", [16, D])
        self.st_wkv = dram_in("state_wkv", [DEPTH, 16, 6, 64, 64])
        self.st_shift = dram_in("state_shift", [DEPTH, 16, D])
        self.st_hgrn = dram_in("state_hgrn", [DEPTH, 16, 6, 64, 64])
        self.st_lru = dram_in("state_lru", [DEPTH, 16, 256])
        self.st_conv = dram_in("state_conv", [DEPTH, 16, 3, 256])
        self.y_s = dram_out("y_sample", [16, D])
        self.o_swkv = dram_out("s_wkv", [DEPTH, 16, 6, 64, 64])
        self.o_sshift = dram_out("s_shift", [DEPTH, 16, D])
        self.o_shgrn = dram_out("s_hgrn", [DEPTH, 16, 6, 64, 64])
        self.o_slru = dram_out("s_lru", [DEPTH, 16, 256])
        self.o_sconv = dram_out("s_conv", [DEPTH, 16, 3, 256])

        with contextlib.ExitStack() as es:
            self.es = es
            self.sb_off = 0x4000 + 512
            sb = self.sb
            self.banks = [es.enter_context(nc.psum_tensor(f"bank{i}", [128, 512], F32)) for i in range(8)]
            self.bank_i = 0
            self.C = sb("cpk", [128, cp.n])
            S.add('sp', 'dma_start', dma='c0', out=self.C[:], in_=self.cpk)
            self.ident = self.C[:, cp.sl('ident')]
            self.ident_bf = sb("ident_bf", [128, 128], BF16)
            S.add('dve', 'tensor_copy', out=self.ident_bf[:], in_=self.ident)
            self.ones_bf = sb("ones_bf", [128, 128], BF16)
            S.add('dve', 'memset', ap=self.ones_bf[:], constant=1.0)
            self.bmean_bf = sb("bmean_bf", [128, 128], BF16)
            S.add('dve', 'memset', ap=self.bmean_bf[:], constant=0.0)
            S.add('dve', 'memset', ap=self.bmean_bf[0:64, 0:64], constant=1.0 / 64)
            S.add('dve', 'memset', ap=self.bmean_bf[64:128, 64:128], constant=1.0 / 64)
            self.bones_bf = sb("bones_bf", [128, 128], BF16)
            S.add('dve', 'memset', ap=self.bones_bf[:], constant=0.0)
            S.add('dve', 'memset', ap=self.bones_bf[0:64, 0:64], constant=1.0)
            S.add('dve', 'memset', ap=self.bones_bf[64:128, 64:128], constant=1.0)
            self.rmask = sb("rmask", [128, TT])
            S.add('dve', 'memset', ap=self.rmask[:], constant=1.0)
            S.add('dve', 'memset', ap=self.rmask[:].rearrange("p (c t) -> p c t", t=CH)[:, :, 0:1], constant=0.0)
            self.epsc = {}
            for v in (NORM_EPS, GN_EPS, 1e-12, 1.0):
                t = sb(self.nm("eps"), [128, 1])
                S.add('pool', 'memset', ap=t[:], constant=float(v))
                self.epsc[v] = t
            self.lb = sb("lb", [128, 3, DEPTH])
            self.oml = sb("oml", [128, 3, DEPTH])
            self.noml = sb("noml", [128, 3, DEPTH])
            self.hgrn_lb_setup()
            self.c8 = sb("c8", [128, DEPTH, 2])
            self.c16 = sb("c16", [128, DEPTH, 2])
            self.lru_setup()
            self.x = [sb(f"x{t}", [128, KD, TT]) for t in range(NTILE)]
            self.WAin = sb("WAin", [128, KD, C_IN], BF16)
            self.WAout = sb("WAout", [128, KD, D], BF16)
            self.SM = sb("SM", [128, 1024], BF16)
            self.shl = sb("shl", [128, DEPTH, KD])
            S.add('pool', 'memset', ap=self.shl[:], constant=0.0)
            self.Tst = [sb(f"Tst{l}", [128, 3, 64]) for l in range(DEPTH)]
            self.Sst = [sb(f"Sst{l}", [128, 3, 64]) for l in range(DEPTH)]
            self.hst = [sb(f"hst{l}", [128, 2, 1]) for l in range(DEPTH)]
            self.cvh = [sb(f"cvh{l}", [128, 2, 3]) for l in range(DEPTH)]
            self.shh = [sb(f"shh{l}", [128, 11, 1]) for l in range(DEPTH)]
            for l in range(DEPTH):
                for t in (self.Tst[l], self.Sst[l], self.hst[l], self.cvh[l], self.shh[l]):
                    S.add('pool', 'memset', ap=t[:], constant=0.0)
            self.xs = sb("xs", [128, KD, 16])
            self.work_base = self.sb_off
            self.pcfg = Cfg(TT, CH)
            self.scfg = Cfg(16, 1)
            self.smp = False

            self.load_weights_a(0)
            for ps in range(self.npass):
                self.alloc_io()
                self.load_x(ps)
                if self.do_smp and ps == self.npass - 1:
                    self.load_xs()
                S.barrier()
                for l in range(self.nlayers):
                    self.alloc_a()
                    for tt in range(NTILE):
                        last = (ps == self.npass - 1 and tt == NTILE - 1)
                        self.phase_a_tile(l, tt, self.pcfg, last)
                    if self.do_smp and ps == self.npass - 1:
                        self.sample_tile(l)
                    S.barrier()
                    nl, nps = (l + 1, ps) if l + 1 < self.nlayers else (0, ps + 1)
                    nxt = self.weight_loads_a(nl) if nps < self.npass else []
                    if self.do_mlp:
                        self.alloc_b()
                        self.cur_ps = ps
                        self.phase_b(l, nxt)
                        S.barrier()
                    else:
                        for t in nxt:
                            t()
                self.alloc_io()
                self.store_y(ps)
                if self.do_smp and ps == self.npass - 1:
                    self.store_ys()
                S.barrier()
            self.store_states()
            S.barrier()
            S.emit(es)
        return nc

    def mk(self, label):
        import os
        if os.environ.get('K_MARK'):
            print('MARK', label, len(self.S.ops))

    def interleave(self, chain_fns):
        S = self.S
        recs = []
        for fn in chain_fns:
            lst = []
            S.add = (lambda *a, _l=lst, **k: _l.append((a, k)))
            try:
                fn()
            finally:
                del S.add
            recs.append(lst)
        L = max(len(r) for r in recs)
        for w in range(L + len(recs) - 1):
            for j, r in enumerate(recs):
                si = w - j
                if 0 <= si < len(r):
                    a, k = r[si]
                    S.add(*a, **k)

    def wavefront(self, steps, nj):
        for w in range(len(steps) + nj - 1):
            for si in range(len(steps)):
                j = w - si
                if 0 <= j < nj:
                    steps[si](j)

    def nm(self, s):
        self.uid += 1
        return f"{s}_{self.uid}"

    def hgrn_lb_setup(self):
        S, cp = self.S, self.cp
        sb = self.sb
        e = sb("hl_e", [128, 3, DEPTH])
        ssum = sb("hl_s", [128, 3, 1])
        hl = self.C[:, cp.sl('hlb')].rearrange("p (j l) -> p j l", l=DEPTH)
        S.add('act', 'activation', out=e[:], in_=hl, func=AF.Exp)
        S.add('dve', 'tensor_reduce', out=ssum[:], in_=e[:], axis=mybir.AxisListType.X, op=ALU.add)
        S.add('dve', 'reciprocal', out=ssum[:], in_=ssum[:])
        S.add('dve', 'tensor_tensor', out=e[:], in0=e[:], in1=bc(ssum[:], [128, 3, DEPTH]), op=ALU.mult)
        S.add('dve', 'memset', ap=self.lb[:, :, 0:1], constant=0.0)
        for l in range(1, DEPTH):
            S.add('dve', 'tensor_tensor', out=self.lb[:, :, l:l + 1], in0=self.lb[:, :, l - 1:l], in1=e[:, :, l:l + 1], op=ALU.add)
        S.add('dve', 'tensor_scalar', out=self.oml[:], in0=self.lb[:], scalar1=-1.0, scalar2=1.0, op0=ALU.mult, op1=ALU.add)
        S.add('dve', 'tensor_scalar', out=self.noml[:], in0=self.lb[:], scalar1=-1.0, scalar2=None, op0=ALU.add)

    def lru_setup(self):
        S, cp = self.S, self.cp
        t = self.sb("lru_t", [128, DEPTH, 2])
        for l in range(DEPTH):
            S.add('act', 'activation', out=t[:, l, :], in_=self.C[:, cp.sl(f"lam{l}")], func=AF.Exp, scale=-1.0)
        S.add('act', 'activation', out=t[:], in_=t[:], func=AF.Ln, bias=self.epsc[1.0][:])
        S.add('dve', 'tensor_scalar', out=self.c8[:], in0=t[:], scalar1=-8.0, scalar2=None, op0=ALU.mult)
        S.add('dve', 'tensor_scalar', out=self.c16[:], in0=t[:], scalar1=-16.0, scalar2=None, op0=ALU.mult)

    def weight_loads_a(self, l):
        S = self.S
        src = self.w_in[l].rearrange("(k p) c -> p k c", p=128)
        q = C_IN // 4
        th = []
        for i in range(4):
            th.append(lambda i=i: S.add('pool', 'dma_start', dma=f'wain{i}', out=self.WAin[:, :, i * q:(i + 1) * q],
                                        in_=src[:, :, i * q:(i + 1) * q]))
        th.insert(1, lambda: S.add('pool', 'dma_start', dma='sm', out=self.SM[:], in_=self.smat[:, l, :]))
        th.append(lambda: S.add('pool', 'dma_start', dma='waout', out=self.WAout[:], in_=self.w_out[l].rearrange("(k p) c -> p k c", p=128)))
        return th

    def load_weights_a(self, l):
        for t in self.weight_loads_a(l):
            t()

    def alloc_a(self):
        self.sb_off = self.work_base
        sb = self.sb
        u = self.nm("a")
        N = TT
        self.xn = sb(f"xn{u}", [128, KD, N], BF16)
        self.nln = sb(f"nln{u}", [128, N])
        self.nrs = sb(f"nrs{u}", [128, N])
        self.Pr = sb(f"Pr{u}", [128, 11, N + 1])
        self.ds = [sb(f"ds{i}{u}", [128, N]) for i in range(2)]
        self.twxa = sb(f"twxa{u}", [128, N], BF16)
        self.sgg = sb(f"sgg{u}", [128, N], BF16)
        self.A = [sb(f"A{i}{u}", [128, 3, N]) for i in range(6)]
        self._offB0 = self.sb_off
        self.B = [sb(f"B{i}{u}", [128, 3, N], BF16) for i in range(6)]
        self.O = sb(f"O{u}", [128, 3, N])
        self.eM = sb(f"eM{u}", [128, 3, 16])
        self.eC = sb(f"eC{u}", [128, 3, 16])
        self.eCM = sb(f"eCM{u}", [128, 3, 16])
        self.cmid = sb(f"cmid{u}", [128, 3, 16])
        self.AkT = sb(f"AkT{u}", [128, 3, 64], BF16)
        self.TM2 = sb(f"TM2{u}", [128, 6, 64], BF16)
        self.T0f = sb(f"T0f{u}", [128, 3, 64])
        self.T0b = sb(f"T0b{u}", [128, 3, 64], BF16)
        self.mix = sb(f"mix{u}", [128, KD, N], BF16)
        self.nsq = self.mix
        self.xs32 = sb(f"xs32{u}", [128, KD, 16])
        self.P0 = sb(f"P0{u}", [128, 11, 16])
        self.shfm = sb(f"shfm{u}", [128, KD, 16], BF16)
        self._offR2 = self.sb_off
        _stm = sb(f"stm{u}", [16, D])
        self.stm = [_stm, _stm]
        self.Ts = [sb(f"Ts{i}{u}", [128, 3, 64]) for i in range(2)]
        self.Ss = [sb(f"Ss{i}{u}", [128, 3, 64]) for i in range(2)]
        _wkT = sb(f"wkT{u}", [64, 3, 128])
        self.wkT = [_wkT, _wkT]
        self.swi = [sb(f"swi{i}{u}", [64, 6, 64]) for i in range(2)]
        self.CV = sb(f"CV{u}", [128, 6, 16])
        self.H0 = sb(f"H0{u}", [128, 2, 16])
        self.cvtm = sb(f"cvtm{u}", [16, 768])
        self.cvo = sb(f"cvo{u}", [16, 256])
        self.lrtm = sb(f"lrtm{u}", [16, 256])
        self.lro = sb(f"lro{u}", [16, 256])
        self.XBs = sb(f"XBs{u}", [128, 2, 16])
        end_off = self.sb_off
        lim = 0
        self.R = {}
        for nmx, shp in [('ZY0', [128, 2, 3, 64]), ('ZY1', [128, 2, 3, 64]), ('NT', [128, 3, 64]), ('AbT', [128, 3, 64]),
                         ('AkT', [128, 3, 64]), ('PP0', [128, 3, 64]), ('PP1', [128, 3, 64]), ('TM2', [128, 6, 64]),
                         ('TMK', [128, 3, 64]), ('Xb', [128, 3, 64]), ('Ub', [128, 3, 64])]:
            self.R[nmx] = sb(f"R{nmx}{u}", shp)
        self.sb_off = max(self.sb_off, end_off)
        end2 = self.sb_off
        self.sb_off = self._offR2
        self.R2 = dict(self.R)
        for nmx, shp in [('ZY0', [128, 2, 3, 64]), ('ZY1', [128, 2, 3, 64]), ('NT', [128, 3, 64]), ('AbT', [128, 3, 64]),
                         ('AkT', [128, 3, 64]), ('PP0', [128, 3, 64]), ('PP1', [128, 3, 64]), ('TM2', [128, 6, 64]),
                         ('TMK', [128, 3, 64])]:
            self.R2[nmx] = sb(f"R2{nmx}{u}", shp)
        self.HB2 = {'TM2': sb(f"HB2TM2{u}", [128, 6, 64], BF16), 'AkT': sb(f"HB2AkT{u}", [128, 3, 64], BF16)}
        assert self.sb_off <= self._offR2 + 4096 + 4 * 768 + 1536 + 2 * 1536, self.sb_off - self._offR2
        self.sb_off = end2
        if not hasattr(self, '_pa'):
            self._pa = True
            print("phase A work bytes", self.sb_off - self.work_base, "limit", 0x4000 + 212000 - self.work_base)

    def alloc_io(self):
        self.sb_off = self.work_base
        sb = self.sb
        u = self.nm("io")
        N = TT
        self.nsq = sb(f"nsq{u}", [128, KD, N], BF16)
        self.nln = sb(f"nln{u}", [128, N])
        self.nrs = sb(f"nrs{u}", [128, N])
        self.xtm = [sb(f"xtm{i}{u}", [128, D]) for i in range(2)]
        self.yfm = sb(f"yfm{u}", [128, KD, N])
        self.xs32 = sb(f"xs32{u}", [128, KD, 16])
        self.stm = [sb(f"stm{i}{u}", [16, D]) for i in range(2)]

    def load_x(self, ps):
        S = self.S
        for tb in range(TPP // 128):
            buf = self.xtm[tb % 2]
            t0 = ps * TPP + tb * 128
            S.add('sp', 'dma_start', dma=f'xtm{tb % 2}', out=buf[:], in_=self.xp[t0:t0 + 128, :])
            xt = self.x[(tb * 128) // TT]
            tl = (tb * 128) % TT
            for kh in range(2):
                bk = self.bank()
                for kk in range(4):
                    k = kh * 4 + kk
                    S.add('pe', 'transpose', out=bk[:, kk * 128:(kk + 1) * 128],
                          in_=buf[:, k * 128:(k + 1) * 128], identity=self.ident)
                src = bk[:].rearrange("p (k t) -> p k t", k=4)
                dst = xt[:, kh * 4:(kh + 1) * 4, tl:tl + 128]
                if kh == 0:
                    S.add('act', 'activation', out=dst, in_=src, func=AF.Copy)
                else:
                    S.add('dve', 'tensor_copy', out=dst, in_=src)

    def rmsnorm(self, xt, g, out, N, want_last=None):
        S = self.S
        S.add('act', 'activation', out=self.nsq[:, :, :N], in_=xt[:, :, :N], func=AF.Square)
        bk = self.bank()
        for k in range(KD):
            S.add('pe', 'matmul', out=bk[:, :N], lhsT=self.ones_bf[:], rhs=self.nsq[:, k, :N], start=(k == 0), stop=(k == KD - 1))
        S.add('act', 'activation', out=self.nln[:, :N], in_=bk[:, :N], func=AF.Ln, scale=1.0 / D, bias=self.epsc[NORM_EPS][:])
        S.add('act', 'activation', out=self.nrs[:, :N], in_=self.nln[:, :N], func=AF.Exp, scale=-0.5)
        for k in range(KD):
            S.add('dve', 'scalar_tensor_tensor', out=out[:, k, :N], in0=xt[:, k, :N], scalar=g[:, k:k + 1], in1=self.nrs[:, :N],
                  op0=ALU.mult, op1=ALU.mult)
        if want_last is not None:
            S.add('dve', 'tensor_tensor', out=want_last, in0=xt[:, :, N - 1], in1=g, op=ALU.mult)
            S.add('dve', 'tensor_scalar', out=want_last, in0=want_last, scalar1=self.nrs[:, N - 1:N], scalar2=None, op0=ALU.mult)

    def project(self, cols, N, evac, rhs=None):
        S = self.S
        for i in range(0, len(cols), 2):
            pair = cols[i:i + 2]
            bk = self.bank()
            for idx, c in enumerate(pair):
                for k in range(KD):
                    S.add('pe', 'matmul', out=bk[:, idx * 256: idx * 256 + N], lhsT=self.WAin[:, k, c * 128:(c + 1) * 128],
                          rhs=(self.xn if rhs is None else rhs)[:, k, :N], start=(k == 0), stop=(k == KD - 1))
            evac(bk, pair)

    def phase_a_tile(self, l, tt, cfg, last):
        S, cp = self.S, self.cp
        N = cfg.n
        xt = self.x[tt]
        self.rmsnorm(xt, self.C[:, cp.sl(f"g1{l}")], self.xn, N, want_last=self.shl[:, l, :] if last else None)
        import os
        mixs = os.environ.get("K_MIX", "rhl")
        if mixs != "rhl":
            S.add('dve', 'memset', ap=self.mix[:], constant=0.0)
        if 'r' in mixs:
            self.rwkv(l, cfg)
        if 'h' in mixs:
            self.hgrn(l, cfg)
        if 'l' in mixs:
            self.lru(l, cfg, last)
        for i in range(0, KD, 2):
            bk = self.bank()
            for idx in range(2):
                c = i + idx
                for k in range(KD):
                    S.add('pe', 'matmul', out=bk[:, idx * 256: idx * 256 + N], lhsT=self.WAout[:, k, c * 128:(c + 1) * 128],
                          rhs=self.mix[:, k, :N], start=(k == 0), stop=(k == KD - 1))
            S.add('dve', 'tensor_tensor', out=xt[:, i:i + 2, :N], in0=xt[:, i:i + 2, :N],
                  in1=bk[:].rearrange("p (c t) -> p c t", c=2)[:, :, :N], op=ALU.add)

    def tm2fm(self, src, nblk, dst, eng='act'):
        S = self.S
        bk = self.bank()
        for b in range(nblk):
            S.add('pe', 'transpose', out=bk[:, b * 16:(b + 1) * 16], in_=src[:, b * 128:(b + 1) * 128], identity=self.ident[0:16, 0:16])
        srcv = bk[:, 0:nblk * 16].rearrange("p (b t) -> p b t", t=16)
        if eng == 'act':
            S.add('act', 'activation', out=dst, in_=srcv, func=AF.Copy)
        else:
            S.add('dve', 'tensor_copy', out=dst, in_=srcv)

    def fm2tm(self, src, nblk, dst):
        S = self.S
        for b0 in range(0, nblk, 4):
            nb = min(4, nblk - b0)
            bk = self.bank()
            for b in range(nb):
                S.add('pe', 'transpose', out=bk[0:16, b * 128:(b + 1) * 128], in_=src(b0 + b), identity=self.ident)
            S.add('act', 'activation', out=dst[:, b0 * 128:(b0 + nb) * 128], in_=bk[0:16, 0:nb * 128], func=AF.Copy)

    def load_xs(self):
        S = self.S
        S.add('sp', 'dma_start', dma='stm0', out=self.stm[0][:], in_=self.xs_d)
        self.tm2fm(self.stm[0], KD, self.xs[:])

    def store_ys(self):
        S, cp = self.S, self.cp
        self.rmsnorm(self.xs, self.C[:, cp.sl('fg')], self.xs32, 16)
        self.fm2tm(lambda b: self.xs32[:, b, :], KD, self.stm[0])
        S.add('sp', 'dma_start', dma='stm0', out=self.y_s, in_=self.stm[0][:])

    def sample_tile(self, l):
        S, cp = self.S, self.cp
        cfg = self.scfg
        N = 16
        S.barrier()
        self.smp = True
        self.rmsnorm(self.xs, self.C[:, cp.sl(f"g1{l}")], self.xs32, N)
        S.add('act', 'activation', out=self.xn[:, :, :N], in_=self.xs32[:], func=AF.Copy)
        self.fm2tm(lambda b: self.xs32[:, b, :], KD, self.stm[1])
        S.add('sp', 'dma_start', dma='stm0o', out=self.o_sshift[l], in_=self.stm[1][:])
        S.add('sp', 'dma_start', dma='stm0', out=self.stm[0][:], in_=self.st_shift[l])
        self.tm2fm(self.stm[0], KD, self.shfm[:])
        S.add('sp', 'dma_start', dma='cvtm', out=self.cvtm[:], in_=self.st_conv[l].rearrange("b j c -> b (j c)"))
        self.tm2fm(self.cvtm, 6, self.CV[:], eng='dve')
        S.add('sp', 'dma_start', dma='lrtm', out=self.lrtm[:], in_=self.st_lru[l])
        self.tm2fm(self.lrtm, 2, self.H0[:], eng='dve')
        self.rwkv(l, cfg)
        self.hgrn(l, cfg)
        self.lru(l, cfg, False)
        S.add('sp', 'dma_start', dma='cvo0', out=self.o_sconv[l][:, 0:2, :].rearrange("b j c -> b (j c)"), in_=self.cvtm[:, 256:768])
        self.fm2tm(lambda b: self.XBs[:, b, :], 2, self.cvo)
        S.add('sp', 'dma_start', dma='cvo1', out=self.o_sconv[l][:, 2, :], in_=self.cvo[:])
        self.fm2tm(lambda b: self.H0[:, b, :], 2, self.lro)
        S.add('sp', 'dma_start', dma='lro', out=self.o_slru[l], in_=self.lro[:])
        for i in range(0, KD, 2):
            bk = self.bank()
            for idx in range(2):
                c = i + idx
                for k in range(KD):
                    S.add('pe', 'matmul', out=bk[:, idx * 256: idx * 256 + N], lhsT=self.WAout[:, k, c * 128:(c + 1) * 128],
                          rhs=self.mix[:, k, :N], start=(k == 0), stop=(k == KD - 1))
            S.add('dve', 'tensor_tensor', out=self.xs[:, i:i + 2, :N], in0=self.xs[:, i:i + 2, :N],
                  in1=bk[:].rearrange("p (c t) -> p c t", c=2)[:, :, :N], op=ALU.add)
        self.smp = False

    def rwkv(self, l, cfg):
        S, cp = self.S, self.cp
        N, CHL, NCHK = cfg.n, cfg.ch, cfg.nch
        Pr = self.Pr
        C = self.C
        sg, a_, kap, t1, E, bonus = self.A
        rt, kt, bt, kkt, vbf, rkbf = self.B
        cs = lambda nm: C[:, cp.sl(f"{nm}{l}")]
        smp = self.smp
        if not smp:
            S.add('act', 'activation', out=Pr[:, 0:11, 0:1], in_=self.shh[l][:], func=AF.Copy)

        def evac(bk, pair):
            S.add('act', 'activation', out=Pr[:, pair[0]:pair[0] + len(pair), 1:N + 1],
                  in_=bk[:].rearrange("p (c t) -> p c t", c=2)[:, 0:len(pair), :N], func=AF.Copy)
        self.project(list(range(11)), N, evac)
        if smp:
            def evac0(bk, pair):
                S.add('act', 'activation', out=self.P0[:, pair[0]:pair[0] + len(pair), :N],
                      in_=bk[:].rearrange("p (c t) -> p c t", c=2)[:, 0:len(pair), :N], func=AF.Copy)
            self.project(list(range(11)), N, evac0, rhs=self.shfm)
        else:
            S.add('act', 'activation', out=self.shh[l][:], in_=Pr[:, 0:11, N:N + 1], func=AF.Copy)
        self.mk('r_after_proj')
        mu = cs('mu')
        for c in range(11):
            d = self.ds[c % 2]
            S.add('pool', 'tensor_tensor', out=d[:, :N], in0=(self.P0[:, c, :N] if smp else Pr[:, c, 0:N]), in1=Pr[:, c, 1:N + 1], op=ALU.subtract)
            S.add('dve', 'scalar_tensor_tensor', out=Pr[:, c, 1:N + 1], in0=d[:, :N], scalar=mu[:, c:c + 1], in1=Pr[:, c, 1:N + 1],
                  op0=ALU.mult, op1=ALU.add)
        r = Pr[:, 0:3, 1:N + 1]
        k = Pr[:, 3:6, 1:N + 1]
        v = Pr[:, 6:9, 1:N + 1]
        self.mk('r_after_pm')
        S.add('act', 'activation', out=self.twxa[0:64, :N], in_=Pr[0:64, 9, 1:N + 1], func=AF.Tanh)
        S.add('act', 'activation', out=self.twxa[64:128, :N], in_=Pr[64:128, 9, 1:N + 1], func=AF.Copy)
        S.add('act', 'activation', out=self.sgg[:, :N], in_=Pr[:, 10, 1:N + 1], func=AF.Sigmoid)
        c3 = lambda t, j: t[:, j, :N].rearrange("p (c t) -> p c t", t=CHL)
        rj = lambda j: Pr[:, j, 1:N + 1]
        kj = lambda j: Pr[:, 3 + j, 1:N + 1]
        vj = lambda j: Pr[:, 6 + j, 1:N + 1]
        st = {}

        def s_lora(j):
            st[('bk', j)] = self.bank()
            st[('bk2', j)] = self.bank()
            S.add('pe', 'matmul', out=st[('bk', j)][:, 0:N], lhsT=self.SM[0:64, j * 128:(j + 1) * 128], rhs=self.twxa[0:64, :N], start=True, stop=True)
            S.add('pe', 'matmul', out=st[('bk2', j)][:, 0:N], lhsT=self.SM[64:128, j * 128:(j + 1) * 128], rhs=self.twxa[64:128, :N], start=True, stop=True)

        def s_bones1(j):
            st[('b3', j)] = self.bank()
            S.add('pe', 'matmul', out=st[('b3', j)][:, 0:N], lhsT=self.bones_bf[:], rhs=rkbf[:, j, :N], start=True, stop=True)

        def s_bones2(j):
            st[('b4', j)] = self.bank()
            S.add('pe', 'matmul', out=st[('b4', j)][:, 0:N], lhsT=self.bones_bf[:], rhs=rkbf[:, j, :N], start=True, stop=True)

        def s_scan(j):
            if CHL > 1:
                S.add('dve', 'tensor_tensor_scan', out=t1[:, j, :N], data0=self.rmask[:, :N], data1=sg[:, j, :N], initial=0.0, op0=ALU.mult, op1=ALU.add)
            else:
                S.add('dve', 'tensor_copy', out=t1[:, j, :N], in_=sg[:, j, :N])

        steps = [
            s_lora,
            lambda j: S.add('act', 'activation', out=sg[:, j, :N], in_=st[('bk', j)][:, 0:N], func=AF.Sigmoid, bias=cs('w0')[:, j:j + 1]),
            lambda j: S.add('act', 'activation', out=a_[:, j, :N], in_=st[('bk2', j)][:, 0:N], func=AF.Sigmoid, bias=cs('a0')[:, j:j + 1]),
            lambda j: S.add('dve', 'tensor_scalar', out=kap[:, j, :N], in0=kj(j), scalar1=cs('kk')[:, j:j + 1], scalar2=None, op0=ALU.mult),
            lambda j: S.add('act', 'activation', out=rkbf[:, j, :N], in_=kap[:, j, :N], func=AF.Square),
            s_bones1,
            lambda j: S.add('act', 'activation', out=t1[:, j, :N], in_=st[('b3', j)][:, 0:N], func=AF.Ln, bias=self.epsc[1e-12][:]),
            lambda j: S.add('act', 'activation', out=t1[:, j, :N], in_=t1[:, j, :N], func=AF.Exp, scale=-0.5),
            lambda j: S.add('dve', 'tensor_tensor', out=kap[:, j, :N], in0=kap[:, j, :N], in1=t1[:, j, :N], op=ALU.mult),
            lambda j: S.add('dve', 'tensor_scalar', out=t1[:, j, :N], in0=a_[:, j, :N], scalar1=-1.0, scalar2=cs('ka')[:, j:j + 1], op0=ALU.add, op1=ALU.mult),
            lambda j: S.add('dve', 'scalar_tensor_tensor', out=kj(j), in0=t1[:, j, :N], scalar=1.0, in1=kj(j), op0=ALU.add, op1=ALU.mult),
            lambda j: S.add('dve', 'tensor_tensor', out=a_[:, j, :N], in0=a_[:, j, :N], in1=kap[:, j, :N], op=ALU.mult),
            lambda j: S.add('dve', 'scalar_tensor_tensor', out=rkbf[:, j, :N], in0=rj(j), scalar=cs('rk')[:, j:j + 1], in1=kj(j), op0=ALU.mult, op1=ALU.mult),
            s_bones2,
            lambda j: S.add('dve', 'tensor_tensor', out=bonus[:, j, :N], in0=vj(j), in1=st[('b4', j)][:, 0:N], op=ALU.mult),
            s_scan,
            lambda j: S.add('act', 'activation', out=self.eM[:, j, :NCHK], in_=c3(t1, j)[:, :, cfg.mid], func=AF.Exp, scale=-C0),
            lambda j: S.add('act', 'activation', out=self.eC[:, j, :NCHK], in_=c3(t1, j)[:, :, CHL - 1], func=AF.Exp, scale=-C0),
            lambda j: S.add('act', 'activation', out=self.cmid[:, j, :NCHK], in_=c3(t1, j)[:, :, cfg.mid], func=AF.Copy),
            lambda j: S.add('dve', 'tensor_tensor', out=c3(t1, j), in0=c3(t1, j), in1=bc(self.cmid[:, j, :NCHK].unsqueeze(2), [128, NCHK, CHL]), op=ALU.subtract),
            lambda j: S.add('act', 'activation', out=E[:, j, :N], in_=t1[:, j, :N], func=AF.Exp, scale=-C0),
            lambda j: S.add('dve', 'tensor_tensor', out=rj(j), in0=rj(j), in1=E[:, j, :N], op=ALU.mult),
            lambda j: S.add('act', 'activation', out=self.eCM[:, j, :NCHK], in_=c3(E, j)[:, :, CHL - 1], func=AF.Copy),
            lambda j: S.add('act', 'activation', out=E[:, j, :N], in_=t1[:, j, :N], func=AF.Exp, scale=C0),
            lambda j: S.add('dve', 'tensor_tensor', out=a_[:, j, :N], in0=a_[:, j, :N], in1=E[:, j, :N], op=ALU.mult),
            lambda j: S.add('dve', 'tensor_tensor', out=kj(j), in0=kj(j), in1=E[:, j, :N], op=ALU.mult),
            lambda j: S.add('dve', 'tensor_tensor', out=t1[:, j, :N], in0=t1[:, j, :N], in1=sg[:, j, :N], op=ALU.subtract),
            lambda j: S.add('act', 'activation', out=E[:, j, :N], in_=t1[:, j, :N], func=AF.Exp, scale=-C0),
            lambda j: S.add('dve', 'tensor_tensor', out=kap[:, j, :N], in0=kap[:, j, :N], in1=E[:, j, :N], op=ALU.mult),
        ]
        self.wavefront(steps, 3)
        self.mk('r_pre_chunk')
        T = self.Tst[l]
        PS = [slice(0, 128)] if CHL == 64 else [slice(0, CHL), slice(64, 64 + CHL)]
        msun = C[:, cp.sl('msun')]
        msln = C[:, cp.sl('msln')]
        msu = C[:, cp.sl('msu')]
        miu = C[:, cp.sl('miu')]
        id2 = C[:, cp.sl('id2')]
        npz = lambda ps: ps.stop - ps.start
        m3 = lambda m, ps: bc(m[ps, 0:CHL].unsqueeze(1), [npz(ps), 3, CHL])
        hv = lambda bk, ps: bk[ps, 0:192].rearrange("p (j t) -> p j t", j=3)[:, :, 0:CHL]
        sv = lambda t, ps: t[ps, :, 0:CHL]
        hs = lambda hh: slice(hh * 64, hh * 64 + 64)
        ph = lambda hh: slice(hh * 64, hh * 64 + CHL)
        HEADS = [(j, hh) for hh in range(2) for j in range(3)]
        def chunk_gen(ch, R):
            T = self.Tst[l]
            ts = slice(ch * CHL, (ch + 1) * CHL)
            if smp:
                T = self.Ts[ch % 2]
                swi = self.swi[ch % 2]
                if ch == 0:
                    S.add('sp', 'dma_start', dma='swi0', out=swi[:], in_=self.st_wkv[l, 0].rearrange("h v k -> v h k"))
                if ch + 1 < NCHK:
                    S.add('sp', 'dma_start', dma=f'swi{(ch + 1) % 2}', out=self.swi[(ch + 1) % 2][:],
                          in_=self.st_wkv[l, ch + 1].rearrange("h v k -> v h k"))
                bki = self.bank()
                for j in range(3):
                    S.add('pe', 'transpose', out=bki[:, j * 64:(j + 1) * 64], in_=swi[:, 2 * j:2 * j + 2, :], identity=self.ident[0:64, 0:64])
                S.add('act', 'activation', out=T[:], in_=bki[:, 0:192].rearrange("p (j v) -> p j v", j=3), func=AF.Copy)
            bkA, bkB = self.bank(), self.bank()
            for j in range(3):
                for hh in range(2):
                    S.add('pe', 'matmul', out=bkA[ph(hh), j * 64:(j + 1) * 64], lhsT=Pr[:, 6 + j, 1 + ch * CHL:1 + (ch + 1) * CHL], rhs=self.ident[:, hs(hh)], start=True, stop=True)
                for hh in range(2):
                    S.add('pe', 'matmul', out=bkA[ph(hh), 192 + j * 64:192 + (j + 1) * 64], lhsT=a_[:, j, ts], rhs=self.ident[:, hs(hh)], start=True, stop=True)
                for hh in range(2):
                    S.add('pe', 'matmul', out=bkB[ph(hh), j * 64:(j + 1) * 64], lhsT=Pr[:, 3 + j, 1 + ch * CHL:1 + (ch + 1) * CHL], rhs=self.ident[:, hs(hh)], start=True, stop=True)
            for ps in PS:
                S.add('act', 'activation', out=R['TM2'][ps], in_=bkA[ps, 0:384].rearrange("p (q f) -> p q f", f=64), func=AF.Copy)
                S.add('dve', 'tensor_copy', out=R['TMK'][ps], in_=bkB[ps, 0:192].rearrange("p (q f) -> p q f", f=64))
            Vtm = lambda j, hh: R['TM2'][ph(hh), j, :]
            Btm = lambda j, hh: R['TM2'][ph(hh), 3 + j, :]
            Ktm = lambda j, hh: R['TMK'][ph(hh), j, :]
            ts1 = slice(1 + ch * CHL, 1 + (ch + 1) * CHL)
            FMSRC = {'rt': lambda p, j: Pr[p, j, ts1], 'kkt': lambda p, j: Pr[p, 3 + j, ts1], 'v': lambda p, j: Pr[p, 6 + j, ts1],
                     'bt': lambda p, j: a_[p, j, ts], 'kt': lambda p, j: kap[p, j, ts]}
            fm = lambda t, j, hh: FMSRC[t](hs(hh), j)
            cc = lambda t, j, hh: t[ph(hh), j, 0:CHL]
            pc = lambda bk, j, hh: bk[ph(hh), j * 64:j * 64 + CHL]
            self.mk('r_after_tm')
            yield 0
            ONE = (CHL == 1)
            if ONE:
                pab, pak = self.bank(), self.bank()
            else:
                pz, py, pn, pab, pak = [self.bank() for _ in range(5)]
            for (j, hh) in HEADS:
                if not ONE:
                    S.add('pe', 'matmul', out=pc(pz, j, hh), lhsT=fm('bt', j, hh), rhs=fm('kt', j, hh), start=True, stop=True)
                S.add('pe', 'matmul', out=pc(pab, j, hh), lhsT=fm('bt', j, hh), rhs=fm('rt', j, hh), start=True, stop=True)
                if not ONE:
                    S.add('pe', 'matmul', out=pc(pn, j, hh), lhsT=fm('kkt', j, hh), rhs=fm('kt', j, hh), start=True, stop=True)
                S.add('pe', 'matmul', out=pc(pak, j, hh), lhsT=fm('kkt', j, hh), rhs=fm('rt', j, hh), start=True, stop=True)
                if not ONE:
                    S.add('pe', 'matmul', out=pc(py, j, hh), lhsT=fm('kt', j, hh), rhs=fm('bt', j, hh), start=True, stop=True)
            zy = R['ZY0']
            Zt = lambda z: z[:, 0, :, :]
            Yt = lambda z: z[:, 1, :, :]
            P = R['PP0']
            for ps in PS:
                if not ONE:
                    S.add('dve', 'tensor_tensor', out=sv(Zt(zy), ps), in0=hv(pz, ps), in1=m3(msun, ps), op=ALU.mult)
                    S.add('dve', 'tensor_tensor', out=sv(Yt(zy), ps), in0=hv(py, ps), in1=m3(msln, ps), op=ALU.mult)
                    S.add('dve', 'tensor_tensor', out=sv(R['NT'], ps), in0=hv(pn, ps), in1=m3(msu, ps), op=ALU.mult)
                S.add('dve', 'tensor_tensor', out=sv(R['AbT'], ps), in0=hv(pab, ps), in1=m3(miu, ps), op=ALU.mult)
                S.add('dve', 'tensor_tensor', out=sv(R['AkT'], ps), in0=hv(pak, ps), in1=m3(miu, ps), op=ALU.mult)
                if not ONE:
                    S.add('dve', 'tensor_tensor', out=sv(P, ps), in0=sv(Zt(zy), ps), in1=m3(id2, ps), op=ALU.add)
            self.mk('r_after_masks')
            yield 0
            cur = 0
            for lv in range(1, cfg.lv):
                zn = R['ZY%d' % (1 - cur)]
                zo = R['ZY%d' % cur]
                pyb = self.bank()
                for (j, hh) in HEADS:
                    S.add('pe', 'matmul', out=pc(pyb, j, hh), lhsT=cc(Zt(zo), j, hh), rhs=cc(Yt(zo), j, hh), start=True, stop=True)
                for ps in PS:
                    S.add('act', 'activation', out=sv(Yt(zn), ps), in_=hv(pyb, ps), func=AF.Copy)
                if lv < cfg.lv - 1:
                    pzb = self.bank()
                    for (j, hh) in HEADS:
                        S.add('pe', 'matmul', out=pc(pzb, j, hh), lhsT=cc(Yt(zo), j, hh), rhs=cc(Zt(zo), j, hh), start=True, stop=True)
                    for ps in PS:
                        S.add('act', 'activation', out=sv(Zt(zn), ps), in_=hv(pzb, ps), func=AF.Copy)
                yield 0
                ppb = self.bank()
                Pn = R['PP1'] if P is R['PP0'] else R['PP0']
                for (j, hh) in HEADS:
                    S.add('pe', 'matmul', out=pc(ppb, j, hh), lhsT=cc(Yt(zn), j, hh), rhs=cc(P, j, hh), start=True, stop=True)
                for ps in PS:
                    S.add('dve', 'tensor_tensor', out=sv(Pn, ps), in0=hv(ppb, ps), in1=sv(P, ps), op=ALU.add)
                P = Pn
                cur = 1 - cur
                yield 0
            self.mk('r_after_inv')
            yield 'SD'
            S.add('dve', 'tensor_tensor', out=self.T0f[:], in0=T[:], in1=bc(self.eM[:, :, ch:ch + 1], [128, 3, 64]), op=ALU.mult)
            T0h = lambda j, hh: self.T0f[hs(hh), j, :]
            pv = lambda bk, j, hh: bk[ph(hh), j * 64:(j + 1) * 64]
            px = self.bank()
            for (j, hh) in HEADS:
                if not ONE:
                    S.add('pe', 'matmul', out=pv(px, j, hh), lhsT=cc(R['NT'], j, hh), rhs=Vtm(j, hh), start=True, stop=False)
                S.add('pe', 'matmul', out=pv(px, j, hh), lhsT=fm('kt', j, hh), rhs=T0h(j, hh), start=ONE, stop=True)
            if ONE:
                for ps in PS:
                    S.add('act', 'activation', out=R['Ub'][ps], in_=px[ps, 0:192].rearrange("p (j v) -> p j v", j=3), func=AF.Copy, scale=-1.0)
            else:
                for ps in PS:
                    S.add('act', 'activation', out=R['Xb'][ps], in_=px[ps, 0:192].rearrange("p (j v) -> p j v", j=3), func=AF.Copy)
                self.mk('r_after_x')
                pu = self.bank()
                for (j, hh) in HEADS:
                    S.add('pe', 'matmul', out=pv(pu, j, hh), lhsT=cc(P, j, hh), rhs=R['Xb'][ph(hh), j, :], start=True, stop=True)
                for ps in PS:
                    S.add('act', 'activation', out=R['Ub'][ps], in_=pu[ps, 0:192].rearrange("p (j v) -> p j v", j=3), func=AF.Copy, scale=-1.0)
            self.mk('r_after_u')
            po = self.bank()
            if ONE:
                po2 = self.bank()
                for (j, hh) in HEADS:
                    S.add('pe', 'matmul', out=po[hs(hh), j * CHL:(j + 1) * CHL], lhsT=T0h(j, hh), rhs=fm('rt', j, hh), start=True, stop=True)
                for (j, hh) in HEADS:
                    o_ap = po2[hs(hh), j * CHL:(j + 1) * CHL]
                    S.add('pe', 'matmul', out=o_ap, lhsT=R['Ub'][ph(hh), j, :], rhs=cc(R['AbT'], j, hh), start=True, stop=False)
                    S.add('pe', 'matmul', out=o_ap, lhsT=Vtm(j, hh), rhs=cc(R['AkT'], j, hh), start=False, stop=True)
                S.add('act', 'activation', out=self.O[:, :, ts], in_=po[:, 0:3 * CHL].rearrange("p (j t) -> p j t", j=3), func=AF.Copy)
                S.add('dve', 'tensor_tensor', out=self.O[:, :, ts], in0=self.O[:, :, ts], in1=po2[:, 0:3 * CHL].rearrange("p (j t) -> p j t", j=3), op=ALU.add)
            else:
                for (j, hh) in HEADS:
                    o_ap = po[hs(hh), j * CHL:(j + 1) * CHL]
                    S.add('pe', 'matmul', out=o_ap, lhsT=T0h(j, hh), rhs=fm('rt', j, hh), start=True, stop=False)
                    S.add('pe', 'matmul', out=o_ap, lhsT=R['Ub'][ph(hh), j, :], rhs=cc(R['AbT'], j, hh), start=False, stop=False)
                    S.add('pe', 'matmul', out=o_ap, lhsT=Vtm(j, hh), rhs=cc(R['AkT'], j, hh), start=False, stop=True)
                S.add('act', 'activation', out=self.O[:, :, ts], in_=po[:, 0:3 * CHL].rearrange("p (j t) -> p j t", j=3), func=AF.Copy)
            self.mk('r_after_o')
            pt = self.bank()
            for (j, hh) in HEADS:
                t_ap = pt[hs(hh), j * 64:(j + 1) * 64]
                S.add('pe', 'matmul', out=t_ap, lhsT=Btm(j, hh), rhs=R['Ub'][ph(hh), j, :], start=True, stop=False)
                S.add('pe', 'matmul', out=t_ap, lhsT=Ktm(j, hh), rhs=Vtm(j, hh), start=False, stop=True)
            S.add('dve', 'tensor_tensor', out=self.T0f[:], in0=self.T0f[:], in1=pt[:, 0:192].rearrange("p (j v) -> p j v", j=3), op=ALU.add)
            S.add('dve', 'tensor_tensor', out=T[:], in0=self.T0f[:], in1=bc(self.eCM[:, :, ch:ch + 1], [128, 3, 64]), op=ALU.mult)
            if smp:
                wk = self.wkT[ch % 2]
                bko = self.bank()
                for j in range(3):
                    S.add('pe', 'transpose', out=bko[0:64, j * 128:(j + 1) * 128], in_=T[:, j, :], identity=self.ident)
                S.add('act', 'activation', out=wk[:], in_=bko[0:64, 0:384].rearrange("p (j f) -> p j f", j=3), func=AF.Copy)
                S.add('sp', 'dma_start', dma='swo0', out=self.o_swkv[l, ch].rearrange("h v k -> v h k"),
                      in_=wk[:].rearrange("p j (hh k) -> p (j hh) k", hh=2))

        GRP = 2 if (CHL == 64 and not smp) else 1
        for c0 in range(0, NCHK, GRP):
            gens = [chunk_gen(c0 + i, self.R if i == 0 else self.R2) for i in range(min(GRP, NCHK - c0))]
            live = list(gens)
            while live:
                nxt = []
                for gq in live:
                    if next(gq) != 'SD':
                        nxt.append(gq)
                live = nxt
            for gq in gens:
                for _ in gq:
                    pass
        self.mk('r_post')
        O = self.O
        pst = {}

        def p_mm(key, rhs_fn, lhsT):
            def f(j):
                pst[(key, j)] = self.bank()
                S.add('pe', 'matmul', out=pst[(key, j)][:, 0:N], lhsT=lhsT(j), rhs=rhs_fn(j), start=True, stop=True)
            return f
        psteps = [
            lambda j: S.add('act', 'activation', out=rkbf[:, j, :N], in_=O[:, j, :N], func=AF.Copy),
            p_mm('m', lambda j: rkbf[:, j, :N], lambda j: self.bmean_bf[:]),
            lambda j: S.add('dve', 'tensor_tensor', out=O[:, j, :N], in0=O[:, j, :N], in1=pst[('m', j)][:, 0:N], op=ALU.subtract),
            lambda j: S.add('act', 'activation', out=rkbf[:, j, :N], in_=O[:, j, :N], func=AF.Square),
            p_mm('v', lambda j: rkbf[:, j, :N], lambda j: self.bmean_bf[:]),
            lambda j: S.add('act', 'activation', out=t1[:, j, :N], in_=pst[('v', j)][:, 0:N], func=AF.Ln, bias=self.epsc[GN_EPS][:]),
            lambda j: S.add('act', 'activation', out=t1[:, j, :N], in_=t1[:, j, :N], func=AF.Exp, scale=-0.5),
            lambda j: S.add('dve', 'tensor_tensor', out=O[:, j, :N], in0=O[:, j, :N], in1=t1[:, j, :N], op=ALU.mult),
            lambda j: S.add('dve', 'tensor_scalar', out=O[:, j, :N], in0=O[:, j, :N], scalar1=cs('lnw')[:, j:j + 1], scalar2=cs('lnb')[:, j:j + 1],
                            op0=ALU.mult, op1=ALU.add),
            lambda j: S.add('dve', 'tensor_tensor', out=O[:, j, :N], in0=O[:, j, :N], in1=bonus[:, j, :N], op=ALU.add),
            p_mm('g', lambda j: self.sgg[:, :N], lambda j: self.SM[:, 384 + j * 128:384 + (j + 1) * 128]),
            lambda j: S.add('dve', 'tensor_tensor', out=self.mix[:, j, :N], in0=O[:, j, :N], in1=pst[('g', j)][:, 0:N], op=ALU.mult),
        ]
        self.wavefront(psteps, 3)

    def hgrn(self, l, cfg):
        S, cp = self.S, self.cp
        N, CHL, NCHK = cfg.n, cfg.ch, cfg.nch
        Pr = self.Pr
        C = self.C
        q32, lf, kin, cum, E, sil = self.A
        qt, kkt, vbf, tmpb = self.B[0], self.B[1], self.B[2], self.B[3]
        oml = lambda j: self.oml[:, j, l:l + 1]
        noml = lambda j: self.noml[:, j, l:l + 1]
        lbj = lambda j: self.lb[:, j, l:l + 1]
        self.mk('h_start')
        def evac(bk, pair):
            v2 = bk[:].rearrange("p (c t) -> p c t", c=2)
            for idx, c in enumerate(pair):
                g_, j = (c - 11) // 3, (c - 11) % 3
                src = v2[:, idx, :N]
                if g_ == 0:
                    S.add('act', 'activation', out=q32[:, j, :N], in_=src, func=AF.Copy)
                elif g_ == 1:
                    S.add('act', 'activation', out=kin[:, j, :N], in_=src, func=AF.Sigmoid)
                elif g_ == 2:
                    S.add('act', 'activation', out=vbf[:, j, :N], in_=src, func=AF.Copy)
                else:
                    S.add('act', 'activation', out=sil[:, j, :N], in_=src, func=AF.Silu)
        self.project(list(range(11, 23)), N, evac)
        self.mk('h_after_proj')
        c3 = lambda t, j: t[:, j, :N].rearrange("p (c t) -> p c t", t=CHL)

        def h_scan(j):
            if CHL > 1:
                S.add('dve', 'tensor_tensor_scan', out=cum[:, j, :N], data0=self.rmask[:, :N], data1=lf[:, j, :N], initial=0.0, op0=ALU.mult, op1=ALU.add)
            else:
                S.add('dve', 'tensor_copy', out=cum[:, j, :N], in_=lf[:, j, :N])
        hsteps = [
            lambda j: S.add('act', 'activation', out=lf[:, j, :N], in_=kin[:, j, :N], func=AF.Ln, scale=oml(j), bias=lbj(j)),
            lambda j: S.add('dve', 'tensor_scalar', out=kin[:, j, :N], in0=kin[:, j, :N], scalar1=-1.0, scalar2=noml(j), op0=ALU.add, op1=ALU.mult),
            h_scan,
            lambda j: S.add('act', 'activation', out=self.eM[:, j, :NCHK], in_=c3(cum, j)[:, :, cfg.mid], func=AF.Exp),
            lambda j: S.add('act', 'activation', out=self.cmid[:, j, :NCHK], in_=c3(cum, j)[:, :, cfg.mid], func=AF.Copy),
            lambda j: S.add('dve', 'tensor_tensor', out=c3(cum, j), in0=c3(cum, j), in1=bc(self.cmid[:, j, :NCHK].unsqueeze(2), [128, NCHK, CHL]), op=ALU.subtract),
            lambda j: S.add('act', 'activation', out=E[:, j, :N], in_=cum[:, j, :N], func=AF.Exp),
            lambda j: S.add('dve', 'tensor_tensor', out=qt[:, j, :N], in0=q32[:, j, :N], in1=E[:, j, :N], op=ALU.mult),
            lambda j: S.add('act', 'activation', out=self.eCM[:, j, :NCHK], in_=c3(E, j)[:, :, CHL - 1], func=AF.Copy),
            lambda j: S.add('act', 'activation', out=E[:, j, :N], in_=cum[:, j, :N], func=AF.Exp, scale=-1.0),
            lambda j: S.add('dve', 'tensor_tensor', out=kkt[:, j, :N], in0=kin[:, j, :N], in1=E[:, j, :N], op=ALU.mult),
        ]
        self.wavefront(hsteps, 3)
        self.mk('h_pre_chunk')
        Sst = self.Sst[l]
        smp = self.smp
        PS = [slice(0, 128)] if CHL == 64 else [slice(0, CHL), slice(64, 64 + CHL)]
        miu = C[:, cp.sl('miu')]
        npz = lambda ps: ps.stop - ps.start
        m3 = lambda m, ps: bc(m[ps, 0:CHL].unsqueeze(1), [npz(ps), 3, CHL])
        hv = lambda bk, ps: bk[ps, 0:192].rearrange("p (j t) -> p j t", j=3)[:, :, 0:CHL]
        sv = lambda t, ps: t[ps, :, 0:CHL]
        hs = lambda hh: slice(hh * 64, hh * 64 + 64)
        ph = lambda hh: slice(hh * 64, hh * 64 + CHL)
        HEADS = [(j, hh) for hh in range(2) for j in range(3)]
        def hchunk_gen(ch, HB):
            Sst = self.Sst[l]
            ts = slice(ch * CHL, (ch + 1) * CHL)
            if smp:
                Sst = self.Ss[ch % 2]
                for cn in ([0, 1] if ch == 0 else [ch + 1]):
                    if cn < NCHK:
                        for hh in range(2):
                            S.add('sp', 'dma_start', dma=f'shi{cn % 2}', out=self.Ss[cn % 2][hh * 64:(hh + 1) * 64, :, :],
                                  in_=self.st_hgrn[l, cn].rearrange("(j hh) k v -> hh k j v", hh=2)[hh])
            bkA = self.bank()
            for j in range(3):
                for hh in range(2):
                    S.add('pe', 'matmul', out=bkA[ph(hh), j * 64:(j + 1) * 64], lhsT=vbf[:, j, ts], rhs=self.ident_bf[:, hs(hh)], start=True, stop=True)
                for hh in range(2):
                    S.add('pe', 'matmul', out=bkA[ph(hh), 192 + j * 64:192 + (j + 1) * 64], lhsT=kkt[:, j, ts], rhs=self.ident_bf[:, hs(hh)], start=True, stop=True)
            for ps in PS:
                S.add('act', 'activation', out=HB['TM2'][ps], in_=bkA[ps, 0:384].rearrange("p (q f) -> p q f", f=64), func=AF.Copy)
            Vtm = lambda j, hh: HB['TM2'][ph(hh), j, :]
            Ktm = lambda j, hh: HB['TM2'][ph(hh), 3 + j, :]
            fm = lambda t, j, hh: t[hs(hh), j, ts]
            cc = lambda t, j, hh: t[ph(hh), j, 0:CHL]
            pc = lambda bk, j, hh: bk[ph(hh), j * 64:j * 64 + CHL]
            pa = self.bank()
            for (j, hh) in HEADS:
                S.add('pe', 'matmul', out=pc(pa, j, hh), lhsT=fm(kkt, j, hh), rhs=fm(qt, j, hh), start=True, stop=True)
            for ps in PS:
                S.add('dve', 'tensor_tensor', out=sv(HB['AkT'], ps), in0=hv(pa, ps), in1=m3(miu, ps), op=ALU.mult)
            yield 'SD'
            S.add('dve', 'tensor_tensor', out=self.T0f[:], in0=Sst[:], in1=bc(self.eM[:, :, ch:ch + 1], [128, 3, 64]), op=ALU.mult)
            S.add('act', 'activation', out=self.T0b[:], in_=self.T0f[:], func=AF.Copy)
            T0h = lambda j, hh: self.T0b[hs(hh), j, :]
            po = self.bank()
            if CHL == 1:
                po2 = self.bank()
                for (j, hh) in HEADS:
                    S.add('pe', 'matmul', out=po[hs(hh), j * CHL:(j + 1) * CHL], lhsT=T0h(j, hh), rhs=fm(qt, j, hh), start=True, stop=True)
                for (j, hh) in HEADS:
                    S.add('pe', 'matmul', out=po2[hs(hh), j * CHL:(j + 1) * CHL], lhsT=Vtm(j, hh), rhs=cc(HB['AkT'], j, hh), start=True, stop=True)
                S.add('act', 'activation', out=self.O[:, :, ts], in_=po[:, 0:3 * CHL].rearrange("p (j t) -> p j t", j=3), func=AF.Copy)
                S.add('dve', 'tensor_tensor', out=self.O[:, :, ts], in0=self.O[:, :, ts], in1=po2[:, 0:3 * CHL].rearrange("p (j t) -> p j t", j=3), op=ALU.add)
            else:
                for (j, hh) in HEADS:
                    o_ap = po[hs(hh), j * CHL:(j + 1) * CHL]
                    S.add('pe', 'matmul', out=o_ap, lhsT=T0h(j, hh), rhs=fm(qt, j, hh), start=True, stop=False)
                    S.add('pe', 'matmul', out=o_ap, lhsT=Vtm(j, hh), rhs=cc(HB['AkT'], j, hh), start=False, stop=True)
                S.add('act', 'activation', out=self.O[:, :, ts], in_=po[:, 0:3 * CHL].rearrange("p (j t) -> p j t", j=3), func=AF.Copy)
            pt = self.bank()
            for (j, hh) in HEADS:
                t_ap = pt[hs(hh), j * 64:(j + 1) * 64]
                S.add('pe', 'matmul', out=t_ap, lhsT=Ktm(j, hh), rhs=Vtm(j, hh), start=True, stop=True)
            S.add('dve', 'tensor_tensor', out=self.T0f[:], in0=self.T0f[:], in1=pt[:, 0:192].rearrange("p (j v) -> p j v", j=3), op=ALU.add)
            S.add('dve', 'tensor_tensor', out=Sst[:], in0=self.T0f[:], in1=bc(self.eCM[:, :, ch:ch + 1], [128, 3, 64]), op=ALU.mult)
            if smp:
                for hh in range(2):
                    S.add('sp', 'dma_start', dma=f'sho{ch % 2}', out=self.o_shgrn[l, ch].rearrange("(j hh) k v -> hh k j v", hh=2)[hh],
                          in_=Sst[hh * 64:(hh + 1) * 64, :, :])

        GRP = 2 if (CHL == 64 and not smp) else 1
        HB0 = {'TM2': self.TM2, 'AkT': self.AkT}
        for c0 in range(0, NCHK, GRP):
            gens = [hchunk_gen(c0 + i, HB0 if i == 0 else self.HB2) for i in range(min(GRP, NCHK - c0))]
            for gq in gens:
                next(gq)
            for gq in gens:
                for _ in gq:
                    pass
        self.mk('h_post')
        O = self.O
        hng = C[:, cp.sl(f"hng{l}")]
        hst_ = {}

        def hp_mm(j):
            hst_[j] = self.bank()
            S.add('pe', 'matmul', out=hst_[j][:, 0:N], lhsT=self.bmean_bf[:], rhs=tmpb[:, j, :N], start=True, stop=True)
        hpsteps = [
            lambda j: S.add('act', 'activation', out=tmpb[:, j, :N], in_=O[:, j, :N], func=AF.Square),
            hp_mm,
            lambda j: S.add('act', 'activation', out=E[:, j, :N], in_=hst_[j][:, 0:N], func=AF.Ln, bias=self.epsc[NORM_EPS][:]),
            lambda j: S.add('act', 'activation', out=E[:, j, :N], in_=E[:, j, :N], func=AF.Exp, scale=-0.5),
            lambda j: S.add('dve', 'scalar_tensor_tensor', out=O[:, j, :N], in0=O[:, j, :N], scalar=hng[:, j:j + 1], in1=E[:, j, :N], op0=ALU.mult, op1=ALU.mult),
            lambda j: S.add('dve', 'tensor_tensor', out=self.mix[:, 3 + j, :N], in0=O[:, j, :N], in1=sil[:, j, :N], op=ALU.mult),
        ]
        self.wavefront(hpsteps, 3)

    def lru(self, l, cfg, last):
        S, cp = self.S, self.cp
        N = cfg.n
        C = self.C
        xb = self.Pr
        XB = self.A[0]
        xbv = XB[:].rearrange("p j n -> p (j n)")[:, 0:2 * (N + 3)].rearrange("p (c n) -> p c n", c=2)
        xc, rr, ii, aa, uu, gt = self.A[1], self.A[2], self.A[3], self.A[4], self.A[5], self.O
        xcb = self.B[0]
        cw = C[:, cp.sl(f"cw{l}")].rearrange("p (j c) -> p j c", c=2)
        smp = self.smp
        if not smp:
            S.add('act', 'activation', out=xbv[:, :, 0:3], in_=self.cvh[l][:], func=AF.Copy)

        def evac(bk, pair):
            v2 = bk[:].rearrange("p (c t) -> p c t", c=2)
            if pair[0] == 23:
                S.add('act', 'activation', out=xbv[:, :, 3:N + 3], in_=v2[:, :, :N], func=AF.Copy)
            else:
                S.add('act', 'activation', out=gt[:, 0:2, :N], in_=v2[:, :, :N], func=AF.Copy)
        self.project([23, 24, 25, 26], N, evac)
        if not smp:
            S.add('act', 'activation', out=self.cvh[l][:], in_=xbv[:, :, N:N + 3], func=AF.Copy)
        else:
            S.add('act', 'activation', out=self.XBs[:], in_=xbv[:, :, 3:N + 3], func=AF.Copy)
        for c in range(2):
            S.add('dve', 'tensor_scalar', out=xc[:, c, :N], in0=xbv[:, c, 3:N + 3], scalar1=cw[:, 3, c:c + 1], scalar2=C[:, cp.sl(f"cb{l}")][:, c:c + 1],
                  op0=ALU.mult, op1=ALU.add)
            for jj in range(3):
                prevj = self.CV[:, jj * 2 + c, :N] if smp else xbv[:, c, jj:jj + N]
                S.add('dve', 'scalar_tensor_tensor', out=xc[:, c, :N], in0=prevj, scalar=cw[:, jj, c:c + 1], in1=xc[:, c, :N],
                      op0=ALU.mult, op1=ALU.add)
        S.add('act', 'activation', out=xcb[:, 0:2, :N], in_=xc[:, 0:2, :N], func=AF.Copy)

        def lru_chain(c):
            bk = self.bank()
            for nn in range(2):
                pr_ = slice(nn * 64, nn * 64 + 64)
                S.add('pe', 'matmul', out=bk[pr_, 0:N], lhsT=self.SM[pr_, 768 + c * 64:768 + (c + 1) * 64], rhs=xcb[pr_, c, :N], start=True, stop=True)
                S.add('pe', 'matmul', out=bk[pr_, 256:256 + N], lhsT=self.SM[pr_, 896 + c * 64:896 + (c + 1) * 64], rhs=xcb[pr_, c, :N], start=True, stop=True)
            S.add('act', 'activation', out=rr[:, c, :N], in_=bk[:, 0:N], func=AF.Sigmoid, bias=C[:, cp.sl(f"ba{l}")][:, c:c + 1])
            S.add('act', 'activation', out=ii[:, c, :N], in_=bk[:, 256:256 + N], func=AF.Sigmoid, bias=C[:, cp.sl(f"bx{l}")][:, c:c + 1])
            S.add('act', 'activation', out=aa[:, c, :N], in_=rr[:, c, :N], func=AF.Exp, scale=self.c8[:, l, c:c + 1])
            S.add('act', 'activation', out=rr[:, c, :N], in_=rr[:, c, :N], func=AF.Exp, scale=self.c16[:, l, c:c + 1])
            S.add('dve', 'tensor_scalar', out=rr[:, c, :N], in0=rr[:, c, :N], scalar1=-1.0, scalar2=1.0, op0=ALU.mult, op1=ALU.add)
            S.add('dve', 'tensor_scalar', out=rr[:, c, :N], in0=rr[:, c, :N], scalar1=1e-12, scalar2=None, op0=ALU.max)
            S.add('act', 'activation', out=rr[:, c, :N], in_=rr[:, c, :N], func=AF.Sqrt)
            S.add('dve', 'tensor_tensor', out=uu[:, c, :N], in0=xc[:, c, :N], in1=ii[:, c, :N], op=ALU.mult)
            S.add('dve', 'tensor_tensor', out=uu[:, c, :N], in0=uu[:, c, :N], in1=rr[:, c, :N], op=ALU.mult)
            if smp:
                S.add('dve', 'tensor_tensor', out=ii[:, c, :N], in0=aa[:, c, :N], in1=self.H0[:, c, :N], op=ALU.mult)
                S.add('dve', 'tensor_tensor', out=ii[:, c, :N], in0=ii[:, c, :N], in1=uu[:, c, :N], op=ALU.add)
                S.add('act', 'activation', out=self.H0[:, c, :N], in_=ii[:, c, :N], func=AF.Copy)
            else:
                S.add('dve', 'tensor_tensor_scan', out=ii[:, c, :N], data0=aa[:, c, :N], data1=uu[:, c, :N], initial=self.hst[l][:, c, :], op0=ALU.mult, op1=ALU.add)
                S.add('act', 'activation', out=self.hst[l][:, c, :], in_=ii[:, c, N - 1:N], func=AF.Copy)
            S.add('act', 'activation', out=uu[:, c, :N], in_=gt[:, c, :N], func=AF.Square)
            S.add('dve', 'tensor_scalar', out=uu[:, c, :N], in0=uu[:, c, :N], scalar1=0.044715, scalar2=1.0, op0=ALU.mult, op1=ALU.add)
            S.add('dve', 'tensor_tensor', out=uu[:, c, :N], in0=uu[:, c, :N], in1=gt[:, c, :N], op=ALU.mult)
            S.add('act', 'activation', out=uu[:, c, :N], in_=uu[:, c, :N], func=AF.Sigmoid, scale=1.5957691216057308)
            S.add('dve', 'tensor_tensor', out=uu[:, c, :N], in0=uu[:, c, :N], in1=gt[:, c, :N], op=ALU.mult)
            S.add('dve', 'tensor_tensor', out=self.mix[:, 6 + c, :N], in0=uu[:, c, :N], in1=ii[:, c, :N], op=ALU.mult)
        self.interleave([lambda c=c: lru_chain(c) for c in range(2)])

    def alloc_b(self):
        self.sb_off = self.work_base
        sb = self.sb
        u = self.nm("b")
        self.xn2 = sb(f"xn2{u}", [128, KD, TPP], BF16)
        self.nsq = sb(f"nsq{u}", [128, KD, TT], BF16)
        self.nln = sb(f"nln{u}", [128, TT])
        self.nrs = sb(f"nrs{u}", [128, TT])
        self.w1g = [sb(f"w1g{i}{u}", [128, KD, FG], BF16) for i in range(2)]
        self.w2g = [sb(f"w2g{i}{u}", [128, FG // 128, D], BF16) for i in range(2)]
        self.hb = [sb(f"hb{i}{u}", [128, FG // 128, 512], BF16) for i in range(2)]
        self.xn2s = sb(f"xn2s{u}", [128, KD, 16], BF16)
        self.hbs = sb(f"hbs{u}", [128, FG // 128, 16], BF16)
        self.rls = sb(f"rls{u}", [128, FG // 128, 16])
        self.rl = [sb(f"rl{i}{u}", [128, 512]) for i in range(2)]

    def phase_b(self, l, nxt_loads=()):
        S, cp = self.S, self.cp
        g2 = self.C[:, cp.sl(f"g2{l}")]
        def load_group(g):
            s = g % 2
            S.add('pool', 'dma_start', dma=f'w1g{s}', out=self.w1g[s][:], in_=self.w1[l][:, g * FG:(g + 1) * FG].rearrange("(k p) c -> p k c", p=128))
            S.add('pool', 'dma_start', dma=f'w2g{s}', out=self.w2g[s][:], in_=self.w2[l][g * FG:(g + 1) * FG, :].rearrange("(k p) c -> p k c", p=128))
        nxt_loads = list(nxt_loads)
        load_group(0)
        load_group(1)
        if nxt_loads:
            nxt_loads.pop(0)()
        for tt in range(NTILE):
            self.rmsnorm_into(self.x[tt], g2, tt)
        nmt = TPP // 512
        do_s = self.do_smp and self.cur_ps == self.npass - 1
        if do_s:
            self.rmsnorm(self.xs, g2, self.xn2s, 16)

        def sample_group(g):
            s_ = g % 2
            bk = self.bank()
            for c in range(FG // 128):
                for k in range(KD):
                    S.add('pe', 'matmul', out=bk[:, c * 16:(c + 1) * 16], lhsT=self.w1g[s_][:, k, c * 128:(c + 1) * 128], rhs=self.xn2s[:, k, :],
                          start=(k == 0), stop=(k == KD - 1))
            S.add('act', 'activation', out=self.rls[:], in_=bk[:, 0:(FG // 128) * 16].rearrange("p (c t) -> p c t", t=16), func=AF.Relu)
            S.add('pool', 'tensor_tensor', out=self.hbs[:], in0=self.rls[:], in1=self.rls[:], op=ALU.mult)
            bk = self.bank()
            for c in range(KD):
                for k in range(FG // 128):
                    S.add('pe', 'matmul', out=bk[:, c * 16:(c + 1) * 16], lhsT=self.w2g[s_][:, k, c * 128:(c + 1) * 128], rhs=self.hbs[:, k, :],
                          start=(k == 0), stop=(k == FG // 128 - 1))
            S.add('dve', 'tensor_tensor', out=self.xs[:], in0=self.xs[:], in1=bk[:, 0:KD * 16].rearrange("p (c t) -> p c t", t=16), op=ALU.add)

        def stage1(u, g, mt):
            s_ = g % 2
            hb = self.hb[u % 2]
            for c in range(FG // 128):
                bk = self.bank()
                for k in range(KD):
                    S.add('pe', 'matmul', out=bk[:], lhsT=self.w1g[s_][:, k, c * 128:(c + 1) * 128], rhs=self.xn2[:, k, mt * 512:(mt + 1) * 512],
                          start=(k == 0), stop=(k == KD - 1))
                rl = self.rl[c % 2]
                S.add('act', 'activation', out=rl[:], in_=bk[:], func=AF.Relu)
                S.add('pool', 'tensor_tensor', out=hb[:, c, :], in0=rl[:], in1=rl[:], op=ALU.mult)

        def stage2(u, g, mt):
            s_ = g % 2
            hb = self.hb[u % 2]
            for c in range(KD):
                bk = self.bank()
                for k in range(FG // 128):
                    S.add('pe', 'matmul', out=bk[:], lhsT=self.w2g[s_][:, k, c * 128:(c + 1) * 128], rhs=hb[:, k, :],
                          start=(k == 0), stop=(k == FG // 128 - 1))
                for hh in range(512 // TT):
                    xt = self.x[mt * (512 // TT) + hh]
                    S.add('dve', 'tensor_tensor', out=xt[:, c, :], in0=xt[:, c, :], in1=bk[:, hh * TT:(hh + 1) * TT], op=ALU.add)

        units = [(g, mt) for g in range(NFG) for mt in range(nmt)]
        stage1(0, *units[0])
        for u, (g, mt) in enumerate(units):
            if u + 1 < len(units):
                stage1(u + 1, *units[u + 1])
            stage2(u, g, mt)
            if mt == nmt - 1:
                if do_s:
                    sample_group(g)
                if g + 2 < NFG:
                    load_group(g + 2)
                if nxt_loads:
                    nxt_loads.pop(0)()
        for t in nxt_loads:
            t()

    def rmsnorm_into(self, xt, g, tt):
        S = self.S
        N = TT
        S.add('act', 'activation', out=self.nsq[:], in_=xt[:], func=AF.Square)
        bk = self.bank()
        for k in range(KD):
            S.add('pe', 'matmul', out=bk[:, :N], lhsT=self.ones_bf[:], rhs=self.nsq[:, k, :], start=(k == 0), stop=(k == KD - 1))
        S.add('act', 'activation', out=self.nln[:], in_=bk[:, :N], func=AF.Ln, scale=1.0 / D, bias=self.epsc[NORM_EPS][:])
        S.add('act', 'activation', out=self.nrs[:], in_=self.nln[:], func=AF.Exp, scale=-0.5)
        for k in range(KD):
            S.add('dve', 'scalar_tensor_tensor', out=self.xn2[:, k, tt * TT:(tt + 1) * TT], in0=xt[:, k, :], scalar=g[:, k:k + 1], in1=self.nrs[:],
                  op0=ALU.mult, op1=ALU.mult)

    def store_y(self, ps):
        S, cp = self.S, self.cp
        g = self.C[:, cp.sl('fg')]
        for tt in range(NTILE):
            self.rmsnorm(self.x[tt], g, self.yfm, TT)
            for tb in range(TT // 128):
                ob = self.xtm[tb % 2]
                for kh in range(2):
                    bk = self.bank()
                    for kk in range(4):
                        k = kh * 4 + kk
                        S.add('pe', 'transpose', out=bk[:, kk * 128:(kk + 1) * 128],
                              in_=self.yfm[:, k, tb * 128:(tb + 1) * 128], identity=self.ident)
                    S.add('act', 'activation', out=ob[:, kh * 512:(kh + 1) * 512], in_=bk[:], func=AF.Copy)
                t0 = ps * TPP + tt * TT + tb * 128
                S.add('sp', 'dma_start', dma=f'ytm{tb % 2}', out=self.y_p[t0:t0 + 128, :], in_=ob[:])

    def store_states(self):
        S, nc = self.S, self.nc
        self.sb_off = self.work_base
        wk = self.sb("wkvT", [64, DEPTH, 3, 128])
        with nc.allow_non_contiguous_dma(reason="small strided state outputs"):
            S.add('sp', 'dma_start', dma='o_small', allow_slow_non_contiguous=True, out=self.o_pshift.rearrange("l (k p) -> p l k", p=128), in_=self.shl[:])
            for l in range(DEPTH):
                bk = self.bank()
                for j in range(3):
                    S.add('pe', 'transpose', out=bk[0:64, j * 128:(j + 1) * 128], in_=self.Tst[l][:, j, :], identity=self.ident)
                S.add('act', 'activation', out=wk[:, l, :, :], in_=bk[0:64, 0:384].rearrange("p (j f) -> p j f", j=3), func=AF.Copy)
                S.add('sp', 'dma_start', dma=f'o_wkv{l}', allow_slow_non_contiguous=True, out=self.o_pwkv[l].rearrange("h v k -> v h k"),
                      in_=wk[:, l, :, :].rearrange("p j (hh k) -> p (j hh) k", hh=2))
                for hh in range(2):
                    S.add('sp', 'dma_start', dma='o_small', allow_slow_non_contiguous=True, out=self.o_phgrn[l].rearrange("(j hh) k v -> hh k j v", hh=2)[hh],
                          in_=self.Sst[l][hh * 64:(hh + 1) * 64, :, :])
                S.add('sp', 'dma_start', dma='o_small', allow_slow_non_contiguous=True, out=self.o_plru[l].rearrange("(c p) -> p c", p=128), in_=self.hst[l][:, :, 0])
                for c in range(2):
                    S.add('sp', 'dma_start', dma='o_small', allow_slow_non_contiguous=True, out=self.o_pconv[l][:, c * 128:(c + 1) * 128].rearrange("j p -> p j"), in_=self.cvh[l][:, c, :])


def kernel(**inputs):
    inp = {k: np.asarray(v) for k, v in inputs.items()}
    import os
    b = Builder(nlayers=int(os.environ.get("K_NL", DEPTH)), npass=int(os.environ.get("K_NP", NPASS)),
                do_mlp=os.environ.get("K_MLP", "1") == "1", do_smp=os.environ.get("K_SMP", "1") == "1")
    nc = b.build()
    cpk = pack_consts(inp, b.cp)
    smat = pack_small_mats(inp)
    in_maps = []
    for c in range(NCORES):
        in_maps.append({
            "x_prompt": np.ascontiguousarray(inp['x_prompt'][c]),
            "x_sample": np.ascontiguousarray(inp['x_sample'][c * 16:(c + 1) * 16, 0]),
            "state_wkv": np.ascontiguousarray(inp['state_wkv'][:, c * 16:(c + 1) * 16]),
            "state_shift": np.ascontiguousarray(inp['state_shift'][:, c * 16:(c + 1) * 16]),
            "state_hgrn": np.ascontiguousarray(inp['state_hgrn'][:, c * 16:(c + 1) * 16]),
            "state_lru": np.ascontiguousarray(inp['state_lru'][:, c * 16:(c + 1) * 16]),
            "state_conv": np.ascontiguousarray(inp['state_conv'][:, c * 16:(c + 1) * 16]),
            "cpack": cpk, "smat": smat,
            "w_in": inp['w_in'][:b.nlayers], "w_out": inp['w_out'][:b.nlayers],
            "mlp_w1": inp['mlp_w1'][:(b.nlayers if b.do_mlp else 1)], "mlp_w2": inp['mlp_w2'][:(b.nlayers if b.do_mlp else 1)],
        })
    ncr = int(os.environ.get("K_CORES", NCORES))
    res = run_bass_kernel_spmd(nc, in_maps[:ncr], core_ids=list(range(ncr)))
    R = list(res.results) + [res.results[0]] * (NCORES - ncr)
    print("total ops", len(b.S.ops))
    y_prompt = np.stack([R[c]["y_prompt"] for c in range(NCORES)], 0)
    p_shift = np.stack([R[c]["p_shift"] for c in range(NCORES)], 1)
    p_wkv = np.stack([R[c]["p_wkv"] for c in range(NCORES)], 1)
    p_hgrn = np.stack([R[c]["p_hgrn"] for c in range(NCORES)], 1)
    p_lru = np.stack([R[c]["p_lru"] for c in range(NCORES)], 1)
    p_conv = np.stack([R[c]["p_conv"] for c in range(NCORES)], 1)
    cat = lambda n, ax: np.concatenate([R[c][n] for c in range(NCORES)], ax)
    y_sample = cat("y_sample", 0)[:, None, :]
    return (y_prompt, y_sample, p_wkv, p_shift, p_hgrn, p_lru, p_conv,
            cat("s_wkv", 1), cat("s_shift", 1), cat("s_hgrn", 1), cat("s_lru", 1), cat("s_conv", 1))
```

```python
import contextlib
import numpy as np
import concourse.bass as bass
import concourse.mybir as mybir
from concourse.bass_utils import run_bass_kernel_spmd

F32 = mybir.dt.float32
BF16 = mybir.dt.bfloat16
AF = mybir.ActivationFunctionType
ALU = mybir.AluOpType

NCORES = 8
D = 1024
KD = 8
SEQ = 2048
DEPTH = 4
TT = 256
NPASS = 2
TPP = SEQ // NPASS
NTILE = TPP // TT
CH = 64
NCH = TT // CH
MID = CH // 2 - 1
C_IN = 3456
NCT_IN = 27
DFF = 4096
FG = 512
NFG = DFF // FG
NORM_EPS = 1e-6
GN_EPS = 64e-5
C0 = float(np.exp(-0.5))

ENGS = ['pe', 'act', 'dve', 'pool', 'sp']


class Sched:
    def __init__(self, nc):
        self.nc = nc
        self.ops = []
        self.per_eng = {e: [] for e in ENGS}
        self.last_w = {}
        self.readers = {}
        self.synced = {e: {} for e in ENGS}
        self.dma_cum = {}
        self.pe_mode = None
        self.pe_last = None

    @staticmethod
    def _keys(aps):
        ks = []
        for a in aps:
            if a is None or isinstance(a, (int, float)):
                continue
            if isinstance(a, (str, tuple)):
                ks.append(a)
                continue
            t = getattr(a, 'tensor', None)
            if t is None or type(t).__name__ != 'SBTensorHandle' or len(t.shape) < 3 or a.dtype != t.dtype:
                ks.append(a.name)
                continue
            shp = [int(v) for v in t.shape]
            row = int(np.prod(shp[1:]))
            bs = int(np.prod(shp[2:]))
            lo = int(a.offset) % row
            hi = lo
            for step, cnt in list(a.ap)[1:]:
                if step > 0:
                    hi += (int(cnt) - 1) * int(step)
            for b in range(lo // bs, min(hi // bs, shp[1] - 1) + 1):
                ks.append((a.name, b))
        return ks

    maxops = None

    def add(self, eng, method, r=(), w=(), dma=None, **kw):
        if self.maxops is not None and len(self.ops) >= self.maxops:
            return None
        idx = len(self.ops)
        r = list(r)
        w = list(w)
        for kn, v in kw.items():
            if hasattr(v, 'name') and hasattr(v, 'ap'):
                (w if kn in ('out', 'ap', 'accum_out') else r).append(v)
        rk = self._keys(r)
        wk = self._keys(w)
        deps = set()
        for k in rk:
            if k in self.last_w:
                deps.add(self.last_w[k])
        for k in wk:
            if k in self.last_w:
                deps.add(self.last_w[k])
            deps.update(self.readers.get(k, {}).values())
        waits = []
        if eng == 'pe' and method in ('matmul', 'transpose'):
            st_ap = kw['lhsT'] if method == 'matmul' else kw['in_']
            shp = list(st_ap.shape)
            rkk = 32 if shp[0] <= 32 else (64 if shp[0] <= 64 else 128)
            mfree = int(np.prod(shp[1:]))
            rm = 32 if mfree <= 32 else (64 if mfree <= 64 else 128)
            mode = (rkk, rm, method, str(st_ap.dtype))
            if getattr(self, 'pe_mode', None) is not None and mode != self.pe_mode and self.pe_last is not None:
                p = self.ops[self.pe_last]
                p['inc'] = True
                waits.append(('eng', self.pe_last))
            self.pe_mode = mode
        for d in sorted(deps):
            p = self.ops[d]
            if p['dma'] is not None:
                key = ('dma', p['dma'])
                if self.synced[eng].get(key, 0) >= p['cum']:
                    continue
                self.synced[eng][key] = p['cum']
                waits.append(('dma', p['dma'], p['cum']))
            else:
                if p['eng'] == 'pe' and eng == 'pe':
                    continue
                if self.synced[eng].get(p['eng'], -1) >= p['seq']:
                    continue
                self.synced[eng][p['eng']] = p['seq']
                p['inc'] = True
                waits.append(('eng', d))
        o = dict(eng=eng, method=method, kw=kw, waits=waits, dma=dma, seq=len(self.per_eng[eng]),
                 inc=False, cum=None, tick=None)
        if dma is not None:
            self.dma_cum[dma] = self.dma_cum.get(dma, 0) + 16
            o['cum'] = self.dma_cum[dma]
        self.ops.append(o)
        self.per_eng[eng].append(idx)
        if eng == 'pe':
            self.pe_last = idx
        for k in wk:
            self.last_w[k] = idx
            self.readers[k] = {}
        rtag = ('dma', dma) if dma is not None else eng
        for k in rk:
            self.readers.setdefault(k, {})[rtag] = idx
        return idx

    def barrier(self):
        last_real = {}
        for f in ENGS:
            j = len(self.per_eng[f]) - 1
            while j >= 0 and (self.ops[self.per_eng[f][j]]['dma'] is not None
                              or self.ops[self.per_eng[f][j]]['method'] is None):
                j -= 1
            if j >= 0:
                last_real[f] = self.per_eng[f][j]
        cums = dict(self.dma_cum)
        for e in ENGS:
            waits = []
            for f, qi in last_real.items():
                if f == e:
                    continue
                q = self.ops[qi]
                if self.synced[e].get(f, -1) < q['seq']:
                    self.synced[e][f] = q['seq']
                    q['inc'] = True
                    waits.append(('eng', qi))
            for key, cum in cums.items():
                k2 = ('dma', key)
                if self.synced[e].get(k2, 0) < cum:
                    self.synced[e][k2] = cum
                    waits.append(('dma', key, cum))
            o = dict(eng=e, method=None, kw={}, waits=waits, dma=None, seq=len(self.per_eng[e]),
                     inc=False, cum=None, tick=None)
            self.per_eng[e].append(len(self.ops))
            self.ops.append(o)

    def emit(self, es):
        nc = self.nc
        ROT = 20000
        eng_sems = {e: [] for e in ENGS}
        for e in ENGS:
            n = 0
            for i in self.per_eng[e]:
                o = self.ops[i]
                if o['inc']:
                    n += 1
                    o['tick'] = n
            nsem = (n + ROT - 1) // ROT
            for j in range(max(nsem, 1)):
                eng_sems[e].append(es.enter_context(nc.semaphore(f"s_{e}_{j}")))
        dma_sems = {}
        for key in self.dma_cum:
            dma_sems[key] = es.enter_context(nc.semaphore(f"d_{len(dma_sems)}"))
        objs = {'pe': nc.tensor, 'act': nc.scalar, 'dve': nc.vector, 'pool': nc.gpsimd, 'sp': nc.sync}

        def tick_ref(e, tick):
            j = (tick - 1) // ROT
            return eng_sems[e][j], tick - j * ROT

        def run(e, eng):
            for i in self.per_eng[e]:
                o = self.ops[i]
                for wt in o['waits']:
                    if wt[0] == 'dma':
                        eng.wait_ge(dma_sems[wt[1]], wt[2])
                    else:
                        p = self.ops[wt[1]]
                        s, v = tick_ref(p['eng'], p['tick'])
                        eng.wait_ge(s, v)
                if o['method'] is None:
                    continue
                ins = getattr(eng, o['method'])(**o['kw'])
                if o['dma'] is not None:
                    ins.then_inc(dma_sems[o['dma']], 16)
                elif o['inc']:
                    s, v = tick_ref(e, o['tick'])
                    ins.then_inc(s, 1)

        with nc.Block() as block:
            @block.tensor
            def _(eng):
                run('pe', eng)

            @block.scalar
            def _(eng):
                run('act', eng)

            @block.vector
            def _(eng):
                run('dve', eng)

            @block.gpsimd
            def _(eng):
                run('pool', eng)

            @block.sync
            def _(eng):
                run('sp', eng)


def _fm(v, nt):
    return np.ascontiguousarray(np.asarray(v, np.float32).reshape(nt, 128).T)


class CP:
    def __init__(self):
        self.cols = {}
        self.n = 0

    def alloc(self, name, w):
        self.cols[name] = (self.n, w)
        self.n += w

    def sl(self, name):
        a, w = self.cols[name]
        return slice(a, a + w)


def make_cp():
    cp = CP()
    for l in range(DEPTH):
        for nm, w in [('g1', 8), ('g2', 8), ('mu', 11), ('w0', 3), ('a0', 3), ('kk', 3), ('ka', 3), ('rk', 3),
                      ('lnw', 3), ('lnb', 3), ('hng', 3), ('cw', 8), ('cb', 2), ('ba', 2), ('bx', 2), ('lam', 2)]:
            cp.alloc(f"{nm}{l}", w)
    cp.alloc('hlb', 12)
    cp.alloc('fg', 8)
    cp.alloc('ident', 128)
    cp.alloc('msu', 64)
    cp.alloc('msl', 64)
    cp.alloc('miu', 64)
    cp.alloc('msun', 64)
    cp.alloc('msln', 64)
    cp.alloc('id2', 64)
    return cp


def pack_consts(inp, cp):
    A = np.zeros((128, cp.n), np.float32)
    for l in range(DEPTH):
        A[:, cp.sl(f"g1{l}")] = _fm(inp['norm1_g'][l], 8)
        A[:, cp.sl(f"g2{l}")] = _fm(inp['norm2_g'][l], 8)
        A[:, cp.sl(f"mu{l}")] = _fm(inp['mu_shift'][l], 11)
        A[:, cp.sl(f"w0{l}")] = _fm(inp['rwkv_w0'][l], 3)
        A[:, cp.sl(f"a0{l}")] = _fm(inp['rwkv_a0'][l], 3)
        A[:, cp.sl(f"kk{l}")] = _fm(inp['rwkv_k_k'][l], 3)
        A[:, cp.sl(f"ka{l}")] = _fm(inp['rwkv_k_a'][l], 3)
        A[:, cp.sl(f"rk{l}")] = _fm(inp['rwkv_r_k'][l].reshape(-1), 3)
        A[:, cp.sl(f"lnw{l}")] = _fm(inp['rwkv_ln_w'][l], 3)
        A[:, cp.sl(f"lnb{l}")] = _fm(inp['rwkv_ln_b'][l], 3)
        A[:, cp.sl(f"hng{l}")] = _fm(inp['hgrn_norm_g'][l], 3)
        cw = np.stack([_fm(inp['lru_conv_w'][l, j], 2) for j in range(4)], axis=1)
        A[:, cp.sl(f"cw{l}")] = cw.reshape(128, 8)
        A[:, cp.sl(f"cb{l}")] = _fm(inp['lru_conv_b'][l], 2)
        A[:, cp.sl(f"ba{l}")] = _fm(inp['lru_ba'][l], 2)
        A[:, cp.sl(f"bx{l}")] = _fm(inp['lru_bx'][l], 2)
        A[:, cp.sl(f"lam{l}")] = _fm(inp['lru_lambda'][l], 2)
    hl = np.stack([_fm(inp['hgrn_lb'][l], 3) for l in range(DEPTH)], axis=2)
    A[:, cp.sl('hlb')] = hl.reshape(128, 12)
    A[:, cp.sl('fg')] = _fm(inp['final_g'], 8)
    A[:, cp.sl('ident')] = np.eye(128, dtype=np.float32)
    r = np.arange(64)
    for hb in (0, 64):
        A[hb:hb + 64, cp.sl('msu')] = (r[:, None] < r[None, :]).astype(np.float32)
        A[hb:hb + 64, cp.sl('msl')] = (r[:, None] > r[None, :]).astype(np.float32)
        A[hb:hb + 64, cp.sl('miu')] = (r[:, None] <= r[None, :]).astype(np.float32)
        A[hb:hb + 64, cp.sl('msun')] = -(r[:, None] < r[None, :]).astype(np.float32)
        A[hb:hb + 64, cp.sl('msln')] = -(r[:, None] > r[None, :]).astype(np.float32)
        A[hb:hb + 64, cp.sl('id2')] = np.eye(64, dtype=np.float32)
    return A


def pack_small_mats(inp):
    W = np.zeros((128, DEPTH, 384 * 2 + 256), np.float32)
    for l in range(DEPTH):
        W[:64, l, 0:384] = inp['rwkv_w_up'][l]
        W[64:, l, 0:384] = inp['rwkv_a_up'][l]
        W[:, l, 384:768] = inp['rwkv_g_up'][l]
        wa = np.asarray(inp['lru_wa'][l]).reshape(2, 2, 64, 64)
        wx = np.asarray(inp['lru_wx'][l]).reshape(2, 2, 64, 64)
        for t in range(2):
            W[:, l, 768 + t * 64: 768 + (t + 1) * 64] = wa[t].reshape(128, 64)
            W[:, l, 896 + t * 64: 896 + (t + 1) * 64] = wx[t].reshape(128, 64)
    return W


def bc(ap, shape):
    return ap.broadcast_to(list(shape))


class Cfg:
    def __init__(self, n, ch):
        self.n = n
        self.ch = ch
        self.nch = n // ch
        self.mid = max(ch // 2 - 1, 0)
        lv = 0
        while (1 << lv) < ch:
            lv += 1
        self.lv = lv


class Builder:
    def __init__(self, nlayers=DEPTH, npass=NPASS, do_mlp=True, do_smp=True):
        self.do_smp = do_smp
        self.nlayers = nlayers
        self.npass = npass
        self.do_mlp = do_mlp
        self.cp = make_cp()
        self.uid = 0

    def sb(self, name, shape, dt=F32):
        nb = int(np.prod(shape[1:])) * (2 if dt == BF16 else 4)
        nb = (nb + 31) // 32 * 32
        t = self.nc.alloc_sbuf_tensor_at(name, list(shape), dt, offset=self.sb_off)
        self.sb_off += nb
        assert self.sb_off <= 229376 - 256, (name, self.sb_off)
        return t

    def bank(self):
        b = self.banks[self.bank_i % 8]
        self.bank_i += 1
        return b

    def build(self):
        nc = bass.Bass("TRN2", target_bir_lowering=False)
        self.nc = nc
        S = Sched(nc)
        import os as _os
        if _os.environ.get("K_MAXOPS"):
            S.maxops = int(_os.environ["K_MAXOPS"])
        self.S = S
        cp = self.cp
        dram_in = lambda n, shp: nc.dram_tensor(n, list(shp), F32, kind="ExternalInput").ap()
        dram_out = lambda n, shp: nc.dram_tensor(n, list(shp), F32, kind="ExternalOutput").ap()
        self.xp = dram_in("x_prompt", [SEQ, D])
        self.cpk = dram_in("cpack", [128, cp.n])
        self.smat = dram_in("smat", [128, DEPTH, 1024])
        self.w_in = dram_in("w_in", [self.nlayers, D, C_IN])
        self.w_out = dram_in("w_out", [self.nlayers, D, D])
        self.w1 = dram_in("mlp_w1", [self.nlayers if self.do_mlp else 1, D, DFF])
        self.w2 = dram_in("mlp_w2", [self.nlayers if self.do_mlp else 1, DFF, D])
        self.y_p = dram_out("y_prompt", [SEQ, D])
        self.o_pshift = dram_out("p_shift", [DEPTH, D])
        self.o_pwkv = dram_out("p_wkv", [DEPTH, 6, 64, 64])
        self.o_phgrn = dram_out("p_hgrn", [DEPTH, 6, 64, 64])
        self.o_plru = dram_out("p_lru", [DEPTH, 256])
        self.o_pconv = dram_out("p_conv", [DEPTH, 3, 256])
        self.xs_d = dram_in("x_sample", [16, D])
        self.st_wkv = dram_in("state_wkv", [DEPTH, 16, 6, 64, 64])
        self.st_shift = dram_in("state_shift", [DEPTH, 16, D])
        self.st_hgrn = dram_in("state_hgrn", [DEPTH, 16, 6, 64, 64])
        self.st_lru = dram_in("state_lru", [DEPTH, 16, 256])
        self.st_conv = dram_in("state_conv", [DEPTH, 16, 3, 256])
        self.y_s = dram_out("y_sample", [16, D])
        self.o_swkv = dram_out("s_wkv", [DEPTH, 16, 6, 64, 64])
        self.o_sshift = dram_out("s_shift", [DEPTH, 16, D])
        self.o_shgrn = dram_out("s_hgrn", [DEPTH, 16, 6, 64, 64])
        self.o_slru = dram_out("s_lru", [DEPTH, 16, 256])
        self.o_sconv = dram_out("s_conv", [DEPTH, 16, 3, 256])

        with contextlib.ExitStack() as es:
            self.es = es
            self.sb_off = 0x4000 + 512
            sb = self.sb
            self.banks = [es.enter_context(nc.psum_tensor(f"bank{i}", [128, 512], F32)) for i in range(8)]
            self.bank_i = 0
            self.C = sb("cpk", [128, cp.n])
            S.add('sp', 'dma_start', dma='c0', out=self.C[:], in_=self.cpk)
            self.ident = self.C[:, cp.sl('ident')]
            self.ident_bf = sb("ident_bf", [128, 128], BF16)
            S.add('dve', 'tensor_copy', out=self.ident_bf[:], in_=self.ident)
            self.ones_bf = sb("ones_bf", [128, 128], BF16)
            S.add('dve', 'memset', ap=self.ones_bf[:], constant=1.0)
            self.bmean_bf = sb("bmean_bf", [128, 128], BF16)
            S.add('dve', 'memset', ap=self.bmean_bf[:], constant=0.0)
            S.add('dve', 'memset', ap=self.bmean_bf[0:64, 0:64], constant=1.0 / 64)
            S.add('dve', 'memset', ap=self.bmean_bf[64:128, 64:128], constant=1.0 / 64)
            self.bones_bf = sb("bones_bf", [128, 128], BF16)
            S.add('dve', 'memset', ap=self.bones_bf[:], constant=0.0)
            S.add('dve', 'memset', ap=self.bones_bf[0:64, 0:64], constant=1.0)
            S.add('dve', 'memset', ap=self.bones_bf[64:128, 64:128], constant=1.0)
            self.rmask = sb("rmask", [128, TT])
            S.add('dve', 'memset', ap=self.rmask[:], constant=1.0)
            S.add('dve', 'memset', ap=self.rmask[:].rearrange("p (c t) -> p c t", t=CH)[:, :, 0:1], constant=0.0)
            self.epsc = {}
            for v in (NORM_EPS, GN_EPS, 1e-12, 1.0):
                t = sb(self.nm("eps"), [128, 1])
                S.add('pool', 'memset', ap=t[:], constant=float(v))
                self.epsc[v] = t
            self.lb = sb("lb", [128, 3, DEPTH])
            self.oml = sb("oml", [128, 3, DEPTH])
            self.noml = sb("noml", [128, 3, DEPTH])
            self.hgrn_lb_setup()
            self.c8 = sb("c8", [128, DEPTH, 2])
            self.c16 = sb("c16", [128, DEPTH, 2])
            self.lru_setup()
            self.x = [sb(f"x{t}", [128, KD, TT]) for t in range(NTILE)]
            self.WAin = sb("WAin", [128, KD, C_IN], BF16)
            self.WAout = sb("WAout", [128, KD, D], BF16)
            self.SM = sb("SM", [128, 1024], BF16)
            self.shl = sb("shl", [128, DEPTH, KD])
            S.add('pool', 'memset', ap=self.shl[:], constant=0.0)
            self.Tst = [sb(f"Tst{l}", [128, 3, 64]) for l in range(DEPTH)]
            self.Sst = [sb(f"Sst{l}", [128, 3, 64]) for l in range(DEPTH)]
            self.hst = [sb(f"hst{l}", [128, 2, 1]) for l in range(DEPTH)]
            self.cvh = [sb(f"cvh{l}", [128, 2, 3]) for l in range(DEPTH)]
            self.shh = [sb(f"shh{l}", [128, 11, 1]) for l in range(DEPTH)]
            for l in range(DEPTH):
                for t in (self.Tst[l], self.Sst[l], self.hst[l], self.cvh[l], self.shh[l]):
                    S.add('pool', 'memset', ap=t[:], constant=0.0)
            self.xs = sb("xs", [128, KD, 16])
            self.work_base = self.sb_off
            self.pcfg = Cfg(TT, CH)
            self.scfg = Cfg(16, 1)
            self.smp = False

            self.load_weights_a(0)
            for ps in range(self.npass):
                self.alloc_io()
                self.load_x(ps)
                if self.do_smp and ps == self.npass - 1:
                    self.load_xs()
                S.barrier()
                for l in range(self.nlayers):
                    self.alloc_a()
                    for tt in range(NTILE):
                        last = (ps == self.npass - 1 and tt == NTILE - 1)
                        self.phase_a_tile(l, tt, self.pcfg, last)
                    if self.do_smp and ps == self.npass - 1:
                        self.sample_tile(l)
                    S.barrier()
                    nl, nps = (l + 1, ps) if l + 1 < self.nlayers else (0, ps + 1)
                    nxt = self.weight_loads_a(nl) if nps < self.npass else []
                    if self.do_mlp:
                        self.alloc_b()
                        self.cur_ps = ps
                        self.phase_b(l, nxt)
                        S.barrier()
                    else:
                        for t in nxt:
                            t()
                self.alloc_io()
                self.store_y(ps)
                if self.do_smp and ps == self.npass - 1:
                    self.store_ys()
                S.barrier()
            self.store_states()
            S.barrier()
            S.emit(es)
        return nc

    def mk(self, label):
        import os
        if os.environ.get('K_MARK'):
            print('MARK', label, len(self.S.ops))

    def interleave(self, chain_fns):
        S = self.S
        recs = []
        for fn in chain_fns:
            lst = []
            S.add = (lambda *a, _l=lst, **k: _l.append((a, k)))
            try:
                fn()
            finally:
                del S.add
            recs.append(lst)
        L = max(len(r) for r in recs)
        for w in range(L + len(recs) - 1):
            for j, r in enumerate(recs):
                si = w - j
                if 0 <= si < len(r):
                    a, k = r[si]
                    S.add(*a, **k)

    def wavefront(self, steps, nj):
        for w in range(len(steps) + nj - 1):
            for si in range(len(steps)):
                j = w - si
                if 0 <= j < nj:
                    steps[si](j)

    def nm(self, s):
        self.uid += 1
        return f"{s}_{self.uid}"

    def hgrn_lb_setup(self):
        S, cp = self.S, self.cp
        sb = self.sb
        e = sb("hl_e", [128, 3, DEPTH])
        ssum = sb("hl_s", [128, 3, 1])
        hl = self.C[:, cp.sl('hlb')].rearrange("p (j l) -> p j l", l=DEPTH)
        S.add('act', 'activation', out=e[:], in_=hl, func=AF.Exp)
        S.add('dve', 'tensor_reduce', out=ssum[:], in_=e[:], axis=mybir.AxisListType.X, op=ALU.add)
        S.add('dve', 'reciprocal', out=ssum[:], in_=ssum[:])
        S.add('dve', 'tensor_tensor', out=e[:], in0=e[:], in1=bc(ssum[:], [128, 3, DEPTH]), op=ALU.mult)
        S.add('dve', 'memset', ap=self.lb[:, :, 0:1], constant=0.0)
        for l in range(1, DEPTH):
            S.add('dve', 'tensor_tensor', out=self.lb[:, :, l:l + 1], in0=self.lb[:, :, l - 1:l], in1=e[:, :, l:l + 1], op=ALU.add)
        S.add('dve', 'tensor_scalar', out=self.oml[:], in0=self.lb[:], scalar1=-1.0, scalar2=1.0, op0=ALU.mult, op1=ALU.add)
        S.add('dve', 'tensor_scalar', out=self.noml[:], in0=self.lb[:], scalar1=-1.0, scalar2=None, op0=ALU.add)

    def lru_setup(self):
        S, cp = self.S, self.cp
        t = self.sb("lru_t", [128, DEPTH, 2])
        for l in range(DEPTH):
            S.add('act', 'activation', out=t[:, l, :], in_=self.C[:, cp.sl(f"lam{l}")], func=AF.Exp, scale=-1.0)
        S.add('act', 'activation', out=t[:], in_=t[:], func=AF.Ln, bias=self.epsc[1.0][:])
        S.add('dve', 'tensor_scalar', out=self.c8[:], in0=t[:], scalar1=-8.0, scalar2=None, op0=ALU.mult)
        S.add('dve', 'tensor_scalar', out=self.c16[:], in0=t[:], scalar1=-16.0, scalar2=None, op0=ALU.mult)

    def weight_loads_a(self, l):
        S = self.S
        src = self.w_in[l].rearrange("(k p) c -> p k c", p=128)
        q = C_IN // 4
        th = []
        for i in range(4):
            th.append(lambda i=i: S.add('pool', 'dma_start', dma=f'wain{i}', out=self.WAin[:, :, i * q:(i + 1) * q],
                                        in_=src[:, :, i * q:(i + 1) * q]))
        th.insert(1, lambda: S.add('pool', 'dma_start', dma='sm', out=self.SM[:], in_=self.smat[:, l, :]))
        th.append(lambda: S.add('pool', 'dma_start', dma='waout', out=self.WAout[:], in_=self.w_out[l].rearrange("(k p) c -> p k c", p=128)))
        return th

    def load_weights_a(self, l):
        for t in self.weight_loads_a(l):
            t()

    def alloc_a(self):
        self.sb_off = self.work_base
        sb = self.sb
        u = self.nm("a")
        N = TT
        self.xn = sb(f"xn{u}", [128, KD, N], BF16)
        self.nln = sb(f"nln{u}", [128, N])
        self.nrs = sb(f"nrs{u}", [128, N])
        self.Pr = sb(f"Pr{u}", [128, 11, N + 1])
        self.ds = [sb(f"ds{i}{u}", [128, N]) for i in range(2)]
        self.twxa = sb(f"twxa{u}", [128, N], BF16)
        self.sgg = sb(f"sgg{u}", [128, N], BF16)
        self.A = [sb(f"A{i}{u}", [128, 3, N]) for i in range(6)]
        self._offB0 = self.sb_off
        self.B = [sb(f"B{i}{u}", [128, 3, N], BF16) for i in range(6)]
        self.O = sb(f"O{u}", [128, 3, N])
        self.eM = sb(f"eM{u}", [128, 3, 16])
        self.eC = sb(f"eC{u}", [128, 3, 16])
        self.eCM = sb(f"eCM{u}", [128, 3, 16])
        self.cmid = sb(f"cmid{u}", [128, 3, 16])
        self.AkT = sb(f"AkT{u}", [128, 3, 64], BF16)
        self.TM2 = sb(f"TM2{u}", [128, 6, 64], BF16)
        self.T0f = sb(f"T0f{u}", [128, 3, 64])
        self.T0b = sb(f"T0b{u}", [128, 3, 64], BF16)
        self.mix = sb(f"mix{u}", [128, KD, N], BF16)
        self.nsq = self.mix
        self.xs32 = sb(f"xs32{u}", [128, KD, 16])
        self.P0 = sb(f"P0{u}", [128, 11, 16])
        self.shfm = sb(f"shfm{u}", [128, KD, 16], BF16)
        self._offR2 = self.sb_off
        _stm = sb(f"stm{u}", [16, D])
        self.stm = [_stm, _stm]
        self.Ts = [sb(f"Ts{i}{u}", [128, 3, 64]) for i in range(2)]
        self.Ss = [sb(f"Ss{i}{u}", [128, 3, 64]) for i in range(2)]
        _wkT = sb(f"wkT{u}", [64, 3, 128])
        self.wkT = [_wkT, _wkT]
        self.swi = [sb(f"swi{i}{u}", [64, 6, 64]) for i in range(2)]
        self.CV = sb(f"CV{u}", [128, 6, 16])
        self.H0 = sb(f"H0{u}", [128, 2, 16])
        self.cvtm = sb(f"cvtm{u}", [16, 768])
        self.cvo = sb(f"cvo{u}", [16, 256])
        self.lrtm = sb(f"lrtm{u}", [16, 256])
        self.lro = sb(f"lro{u}", [16, 256])
        self.XBs = sb(f"XBs{u}", [128, 2, 16])
        end_off = self.sb_off
        lim = 0
        self.R = {}
        for nmx, shp in [('ZY0', [128, 2, 3, 64]), ('ZY1', [128, 2, 3, 64]), ('NT', [128, 3, 64]), ('AbT', [128, 3, 64]),
                         ('AkT', [128, 3, 64]), ('PP0', [128, 3, 64]), ('PP1', [128, 3, 64]), ('TM2', [128, 6, 64]),
                         ('TMK', [128, 3, 64]), ('Xb', [128, 3, 64]), ('Ub', [128, 3, 64])]:
            self.R[nmx] = sb(f"R{nmx}{u}", shp)
        self.sb_off = max(self.sb_off, end_off)
        end2 = self.sb_off
        self.sb_off = self._offR2
        self.R2 = dict(self.R)
        for nmx, shp in [('ZY0', [128, 2, 3, 64]), ('ZY1', [128, 2, 3, 64]), ('NT', [128, 3, 64]), ('AbT', [128, 3, 64]),
                         ('AkT', [128, 3, 64]), ('PP0', [128, 3, 64]), ('PP1', [128, 3, 64]), ('TM2', [128, 6, 64]),
                         ('TMK', [128, 3, 64])]:
            self.R2[nmx] = sb(f"R2{nmx}{u}", shp)
        self.HB2 = {'TM2': sb(f"HB2TM2{u}", [128, 6, 64], BF16), 'AkT': sb(f"HB2AkT{u}", [128, 3, 64], BF16)}
        assert self.sb_off <= self._offR2 + 4096 + 4 * 768 + 1536 + 2 * 1536, self.sb_off - self._offR2
        self.sb_off = end2
        if not hasattr(self, '_pa'):
            self._pa = True
            print("phase A work bytes", self.sb_off - self.work_base, "limit", 0x4000 + 212000 - self.work_base)

    def alloc_io(self):
        self.sb_off = self.work_base
        sb = self.sb
        u = self.nm("io")
        N = TT
        self.nsq = sb(f"nsq{u}", [128, KD, N], BF16)
        self.nln = sb(f"nln{u}", [128, N])
        self.nrs = sb(f"nrs{u}", [128, N])
        self.xtm = [sb(f"xtm{i}{u}", [128, D]) for i in range(2)]
        self.yfm = sb(f"yfm{u}", [128, KD, N])
        self.xs32 = sb(f"xs32{u}", [128, KD, 16])
        self.stm = [sb(f"stm{i}{u}", [16, D]) for i in range(2)]

    def load_x(self, ps):
        S = self.S
        for tb in range(TPP // 128):
            buf = self.xtm[tb % 2]
            t0 = ps * TPP + tb * 128
            S.add('sp', 'dma_start', dma=f'xtm{tb % 2}', out=buf[:], in_=self.xp[t0:t0 + 128, :])
            xt = self.x[(tb * 128) // TT]
            tl = (tb * 128) % TT
            for kh in range(2):
                bk = self.bank()
                for kk in range(4):
                    k = kh * 4 + kk
                    S.add('pe', 'transpose', out=bk[:, kk * 128:(kk + 1) * 128],
                          in_=buf[:, k * 128:(k + 1) * 128], identity=self.ident)
                src = bk[:].rearrange("p (k t) -> p k t", k=4)
                dst = xt[:, kh * 4:(kh + 1) * 4, tl:tl + 128]
                if kh == 0:
                    S.add('act', 'activation', out=dst, in_=src, func=AF.Copy)
                else:
                    S.add('dve', 'tensor_copy', out=dst, in_=src)

    def rmsnorm(self, xt, g, out, N, want_last=None):
        S = self.S
        S.add('act', 'activation', out=self.nsq[:, :, :N], in_=xt[:, :, :N], func=AF.Square)
        bk = self.bank()
        for k in range(KD):
            S.add('pe', 'matmul', out=bk[:, :N], lhsT=self.ones_bf[:], rhs=self.nsq[:, k, :N], start=(k == 0), stop=(k == KD - 1))
        S.add('act', 'activation', out=self.nln[:, :N], in_=bk[:, :N], func=AF.Ln, scale=1.0 / D, bias=self.epsc[NORM_EPS][:])
        S.add('act', 'activation', out=self.nrs[:, :N], in_=self.nln[:, :N], func=AF.Exp, scale=-0.5)
        for k in range(KD):
            S.add('dve', 'scalar_tensor_tensor', out=out[:, k, :N], in0=xt[:, k, :N], scalar=g[:, k:k + 1], in1=self.nrs[:, :N],
                  op0=ALU.mult, op1=ALU.mult)
        if want_last is not None:
            S.add('dve', 'tensor_tensor', out=want_last, in0=xt[:, :, N - 1], in1=g, op=ALU.mult)
            S.add('dve', 'tensor_scalar', out=want_last, in0=want_last, scalar1=self.nrs[:, N - 1:N], scalar2=None, op0=ALU.mult)

    def project(self, cols, N, evac, rhs=None):
        S = self.S
        for i in range(0, len(cols), 2):
            pair = cols[i:i + 2]
            bk = self.bank()
            for idx, c in enumerate(pair):
                for k in range(KD):
                    S.add('pe', 'matmul', out=bk[:, idx * 256: idx * 256 + N], lhsT=self.WAin[:, k, c * 128:(c + 1) * 128],
                          rhs=(self.xn if rhs is None else rhs)[:, k, :N], start=(k == 0), stop=(k == KD - 1))
            evac(bk, pair)

    def phase_a_tile(self, l, tt, cfg, last):
        S, cp = self.S, self.cp
        N = cfg.n
        xt = self.x[tt]
        self.rmsnorm(xt, self.C[:, cp.sl(f"g1{l}")], self.xn, N, want_last=self.shl[:, l, :] if last else None)
        import os
        mixs = os.environ.get("K_MIX", "rhl")
        if mixs != "rhl":
            S.add('dve', 'memset', ap=self.mix[:], constant=0.0)
        if 'r' in mixs:
            self.rwkv(l, cfg)
        if 'l' in mixs:
            self.lru(l, cfg, last)
        if 'h' in mixs:
            self.hgrn(l, cfg)
        for i in range(0, KD, 2):
            bk = self.bank()
            for idx in range(2):
                c = i + idx
                for k in range(KD):
                    S.add('pe', 'matmul', out=bk[:, idx * 256: idx * 256 + N], lhsT=self.WAout[:, k, c * 128:(c + 1) * 128],
                          rhs=self.mix[:, k, :N], start=(k == 0), stop=(k == KD - 1))
            S.add('dve', 'tensor_tensor', out=xt[:, i:i + 2, :N], in0=xt[:, i:i + 2, :N],
                  in1=bk[:].rearrange("p (c t) -> p c t", c=2)[:, :, :N], op=ALU.add)

    def tm2fm(self, src, nblk, dst, eng='act'):
        S = self.S
        bk = self.bank()
        for b in range(nblk):
            S.add('pe', 'transpose', out=bk[:, b * 16:(b + 1) * 16], in_=src[:, b * 128:(b + 1) * 128], identity=self.ident[0:16, 0:16])
        srcv = bk[:, 0:nblk * 16].rearrange("p (b t) -> p b t", t=16)
        if eng == 'act':
            S.add('act', 'activation', out=dst, in_=srcv, func=AF.Copy)
        else:
            S.add('dve', 'tensor_copy', out=dst, in_=srcv)

    def fm2tm(self, src, nblk, dst):
        S = self.S
        for b0 in range(0, nblk, 4):
            nb = min(4, nblk - b0)
            bk = self.bank()
            for b in range(nb):
                S.add('pe', 'transpose', out=bk[0:16, b * 128:(b + 1) * 128], in_=src(b0 + b), identity=self.ident)
            S.add('act', 'activation', out=dst[:, b0 * 128:(b0 + nb) * 128], in_=bk[0:16, 0:nb * 128], func=AF.Copy)

    def load_xs(self):
        S = self.S
        S.add('sp', 'dma_start', dma='stm0', out=self.stm[0][:], in_=self.xs_d)
        self.tm2fm(self.stm[0], KD, self.xs[:])

    def store_ys(self):
        S, cp = self.S, self.cp
        self.rmsnorm(self.xs, self.C[:, cp.sl('fg')], self.xs32, 16)
        self.fm2tm(lambda b: self.xs32[:, b, :], KD, self.stm[0])
        S.add('sp', 'dma_start', dma='stm0', out=self.y_s, in_=self.stm[0][:])

    def sample_tile(self, l):
        S, cp = self.S, self.cp
        cfg = self.scfg
        N = 16
        S.barrier()
        self.smp = True
        self.rmsnorm(self.xs, self.C[:, cp.sl(f"g1{l}")], self.xs32, N)
        S.add('act', 'activation', out=self.xn[:, :, :N], in_=self.xs32[:], func=AF.Copy)
        self.fm2tm(lambda b: self.xs32[:, b, :], KD, self.stm[1])
        S.add('sp', 'dma_start', dma='stm0o', out=self.o_sshift[l], in_=self.stm[1][:])
        S.add('sp', 'dma_start', dma='stm0', out=self.stm[0][:], in_=self.st_shift[l])
        self.tm2fm(self.stm[0], KD, self.shfm[:])
        S.add('sp', 'dma_start', dma='cvtm', out=self.cvtm[:], in_=self.st_conv[l].rearrange("b j c -> b (j c)"))
        self.tm2fm(self.cvtm, 6, self.CV[:], eng='dve')
        S.add('sp', 'dma_start', dma='lrtm', out=self.lrtm[:], in_=self.st_lru[l])
        self.tm2fm(self.lrtm, 2, self.H0[:], eng='dve')
        self.rwkv(l, cfg)
        self.hgrn(l, cfg)
        self.lru(l, cfg, False)
        S.add('sp', 'dma_start', dma='cvo0', out=self.o_sconv[l][:, 0:2, :].rearrange("b j c -> b (j c)"), in_=self.cvtm[:, 256:768])
        self.fm2tm(lambda b: self.XBs[:, b, :], 2, self.cvo)
        S.add('sp', 'dma_start', dma='cvo1', out=self.o_sconv[l][:, 2, :], in_=self.cvo[:])
        self.fm2tm(lambda b: self.H0[:, b, :], 2, self.lro)
        S.add('sp', 'dma_start', dma='lro', out=self.o_slru[l], in_=self.lro[:])
        for i in range(0, KD, 2):
            bk = self.bank()
            for idx in range(2):
                c = i + idx
                for k in range(KD):
                    S.add('pe', 'matmul', out=bk[:, idx * 256: idx * 256 + N], lhsT=self.WAout[:, k, c * 128:(c + 1) * 128],
                          rhs=self.mix[:, k, :N], start=(k == 0), stop=(k == KD - 1))
            S.add('dve', 'tensor_tensor', out=self.xs[:, i:i + 2, :N], in0=self.xs[:, i:i + 2, :N],
                  in1=bk[:].rearrange("p (c t) -> p c t", c=2)[:, :, :N], op=ALU.add)
        self.smp = False

    def rwkv(self, l, cfg):
        S, cp = self.S, self.cp
        N, CHL, NCHK = cfg.n, cfg.ch, cfg.nch
        Pr = self.Pr
        C = self.C
        sg, a_, kap, t1, E, bonus = self.A
        rt, kt, bt, kkt, vbf, rkbf = self.B
        cs = lambda nm: C[:, cp.sl(f"{nm}{l}")]
        smp = self.smp
        if not smp:
            S.add('act', 'activation', out=Pr[:, 0:11, 0:1], in_=self.shh[l][:], func=AF.Copy)

        def evac(bk, pair):
            S.add('act', 'activation', out=Pr[:, pair[0]:pair[0] + len(pair), 1:N + 1],
                  in_=bk[:].rearrange("p (c t) -> p c t", c=2)[:, 0:len(pair), :N], func=AF.Copy)
        self.project(list(range(11)), N, evac)
        if smp:
            def evac0(bk, pair):
                S.add('act', 'activation', out=self.P0[:, pair[0]:pair[0] + len(pair), :N],
                      in_=bk[:].rearrange("p (c t) -> p c t", c=2)[:, 0:len(pair), :N], func=AF.Copy)
            self.project(list(range(11)), N, evac0, rhs=self.shfm)
        else:
            S.add('act', 'activation', out=self.shh[l][:], in_=Pr[:, 0:11, N:N + 1], func=AF.Copy)
        self.mk('r_after_proj')
        mu = cs('mu')
        for c in range(11):
            d = self.ds[c % 2]
            S.add('pool', 'tensor_tensor', out=d[:, :N], in0=(self.P0[:, c, :N] if smp else Pr[:, c, 0:N]), in1=Pr[:, c, 1:N + 1], op=ALU.subtract)
            S.add('dve', 'scalar_tensor_tensor', out=Pr[:, c, 1:N + 1], in0=d[:, :N], scalar=mu[:, c:c + 1], in1=Pr[:, c, 1:N + 1],
                  op0=ALU.mult, op1=ALU.add)
        r = Pr[:, 0:3, 1:N + 1]
        k = Pr[:, 3:6, 1:N + 1]
        v = Pr[:, 6:9, 1:N + 1]
        self.mk('r_after_pm')
        S.add('act', 'activation', out=self.twxa[0:64, :N], in_=Pr[0:64, 9, 1:N + 1], func=AF.Tanh)
        S.add('act', 'activation', out=self.twxa[64:128, :N], in_=Pr[64:128, 9, 1:N + 1], func=AF.Copy)
        S.add('act', 'activation', out=self.sgg[:, :N], in_=Pr[:, 10, 1:N + 1], func=AF.Sigmoid)
        c3 = lambda t, j: t[:, j, :N].rearrange("p (c t) -> p c t", t=CHL)
        rj = lambda j: Pr[:, j, 1:N + 1]
        kj = lambda j: Pr[:, 3 + j, 1:N + 1]
        vj = lambda j: Pr[:, 6 + j, 1:N + 1]
        st = {}

        def s_lora(j):
            st[('bk', j)] = self.bank()
            st[('bk2', j)] = self.bank()
            S.add('pe', 'matmul', out=st[('bk', j)][:, 0:N], lhsT=self.SM[0:64, j * 128:(j + 1) * 128], rhs=self.twxa[0:64, :N], start=True, stop=True)
            S.add('pe', 'matmul', out=st[('bk2', j)][:, 0:N], lhsT=self.SM[64:128, j * 128:(j + 1) * 128], rhs=self.twxa[64:128, :N], start=True, stop=True)

        def s_bones1(j):
            st[('b3', j)] = self.bank()
            S.add('pe', 'matmul', out=st[('b3', j)][:, 0:N], lhsT=self.bones_bf[:], rhs=rkbf[:, j, :N], start=True, stop=True)

        def s_bones2(j):
            st[('b4', j)] = self.bank()
            S.add('pe', 'matmul', out=st[('b4', j)][:, 0:N], lhsT=self.bones_bf[:], rhs=rkbf[:, j, :N], start=True, stop=True)

        def s_scan(j):
            if CHL > 1:
                S.add('dve', 'tensor_tensor_scan', out=t1[:, j, :N], data0=self.rmask[:, :N], data1=sg[:, j, :N], initial=0.0, op0=ALU.mult, op1=ALU.add)
            else:
                S.add('dve', 'tensor_copy', out=t1[:, j, :N], in_=sg[:, j, :N])

        steps = [
            s_lora,
            lambda j: S.add('act', 'activation', out=sg[:, j, :N], in_=st[('bk', j)][:, 0:N], func=AF.Sigmoid, bias=cs('w0')[:, j:j + 1]),
            lambda j: S.add('act', 'activation', out=a_[:, j, :N], in_=st[('bk2', j)][:, 0:N], func=AF.Sigmoid, bias=cs('a0')[:, j:j + 1]),
            lambda j: S.add('dve', 'tensor_scalar', out=kap[:, j, :N], in0=kj(j), scalar1=cs('kk')[:, j:j + 1], scalar2=None, op0=ALU.mult),
            lambda j: S.add('act', 'activation', out=rkbf[:, j, :N], in_=kap[:, j, :N], func=AF.Square),
            s_bones1,
            lambda j: S.add('act', 'activation', out=t1[:, j, :N], in_=st[('b3', j)][:, 0:N], func=AF.Ln, bias=self.epsc[1e-12][:]),
            lambda j: S.add('act', 'activation', out=t1[:, j, :N], in_=t1[:, j, :N], func=AF.Exp, scale=-0.5),
            lambda j: S.add('dve', 'tensor_tensor', out=kap[:, j, :N], in0=kap[:, j, :N], in1=t1[:, j, :N], op=ALU.mult),
            lambda j: S.add('dve', 'tensor_scalar', out=t1[:, j, :N], in0=a_[:, j, :N], scalar1=-1.0, scalar2=cs('ka')[:, j:j + 1], op0=ALU.add, op1=ALU.mult),
            lambda j: S.add('dve', 'scalar_tensor_tensor', out=kj(j), in0=t1[:, j, :N], scalar=1.0, in1=kj(j), op0=ALU.add, op1=ALU.mult),
            lambda j: S.add('dve', 'tensor_tensor', out=a_[:, j, :N], in0=a_[:, j, :N], in1=kap[:, j, :N], op=ALU.mult),
            lambda j: S.add('dve', 'scalar_tensor_tensor', out=rkbf[:, j, :N], in0=rj(j), scalar=cs('rk')[:, j:j + 1], in1=kj(j), op0=ALU.mult, op1=ALU.mult),
            s_bones2,
            lambda j: S.add('dve', 'tensor_tensor', out=bonus[:, j, :N], in0=vj(j), in1=st[('b4', j)][:, 0:N], op=ALU.mult),
            s_scan,
            lambda j: S.add('act', 'activation', out=self.eM[:, j, :NCHK], in_=c3(t1, j)[:, :, cfg.mid], func=AF.Exp, scale=-C0),
            lambda j: S.add('act', 'activation', out=self.eC[:, j, :NCHK], in_=c3(t1, j)[:, :, CHL - 1], func=AF.Exp, scale=-C0),
            lambda j: S.add('act', 'activation', out=self.cmid[:, j, :NCHK], in_=c3(t1, j)[:, :, cfg.mid], func=AF.Copy),
            lambda j: S.add('dve', 'tensor_tensor', out=c3(t1, j), in0=c3(t1, j), in1=bc(self.cmid[:, j, :NCHK].unsqueeze(2), [128, NCHK, CHL]), op=ALU.subtract),
            lambda j: S.add('act', 'activation', out=E[:, j, :N], in_=t1[:, j, :N], func=AF.Exp, scale=-C0),
            lambda j: S.add('dve', 'tensor_tensor', out=rj(j), in0=rj(j), in1=E[:, j, :N], op=ALU.mult),
            lambda j: S.add('act', 'activation', out=self.eCM[:, j, :NCHK], in_=c3(E, j)[:, :, CHL - 1], func=AF.Copy),
            lambda j: S.add('act', 'activation', out=E[:, j, :N], in_=t1[:, j, :N], func=AF.Exp, scale=C0),
            lambda j: S.add('dve', 'tensor_tensor', out=a_[:, j, :N], in0=a_[:, j, :N], in1=E[:, j, :N], op=ALU.mult),
            lambda j: S.add('dve', 'tensor_tensor', out=kj(j), in0=kj(j), in1=E[:, j, :N], op=ALU.mult),
            lambda j: S.add('dve', 'tensor_tensor', out=t1[:, j, :N], in0=t1[:, j, :N], in1=sg[:, j, :N], op=ALU.subtract),
            lambda j: S.add('act', 'activation', out=E[:, j, :N], in_=t1[:, j, :N], func=AF.Exp, scale=-C0),
            lambda j: S.add('dve', 'tensor_tensor', out=kap[:, j, :N], in0=kap[:, j, :N], in1=E[:, j, :N], op=ALU.mult),
        ]
        self.wavefront(steps, 3)
        self.mk('r_pre_chunk')
        T = self.Tst[l]
        PS = [slice(0, 128)] if CHL == 64 else [slice(0, CHL), slice(64, 64 + CHL)]
        msun = C[:, cp.sl('msun')]
        msln = C[:, cp.sl('msln')]
        msu = C[:, cp.sl('msu')]
        miu = C[:, cp.sl('miu')]
        id2 = C[:, cp.sl('id2')]
        npz = lambda ps: ps.stop - ps.start
        m3 = lambda m, ps: bc(m[ps, 0:CHL].unsqueeze(1), [npz(ps), 3, CHL])
        hv = lambda bk, ps: bk[ps, 0:192].rearrange("p (j t) -> p j t", j=3)[:, :, 0:CHL]
        sv = lambda t, ps: t[ps, :, 0:CHL]
        hs = lambda hh: slice(hh * 64, hh * 64 + 64)
        ph = lambda hh: slice(hh * 64, hh * 64 + CHL)
        HEADS = [(j, hh) for hh in range(2) for j in range(3)]
        def chunk_gen(ch, R):
            T = self.Tst[l]
            ts = slice(ch * CHL, (ch + 1) * CHL)
            if smp:
                T = self.Ts[ch % 2]
                swi = self.swi[ch % 2]
                if ch == 0:
                    S.add('sp', 'dma_start', dma='swi0', out=swi[:], in_=self.st_wkv[l, 0].rearrange("h v k -> v h k"))
                if ch + 1 < NCHK:
                    S.add('sp', 'dma_start', dma=f'swi{(ch + 1) % 2}', out=self.swi[(ch + 1) % 2][:],
                          in_=self.st_wkv[l, ch + 1].rearrange("h v k -> v h k"))
                bki = self.bank()
                for j in range(3):
                    S.add('pe', 'transpose', out=bki[:, j * 64:(j + 1) * 64], in_=swi[:, 2 * j:2 * j + 2, :], identity=self.ident[0:64, 0:64])
                S.add('act', 'activation', out=T[:], in_=bki[:, 0:192].rearrange("p (j v) -> p j v", j=3), func=AF.Copy)
            bkA, bkB = self.bank(), self.bank()
            for j in range(3):
                for hh in range(2):
                    S.add('pe', 'matmul', out=bkA[ph(hh), j * 64:(j + 1) * 64], lhsT=Pr[:, 6 + j, 1 + ch * CHL:1 + (ch + 1) * CHL], rhs=self.ident[:, hs(hh)], start=True, stop=True)
                for hh in range(2):
                    S.add('pe', 'matmul', out=bkA[ph(hh), 192 + j * 64:192 + (j + 1) * 64], lhsT=a_[:, j, ts], rhs=self.ident[:, hs(hh)], start=True, stop=True)
                for hh in range(2):
                    S.add('pe', 'matmul', out=bkB[ph(hh), j * 64:(j + 1) * 64], lhsT=Pr[:, 3 + j, 1 + ch * CHL:1 + (ch + 1) * CHL], rhs=self.ident[:, hs(hh)], start=True, stop=True)
            for ps in PS:
                S.add('act', 'activation', out=R['TM2'][ps], in_=bkA[ps, 0:384].rearrange("p (q f) -> p q f", f=64), func=AF.Copy)
                S.add('dve', 'tensor_copy', out=R['TMK'][ps], in_=bkB[ps, 0:192].rearrange("p (q f) -> p q f", f=64))
            Vtm = lambda j, hh: R['TM2'][ph(hh), j, :]
            Btm = lambda j, hh: R['TM2'][ph(hh), 3 + j, :]
            Ktm = lambda j, hh: R['TMK'][ph(hh), j, :]
            ts1 = slice(1 + ch * CHL, 1 + (ch + 1) * CHL)
            FMSRC = {'rt': lambda p, j: Pr[p, j, ts1], 'kkt': lambda p, j: Pr[p, 3 + j, ts1], 'v': lambda p, j: Pr[p, 6 + j, ts1],
                     'bt': lambda p, j: a_[p, j, ts], 'kt': lambda p, j: kap[p, j, ts]}
            fm = lambda t, j, hh: FMSRC[t](hs(hh), j)
            cc = lambda t, j, hh: t[ph(hh), j, 0:CHL]
            pc = lambda bk, j, hh: bk[ph(hh), j * 64:j * 64 + CHL]
            self.mk('r_after_tm')
            yield 0
            ONE = (CHL == 1)
            if ONE:
                pab, pak = self.bank(), self.bank()
            else:
                pz, py, pn, pab, pak = [self.bank() for _ in range(5)]
            for (j, hh) in HEADS:
                if not ONE:
                    S.add('pe', 'matmul', out=pc(pz, j, hh), lhsT=fm('bt', j, hh), rhs=fm('kt', j, hh), start=True, stop=True)
                S.add('pe', 'matmul', out=pc(pab, j, hh), lhsT=fm('bt', j, hh), rhs=fm('rt', j, hh), start=True, stop=True)
                if not ONE:
                    S.add('pe', 'matmul', out=pc(pn, j, hh), lhsT=fm('kkt', j, hh), rhs=fm('kt', j, hh), start=True, stop=True)
                S.add('pe', 'matmul', out=pc(pak, j, hh), lhsT=fm('kkt', j, hh), rhs=fm('rt', j, hh), start=True, stop=True)
                if not ONE:
                    S.add('pe', 'matmul', out=pc(py, j, hh), lhsT=fm('kt', j, hh), rhs=fm('bt', j, hh), start=True, stop=True)
            zy = R['ZY0']
            Zt = lambda z: z[:, 0, :, :]
            Yt = lambda z: z[:, 1, :, :]
            P = R['PP0']
            for ps in PS:
                if not ONE:
                    S.add('dve', 'tensor_tensor', out=sv(Zt(zy), ps), in0=hv(pz, ps), in1=m3(msun, ps), op=ALU.mult)
                    S.add('dve', 'tensor_tensor', out=sv(Yt(zy), ps), in0=hv(py, ps), in1=m3(msln, ps), op=ALU.mult)
                    S.add('dve', 'tensor_tensor', out=sv(R['NT'], ps), in0=hv(pn, ps), in1=m3(msu, ps), op=ALU.mult)
                S.add('dve', 'tensor_tensor', out=sv(R['AbT'], ps), in0=hv(pab, ps), in1=m3(miu, ps), op=ALU.mult)
                S.add('dve', 'tensor_tensor', out=sv(R['AkT'], ps), in0=hv(pak, ps), in1=m3(miu, ps), op=ALU.mult)
                if not ONE:
                    S.add('dve', 'tensor_tensor', out=sv(P, ps), in0=sv(Zt(zy), ps), in1=m3(id2, ps), op=ALU.add)
            self.mk('r_after_masks')
            yield 0
            cur = 0
            for lv in range(1, cfg.lv):
                zn = R['ZY%d' % (1 - cur)]
                zo = R['ZY%d' % cur]
                pyb = self.bank()
                for (j, hh) in HEADS:
                    S.add('pe', 'matmul', out=pc(pyb, j, hh), lhsT=cc(Zt(zo), j, hh), rhs=cc(Yt(zo), j, hh), start=True, stop=True)
                for ps in PS:
                    S.add('act', 'activation', out=sv(Yt(zn), ps), in_=hv(pyb, ps), func=AF.Copy)
                if lv < cfg.lv - 1:
                    pzb = self.bank()
                    for (j, hh) in HEADS:
                        S.add('pe', 'matmul', out=pc(pzb, j, hh), lhsT=cc(Yt(zo), j, hh), rhs=cc(Zt(zo), j, hh), start=True, stop=True)
                    for ps in PS:
                        S.add('act', 'activation', out=sv(Zt(zn), ps), in_=hv(pzb, ps), func=AF.Copy)
                yield 0
                ppb = self.bank()
                Pn = R['PP1'] if P is R['PP0'] else R['PP0']
                for (j, hh) in HEADS:
                    S.add('pe', 'matmul', out=pc(ppb, j, hh), lhsT=cc(Yt(zn), j, hh), rhs=cc(P, j, hh), start=True, stop=True)
                for ps in PS:
                    S.add('dve', 'tensor_tensor', out=sv(Pn, ps), in0=hv(ppb, ps), in1=sv(P, ps), op=ALU.add)
                P = Pn
                cur = 1 - cur
                yield 0
            self.mk('r_after_inv')
            yield 'SD'
            S.add('dve', 'tensor_tensor', out=self.T0f[:], in0=T[:], in1=bc(self.eM[:, :, ch:ch + 1], [128, 3, 64]), op=ALU.mult)
            T0h = lambda j, hh: self.T0f[hs(hh), j, :]
            pv = lambda bk, j, hh: bk[ph(hh), j * 64:(j + 1) * 64]
            px = self.bank()
            for (j, hh) in HEADS:
                if not ONE:
                    S.add('pe', 'matmul', out=pv(px, j, hh), lhsT=cc(R['NT'], j, hh), rhs=Vtm(j, hh), start=True, stop=False)
                S.add('pe', 'matmul', out=pv(px, j, hh), lhsT=fm('kt', j, hh), rhs=T0h(j, hh), start=ONE, stop=True)
            if ONE:
                for ps in PS:
                    S.add('act', 'activation', out=R['Ub'][ps], in_=px[ps, 0:192].rearrange("p (j v) -> p j v", j=3), func=AF.Copy, scale=-1.0)
            else:
                for ps in PS:
                    S.add('act', 'activation', out=R['Xb'][ps], in_=px[ps, 0:192].rearrange("p (j v) -> p j v", j=3), func=AF.Copy)
                self.mk('r_after_x')
                pu = self.bank()
                for (j, hh) in HEADS:
                    S.add('pe', 'matmul', out=pv(pu, j, hh), lhsT=cc(P, j, hh), rhs=R['Xb'][ph(hh), j, :], start=True, stop=True)
                for ps in PS:
                    S.add('act', 'activation', out=R['Ub'][ps], in_=pu[ps, 0:192].rearrange("p (j v) -> p j v", j=3), func=AF.Copy, scale=-1.0)
            self.mk('r_after_u')
            po = self.bank()
            if ONE:
                po2 = self.bank()
                for (j, hh) in HEADS:
                    S.add('pe', 'matmul', out=po[hs(hh), j * CHL:(j + 1) * CHL], lhsT=T0h(j, hh), rhs=fm('rt', j, hh), start=True, stop=True)
                for (j, hh) in HEADS:
                    o_ap = po2[hs(hh), j * CHL:(j + 1) * CHL]
                    S.add('pe', 'matmul', out=o_ap, lhsT=R['Ub'][ph(hh), j, :], rhs=cc(R['AbT'], j, hh), start=True, stop=False)
                    S.add('pe', 'matmul', out=o_ap, lhsT=Vtm(j, hh), rhs=cc(R['AkT'], j, hh), start=False, stop=True)
                S.add('act', 'activation', out=self.O[:, :, ts], in_=po[:, 0:3 * CHL].rearrange("p (j t) -> p j t", j=3), func=AF.Copy)
                S.add('dve', 'tensor_tensor', out=self.O[:, :, ts], in0=self.O[:, :, ts], in1=po2[:, 0:3 * CHL].rearrange("p (j t) -> p j t", j=3), op=ALU.add)
            else:
                for (j, hh) in HEADS:
                    o_ap = po[hs(hh), j * CHL:(j + 1) * CHL]
                    S.add('pe', 'matmul', out=o_ap, lhsT=T0h(j, hh), rhs=fm('rt', j, hh), start=True, stop=False)
                    S.add('pe', 'matmul', out=o_ap, lhsT=R['Ub'][ph(hh), j, :], rhs=cc(R['AbT'], j, hh), start=False, stop=False)
                    S.add('pe', 'matmul', out=o_ap, lhsT=Vtm(j, hh), rhs=cc(R['AkT'], j, hh), start=False, stop=True)
                S.add('act', 'activation', out=self.O[:, :, ts], in_=po[:, 0:3 * CHL].rearrange("p (j t) -> p j t", j=3), func=AF.Copy)
            self.mk('r_after_o')
            pt = self.bank()
            for (j, hh) in HEADS:
                t_ap = pt[hs(hh), j * 64:(j + 1) * 64]
                S.add('pe', 'matmul', out=t_ap, lhsT=Btm(j, hh), rhs=R['Ub'][ph(hh), j, :], start=True, stop=False)
                S.add('pe', 'matmul', out=t_ap, lhsT=Ktm(j, hh), rhs=Vtm(j, hh), start=False, stop=True)
            S.add('dve', 'tensor_tensor', out=self.T0f[:], in0=self.T0f[:], in1=pt[:, 0:192].rearrange("p (j v) -> p j v", j=3), op=ALU.add)
            S.add('dve', 'tensor_tensor', out=T[:], in0=self.T0f[:], in1=bc(self.eCM[:, :, ch:ch + 1], [128, 3, 64]), op=ALU.mult)
            if smp:
                wk = self.wkT[ch % 2]
                bko = self.bank()
                for j in range(3):
                    S.add('pe', 'transpose', out=bko[0:64, j * 128:(j + 1) * 128], in_=T[:, j, :], identity=self.ident)
                S.add('act', 'activation', out=wk[:], in_=bko[0:64, 0:384].rearrange("p (j f) -> p j f", j=3), func=AF.Copy)
                S.add('sp', 'dma_start', dma='swo0', out=self.o_swkv[l, ch].rearrange("h v k -> v h k"),
                      in_=wk[:].rearrange("p j (hh k) -> p (j hh) k", hh=2))

        GRP = 2 if (CHL == 64 and not smp) else 1
        for c0 in range(0, NCHK, GRP):
            gens = [chunk_gen(c0 + i, self.R if i == 0 else self.R2) for i in range(min(GRP, NCHK - c0))]
            live = list(gens)
            while live:
                nxt = []
                for gq in live:
                    if next(gq) != 'SD':
                        nxt.append(gq)
                live = nxt
            for gq in gens:
                for _ in gq:
                    pass
        self.mk('r_post')
        O = self.O
        pst = {}

        def p_mm(key, rhs_fn, lhsT):
            def f(j):
                pst[(key, j)] = self.bank()
                S.add('pe', 'matmul', out=pst[(key, j)][:, 0:N], lhsT=lhsT(j), rhs=rhs_fn(j), start=True, stop=True)
            return f
        psteps = [
            lambda j: S.add('act', 'activation', out=rkbf[:, j, :N], in_=O[:, j, :N], func=AF.Copy),
            p_mm('m', lambda j: rkbf[:, j, :N], lambda j: self.bmean_bf[:]),
            lambda j: S.add('dve', 'tensor_tensor', out=O[:, j, :N], in0=O[:, j, :N], in1=pst[('m', j)][:, 0:N], op=ALU.subtract),
            lambda j: S.add('act', 'activation', out=rkbf[:, j, :N], in_=O[:, j, :N], func=AF.Square),
            p_mm('v', lambda j: rkbf[:, j, :N], lambda j: self.bmean_bf[:]),
            lambda j: S.add('act', 'activation', out=t1[:, j, :N], in_=pst[('v', j)][:, 0:N], func=AF.Ln, bias=self.epsc[GN_EPS][:]),
            lambda j: S.add('act', 'activation', out=t1[:, j, :N], in_=t1[:, j, :N], func=AF.Exp, scale=-0.5),
            lambda j: S.add('dve', 'tensor_tensor', out=O[:, j, :N], in0=O[:, j, :N], in1=t1[:, j, :N], op=ALU.mult),
            lambda j: S.add('dve', 'tensor_scalar', out=O[:, j, :N], in0=O[:, j, :N], scalar1=cs('lnw')[:, j:j + 1], scalar2=cs('lnb')[:, j:j + 1],
                            op0=ALU.mult, op1=ALU.add),
            lambda j: S.add('dve', 'tensor_tensor', out=O[:, j, :N], in0=O[:, j, :N], in1=bonus[:, j, :N], op=ALU.add),
            p_mm('g', lambda j: self.sgg[:, :N], lambda j: self.SM[:, 384 + j * 128:384 + (j + 1) * 128]),
            lambda j: S.add('dve', 'tensor_tensor', out=self.mix[:, j, :N], in0=O[:, j, :N], in1=pst[('g', j)][:, 0:N], op=ALU.mult),
        ]
        self.wavefront(psteps, 3)

    def hgrn(self, l, cfg):
        S, cp = self.S, self.cp
        N, CHL, NCHK = cfg.n, cfg.ch, cfg.nch
        Pr = self.Pr
        C = self.C
        q32, lf, kin, cum, E, sil = self.A
        qt, kkt, vbf, tmpb = self.B[0], self.B[1], self.B[2], self.B[3]
        oml = lambda j: self.oml[:, j, l:l + 1]
        noml = lambda j: self.noml[:, j, l:l + 1]
        lbj = lambda j: self.lb[:, j, l:l + 1]
        self.mk('h_start')
        def evac(bk, pair):
            v2 = bk[:].rearrange("p (c t) -> p c t", c=2)
            for idx, c in enumerate(pair):
                g_, j = (c - 11) // 3, (c - 11) % 3
                src = v2[:, idx, :N]
                if g_ == 0:
                    S.add('act', 'activation', out=q32[:, j, :N], in_=src, func=AF.Copy)
                elif g_ == 1:
                    S.add('act', 'activation', out=kin[:, j, :N], in_=src, func=AF.Sigmoid)
                elif g_ == 2:
                    S.add('act', 'activation', out=vbf[:, j, :N], in_=src, func=AF.Copy)
                else:
                    S.add('act', 'activation', out=sil[:, j, :N], in_=src, func=AF.Silu)
        self.project(list(range(11, 23)), N, evac)
        self.mk('h_after_proj')
        c3 = lambda t, j: t[:, j, :N].rearrange("p (c t) -> p c t", t=CHL)

        def h_scan(j):
            if CHL > 1:
                S.add('dve', 'tensor_tensor_scan', out=cum[:, j, :N], data0=self.rmask[:, :N], data1=lf[:, j, :N], initial=0.0, op0=ALU.mult, op1=ALU.add)
            else:
                S.add('dve', 'tensor_copy', out=cum[:, j, :N], in_=lf[:, j, :N])
        hsteps = [
            lambda j: S.add('act', 'activation', out=lf[:, j, :N], in_=kin[:, j, :N], func=AF.Ln, scale=oml(j), bias=lbj(j)),
            lambda j: S.add('dve', 'tensor_scalar', out=kin[:, j, :N], in0=kin[:, j, :N], scalar1=-1.0, scalar2=noml(j), op0=ALU.add, op1=ALU.mult),
            h_scan,
            lambda j: S.add('act', 'activation', out=self.eM[:, j, :NCHK], in_=c3(cum, j)[:, :, cfg.mid], func=AF.Exp),
            lambda j: S.add('act', 'activation', out=self.cmid[:, j, :NCHK], in_=c3(cum, j)[:, :, cfg.mid], func=AF.Copy),
            lambda j: S.add('dve', 'tensor_tensor', out=c3(cum, j), in0=c3(cum, j), in1=bc(self.cmid[:, j, :NCHK].unsqueeze(2), [128, NCHK, CHL]), op=ALU.subtract),
            lambda j: S.add('act', 'activation', out=E[:, j, :N], in_=cum[:, j, :N], func=AF.Exp),
            lambda j: S.add('dve', 'tensor_tensor', out=qt[:, j, :N], in0=q32[:, j, :N], in1=E[:, j, :N], op=ALU.mult),
            lambda j: S.add('act', 'activation', out=self.eCM[:, j, :NCHK], in_=c3(E, j)[:, :, CHL - 1], func=AF.Copy),
            lambda j: S.add('act', 'activation', out=E[:, j, :N], in_=cum[:, j, :N], func=AF.Exp, scale=-1.0),
            lambda j: S.add('dve', 'tensor_tensor', out=kkt[:, j, :N], in0=kin[:, j, :N], in1=E[:, j, :N], op=ALU.mult),
        ]
        self.wavefront(hsteps, 3)
        self.mk('h_pre_chunk')
        Sst = self.Sst[l]
        smp = self.smp
        PS = [slice(0, 128)] if CHL == 64 else [slice(0, CHL), slice(64, 64 + CHL)]
        miu = C[:, cp.sl('miu')]
        npz = lambda ps: ps.stop - ps.start
        m3 = lambda m, ps: bc(m[ps, 0:CHL].unsqueeze(1), [npz(ps), 3, CHL])
        hv = lambda bk, ps: bk[ps, 0:192].rearrange("p (j t) -> p j t", j=3)[:, :, 0:CHL]
        sv = lambda t, ps: t[ps, :, 0:CHL]
        hs = lambda hh: slice(hh * 64, hh * 64 + 64)
        ph = lambda hh: slice(hh * 64, hh * 64 + CHL)
        HEADS = [(j, hh) for hh in range(2) for j in range(3)]
        def hchunk_gen(ch, HB):
            Sst = self.Sst[l]
            ts = slice(ch * CHL, (ch + 1) * CHL)
            if smp:
                Sst = self.Ss[ch % 2]
                for cn in ([0, 1] if ch == 0 else [ch + 1]):
                    if cn < NCHK:
                        for hh in range(2):
                            S.add('sp', 'dma_start', dma=f'shi{cn % 2}', out=self.Ss[cn % 2][hh * 64:(hh + 1) * 64, :, :],
                                  in_=self.st_hgrn[l, cn].rearrange("(j hh) k v -> hh k j v", hh=2)[hh])
            bkA = self.bank()
            for j in range(3):
                for hh in range(2):
                    S.add('pe', 'matmul', out=bkA[ph(hh), j * 64:(j + 1) * 64], lhsT=vbf[:, j, ts], rhs=self.ident_bf[:, hs(hh)], start=True, stop=True)
                for hh in range(2):
                    S.add('pe', 'matmul', out=bkA[ph(hh), 192 + j * 64:192 + (j + 1) * 64], lhsT=kkt[:, j, ts], rhs=self.ident_bf[:, hs(hh)], start=True, stop=True)
            for ps in PS:
                S.add('act', 'activation', out=HB['TM2'][ps], in_=bkA[ps, 0:384].rearrange("p (q f) -> p q f", f=64), func=AF.Copy)
            Vtm = lambda j, hh: HB['TM2'][ph(hh), j, :]
            Ktm = lambda j, hh: HB['TM2'][ph(hh), 3 + j, :]
            fm = lambda t, j, hh: t[hs(hh), j, ts]
            cc = lambda t, j, hh: t[ph(hh), j, 0:CHL]
            pc = lambda bk, j, hh: bk[ph(hh), j * 64:j * 64 + CHL]
            pa = self.bank()
            for (j, hh) in HEADS:
                S.add('pe', 'matmul', out=pc(pa, j, hh), lhsT=fm(kkt, j, hh), rhs=fm(qt, j, hh), start=True, stop=True)
            for ps in PS:
                S.add('dve', 'tensor_tensor', out=sv(HB['AkT'], ps), in0=hv(pa, ps), in1=m3(miu, ps), op=ALU.mult)
            yield 'SD'
            S.add('dve', 'tensor_tensor', out=self.T0f[:], in0=Sst[:], in1=bc(self.eM[:, :, ch:ch + 1], [128, 3, 64]), op=ALU.mult)
            S.add('act', 'activation', out=self.T0b[:], in_=self.T0f[:], func=AF.Copy)
            T0h = lambda j, hh: self.T0b[hs(hh), j, :]
            po = self.bank()
            if CHL == 1:
                po2 = self.bank()
                for (j, hh) in HEADS:
                    S.add('pe', 'matmul', out=po[hs(hh), j * CHL:(j + 1) * CHL], lhsT=T0h(j, hh), rhs=fm(qt, j, hh), start=True, stop=True)
                for (j, hh) in HEADS:
                    S.add('pe', 'matmul', out=po2[hs(hh), j * CHL:(j + 1) * CHL], lhsT=Vtm(j, hh), rhs=cc(HB['AkT'], j, hh), start=True, stop=True)
                S.add('act', 'activation', out=self.O[:, :, ts], in_=po[:, 0:3 * CHL].rearrange("p (j t) -> p j t", j=3), func=AF.Copy)
                S.add('dve', 'tensor_tensor', out=self.O[:, :, ts], in0=self.O[:, :, ts], in1=po2[:, 0:3 * CHL].rearrange("p (j t) -> p j t", j=3), op=ALU.add)
            else:
                for (j, hh) in HEADS:
                    o_ap = po[hs(hh), j * CHL:(j + 1) * CHL]
                    S.add('pe', 'matmul', out=o_ap, lhsT=T0h(j, hh), rhs=fm(qt, j, hh), start=True, stop=False)
                    S.add('pe', 'matmul', out=o_ap, lhsT=Vtm(j, hh), rhs=cc(HB['AkT'], j, hh), start=False, stop=True)
                S.add('act', 'activation', out=self.O[:, :, ts], in_=po[:, 0:3 * CHL].rearrange("p (j t) -> p j t", j=3), func=AF.Copy)
            pt = self.bank()
            for (j, hh) in HEADS:
                t_ap = pt[hs(hh), j * 64:(j + 1) * 64]
                S.add('pe', 'matmul', out=t_ap, lhsT=Ktm(j, hh), rhs=Vtm(j, hh), start=True, stop=True)
            S.add('dve', 'tensor_tensor', out=self.T0f[:], in0=self.T0f[:], in1=pt[:, 0:192].rearrange("p (j v) -> p j v", j=3), op=ALU.add)
            S.add('dve', 'tensor_tensor', out=Sst[:], in0=self.T0f[:], in1=bc(self.eCM[:, :, ch:ch + 1], [128, 3, 64]), op=ALU.mult)
            if smp:
                for hh in range(2):
                    S.add('sp', 'dma_start', dma=f'sho{ch % 2}', out=self.o_shgrn[l, ch].rearrange("(j hh) k v -> hh k j v", hh=2)[hh],
                          in_=Sst[hh * 64:(hh + 1) * 64, :, :])

        GRP = 2 if (CHL == 64 and not smp) else 1
        HB0 = {'TM2': self.TM2, 'AkT': self.AkT}
        for c0 in range(0, NCHK, GRP):
            gens = [hchunk_gen(c0 + i, HB0 if i == 0 else self.HB2) for i in range(min(GRP, NCHK - c0))]
            for gq in gens:
                next(gq)
            for gq in gens:
                for _ in gq:
                    pass
        self.mk('h_post')
        O = self.O
        hng = C[:, cp.sl(f"hng{l}")]
        hst_ = {}

        def hp_mm(j):
            hst_[j] = self.bank()
            S.add('pe', 'matmul', out=hst_[j][:, 0:N], lhsT=self.bmean_bf[:], rhs=tmpb[:, j, :N], start=True, stop=True)
        hpsteps = [
            lambda j: S.add('act', 'activation', out=tmpb[:, j, :N], in_=O[:, j, :N], func=AF.Square),
            hp_mm,
            lambda j: S.add('act', 'activation', out=E[:, j, :N], in_=hst_[j][:, 0:N], func=AF.Ln, bias=self.epsc[NORM_EPS][:]),
            lambda j: S.add('act', 'activation', out=E[:, j, :N], in_=E[:, j, :N], func=AF.Exp, scale=-0.5),
            lambda j: S.add('dve', 'scalar_tensor_tensor', out=O[:, j, :N], in0=O[:, j, :N], scalar=hng[:, j:j + 1], in1=E[:, j, :N], op0=ALU.mult, op1=ALU.mult),
            lambda j: S.add('dve', 'tensor_tensor', out=self.mix[:, 3 + j, :N], in0=O[:, j, :N], in1=sil[:, j, :N], op=ALU.mult),
        ]
        self.wavefront(hpsteps, 3)

    def lru(self, l, cfg, last):
        S, cp = self.S, self.cp
        N = cfg.n
        C = self.C
        xb = self.Pr
        XB = self.A[0]
        xbv = XB[:].rearrange("p j n -> p (j n)")[:, 0:2 * (N + 3)].rearrange("p (c n) -> p c n", c=2)
        xc, rr, ii, aa, uu, gt = self.A[1], self.A[2], self.A[3], self.A[4], self.A[5], self.O
        xcb = self.B[0]
        cw = C[:, cp.sl(f"cw{l}")].rearrange("p (j c) -> p j c", c=2)
        smp = self.smp
        if not smp:
            S.add('act', 'activation', out=xbv[:, :, 0:3], in_=self.cvh[l][:], func=AF.Copy)

        def evac(bk, pair):
            v2 = bk[:].rearrange("p (c t) -> p c t", c=2)
            if pair[0] == 23:
                S.add('act', 'activation', out=xbv[:, :, 3:N + 3], in_=v2[:, :, :N], func=AF.Copy)
            else:
                S.add('act', 'activation', out=gt[:, 0:2, :N], in_=v2[:, :, :N], func=AF.Copy)
        self.project([23, 24, 25, 26], N, evac)
        if not smp:
            S.add('act', 'activation', out=self.cvh[l][:], in_=xbv[:, :, N:N + 3], func=AF.Copy)
        else:
            S.add('act', 'activation', out=self.XBs[:], in_=xbv[:, :, 3:N + 3], func=AF.Copy)
        for c in range(2):
            S.add('dve', 'tensor_scalar', out=xc[:, c, :N], in0=xbv[:, c, 3:N + 3], scalar1=cw[:, 3, c:c + 1], scalar2=C[:, cp.sl(f"cb{l}")][:, c:c + 1],
                  op0=ALU.mult, op1=ALU.add)
            for jj in range(3):
                prevj = self.CV[:, jj * 2 + c, :N] if smp else xbv[:, c, jj:jj + N]
                S.add('dve', 'scalar_tensor_tensor', out=xc[:, c, :N], in0=prevj, scalar=cw[:, jj, c:c + 1], in1=xc[:, c, :N],
                      op0=ALU.mult, op1=ALU.add)
        S.add('act', 'activation', out=xcb[:, 0:2, :N], in_=xc[:, 0:2, :N], func=AF.Copy)

        def lru_chain(c):
            bk = self.bank()
            for nn in range(2):
                pr_ = slice(nn * 64, nn * 64 + 64)
                S.add('pe', 'matmul', out=bk[pr_, 0:N], lhsT=self.SM[pr_, 768 + c * 64:768 + (c + 1) * 64], rhs=xcb[pr_, c, :N], start=True, stop=True)
                S.add('pe', 'matmul', out=bk[pr_, 256:256 + N], lhsT=self.SM[pr_, 896 + c * 64:896 + (c + 1) * 64], rhs=xcb[pr_, c, :N], start=True, stop=True)
            S.add('act', 'activation', out=rr[:, c, :N], in_=bk[:, 0:N], func=AF.Sigmoid, bias=C[:, cp.sl(f"ba{l}")][:, c:c + 1])
            S.add('act', 'activation', out=ii[:, c, :N], in_=bk[:, 256:256 + N], func=AF.Sigmoid, bias=C[:, cp.sl(f"bx{l}")][:, c:c + 1])
            S.add('act', 'activation', out=aa[:, c, :N], in_=rr[:, c, :N], func=AF.Exp, scale=self.c8[:, l, c:c + 1])
            S.add('act', 'activation', out=rr[:, c, :N], in_=rr[:, c, :N], func=AF.Exp, scale=self.c16[:, l, c:c + 1])
            S.add('dve', 'tensor_scalar', out=rr[:, c, :N], in0=rr[:, c, :N], scalar1=-1.0, scalar2=1.0, op0=ALU.mult, op1=ALU.add)
            S.add('dve', 'tensor_scalar', out=rr[:, c, :N], in0=rr[:, c, :N], scalar1=1e-12, scalar2=None, op0=ALU.max)
            S.add('act', 'activation', out=rr[:, c, :N], in_=rr[:, c, :N], func=AF.Sqrt)
            S.add('dve', 'tensor_tensor', out=uu[:, c, :N], in0=xc[:, c, :N], in1=ii[:, c, :N], op=ALU.mult)
            S.add('dve', 'tensor_tensor', out=uu[:, c, :N], in0=uu[:, c, :N], in1=rr[:, c, :N], op=ALU.mult)
            if smp:
                S.add('dve', 'tensor_tensor', out=ii[:, c, :N], in0=aa[:, c, :N], in1=self.H0[:, c, :N], op=ALU.mult)
                S.add('dve', 'tensor_tensor', out=ii[:, c, :N], in0=ii[:, c, :N], in1=uu[:, c, :N], op=ALU.add)
                S.add('act', 'activation', out=self.H0[:, c, :N], in_=ii[:, c, :N], func=AF.Copy)
            else:
                S.add('dve', 'tensor_tensor_scan', out=ii[:, c, :N], data0=aa[:, c, :N], data1=uu[:, c, :N], initial=self.hst[l][:, c, :], op0=ALU.mult, op1=ALU.add)
                S.add('act', 'activation', out=self.hst[l][:, c, :], in_=ii[:, c, N - 1:N], func=AF.Copy)
            S.add('act', 'activation', out=uu[:, c, :N], in_=gt[:, c, :N], func=AF.Square)
            S.add('dve', 'tensor_scalar', out=uu[:, c, :N], in0=uu[:, c, :N], scalar1=0.044715, scalar2=1.0, op0=ALU.mult, op1=ALU.add)
            S.add('dve', 'tensor_tensor', out=uu[:, c, :N], in0=uu[:, c, :N], in1=gt[:, c, :N], op=ALU.mult)
            S.add('act', 'activation', out=uu[:, c, :N], in_=uu[:, c, :N], func=AF.Sigmoid, scale=1.5957691216057308)
            S.add('dve', 'tensor_tensor', out=uu[:, c, :N], in0=uu[:, c, :N], in1=gt[:, c, :N], op=ALU.mult)
            S.add('dve', 'tensor_tensor', out=self.mix[:, 6 + c, :N], in0=uu[:, c, :N], in1=ii[:, c, :N], op=ALU.mult)
        self.interleave([lambda c=c: lru_chain(c) for c in range(2)])

    def alloc_b(self):
        self.sb_off = self.work_base
        sb = self.sb
        u = self.nm("b")
        self.xn2 = sb(f"xn2{u}", [128, KD, TPP], BF16)
        self.nsq = sb(f"nsq{u}", [128, KD, TT], BF16)
        self.nln = sb(f"nln{u}", [128, TT])
        self.nrs = sb(f"nrs{u}", [128, TT])
        self.w1g = [sb(f"w1g{i}{u}", [128, KD, FG], BF16) for i in range(2)]
        self.w2g = [sb(f"w2g{i}{u}", [128, FG // 128, D], BF16) for i in range(2)]
        self.hb = [sb(f"hb{i}{u}", [128, FG // 128, 512], BF16) for i in range(2)]
        self.xn2s = sb(f"xn2s{u}", [128, KD, 16], BF16)
        self.hbs = sb(f"hbs{u}", [128, FG // 128, 16], BF16)
        self.rls = sb(f"rls{u}", [128, FG // 128, 16])
        self.rl = [sb(f"rl{i}{u}", [128, 512]) for i in range(2)]

    def phase_b(self, l, nxt_loads=()):
        S, cp = self.S, self.cp
        g2 = self.C[:, cp.sl(f"g2{l}")]
        def load_group(g):
            s = g % 2
            S.add('pool', 'dma_start', dma=f'w1g{s}', out=self.w1g[s][:], in_=self.w1[l][:, g * FG:(g + 1) * FG].rearrange("(k p) c -> p k c", p=128))
            S.add('pool', 'dma_start', dma=f'w2g{s}', out=self.w2g[s][:], in_=self.w2[l][g * FG:(g + 1) * FG, :].rearrange("(k p) c -> p k c", p=128))
        nxt_loads = list(nxt_loads)
        load_group(0)
        load_group(1)
        if nxt_loads:
            nxt_loads.pop(0)()
        for tt in range(NTILE):
            self.rmsnorm_into(self.x[tt], g2, tt)
        nmt = TPP // 512
        do_s = self.do_smp and self.cur_ps == self.npass - 1
        if do_s:
            self.rmsnorm(self.xs, g2, self.xn2s, 16)

        def sample_group(g):
            s_ = g % 2
            bk = self.bank()
            for c in range(FG // 128):
                for k in range(KD):
                    S.add('pe', 'matmul', out=bk[:, c * 16:(c + 1) * 16], lhsT=self.w1g[s_][:, k, c * 128:(c + 1) * 128], rhs=self.xn2s[:, k, :],
                          start=(k == 0), stop=(k == KD - 1))
            S.add('act', 'activation', out=self.rls[:], in_=bk[:, 0:(FG // 128) * 16].rearrange("p (c t) -> p c t", t=16), func=AF.Relu)
            S.add('pool', 'tensor_tensor', out=self.hbs[:], in0=self.rls[:], in1=self.rls[:], op=ALU.mult)
            bk = self.bank()
            for c in range(KD):
                for k in range(FG // 128):
                    S.add('pe', 'matmul', out=bk[:, c * 16:(c + 1) * 16], lhsT=self.w2g[s_][:, k, c * 128:(c + 1) * 128], rhs=self.hbs[:, k, :],
                          start=(k == 0), stop=(k == FG // 128 - 1))
            S.add('dve', 'tensor_tensor', out=self.xs[:], in0=self.xs[:], in1=bk[:, 0:KD * 16].rearrange("p (c t) -> p c t", t=16), op=ALU.add)

        def stage1(u, g, mt):
            s_ = g % 2
            hb = self.hb[u % 2]
            for c in range(FG // 128):
                bk = self.bank()
                for k in range(KD):
                    S.add('pe', 'matmul', out=bk[:], lhsT=self.w1g[s_][:, k, c * 128:(c + 1) * 128], rhs=self.xn2[:, k, mt * 512:(mt + 1) * 512],
                          start=(k == 0), stop=(k == KD - 1))
                rl = self.rl[c % 2]
                S.add('act', 'activation', out=rl[:], in_=bk[:], func=AF.Relu)
                S.add('pool', 'tensor_tensor', out=hb[:, c, :], in0=rl[:], in1=rl[:], op=ALU.mult)

        def stage2(u, g, mt):
            s_ = g % 2
            hb = self.hb[u % 2]
            for c in range(KD):
                bk = self.bank()
                for k in range(FG // 128):
                    S.add('pe', 'matmul', out=bk[:], lhsT=self.w2g[s_][:, k, c * 128:(c + 1) * 128], rhs=hb[:, k, :],
                          start=(k == 0), stop=(k == FG // 128 - 1))
                for hh in range(512 // TT):
                    xt = self.x[mt * (512 // TT) + hh]
                    S.add('dve', 'tensor_tensor', out=xt[:, c, :], in0=xt[:, c, :], in1=bk[:, hh * TT:(hh + 1) * TT], op=ALU.add)

        units = [(g, mt) for g in range(NFG) for mt in range(nmt)]
        stage1(0, *units[0])
        for u, (g, mt) in enumerate(units):
            if u + 1 < len(units):
                stage1(u + 1, *units[u + 1])
            stage2(u, g, mt)
            if mt == nmt - 1:
                if do_s:
                    sample_group(g)
                if g + 2 < NFG:
                    load_group(g + 2)
                if nxt_loads:
                    nxt_loads.pop(0)()
        for t in nxt_loads:
            t()

    def rmsnorm_into(self, xt, g, tt):
        S = self.S
        N = TT
        S.add('act', 'activation', out=self.nsq[:], in_=xt[:], func=AF.Square)
        bk = self.bank()
        for k in range(KD):
            S.add('pe', 'matmul', out=bk[:, :N], lhsT=self.ones_bf[:], rhs=self.nsq[:, k, :], start=(k == 0), stop=(k == KD - 1))
        S.add('act', 'activation', out=self.nln[:], in_=bk[:, :N], func=AF.Ln, scale=1.0 / D, bias=self.epsc[NORM_EPS][:])
        S.add('act', 'activation', out=self.nrs[:], in_=self.nln[:], func=AF.Exp, scale=-0.5)
        for k in range(KD):
            S.add('dve', 'scalar_tensor_tensor', out=self.xn2[:, k, tt * TT:(tt + 1) * TT], in0=xt[:, k, :], scalar=g[:, k:k + 1], in1=self.nrs[:],
                  op0=ALU.mult, op1=ALU.mult)

    def store_y(self, ps):
        S, cp = self.S, self.cp
        g = self.C[:, cp.sl('fg')]
        for tt in range(NTILE):
            self.rmsnorm(self.x[tt], g, self.yfm, TT)
            for tb in range(TT // 128):
                ob = self.xtm[tb % 2]
                for kh in range(2):
                    bk = self.bank()
                    for kk in range(4):
                        k = kh * 4 + kk
                        S.add('pe', 'transpose', out=bk[:, kk * 128:(kk + 1) * 128],
                              in_=self.yfm[:, k, tb * 128:(tb + 1) * 128], identity=self.ident)
                    S.add('act', 'activation', out=ob[:, kh * 512:(kh + 1) * 512], in_=bk[:], func=AF.Copy)
                t0 = ps * TPP + tt * TT + tb * 128
                S.add('sp', 'dma_start', dma=f'ytm{tb % 2}', out=self.y_p[t0:t0 + 128, :], in_=ob[:])

    def store_states(self):
        S, nc = self.S, self.nc
        self.sb_off = self.work_base
        wk = self.sb("wkvT", [64, DEPTH, 3, 128])
        with nc.allow_non_contiguous_dma(reason="small strided state outputs"):
            S.add('sp', 'dma_start', dma='o_small', allow_slow_non_contiguous=True, out=self.o_pshift.rearrange("l (k p) -> p l k", p=128), in_=self.shl[:])
            for l in range(DEPTH):
                bk = self.bank()
                for j in range(3):
                    S.add('pe', 'transpose', out=bk[0:64, j * 128:(j + 1) * 128], in_=self.Tst[l][:, j, :], identity=self.ident)
                S.add('act', 'activation', out=wk[:, l, :, :], in_=bk[0:64, 0:384].rearrange("p (j f) -> p j f", j=3), func=AF.Copy)
                S.add('sp', 'dma_start', dma=f'o_wkv{l}', allow_slow_non_contiguous=True, out=self.o_pwkv[l].rearrange("h v k -> v h k"),
                      in_=wk[:, l, :, :].rearrange("p j (hh k) -> p (j hh) k", hh=2))
                for hh in range(2):
                    S.add('sp', 'dma_start', dma='o_small', allow_slow_non_contiguous=True, out=self.o_phgrn[l].rearrange("(j hh) k v -> hh k j v", hh=2)[hh],
                          in_=self.Sst[l][hh * 64:(hh + 1) * 64, :, :])
                S.add('sp', 'dma_start', dma='o_small', allow_slow_non_contiguous=True, out=self.o_plru[l].rearrange("(c p) -> p c", p=128), in_=self.hst[l][:, :, 0])
                for c in range(2):
                    S.add('sp', 'dma_start', dma='o_small', allow_slow_non_contiguous=True, out=self.o_pconv[l][:, c * 128:(c + 1) * 128].rearrange("j p -> p j"), in_=self.cvh[l][:, c, :])


def kernel(**inputs):
    inp = {k: np.asarray(v) for k, v in inputs.items()}
    import os
    b = Builder(nlayers=int(os.environ.get("K_NL", DEPTH)), npass=int(os.environ.get("K_NP", NPASS)),
                do_mlp=os.environ.get("K_MLP", "1") == "1", do_smp=os.environ.get("K_SMP", "1") == "1")
    nc = b.build()
    cpk = pack_consts(inp, b.cp)
    smat = pack_small_mats(inp)
    in_maps = []
    for c in range(NCORES):
        in_maps.append({
            "x_prompt": np.ascontiguousarray(inp['x_prompt'][c]),
            "x_sample": np.ascontiguousarray(inp['x_sample'][c * 16:(c + 1) * 16, 0]),
            "state_wkv": np.ascontiguousarray(inp['state_wkv'][:, c * 16:(c + 1) * 16]),
            "state_shift": np.ascontiguousarray(inp['state_shift'][:, c * 16:(c + 1) * 16]),
            "state_hgrn": np.ascontiguousarray(inp['state_hgrn'][:, c * 16:(c + 1) * 16]),
            "state_lru": np.ascontiguousarray(inp['state_lru'][:, c * 16:(c + 1) * 16]),
            "state_conv": np.ascontiguousarray(inp['state_conv'][:, c * 16:(c + 1) * 16]),
            "cpack": cpk, "smat": smat,
            "w_in": inp['w_in'][:b.nlayers], "w_out": inp['w_out'][:b.nlayers],
            "mlp_w1": inp['mlp_w1'][:(b.nlayers if b.do_mlp else 1)], "mlp_w2": inp['mlp_w2'][:(b.nlayers if b.do_mlp else 1)],
        })
    ncr = int(os.environ.get("K_CORES", NCORES))
    res = run_bass_kernel_spmd(nc, in_maps[:ncr], core_ids=list(range(ncr)))
    R = list(res.results) + [res.results[0]] * (NCORES - ncr)
    print("total ops", len(b.S.ops))
    y_prompt = np.stack([R[c]["y_prompt"] for c in range(NCORES)], 0)
    p_shift = np.stack([R[c]["p_shift"] for c in range(NCORES)], 1)
    p_wkv = np.stack([R[c]["p_wkv"] for c in range(NCORES)], 1)
    p_hgrn = np.stack([R[c]["p_hgrn"] for c in range(NCORES)], 1)
    p_lru = np.stack([R[c]["p_lru"] for c in range(NCORES)], 1)
    p_conv = np.stack([R[c]["p_conv"] for c in range(NCORES)], 1)
    cat = lambda n, ax: np.concatenate([R[c][n] for c in range(NCORES)], ax)
    y_sample = cat("y_sample", 0)[:, None, :]
    return (y_prompt, y_sample, p_wkv, p_shift, p_hgrn, p_lru, p_conv,
            cat("s_wkv", 1), cat("s_shift", 1), cat("s_hgrn", 1), cat("s_lru", 1), cat("s_conv", 1))
```

```python
import contextlib
import numpy as np
import concourse.bass as bass
import concourse.mybir as mybir
from concourse.bass_utils import run_bass_kernel_spmd

F32 = mybir.dt.float32
BF16 = mybir.dt.bfloat16
AF = mybir.ActivationFunctionType
ALU = mybir.AluOpType

NCORES = 8
D = 1024
KD = 8
SEQ = 2048
DEPTH = 4
TT = 256
NPASS = 2
TPP = SEQ // NPASS
NTILE = TPP // TT
CH = 64
NCH = TT // CH
MID = CH // 2 - 1
C_IN = 3456
NCT_IN = 27
DFF = 4096
FG = 512
NFG = DFF // FG
NORM_EPS = 1e-6
GN_EPS = 64e-5
C0 = float(np.exp(-0.5))

ENGS = ['pe', 'act', 'dve', 'pool', 'sp']


class Sched:
    def __init__(self, nc):
        self.nc = nc
        self.ops = []
        self.per_eng = {e: [] for e in ENGS}
        self.last_w = {}
        self.readers = {}
        self.synced = {e: {} for e in ENGS}
        self.dma_cum = {}
        self.pe_mode = None
        self.pe_last = None

    @staticmethod
    def _keys(aps):
        ks = []
        for a in aps:
            if a is None or isinstance(a, (int, float)):
                continue
            if isinstance(a, (str, tuple)):
                ks.append(a)
                continue
            t = getattr(a, 'tensor', None)
            if t is None or type(t).__name__ != 'SBTensorHandle' or len(t.shape) < 3 or a.dtype != t.dtype:
                ks.append(a.name)
                continue
            shp = [int(v) for v in t.shape]
            row = int(np.prod(shp[1:]))
            bs = int(np.prod(shp[2:]))
            lo = int(a.offset) % row
            hi = lo
            for step, cnt in list(a.ap)[1:]:
                if step > 0:
                    hi += (int(cnt) - 1) * int(step)
            for b in range(lo // bs, min(hi // bs, shp[1] - 1) + 1):
                ks.append((a.name, b))
        return ks

    maxops = None

    def add(self, eng, method, r=(), w=(), dma=None, **kw):
        if self.maxops is not None and len(self.ops) >= self.maxops:
            return None
        idx = len(self.ops)
        r = list(r)
        w = list(w)
        for kn, v in kw.items():
            if hasattr(v, 'name') and hasattr(v, 'ap'):
                (w if kn in ('out', 'ap', 'accum_out') else r).append(v)
        rk = self._keys(r)
        wk = self._keys(w)
        deps = set()
        for k in rk:
            if k in self.last_w:
                deps.add(self.last_w[k])
        for k in wk:
            if k in self.last_w:
                deps.add(self.last_w[k])
            deps.update(self.readers.get(k, {}).values())
        waits = []
        if eng == 'pe' and method in ('matmul', 'transpose'):
            st_ap = kw['lhsT'] if method == 'matmul' else kw['in_']
            shp = list(st_ap.shape)
            rkk = 32 if shp[0] <= 32 else (64 if shp[0] <= 64 else 128)
            mfree = int(np.prod(shp[1:]))
            rm = 32 if mfree <= 32 else (64 if mfree <= 64 else 128)
            mode = (rkk, rm, method, str(st_ap.dtype))
            if getattr(self, 'pe_mode', None) is not None and mode != self.pe_mode and self.pe_last is not None:
                p = self.ops[self.pe_last]
                p['inc'] = True
                waits.append(('eng', self.pe_last))
            self.pe_mode = mode
        for d in sorted(deps):
            p = self.ops[d]
            if p['dma'] is not None:
                key = ('dma', p['dma'])
                if self.synced[eng].get(key, 0) >= p['cum']:
                    continue
                self.synced[eng][key] = p['cum']
                waits.append(('dma', p['dma'], p['cum']))
            else:
                if p['eng'] == 'pe' and eng == 'pe':
                    continue
                if self.synced[eng].get(p['eng'], -1) >= p['seq']:
                    continue
                self.synced[eng][p['eng']] = p['seq']
                p['inc'] = True
                waits.append(('eng', d))
        o = dict(eng=eng, method=method, kw=kw, waits=waits, dma=dma, seq=len(self.per_eng[eng]),
                 inc=False, cum=None, tick=None)
        if dma is not None:
            self.dma_cum[dma] = self.dma_cum.get(dma, 0) + 16
            o['cum'] = self.dma_cum[dma]
        self.ops.append(o)
        self.per_eng[eng].append(idx)
        if eng == 'pe':
            self.pe_last = idx
        for k in wk:
            self.last_w[k] = idx
            self.readers[k] = {}
        rtag = ('dma', dma) if dma is not None else eng
        for k in rk:
            self.readers.setdefault(k, {})[rtag] = idx
        return idx

    def barrier(self):
        last_real = {}
        for f in ENGS:
            j = len(self.per_eng[f]) - 1
            while j >= 0 and (self.ops[self.per_eng[f][j]]['dma'] is not None
                              or self.ops[self.per_eng[f][j]]['method'] is None):
                j -= 1
            if j >= 0:
                last_real[f] = self.per_eng[f][j]
        cums = dict(self.dma_cum)
        for e in ENGS:
            waits = []
            for f, qi in last_real.items():
                if f == e:
                    continue
                q = self.ops[qi]
                if self.synced[e].get(f, -1) < q['seq']:
                    self.synced[e][f] = q['seq']
                    q['inc'] = True
                    waits.append(('eng', qi))
            for key, cum in cums.items():
                k2 = ('dma', key)
                if self.synced[e].get(k2, 0) < cum:
                    self.synced[e][k2] = cum
                    waits.append(('dma', key, cum))
            o = dict(eng=e, method=None, kw={}, waits=waits, dma=None, seq=len(self.per_eng[e]),
                     inc=False, cum=None, tick=None)
            self.per_eng[e].append(len(self.ops))
            self.ops.append(o)

    def emit(self, es):
        nc = self.nc
        ROT = 20000
        eng_sems = {e: [] for e in ENGS}
        for e in ENGS:
            n = 0
            for i in self.per_eng[e]:
                o = self.ops[i]
                if o['inc']:
                    n += 1
                    o['tick'] = n
            nsem = (n + ROT - 1) // ROT
            for j in range(max(nsem, 1)):
                eng_sems[e].append(es.enter_context(nc.semaphore(f"s_{e}_{j}")))
        dma_sems = {}
        for key in self.dma_cum:
            dma_sems[key] = es.enter_context(nc.semaphore(f"d_{len(dma_sems)}"))
        objs = {'pe': nc.tensor, 'act': nc.scalar, 'dve': nc.vector, 'pool': nc.gpsimd, 'sp': nc.sync}

        def tick_ref(e, tick):
            j = (tick - 1) // ROT
            return eng_sems[e][j], tick - j * ROT

        def run(e, eng):
            for i in self.per_eng[e]:
                o = self.ops[i]
                for wt in o['waits']:
                    if wt[0] == 'dma':
                        eng.wait_ge(dma_sems[wt[1]], wt[2])
                    else:
                        p = self.ops[wt[1]]
                        s, v = tick_ref(p['eng'], p['tick'])
                        eng.wait_ge(s, v)
                if o['method'] is None:
                    continue
                ins = getattr(eng, o['method'])(**o['kw'])
                if o['dma'] is not None:
                    ins.then_inc(dma_sems[o['dma']], 16)
                elif o['inc']:
                    s, v = tick_ref(e, o['tick'])
                    ins.then_inc(s, 1)

        with nc.Block() as block:
            @block.tensor
            def _(eng):
                run('pe', eng)

            @block.scalar
            def _(eng):
                run('act', eng)

            @block.vector
            def _(eng):
                run('dve', eng)

            @block.gpsimd
            def _(eng):
                run('pool', eng)

            @block.sync
            def _(eng):
                run('sp', eng)


def _fm(v, nt):
    return np.ascontiguousarray(np.asarray(v, np.float32).reshape(nt, 128).T)


class CP:
    def __init__(self):
        self.cols = {}
        self.n = 0

    def alloc(self, name, w):
        self.cols[name] = (self.n, w)
        self.n += w

    def sl(self, name):
        a, w = self.cols[name]
        return slice(a, a + w)


def make_cp():
    cp = CP()
    for l in range(DEPTH):
        for nm, w in [('g1', 8), ('g2', 8), ('mu', 11), ('w0', 3), ('a0', 3), ('kk', 3), ('ka', 3), ('rk', 3),
                      ('lnw', 3), ('lnb', 3), ('hng', 3), ('cw', 8), ('cb', 2), ('ba', 2), ('bx', 2), ('lam', 2)]:
            cp.alloc(f"{nm}{l}", w)
    cp.alloc('hlb', 12)
    cp.alloc('fg', 8)
    cp.alloc('ident', 128)
    cp.alloc('msu', 64)
    cp.alloc('msl', 64)
    cp.alloc('miu', 64)
    cp.alloc('msun', 64)
    cp.alloc('msln', 64)
    cp.alloc('id2', 64)
    return cp


def pack_consts(inp, cp):
    A = np.zeros((128, cp.n), np.float32)
    for l in range(DEPTH):
        A[:, cp.sl(f"g1{l}")] = _fm(inp['norm1_g'][l], 8)
        A[:, cp.sl(f"g2{l}")] = _fm(inp['norm2_g'][l], 8)
        A[:, cp.sl(f"mu{l}")] = _fm(inp['mu_shift'][l], 11)
        A[:, cp.sl(f"w0{l}")] = _fm(inp['rwkv_w0'][l], 3)
        A[:, cp.sl(f"a0{l}")] = _fm(inp['rwkv_a0'][l], 3)
        A[:, cp.sl(f"kk{l}")] = _fm(inp['rwkv_k_k'][l], 3)
        A[:, cp.sl(f"ka{l}")] = _fm(inp['rwkv_k_a'][l], 3)
        A[:, cp.sl(f"rk{l}")] = _fm(inp['rwkv_r_k'][l].reshape(-1), 3)
        A[:, cp.sl(f"lnw{l}")] = _fm(inp['rwkv_ln_w'][l], 3)
        A[:, cp.sl(f"lnb{l}")] = _fm(inp['rwkv_ln_b'][l], 3)
        A[:, cp.sl(f"hng{l}")] = _fm(inp['hgrn_norm_g'][l], 3)
        cw = np.stack([_fm(inp['lru_conv_w'][l, j], 2) for j in range(4)], axis=1)
        A[:, cp.sl(f"cw{l}")] = cw.reshape(128, 8)
        A[:, cp.sl(f"cb{l}")] = _fm(inp['lru_conv_b'][l], 2)
        A[:, cp.sl(f"ba{l}")] = _fm(inp['lru_ba'][l], 2)
        A[:, cp.sl(f"bx{l}")] = _fm(inp['lru_bx'][l], 2)
        A[:, cp.sl(f"lam{l}")] = _fm(inp['lru_lambda'][l], 2)
    hl = np.stack([_fm(inp['hgrn_lb'][l], 3) for l in range(DEPTH)], axis=2)
    A[:, cp.sl('hlb')] = hl.reshape(128, 12)
    A[:, cp.sl('fg')] = _fm(inp['final_g'], 8)
    A[:, cp.sl('ident')] = np.eye(128, dtype=np.float32)
    r = np.arange(64)
    for hb in (0, 64):
        A[hb:hb + 64, cp.sl('msu')] = (r[:, None] < r[None, :]).astype(np.float32)
        A[hb:hb + 64, cp.sl('msl')] = (r[:, None] > r[None, :]).astype(np.float32)
        A[hb:hb + 64, cp.sl('miu')] = (r[:, None] <= r[None, :]).astype(np.float32)
        A[hb:hb + 64, cp.sl('msun')] = -(r[:, None] < r[None, :]).astype(np.float32)
        A[hb:hb + 64, cp.sl('msln')] = -(r[:, None] > r[None, :]).astype(np.float32)
        A[hb:hb + 64, cp.sl('id2')] = np.eye(64, dtype=np.float32)
    return A


def pack_small_mats(inp):
    W = np.zeros((128, DEPTH, 384 * 2 + 256), np.float32)
    for l in range(DEPTH):
        W[:64, l, 0:384] = inp['rwkv_w_up'][l]
        W[64:, l, 0:384] = inp['rwkv_a_up'][l]
        W[:, l, 384:768] = inp['rwkv_g_up'][l]
        wa = np.asarray(inp['lru_wa'][l]).reshape(2, 2, 64, 64)
        wx = np.asarray(inp['lru_wx'][l]).reshape(2, 2, 64, 64)
        for t in range(2):
            W[:, l, 768 + t * 64: 768 + (t + 1) * 64] = wa[t].reshape(128, 64)
            W[:, l, 896 + t * 64: 896 + (t + 1) * 64] = wx[t].reshape(128, 64)
    return W


def bc(ap, shape):
    return ap.broadcast_to(list(shape))


class Cfg:
    def __init__(self, n, ch):
        self.n = n
        self.ch = ch
        self.nch = n // ch
        self.mid = max(ch // 2 - 1, 0)
        lv = 0
        while (1 << lv) < ch:
            lv += 1
        self.lv = lv


class Builder:
    def __init__(self, nlayers=DEPTH, npass=NPASS, do_mlp=True, do_smp=True):
        self.do_smp = do_smp
        self.nlayers = nlayers
        self.npass = npass
        self.do_mlp = do_mlp
        self.cp = make_cp()
        self.uid = 0

    def sb(self, name, shape, dt=F32):
        nb = int(np.prod(shape[1:])) * (2 if dt == BF16 else 4)
        nb = (nb + 31) // 32 * 32
        t = self.nc.alloc_sbuf_tensor_at(name, list(shape), dt, offset=self.sb_off)
        self.sb_off += nb
        assert self.sb_off <= 229376 - 256, (name, self.sb_off)
        return t

    def bank(self):
        b = self.banks[self.bank_i % 8]
        self.bank_i += 1
        return b

    def build(self):
        nc = bass.Bass("TRN2", target_bir_lowering=False)
        self.nc = nc
        S = Sched(nc)
        import os as _os
        if _os.environ.get("K_MAXOPS"):
            S.maxops = int(_os.environ["K_MAXOPS"])
        self.S = S
        cp = self.cp
        dram_in = lambda n, shp: nc.dram_tensor(n, list(shp), F32, kind="ExternalInput").ap()
        dram_out = lambda n, shp: nc.dram_tensor(n, list(shp), F32, kind="ExternalOutput").ap()
        self.xp = dram_in("x_prompt", [SEQ, D])
        self.cpk = dram_in("cpack", [128, cp.n])
        self.smat = dram_in("smat", [128, DEPTH, 1024])
        self.w_in = dram_in("w_in", [self.nlayers, D, C_IN])
        self.w_out = dram_in("w_out", [self.nlayers, D, D])
        self.w1 = dram_in("mlp_w1", [self.nlayers if self.do_mlp else 1, D, DFF])
        self.w2 = dram_in("mlp_w2", [self.nlayers if self.do_mlp else 1, DFF, D])
        self.y_p = dram_out("y_prompt", [SEQ, D])
        self.o_pshift = dram_out("p_shift", [DEPTH, D])
        self.o_pwkv = dram_out("p_wkv", [DEPTH, 6, 64, 64])
        self.o_phgrn = dram_out("p_hgrn", [DEPTH, 6, 64, 64])
        self.o_plru = dram_out("p_lru", [DEPTH, 256])
        self.o_pconv = dram_out("p_conv", [DEPTH, 3, 256])
        self.xs_d = dram_in("x_sample", [16, D])
        self.st_wkv = dram_in("state_wkv", [DEPTH, 16, 6, 64, 64])
        self.st_shift = dram_in("state_shift", [DEPTH, 16, D])
        self.st_hgrn = dram_in("state_hgrn", [DEPTH, 16, 6, 64, 64])
        self.st_lru = dram_in("state_lru", [DEPTH, 16, 256])
        self.st_conv = dram_in("state_conv", [DEPTH, 16, 3, 256])
        self.y_s = dram_out("y_sample", [16, D])
        self.o_swkv = dram_out("s_wkv", [DEPTH, 16, 6, 64, 64])
        self.o_sshift = dram_out("s_shift", [DEPTH, 16, D])
        self.o_shgrn = dram_out("s_hgrn", [DEPTH, 16, 6, 64, 64])
        self.o_slru = dram_out("s_lru", [DEPTH, 16, 256])
        self.o_sconv = dram_out("s_conv", [DEPTH, 16, 3, 256])

        with contextlib.ExitStack() as es:
            self.es = es
            self.sb_off = 0x4000 + 512
            sb = self.sb
            self.banks = [es.enter_context(nc.psum_tensor(f"bank{i}", [128, 512], F32)) for i in range(8)]
            self.bank_i = 0
            self.C = sb("cpk", [128, cp.n])
            S.add('sp', 'dma_start', dma='c0', out=self.C[:], in_=self.cpk)
            self.ident = self.C[:, cp.sl('ident')]
            self.ident_bf = sb("ident_bf", [128, 128], BF16)
            S.add('dve', 'tensor_copy', out=self.ident_bf[:], in_=self.ident)
            self.ones_bf = sb("ones_bf", [128, 128], BF16)
            S.add('dve', 'memset', ap=self.ones_bf[:], constant=1.0)
            self.bmean_bf = sb("bmean_bf", [128, 128], BF16)
            S.add('dve', 'memset', ap=self.bmean_bf[:], constant=0.0)
            S.add('dve', 'memset', ap=self.bmean_bf[0:64, 0:64], constant=1.0 / 64)
            S.add('dve', 'memset', ap=self.bmean_bf[64:128, 64:128], constant=1.0 / 64)
            self.bones_bf = sb("bones_bf", [128, 128], BF16)
            S.add('dve', 'memset', ap=self.bones_bf[:], constant=0.0)
            S.add('dve', 'memset', ap=self.bones_bf[0:64, 0:64], constant=1.0)
            S.add('dve', 'memset', ap=self.bones_bf[64:128, 64:128], constant=1.0)
            self.omu = sb("omu", [128, DEPTH, 11])
            for l_ in range(DEPTH):
                S.add('dve', 'tensor_scalar', out=self.omu[:, l_, :], in0=self.C[:, cp.sl(f"mu{l_}")], scalar1=-1.0, scalar2=1.0, op0=ALU.mult, op1=ALU.add)
            self.rmask = sb("rmask", [128, TT])
            S.add('dve', 'memset', ap=self.rmask[:], constant=1.0)
            S.add('dve', 'memset', ap=self.rmask[:].rearrange("p (c t) -> p c t", t=CH)[:, :, 0:1], constant=0.0)
            self.epsc = {}
            for v in (NORM_EPS, GN_EPS, 1e-12, 1.0):
                t = sb(self.nm("eps"), [128, 1])
                S.add('pool', 'memset', ap=t[:], constant=float(v))
                self.epsc[v] = t
            self.lb = sb("lb", [128, 3, DEPTH])
            self.oml = sb("oml", [128, 3, DEPTH])
            self.noml = sb("noml", [128, 3, DEPTH])
            self.hgrn_lb_setup()
            self.c8 = sb("c8", [128, DEPTH, 2])
            self.c16 = sb("c16", [128, DEPTH, 2])
            self.lru_setup()
            self.x = [sb(f"x{t}", [128, KD, TT]) for t in range(NTILE)]
            self.WAin = sb("WAin", [128, KD, C_IN], BF16)
            self.WAout = sb("WAout", [128, KD, D], BF16)
            self.SM = sb("SM", [128, 1024], BF16)
            self.shl = sb("shl", [128, DEPTH, KD])
            S.add('pool', 'memset', ap=self.shl[:], constant=0.0)
            self.Tst = [sb(f"Tst{l}", [128, 3, 64]) for l in range(DEPTH)]
            self.Sst = [sb(f"Sst{l}", [128, 3, 64]) for l in range(DEPTH)]
            self.hst = [sb(f"hst{l}", [128, 2, 1]) for l in range(DEPTH)]
            self.cvh = [sb(f"cvh{l}", [128, 2, 3]) for l in range(DEPTH)]
            self.shh = [sb(f"shh{l}", [128, 11, 1]) for l in range(DEPTH)]
            for l in range(DEPTH):
                for t in (self.Tst[l], self.Sst[l], self.hst[l], self.cvh[l], self.shh[l]):
                    S.add('pool', 'memset', ap=t[:], constant=0.0)
            self.xs = sb("xs", [128, KD, 16])
            self.work_base = self.sb_off
            self.pcfg = Cfg(TT, CH)
            self.scfg = Cfg(16, 1)
            self.smp = False

            self.load_weights_a(0)
            for ps in range(self.npass):
                self.alloc_io()
                self.load_x(ps)
                if self.do_smp and ps == self.npass - 1:
                    self.load_xs()
                S.barrier()
                for l in range(self.nlayers):
                    self.alloc_a()
                    for tt in range(NTILE):
                        last = (ps == self.npass - 1 and tt == NTILE - 1)
                        self.phase_a_tile(l, tt, self.pcfg, last)
                    if self.do_smp and ps == self.npass - 1:
                        self.sample_tile(l)
                    S.barrier()
                    nl, nps = (l + 1, ps) if l + 1 < self.nlayers else (0, ps + 1)
                    nxt = self.weight_loads_a(nl) if nps < self.npass else []
                    if self.do_mlp:
                        self.alloc_b()
                        self.cur_ps = ps
                        self.phase_b(l, nxt)
                        S.barrier()
                    else:
                        for t in nxt:
                            t()
                self.alloc_io()
                self.store_y(ps)
                if self.do_smp and ps == self.npass - 1:
                    self.store_ys()
                S.barrier()
            self.store_states()
            S.barrier()
            S.emit(es)
        return nc

    def mk(self, label):
        import os
        if os.environ.get('K_MARK'):
            print('MARK', label, len(self.S.ops))

    def interleave(self, chain_fns):
        S = self.S
        recs = []
        for fn in chain_fns:
            lst = []
            S.add = (lambda *a, _l=lst, **k: _l.append((a, k)))
            try:
                fn()
            finally:
                del S.add
            recs.append(lst)
        L = max(len(r) for r in recs)
        for w in range(L + len(recs) - 1):
            for j, r in enumerate(recs):
                si = w - j
                if 0 <= si < len(r):
                    a, k = r[si]
                    S.add(*a, **k)

    def wavefront(self, steps, nj):
        for w in range(len(steps) + nj - 1):
            for si in range(len(steps)):
                j = w - si
                if 0 <= j < nj:
                    steps[si](j)

    def nm(self, s):
        self.uid += 1
        return f"{s}_{self.uid}"

    def hgrn_lb_setup(self):
        S, cp = self.S, self.cp
        sb = self.sb
        e = sb("hl_e", [128, 3, DEPTH])
        ssum = sb("hl_s", [128, 3, 1])
        hl = self.C[:, cp.sl('hlb')].rearrange("p (j l) -> p j l", l=DEPTH)
        S.add('act', 'activation', out=e[:], in_=hl, func=AF.Exp)
        S.add('dve', 'tensor_reduce', out=ssum[:], in_=e[:], axis=mybir.AxisListType.X, op=ALU.add)
        S.add('dve', 'reciprocal', out=ssum[:], in_=ssum[:])
        S.add('dve', 'tensor_tensor', out=e[:], in0=e[:], in1=bc(ssum[:], [128, 3, DEPTH]), op=ALU.mult)
        S.add('dve', 'memset', ap=self.lb[:, :, 0:1], constant=0.0)
        for l in range(1, DEPTH):
            S.add('dve', 'tensor_tensor', out=self.lb[:, :, l:l + 1], in0=self.lb[:, :, l - 1:l], in1=e[:, :, l:l + 1], op=ALU.add)
        S.add('dve', 'tensor_scalar', out=self.oml[:], in0=self.lb[:], scalar1=-1.0, scalar2=1.0, op0=ALU.mult, op1=ALU.add)
        S.add('dve', 'tensor_scalar', out=self.noml[:], in0=self.lb[:], scalar1=-1.0, scalar2=None, op0=ALU.add)

    def lru_setup(self):
        S, cp = self.S, self.cp
        t = self.sb("lru_t", [128, DEPTH, 2])
        for l in range(DEPTH):
            S.add('act', 'activation', out=t[:, l, :], in_=self.C[:, cp.sl(f"lam{l}")], func=AF.Exp, scale=-1.0)
        S.add('act', 'activation', out=t[:], in_=t[:], func=AF.Ln, bias=self.epsc[1.0][:])
        S.add('dve', 'tensor_scalar', out=self.c8[:], in0=t[:], scalar1=-8.0, scalar2=None, op0=ALU.mult)
        S.add('dve', 'tensor_scalar', out=self.c16[:], in0=t[:], scalar1=-16.0, scalar2=None, op0=ALU.mult)

    def weight_loads_a(self, l):
        S = self.S
        src = self.w_in[l].rearrange("(k p) c -> p k c", p=128)
        q = C_IN // 4
        th = []
        for i in range(4):
            th.append(lambda i=i: S.add('pool', 'dma_start', dma=f'wain{i}', out=self.WAin[:, :, i * q:(i + 1) * q],
                                        in_=src[:, :, i * q:(i + 1) * q]))
        th.insert(1, lambda: S.add('pool', 'dma_start', dma='sm', out=self.SM[:], in_=self.smat[:, l, :]))
        th.append(lambda: S.add('pool', 'dma_start', dma='waout', out=self.WAout[:], in_=self.w_out[l].rearrange("(k p) c -> p k c", p=128)))
        return th

    def load_weights_a(self, l):
        for t in self.weight_loads_a(l):
            t()

    def alloc_a(self):
        self.sb_off = self.work_base
        sb = self.sb
        u = self.nm("a")
        N = TT
        self.xn = sb(f"xn{u}", [128, KD, N], BF16)
        self.nln = sb(f"nln{u}", [128, N])
        self.nrs = sb(f"nrs{u}", [128, N])
        self.Pr = sb(f"Pr{u}", [128, 11, N + 1])
        self.ds = [sb(f"ds{i}{u}", [128, N + 1]) for i in range(2)]
        self.twxa = sb(f"twxa{u}", [128, N], BF16)
        self.sgg = sb(f"sgg{u}", [128, N], BF16)
        self.A = [sb(f"A{i}{u}", [128, 3, N]) for i in range(6)]
        self._offB0 = self.sb_off
        self.B = [sb(f"B{i}{u}", [128, 3, N], BF16) for i in range(6)]
        self.O = sb(f"O{u}", [128, 3, N])
        self.eM = sb(f"eM{u}", [128, 3, 16])
        self.eC = sb(f"eC{u}", [128, 3, 16])
        self.eCM = sb(f"eCM{u}", [128, 3, 16])
        self.cmid = sb(f"cmid{u}", [128, 3, 16])
        self.AkT = sb(f"AkT{u}", [128, 3, 64], BF16)
        self.TM2 = sb(f"TM2{u}", [128, 6, 64], BF16)
        self.T0f = sb(f"T0f{u}", [128, 3, 64])
        self.T0b = sb(f"T0b{u}", [128, 3, 64], BF16)
        self.mix = sb(f"mix{u}", [128, KD, N], BF16)
        self.nsq = self.mix
        self.xs32 = sb(f"xs32{u}", [128, KD, 16])
        self.P0 = sb(f"P0{u}", [128, 11, 16])
        self.shfm = sb(f"shfm{u}", [128, KD, 16], BF16)
        self._offR2 = self.sb_off
        _stm = sb(f"stm{u}", [16, D])
        self.stm = [_stm, _stm]
        self.Ts = [sb(f"Ts{i}{u}", [128, 3, 64]) for i in range(2)]
        self.Ss = [sb(f"Ss{i}{u}", [128, 3, 64]) for i in range(2)]
        _wkT = sb(f"wkT{u}", [64, 3, 128])
        self.wkT = [_wkT, _wkT]
        self.swi = [sb(f"swi{i}{u}", [64, 6, 64]) for i in range(2)]
        self.CV = sb(f"CV{u}", [128, 6, 16])
        self.H0 = sb(f"H0{u}", [128, 2, 16])
        self.cvtm = sb(f"cvtm{u}", [16, 768])
        self.cvo = sb(f"cvo{u}", [16, 256])
        self.lrtm = sb(f"lrtm{u}", [16, 256])
        self.lro = sb(f"lro{u}", [16, 256])
        self.XBs = sb(f"XBs{u}", [128, 2, 16])
        end_off = self.sb_off
        lim = 0
        self.R = {}
        for nmx, shp in [('ZY0', [128, 2, 3, 64]), ('ZY1', [128, 2, 3, 64]), ('NT', [128, 3, 64]), ('AbT', [128, 3, 64]),
                         ('AkT', [128, 3, 64]), ('PP0', [128, 3, 64]), ('PP1', [128, 3, 64]), ('TM2', [128, 6, 64]),
                         ('TMK', [128, 3, 64]), ('Xb', [128, 3, 64]), ('Ub', [128, 3, 64])]:
            self.R[nmx] = sb(f"R{nmx}{u}", shp)
        self.sb_off = max(self.sb_off, end_off)
        end2 = self.sb_off
        self.sb_off = self._offR2
        self.R2 = dict(self.R)
        for nmx, shp in [('ZY0', [128, 2, 3, 64]), ('ZY1', [128, 2, 3, 64]), ('NT', [128, 3, 64]), ('AbT', [128, 3, 64]),
                         ('AkT', [128, 3, 64]), ('PP0', [128, 3, 64]), ('PP1', [128, 3, 64]), ('TM2', [128, 6, 64]),
                         ('TMK', [128, 3, 64])]:
            self.R2[nmx] = sb(f"R2{nmx}{u}", shp)
        self.HB2 = {'TM2': sb(f"HB2TM2{u}", [128, 6, 64], BF16), 'AkT': sb(f"HB2AkT{u}", [128, 3, 64], BF16)}
        assert self.sb_off <= self._offR2 + 4096 + 4 * 768 + 1536 + 2 * 1536, self.sb_off - self._offR2
        self.sb_off = end2
        if not hasattr(self, '_pa'):
            self._pa = True
            print("phase A work bytes", self.sb_off - self.work_base, "limit", 0x4000 + 212000 - self.work_base)

    def alloc_io(self):
        self.sb_off = self.work_base
        sb = self.sb
        u = self.nm("io")
        N = TT
        self.nsq = sb(f"nsq{u}", [128, KD, N], BF16)
        self.nln = sb(f"nln{u}", [128, N])
        self.nrs = sb(f"nrs{u}", [128, N])
        self.xtm = [sb(f"xtm{i}{u}", [128, D]) for i in range(2)]
        self.yfm = sb(f"yfm{u}", [128, KD, N])
        self.xs32 = sb(f"xs32{u}", [128, KD, 16])
        self.stm = [sb(f"stm{i}{u}", [16, D]) for i in range(2)]

    def load_x(self, ps):
        S = self.S
        for tb in range(TPP // 128):
            buf = self.xtm[tb % 2]
            t0 = ps * TPP + tb * 128
            S.add('sp', 'dma_start', dma=f'xtm{tb % 2}', out=buf[:], in_=self.xp[t0:t0 + 128, :])
            xt = self.x[(tb * 128) // TT]
            tl = (tb * 128) % TT
            for kh in range(2):
                bk = self.bank()
                for kk in range(4):
                    k = kh * 4 + kk
                    S.add('pe', 'transpose', out=bk[:, kk * 128:(kk + 1) * 128],
                          in_=buf[:, k * 128:(k + 1) * 128], identity=self.ident)
                src = bk[:].rearrange("p (k t) -> p k t", k=4)
                dst = xt[:, kh * 4:(kh + 1) * 4, tl:tl + 128]
                if kh == 0:
                    S.add('act', 'activation', out=dst, in_=src, func=AF.Copy)
                else:
                    S.add('dve', 'tensor_copy', out=dst, in_=src)

    def rmsnorm(self, xt, g, out, N, want_last=None):
        S = self.S
        S.add('act', 'activation', out=self.nsq[:, :, :N], in_=xt[:, :, :N], func=AF.Square)
        bk = self.bank()
        for k in range(KD):
            S.add('pe', 'matmul', out=bk[:, :N], lhsT=self.ones_bf[:], rhs=self.nsq[:, k, :N], start=(k == 0), stop=(k == KD - 1))
        S.add('act', 'activation', out=self.nln[:, :N], in_=bk[:, :N], func=AF.Ln, scale=1.0 / D, bias=self.epsc[NORM_EPS][:])
        S.add('act', 'activation', out=self.nrs[:, :N], in_=self.nln[:, :N], func=AF.Exp, scale=-0.5)
        for k in range(KD):
            S.add('dve', 'scalar_tensor_tensor', out=out[:, k, :N], in0=xt[:, k, :N], scalar=g[:, k:k + 1], in1=self.nrs[:, :N],
                  op0=ALU.mult, op1=ALU.mult)
        if want_last is not None:
            S.add('dve', 'tensor_tensor', out=want_last, in0=xt[:, :, N - 1], in1=g, op=ALU.mult)
            S.add('dve', 'tensor_scalar', out=want_last, in0=want_last, scalar1=self.nrs[:, N - 1:N], scalar2=None, op0=ALU.mult)

    def project(self, cols, N, evac, rhs=None):
        S = self.S
        for i in range(0, len(cols), 2):
            pair = cols[i:i + 2]
            bk = self.bank()
            for idx, c in enumerate(pair):
                for k in range(KD):
                    S.add('pe', 'matmul', out=bk[:, idx * 256: idx * 256 + N], lhsT=self.WAin[:, k, c * 128:(c + 1) * 128],
                          rhs=(self.xn if rhs is None else rhs)[:, k, :N], start=(k == 0), stop=(k == KD - 1))
            evac(bk, pair)

    def phase_a_tile(self, l, tt, cfg, last):
        S, cp = self.S, self.cp
        N = cfg.n
        xt = self.x[tt]
        self.rmsnorm(xt, self.C[:, cp.sl(f"g1{l}")], self.xn, N, want_last=self.shl[:, l, :] if last else None)
        import os
        mixs = os.environ.get("K_MIX", "rhl")
        if mixs != "rhl":
            S.add('dve', 'memset', ap=self.mix[:], constant=0.0)
        if 'r' in mixs:
            self.rwkv(l, cfg)
        if 'l' in mixs:
            self.lru(l, cfg, last)
        if 'h' in mixs:
            self.hgrn(l, cfg)
        for i in range(0, KD, 2):
            bk = self.bank()
            for idx in range(2):
                c = i + idx
                for k in range(KD):
                    S.add('pe', 'matmul', out=bk[:, idx * 256: idx * 256 + N], lhsT=self.WAout[:, k, c * 128:(c + 1) * 128],
                          rhs=self.mix[:, k, :N], start=(k == 0), stop=(k == KD - 1))
            S.add('dve', 'tensor_tensor', out=xt[:, i:i + 2, :N], in0=xt[:, i:i + 2, :N],
                  in1=bk[:].rearrange("p (c t) -> p c t", c=2)[:, :, :N], op=ALU.add)

    def tm2fm(self, src, nblk, dst, eng='act'):
        S = self.S
        bk = self.bank()
        for b in range(nblk):
            S.add('pe', 'transpose', out=bk[:, b * 16:(b + 1) * 16], in_=src[:, b * 128:(b + 1) * 128], identity=self.ident[0:16, 0:16])
        srcv = bk[:, 0:nblk * 16].rearrange("p (b t) -> p b t", t=16)
        if eng == 'act':
            S.add('act', 'activation', out=dst, in_=srcv, func=AF.Copy)
        else:
            S.add('dve', 'tensor_copy', out=dst, in_=srcv)

    def fm2tm(self, src, nblk, dst):
        S = self.S
        for b0 in range(0, nblk, 4):
            nb = min(4, nblk - b0)
            bk = self.bank()
            for b in range(nb):
                S.add('pe', 'transpose', out=bk[0:16, b * 128:(b + 1) * 128], in_=src(b0 + b), identity=self.ident)
            S.add('act', 'activation', out=dst[:, b0 * 128:(b0 + nb) * 128], in_=bk[0:16, 0:nb * 128], func=AF.Copy)

    def load_xs(self):
        S = self.S
        S.add('sp', 'dma_start', dma='stm0', out=self.stm[0][:], in_=self.xs_d)
        self.tm2fm(self.stm[0], KD, self.xs[:])

    def store_ys(self):
        S, cp = self.S, self.cp
        self.rmsnorm(self.xs, self.C[:, cp.sl('fg')], self.xs32, 16)
        self.fm2tm(lambda b: self.xs32[:, b, :], KD, self.stm[0])
        S.add('sp', 'dma_start', dma='stm0', out=self.y_s, in_=self.stm[0][:])

    def sample_tile(self, l):
        S, cp = self.S, self.cp
        cfg = self.scfg
        N = 16
        S.barrier()
        self.smp = True
        self.rmsnorm(self.xs, self.C[:, cp.sl(f"g1{l}")], self.xs32, N)
        S.add('act', 'activation', out=self.xn[:, :, :N], in_=self.xs32[:], func=AF.Copy)
        self.fm2tm(lambda b: self.xs32[:, b, :], KD, self.stm[1])
        S.add('sp', 'dma_start', dma='stm0o', out=self.o_sshift[l], in_=self.stm[1][:])
        S.add('sp', 'dma_start', dma='stm0', out=self.stm[0][:], in_=self.st_shift[l])
        self.tm2fm(self.stm[0], KD, self.shfm[:])
        S.add('sp', 'dma_start', dma='cvtm', out=self.cvtm[:], in_=self.st_conv[l].rearrange("b j c -> b (j c)"))
        self.tm2fm(self.cvtm, 6, self.CV[:], eng='dve')
        S.add('sp', 'dma_start', dma='lrtm', out=self.lrtm[:], in_=self.st_lru[l])
        self.tm2fm(self.lrtm, 2, self.H0[:], eng='dve')
        self.rwkv(l, cfg)
        self.hgrn(l, cfg)
        self.lru(l, cfg, False)
        S.add('sp', 'dma_start', dma='cvo0', out=self.o_sconv[l][:, 0:2, :].rearrange("b j c -> b (j c)"), in_=self.cvtm[:, 256:768])
        self.fm2tm(lambda b: self.XBs[:, b, :], 2, self.cvo)
        S.add('sp', 'dma_start', dma='cvo1', out=self.o_sconv[l][:, 2, :], in_=self.cvo[:])
        self.fm2tm(lambda b: self.H0[:, b, :], 2, self.lro)
        S.add('sp', 'dma_start', dma='lro', out=self.o_slru[l], in_=self.lro[:])
        for i in range(0, KD, 2):
            bk = self.bank()
            for idx in range(2):
                c = i + idx
                for k in range(KD):
                    S.add('pe', 'matmul', out=bk[:, idx * 256: idx * 256 + N], lhsT=self.WAout[:, k, c * 128:(c + 1) * 128],
                          rhs=self.mix[:, k, :N], start=(k == 0), stop=(k == KD - 1))
            S.add('dve', 'tensor_tensor', out=self.xs[:, i:i + 2, :N], in0=self.xs[:, i:i + 2, :N],
                  in1=bk[:].rearrange("p (c t) -> p c t", c=2)[:, :, :N], op=ALU.add)
        self.smp = False

    def rwkv(self, l, cfg):
        S, cp = self.S, self.cp
        N, CHL, NCHK = cfg.n, cfg.ch, cfg.nch
        Pr = self.Pr
        C = self.C
        sg, a_, kap, t1, E, bonus = self.A
        rt, kt, bt, kkt, vbf, rkbf = self.B
        cs = lambda nm: C[:, cp.sl(f"{nm}{l}")]
        smp = self.smp
        mu = cs('mu')
        omu = self.omu[:, l, :]
        if smp:
            def evac0(bk, pair):
                for idx, c in enumerate(pair):
                    S.add('act', 'activation', out=self.P0[:, c, :N], in_=bk[:, idx * 256:idx * 256 + N], func=AF.Identity, scale=mu[:, c:c + 1])
            self.project(list(range(11)), N, evac0, rhs=self.shfm)

        def evac(bk, pair):
            for idx, c in enumerate(pair):
                src = bk[:, idx * 256:idx * 256 + N]
                S.add('act', 'activation', out=Pr[:, c, 1:N + 1], in_=src, func=AF.Identity, scale=omu[:, c:c + 1])
                if smp:
                    S.add('dve', 'tensor_tensor', out=Pr[:, c, 1:N + 1], in0=Pr[:, c, 1:N + 1], in1=self.P0[:, c, :N], op=ALU.add)
                else:
                    d = self.ds[c % 2]
                    S.add('pool', 'tensor_copy', out=d[:, 0:1], in_=self.shh[l][:, c, :])
                    S.add('act', 'activation', out=d[:, 1:N + 1], in_=src, func=AF.Identity, scale=mu[:, c:c + 1])
                    S.add('pool', 'tensor_copy', out=self.shh[l][:, c, :], in_=d[:, N:N + 1])
                    S.add('dve', 'tensor_tensor', out=Pr[:, c, 1:N + 1], in0=Pr[:, c, 1:N + 1], in1=d[:, 0:N], op=ALU.add)
        self.project(list(range(11)), N, evac)
        self.mk('r_after_pm')
        S.add('act', 'activation', out=self.twxa[0:64, :N], in_=Pr[0:64, 9, 1:N + 1], func=AF.Tanh)
        S.add('act', 'activation', out=self.twxa[64:128, :N], in_=Pr[64:128, 9, 1:N + 1], func=AF.Copy)
        S.add('act', 'activation', out=self.sgg[:, :N], in_=Pr[:, 10, 1:N + 1], func=AF.Sigmoid)
        c3 = lambda t, j: t[:, j, :N].rearrange("p (c t) -> p c t", t=CHL)
        rj = lambda j: Pr[:, j, 1:N + 1]
        kj = lambda j: Pr[:, 3 + j, 1:N + 1]
        vj = lambda j: Pr[:, 6 + j, 1:N + 1]
        st = {}

        def s_lora(j):
            st[('bk', j)] = self.bank()
            st[('bk2', j)] = self.bank()
            S.add('pe', 'matmul', out=st[('bk', j)][:, 0:N], lhsT=self.SM[0:64, j * 128:(j + 1) * 128], rhs=self.twxa[0:64, :N], start=True, stop=True)
            S.add('pe', 'matmul', out=st[('bk2', j)][:, 0:N], lhsT=self.SM[64:128, j * 128:(j + 1) * 128], rhs=self.twxa[64:128, :N], start=True, stop=True)

        def s_bones1(j):
            st[('b3', j)] = self.bank()
            S.add('pe', 'matmul', out=st[('b3', j)][:, 0:N], lhsT=self.bones_bf[:], rhs=rkbf[:, j, :N], start=True, stop=True)

        def s_bones2(j):
            st[('b4', j)] = self.bank()
            S.add('pe', 'matmul', out=st[('b4', j)][:, 0:N], lhsT=self.bones_bf[:], rhs=rkbf[:, j, :N], start=True, stop=True)

        def s_scan(j):
            if CHL > 1:
                S.add('dve', 'tensor_tensor_scan', out=t1[:, j, :N], data0=self.rmask[:, :N], data1=sg[:, j, :N], initial=0.0, op0=ALU.mult, op1=ALU.add)
            else:
                S.add('dve', 'tensor_copy', out=t1[:, j, :N], in_=sg[:, j, :N])

        steps = [
            s_lora,
            lambda j: S.add('act', 'activation', out=sg[:, j, :N], in_=st[('bk', j)][:, 0:N], func=AF.Sigmoid, bias=cs('w0')[:, j:j + 1]),
            lambda j: S.add('act', 'activation', out=a_[:, j, :N], in_=st[('bk2', j)][:, 0:N], func=AF.Sigmoid, bias=cs('a0')[:, j:j + 1]),
            lambda j: S.add('dve', 'tensor_scalar', out=kap[:, j, :N], in0=kj(j), scalar1=cs('kk')[:, j:j + 1], scalar2=None, op0=ALU.mult),
            lambda j: S.add('act', 'activation', out=rkbf[:, j, :N], in_=kap[:, j, :N], func=AF.Square),
            s_bones1,
            lambda j: S.add('act', 'activation', out=t1[:, j, :N], in_=st[('b3', j)][:, 0:N], func=AF.Ln, bias=self.epsc[1e-12][:]),
            lambda j: S.add('act', 'activation', out=t1[:, j, :N], in_=t1[:, j, :N], func=AF.Exp, scale=-0.5),
            lambda j: S.add('dve', 'tensor_tensor', out=kap[:, j, :N], in0=kap[:, j, :N], in1=t1[:, j, :N], op=ALU.mult),
            lambda j: S.add('dve', 'tensor_scalar', out=t1[:, j, :N], in0=a_[:, j, :N], scalar1=-1.0, scalar2=cs('ka')[:, j:j + 1], op0=ALU.add, op1=ALU.mult),
            lambda j: S.add('dve', 'scalar_tensor_tensor', out=kj(j), in0=t1[:, j, :N], scalar=1.0, in1=kj(j), op0=ALU.add, op1=ALU.mult),
            lambda j: S.add('dve', 'tensor_tensor', out=a_[:, j, :N], in0=a_[:, j, :N], in1=kap[:, j, :N], op=ALU.mult),
            lambda j: S.add('dve', 'scalar_tensor_tensor', out=rkbf[:, j, :N], in0=rj(j), scalar=cs('rk')[:, j:j + 1], in1=kj(j), op0=ALU.mult, op1=ALU.mult),
            s_bones2,
            lambda j: S.add('dve', 'tensor_tensor', out=bonus[:, j, :N], in0=vj(j), in1=st[('b4', j)][:, 0:N], op=ALU.mult),
            s_scan,
            lambda j: S.add('act', 'activation', out=self.eM[:, j, :NCHK], in_=c3(t1, j)[:, :, cfg.mid], func=AF.Exp, scale=-C0),
            lambda j: S.add('act', 'activation', out=self.eC[:, j, :NCHK], in_=c3(t1, j)[:, :, CHL - 1], func=AF.Exp, scale=-C0),
            lambda j: S.add('act', 'activation', out=self.cmid[:, j, :NCHK], in_=c3(t1, j)[:, :, cfg.mid], func=AF.Copy),
            lambda j: S.add('dve', 'tensor_tensor', out=c3(t1, j), in0=c3(t1, j), in1=bc(self.cmid[:, j, :NCHK].unsqueeze(2), [128, NCHK, CHL]), op=ALU.subtract),
            lambda j: S.add('act', 'activation', out=E[:, j, :N], in_=t1[:, j, :N], func=AF.Exp, scale=-C0),
            lambda j: S.add('dve', 'tensor_tensor', out=rj(j), in0=rj(j), in1=E[:, j, :N], op=ALU.mult),
            lambda j: S.add('act', 'activation', out=self.eCM[:, j, :NCHK], in_=c3(E, j)[:, :, CHL - 1], func=AF.Copy),
            lambda j: S.add('act', 'activation', out=E[:, j, :N], in_=t1[:, j, :N], func=AF.Exp, scale=C0),
            lambda j: S.add('dve', 'tensor_tensor', out=a_[:, j, :N], in0=a_[:, j, :N], in1=E[:, j, :N], op=ALU.mult),
            lambda j: S.add('dve', 'tensor_tensor', out=kj(j), in0=kj(j), in1=E[:, j, :N], op=ALU.mult),
            lambda j: S.add('dve', 'tensor_tensor', out=t1[:, j, :N], in0=t1[:, j, :N], in1=sg[:, j, :N], op=ALU.subtract),
            lambda j: S.add('act', 'activation', out=E[:, j, :N], in_=t1[:, j, :N], func=AF.Exp, scale=-C0),
            lambda j: S.add('dve', 'tensor_tensor', out=kap[:, j, :N], in0=kap[:, j, :N], in1=E[:, j, :N], op=ALU.mult),
        ]
        self.wavefront(steps, 3)
        self.mk('r_pre_chunk')
        T = self.Tst[l]
        PS = [slice(0, 128)] if CHL == 64 else [slice(0, CHL), slice(64, 64 + CHL)]
        msun = C[:, cp.sl('msun')]
        msln = C[:, cp.sl('msln')]
        msu = C[:, cp.sl('msu')]
        miu = C[:, cp.sl('miu')]
        id2 = C[:, cp.sl('id2')]
        npz = lambda ps: ps.stop - ps.start
        m3 = lambda m, ps: bc(m[ps, 0:CHL].unsqueeze(1), [npz(ps), 3, CHL])
        hv = lambda bk, ps: bk[ps, 0:192].rearrange("p (j t) -> p j t", j=3)[:, :, 0:CHL]
        sv = lambda t, ps: t[ps, :, 0:CHL]
        hs = lambda hh: slice(hh * 64, hh * 64 + 64)
        ph = lambda hh: slice(hh * 64, hh * 64 + CHL)
        HEADS = [(j, hh) for hh in range(2) for j in range(3)]
        def chunk_gen(ch, R):
            T = self.Tst[l]
            ts = slice(ch * CHL, (ch + 1) * CHL)
            if smp:
                T = self.Ts[ch % 2]
                swi = self.swi[ch % 2]
                if ch == 0:
                    S.add('sp', 'dma_start', dma='swi0', out=swi[:], in_=self.st_wkv[l, 0].rearrange("h v k -> v h k"))
                if ch + 1 < NCHK:
                    S.add('sp', 'dma_start', dma=f'swi{(ch + 1) % 2}', out=self.swi[(ch + 1) % 2][:],
                          in_=self.st_wkv[l, ch + 1].rearrange("h v k -> v h k"))
                bki = self.bank()
                for j in range(3):
                    S.add('pe', 'transpose', out=bki[:, j * 64:(j + 1) * 64], in_=swi[:, 2 * j:2 * j + 2, :], identity=self.ident[0:64, 0:64])
                S.add('act', 'activation', out=T[:], in_=bki[:, 0:192].rearrange("p (j v) -> p j v", j=3), func=AF.Copy)
            bkA, bkB = self.bank(), self.bank()
            for j in range(3):
                for hh in range(2):
                    S.add('pe', 'matmul', out=bkA[ph(hh), j * 64:(j + 1) * 64], lhsT=Pr[:, 6 + j, 1 + ch * CHL:1 + (ch + 1) * CHL], rhs=self.ident[:, hs(hh)], start=True, stop=True)
                for hh in range(2):
                    S.add('pe', 'matmul', out=bkA[ph(hh), 192 + j * 64:192 + (j + 1) * 64], lhsT=a_[:, j, ts], rhs=self.ident[:, hs(hh)], start=True, stop=True)
                for hh in range(2):
                    S.add('pe', 'matmul', out=bkB[ph(hh), j * 64:(j + 1) * 64], lhsT=Pr[:, 3 + j, 1 + ch * CHL:1 + (ch + 1) * CHL], rhs=self.ident[:, hs(hh)], start=True, stop=True)
            for ps in PS:
                S.add('act', 'activation', out=R['TM2'][ps], in_=bkA[ps, 0:384].rearrange("p (q f) -> p q f", f=64), func=AF.Copy)
                S.add('dve', 'tensor_copy', out=R['TMK'][ps], in_=bkB[ps, 0:192].rearrange("p (q f) -> p q f", f=64))
            Vtm = lambda j, hh: R['TM2'][ph(hh), j, :]
            Btm = lambda j, hh: R['TM2'][ph(hh), 3 + j, :]
            Ktm = lambda j, hh: R['TMK'][ph(hh), j, :]
            ts1 = slice(1 + ch * CHL, 1 + (ch + 1) * CHL)
            FMSRC = {'rt': lambda p, j: Pr[p, j, ts1], 'kkt': lambda p, j: Pr[p, 3 + j, ts1], 'v': lambda p, j: Pr[p, 6 + j, ts1],
                     'bt': lambda p, j: a_[p, j, ts], 'kt': lambda p, j: kap[p, j, ts]}
            fm = lambda t, j, hh: FMSRC[t](hs(hh), j)
            cc = lambda t, j, hh: t[ph(hh), j, 0:CHL]
            pc = lambda bk, j, hh: bk[ph(hh), j * 64:j * 64 + CHL]
            self.mk('r_after_tm')
            yield 0
            ONE = (CHL == 1)
            if ONE:
                pab, pak = self.bank(), self.bank()
            else:
                pz, py, pn, pab, pak = [self.bank() for _ in range(5)]
            for (j, hh) in HEADS:
                if not ONE:
                    S.add('pe', 'matmul', out=pc(pz, j, hh), lhsT=fm('bt', j, hh), rhs=fm('kt', j, hh), start=True, stop=True)
                S.add('pe', 'matmul', out=pc(pab, j, hh), lhsT=fm('bt', j, hh), rhs=fm('rt', j, hh), start=True, stop=True)
                if not ONE:
                    S.add('pe', 'matmul', out=pc(pn, j, hh), lhsT=fm('kkt', j, hh), rhs=fm('kt', j, hh), start=True, stop=True)
                S.add('pe', 'matmul', out=pc(pak, j, hh), lhsT=fm('kkt', j, hh), rhs=fm('rt', j, hh), start=True, stop=True)
                if not ONE:
                    S.add('pe', 'matmul', out=pc(py, j, hh), lhsT=fm('kt', j, hh), rhs=fm('bt', j, hh), start=True, stop=True)
            zy = R['ZY0']
            Zt = lambda z: z[:, 0, :, :]
            Yt = lambda z: z[:, 1, :, :]
            P = R['PP0']
            for ps in PS:
                if not ONE:
                    S.add('dve', 'tensor_tensor', out=sv(Zt(zy), ps), in0=hv(pz, ps), in1=m3(msun, ps), op=ALU.mult)
                    S.add('dve', 'tensor_tensor', out=sv(Yt(zy), ps), in0=hv(py, ps), in1=m3(msln, ps), op=ALU.mult)
                    S.add('dve', 'tensor_tensor', out=sv(R['NT'], ps), in0=hv(pn, ps), in1=m3(msu, ps), op=ALU.mult)
                S.add('dve', 'tensor_tensor', out=sv(R['AbT'], ps), in0=hv(pab, ps), in1=m3(miu, ps), op=ALU.mult)
                S.add('dve', 'tensor_tensor', out=sv(R['AkT'], ps), in0=hv(pak, ps), in1=m3(miu, ps), op=ALU.mult)
                if not ONE:
                    S.add('dve', 'tensor_tensor', out=sv(P, ps), in0=sv(Zt(zy), ps), in1=m3(id2, ps), op=ALU.add)
            self.mk('r_after_masks')
            yield 0
            cur = 0
            for lv in range(1, cfg.lv):
                zn = R['ZY%d' % (1 - cur)]
                zo = R['ZY%d' % cur]
                pyb = self.bank()
                for (j, hh) in HEADS:
                    S.add('pe', 'matmul', out=pc(pyb, j, hh), lhsT=cc(Zt(zo), j, hh), rhs=cc(Yt(zo), j, hh), start=True, stop=True)
                for ps in PS:
                    S.add('act', 'activation', out=sv(Yt(zn), ps), in_=hv(pyb, ps), func=AF.Copy)
                if lv < cfg.lv - 1:
                    pzb = self.bank()
                    for (j, hh) in HEADS:
                        S.add('pe', 'matmul', out=pc(pzb, j, hh), lhsT=cc(Yt(zo), j, hh), rhs=cc(Zt(zo), j, hh), start=True, stop=True)
                    for ps in PS:
                        S.add('act', 'activation', out=sv(Zt(zn), ps), in_=hv(pzb, ps), func=AF.Copy)
                yield 0
                ppb = self.bank()
                Pn = R['PP1'] if P is R['PP0'] else R['PP0']
                for (j, hh) in HEADS:
                    S.add('pe', 'matmul', out=pc(ppb, j, hh), lhsT=cc(Yt(zn), j, hh), rhs=cc(P, j, hh), start=True, stop=True)
                for ps in PS:
                    S.add('dve', 'tensor_tensor', out=sv(Pn, ps), in0=hv(ppb, ps), in1=sv(P, ps), op=ALU.add)
                P = Pn
                cur = 1 - cur
                yield 0
            self.mk('r_after_inv')
            yield 'SD'
            S.add('dve', 'tensor_tensor', out=self.T0f[:], in0=T[:], in1=bc(self.eM[:, :, ch:ch + 1], [128, 3, 64]), op=ALU.mult)
            T0h = lambda j, hh: self.T0f[hs(hh), j, :]
            pv = lambda bk, j, hh: bk[ph(hh), j * 64:(j + 1) * 64]
            px = self.bank()
            for (j, hh) in HEADS:
                if not ONE:
                    S.add('pe', 'matmul', out=pv(px, j, hh), lhsT=cc(R['NT'], j, hh), rhs=Vtm(j, hh), start=True, stop=False)
                S.add('pe', 'matmul', out=pv(px, j, hh), lhsT=fm('kt', j, hh), rhs=T0h(j, hh), start=ONE, stop=True)
            if ONE:
                for ps in PS:
                    S.add('act', 'activation', out=R['Ub'][ps], in_=px[ps, 0:192].rearrange("p (j v) -> p j v", j=3), func=AF.Copy, scale=-1.0)
            else:
                for ps in PS:
                    S.add('act', 'activation', out=R['Xb'][ps], in_=px[ps, 0:192].rearrange("p (j v) -> p j v", j=3), func=AF.Copy)
                self.mk('r_after_x')
                pu = self.bank()
                for (j, hh) in HEADS:
                    S.add('pe', 'matmul', out=pv(pu, j, hh), lhsT=cc(P, j, hh), rhs=R['Xb'][ph(hh), j, :], start=True, stop=True)
                for ps in PS:
                    S.add('act', 'activation', out=R['Ub'][ps], in_=pu[ps, 0:192].rearrange("p (j v) -> p j v", j=3), func=AF.Copy, scale=-1.0)
            self.mk('r_after_u')
            po = self.bank()
            if ONE:
                po2 = self.bank()
                for (j, hh) in HEADS:
                    S.add('pe', 'matmul', out=po[hs(hh), j * CHL:(j + 1) * CHL], lhsT=T0h(j, hh), rhs=fm('rt', j, hh), start=True, stop=True)
                for (j, hh) in HEADS:
                    o_ap = po2[hs(hh), j * CHL:(j + 1) * CHL]
                    S.add('pe', 'matmul', out=o_ap, lhsT=R['Ub'][ph(hh), j, :], rhs=cc(R['AbT'], j, hh), start=True, stop=False)
                    S.add('pe', 'matmul', out=o_ap, lhsT=Vtm(j, hh), rhs=cc(R['AkT'], j, hh), start=False, stop=True)
                S.add('act', 'activation', out=self.O[:, :, ts], in_=po[:, 0:3 * CHL].rearrange("p (j t) -> p j t", j=3), func=AF.Copy)
                S.add('dve', 'tensor_tensor', out=self.O[:, :, ts], in0=self.O[:, :, ts], in1=po2[:, 0:3 * CHL].rearrange("p (j t) -> p j t", j=3), op=ALU.add)
            else:
                for (j, hh) in HEADS:
                    o_ap = po[hs(hh), j * CHL:(j + 1) * CHL]
                    S.add('pe', 'matmul', out=o_ap, lhsT=T0h(j, hh), rhs=fm('rt', j, hh), start=True, stop=False)
                    S.add('pe', 'matmul', out=o_ap, lhsT=R['Ub'][ph(hh), j, :], rhs=cc(R['AbT'], j, hh), start=False, stop=False)
                    S.add('pe', 'matmul', out=o_ap, lhsT=Vtm(j, hh), rhs=cc(R['AkT'], j, hh), start=False, stop=True)
                S.add('act', 'activation', out=self.O[:, :, ts], in_=po[:, 0:3 * CHL].rearrange("p (j t) -> p j t", j=3), func=AF.Copy)
            self.mk('r_after_o')
            pt = self.bank()
            for (j, hh) in HEADS:
                t_ap = pt[hs(hh), j * 64:(j + 1) * 64]
                S.add('pe', 'matmul', out=t_ap, lhsT=Btm(j, hh), rhs=R['Ub'][ph(hh), j, :], start=True, stop=False)
                S.add('pe', 'matmul', out=t_ap, lhsT=Ktm(j, hh), rhs=Vtm(j, hh), start=False, stop=True)
            S.add('dve', 'tensor_tensor', out=self.T0f[:], in0=self.T0f[:], in1=pt[:, 0:192].rearrange("p (j v) -> p j v", j=3), op=ALU.add)
            S.add('dve', 'tensor_tensor', out=T[:], in0=self.T0f[:], in1=bc(self.eCM[:, :, ch:ch + 1], [128, 3, 64]), op=ALU.mult)
            if smp:
                wk = self.wkT[ch % 2]
                bko = self.bank()
                for j in range(3):
                    S.add('pe', 'transpose', out=bko[0:64, j * 128:(j + 1) * 128], in_=T[:, j, :], identity=self.ident)
                S.add('act', 'activation', out=wk[:], in_=bko[0:64, 0:384].rearrange("p (j f) -> p j f", j=3), func=AF.Copy)
                S.add('sp', 'dma_start', dma='swo0', out=self.o_swkv[l, ch].rearrange("h v k -> v h k"),
                      in_=wk[:].rearrange("p j (hh k) -> p (j hh) k", hh=2))

        GRP = 2 if (CHL == 64 and not smp) else 1
        for c0 in range(0, NCHK, GRP):
            gens = [chunk_gen(c0 + i, self.R if i == 0 else self.R2) for i in range(min(GRP, NCHK - c0))]
            live = list(gens)
            while live:
                nxt = []
                for gq in live:
                    if next(gq) != 'SD':
                        nxt.append(gq)
                live = nxt
            for gq in gens:
                for _ in gq:
                    pass
        self.mk('r_post')
        O = self.O
        pst = {}

        def p_mm(key, rhs_fn, lhsT):
            def f(j):
                pst[(key, j)] = self.bank()
                S.add('pe', 'matmul', out=pst[(key, j)][:, 0:N], lhsT=lhsT(j), rhs=rhs_fn(j), start=True, stop=True)
            return f
        psteps = [
            lambda j: S.add('act', 'activation', out=rkbf[:, j, :N], in_=O[:, j, :N], func=AF.Copy),
            p_mm('m', lambda j: rkbf[:, j, :N], lambda j: self.bmean_bf[:]),
            lambda j: S.add('dve', 'tensor_tensor', out=O[:, j, :N], in0=O[:, j, :N], in1=pst[('m', j)][:, 0:N], op=ALU.subtract),
            lambda j: S.add('act', 'activation', out=rkbf[:, j, :N], in_=O[:, j, :N], func=AF.Square),
            p_mm('v', lambda j: rkbf[:, j, :N], lambda j: self.bmean_bf[:]),
            lambda j: S.add('act', 'activation', out=t1[:, j, :N], in_=pst[('v', j)][:, 0:N], func=AF.Ln, bias=self.epsc[GN_EPS][:]),
            lambda j: S.add('act', 'activation', out=t1[:, j, :N], in_=t1[:, j, :N], func=AF.Exp, scale=-0.5),
            lambda j: S.add('dve', 'tensor_tensor', out=O[:, j, :N], in0=O[:, j, :N], in1=t1[:, j, :N], op=ALU.mult),
            lambda j: S.add('dve', 'tensor_scalar', out=O[:, j, :N], in0=O[:, j, :N], scalar1=cs('lnw')[:, j:j + 1], scalar2=cs('lnb')[:, j:j + 1],
                            op0=ALU.mult, op1=ALU.add),
            lambda j: S.add('dve', 'tensor_tensor', out=O[:, j, :N], in0=O[:, j, :N], in1=bonus[:, j, :N], op=ALU.add),
            p_mm('g', lambda j: self.sgg[:, :N], lambda j: self.SM[:, 384 + j * 128:384 + (j + 1) * 128]),
            lambda j: S.add('dve', 'tensor_tensor', out=self.mix[:, j, :N], in0=O[:, j, :N], in1=pst[('g', j)][:, 0:N], op=ALU.mult),
        ]
        self.wavefront(psteps, 3)

    def hgrn(self, l, cfg):
        S, cp = self.S, self.cp
        N, CHL, NCHK = cfg.n, cfg.ch, cfg.nch
        Pr = self.Pr
        C = self.C
        q32, lf, kin, cum, E, sil = self.A
        qt, kkt, vbf, tmpb = self.B[0], self.B[1], self.B[2], self.B[3]
        oml = lambda j: self.oml[:, j, l:l + 1]
        noml = lambda j: self.noml[:, j, l:l + 1]
        lbj = lambda j: self.lb[:, j, l:l + 1]
        self.mk('h_start')
        def evac(bk, pair):
            v2 = bk[:].rearrange("p (c t) -> p c t", c=2)
            for idx, c in enumerate(pair):
                g_, j = (c - 11) // 3, (c - 11) % 3
                src = v2[:, idx, :N]
                if g_ == 0:
                    S.add('act', 'activation', out=q32[:, j, :N], in_=src, func=AF.Copy)
                elif g_ == 1:
                    S.add('act', 'activation', out=kin[:, j, :N], in_=src, func=AF.Sigmoid)
                elif g_ == 2:
                    S.add('act', 'activation', out=vbf[:, j, :N], in_=src, func=AF.Copy)
                else:
                    S.add('act', 'activation', out=sil[:, j, :N], in_=src, func=AF.Silu)
        self.project(list(range(11, 23)), N, evac)
        self.mk('h_after_proj')
        c3 = lambda t, j: t[:, j, :N].rearrange("p (c t) -> p c t", t=CHL)

        def h_scan(j):
            if CHL > 1:
                S.add('dve', 'tensor_tensor_scan', out=cum[:, j, :N], data0=self.rmask[:, :N], data1=lf[:, j, :N], initial=0.0, op0=ALU.mult, op1=ALU.add)
            else:
                S.add('dve', 'tensor_copy', out=cum[:, j, :N], in_=lf[:, j, :N])
        hsteps = [
            lambda j: S.add('act', 'activation', out=lf[:, j, :N], in_=kin[:, j, :N], func=AF.Ln, scale=oml(j), bias=lbj(j)),
            lambda j: S.add('dve', 'tensor_scalar', out=kin[:, j, :N], in0=kin[:, j, :N], scalar1=-1.0, scalar2=noml(j), op0=ALU.add, op1=ALU.mult),
            h_scan,
            lambda j: S.add('act', 'activation', out=self.eM[:, j, :NCHK], in_=c3(cum, j)[:, :, cfg.mid], func=AF.Exp),
            lambda j: S.add('act', 'activation', out=self.cmid[:, j, :NCHK], in_=c3(cum, j)[:, :, cfg.mid], func=AF.Copy),
            lambda j: S.add('dve', 'tensor_tensor', out=c3(cum, j), in0=c3(cum, j), in1=bc(self.cmid[:, j, :NCHK].unsqueeze(2), [128, NCHK, CHL]), op=ALU.subtract),
            lambda j: S.add('act', 'activation', out=E[:, j, :N], in_=cum[:, j, :N], func=AF.Exp),
            lambda j: S.add('dve', 'tensor_tensor', out=qt[:, j, :N], in0=q32[:, j, :N], in1=E[:, j, :N], op=ALU.mult),
            lambda j: S.add('act', 'activation', out=self.eCM[:, j, :NCHK], in_=c3(E, j)[:, :, CHL - 1], func=AF.Copy),
            lambda j: S.add('act', 'activation', out=E[:, j, :N], in_=cum[:, j, :N], func=AF.Exp, scale=-1.0),
            lambda j: S.add('dve', 'tensor_tensor', out=kkt[:, j, :N], in0=kin[:, j, :N], in1=E[:, j, :N], op=ALU.mult),
        ]
        self.wavefront(hsteps, 3)
        self.mk('h_pre_chunk')
        Sst = self.Sst[l]
        smp = self.smp
        PS = [slice(0, 128)] if CHL == 64 else [slice(0, CHL), slice(64, 64 + CHL)]
        miu = C[:, cp.sl('miu')]
        npz = lambda ps: ps.stop - ps.start
        m3 = lambda m, ps: bc(m[ps, 0:CHL].unsqueeze(1), [npz(ps), 3, CHL])
        hv = lambda bk, ps: bk[ps, 0:192].rearrange("p (j t) -> p j t", j=3)[:, :, 0:CHL]
        sv = lambda t, ps: t[ps, :, 0:CHL]
        hs = lambda hh: slice(hh * 64, hh * 64 + 64)
        ph = lambda hh: slice(hh * 64, hh * 64 + CHL)
        HEADS = [(j, hh) for hh in range(2) for j in range(3)]
        def hchunk_gen(ch, HB):
            Sst = self.Sst[l]
            ts = slice(ch * CHL, (ch + 1) * CHL)
            if smp:
                Sst = self.Ss[ch % 2]
                for cn in ([0, 1] if ch == 0 else [ch + 1]):
                    if cn < NCHK:
                        for hh in range(2):
                            S.add('sp', 'dma_start', dma=f'shi{cn % 2}', out=self.Ss[cn % 2][hh * 64:(hh + 1) * 64, :, :],
                                  in_=self.st_hgrn[l, cn].rearrange("(j hh) k v -> hh k j v", hh=2)[hh])
            bkA = self.bank()
            for j in range(3):
                for hh in range(2):
                    S.add('pe', 'matmul', out=bkA[ph(hh), j * 64:(j + 1) * 64], lhsT=vbf[:, j, ts], rhs=self.ident_bf[:, hs(hh)], start=True, stop=True)
                for hh in range(2):
                    S.add('pe', 'matmul', out=bkA[ph(hh), 192 + j * 64:192 + (j + 1) * 64], lhsT=kkt[:, j, ts], rhs=self.ident_bf[:, hs(hh)], start=True, stop=True)
            for ps in PS:
                S.add('act', 'activation', out=HB['TM2'][ps], in_=bkA[ps, 0:384].rearrange("p (q f) -> p q f", f=64), func=AF.Copy)
            Vtm = lambda j, hh: HB['TM2'][ph(hh), j, :]
            Ktm = lambda j, hh: HB['TM2'][ph(hh), 3 + j, :]
            fm = lambda t, j, hh: t[hs(hh), j, ts]
            cc = lambda t, j, hh: t[ph(hh), j, 0:CHL]
            pc = lambda bk, j, hh: bk[ph(hh), j * 64:j * 64 + CHL]
            pa = self.bank()
            for (j, hh) in HEADS:
                S.add('pe', 'matmul', out=pc(pa, j, hh), lhsT=fm(kkt, j, hh), rhs=fm(qt, j, hh), start=True, stop=True)
            for ps in PS:
                S.add('dve', 'tensor_tensor', out=sv(HB['AkT'], ps), in0=hv(pa, ps), in1=m3(miu, ps), op=ALU.mult)
            yield 'SD'
            S.add('dve', 'tensor_tensor', out=self.T0f[:], in0=Sst[:], in1=bc(self.eM[:, :, ch:ch + 1], [128, 3, 64]), op=ALU.mult)
            S.add('act', 'activation', out=self.T0b[:], in_=self.T0f[:], func=AF.Copy)
            T0h = lambda j, hh: self.T0b[hs(hh), j, :]
            po = self.bank()
            if CHL == 1:
                po2 = self.bank()
                for (j, hh) in HEADS:
                    S.add('pe', 'matmul', out=po[hs(hh), j * CHL:(j + 1) * CHL], lhsT=T0h(j, hh), rhs=fm(qt, j, hh), start=True, stop=True)
                for (j, hh) in HEADS:
                    S.add('pe', 'matmul', out=po2[hs(hh), j * CHL:(j + 1) * CHL], lhsT=Vtm(j, hh), rhs=cc(HB['AkT'], j, hh), start=True, stop=True)
                S.add('act', 'activation', out=self.O[:, :, ts], in_=po[:, 0:3 * CHL].rearrange("p (j t) -> p j t", j=3), func=AF.Copy)
                S.add('dve', 'tensor_tensor', out=self.O[:, :, ts], in0=self.O[:, :, ts], in1=po2[:, 0:3 * CHL].rearrange("p (j t) -> p j t", j=3), op=ALU.add)
            else:
                for (j, hh) in HEADS:
                    o_ap = po[hs(hh), j * CHL:(j + 1) * CHL]
                    S.add('pe', 'matmul', out=o_ap, lhsT=T0h(j, hh), rhs=fm(qt, j, hh), start=True, stop=False)
                    S.add('pe', 'matmul', out=o_ap, lhsT=Vtm(j, hh), rhs=cc(HB['AkT'], j, hh), start=False, stop=True)
                S.add('act', 'activation', out=self.O[:, :, ts], in_=po[:, 0:3 * CHL].rearrange("p (j t) -> p j t", j=3), func=AF.Copy)
            pt = self.bank()
            for (j, hh) in HEADS:
                t_ap = pt[hs(hh), j * 64:(j + 1) * 64]
                S.add('pe', 'matmul', out=t_ap, lhsT=Ktm(j, hh), rhs=Vtm(j, hh), start=True, stop=True)
            S.add('dve', 'tensor_tensor', out=self.T0f[:], in0=self.T0f[:], in1=pt[:, 0:192].rearrange("p (j v) -> p j v", j=3), op=ALU.add)
            S.add('dve', 'tensor_tensor', out=Sst[:], in0=self.T0f[:], in1=bc(self.eCM[:, :, ch:ch + 1], [128, 3, 64]), op=ALU.mult)
            if smp:
                for hh in range(2):
                    S.add('sp', 'dma_start', dma=f'sho{ch % 2}', out=self.o_shgrn[l, ch].rearrange("(j hh) k v -> hh k j v", hh=2)[hh],
                          in_=Sst[hh * 64:(hh + 1) * 64, :, :])

        GRP = 2 if (CHL == 64 and not smp) else 1
        HB0 = {'TM2': self.TM2, 'AkT': self.AkT}
        for c0 in range(0, NCHK, GRP):
            gens = [hchunk_gen(c0 + i, HB0 if i == 0 else self.HB2) for i in range(min(GRP, NCHK - c0))]
            for gq in gens:
                next(gq)
            for gq in gens:
                for _ in gq:
                    pass
        self.mk('h_post')
        O = self.O
        hng = C[:, cp.sl(f"hng{l}")]
        hst_ = {}

        def hp_mm(j):
            hst_[j] = self.bank()
            S.add('pe', 'matmul', out=hst_[j][:, 0:N], lhsT=self.bmean_bf[:], rhs=tmpb[:, j, :N], start=True, stop=True)
        hpsteps = [
            lambda j: S.add('act', 'activation', out=tmpb[:, j, :N], in_=O[:, j, :N], func=AF.Square),
            hp_mm,
            lambda j: S.add('act', 'activation', out=E[:, j, :N], in_=hst_[j][:, 0:N], func=AF.Ln, bias=self.epsc[NORM_EPS][:]),
            lambda j: S.add('act', 'activation', out=E[:, j, :N], in_=E[:, j, :N], func=AF.Exp, scale=-0.5),
            lambda j: S.add('dve', 'scalar_tensor_tensor', out=O[:, j, :N], in0=O[:, j, :N], scalar=hng[:, j:j + 1], in1=E[:, j, :N], op0=ALU.mult, op1=ALU.mult),
            lambda j: S.add('dve', 'tensor_tensor', out=self.mix[:, 3 + j, :N], in0=O[:, j, :N], in1=sil[:, j, :N], op=ALU.mult),
        ]
        self.wavefront(hpsteps, 3)

    def lru(self, l, cfg, last):
        S, cp = self.S, self.cp
        N = cfg.n
        C = self.C
        xb = self.Pr
        XB = self.A[0]
        xbv = XB[:].rearrange("p j n -> p (j n)")[:, 0:2 * (N + 3)].rearrange("p (c n) -> p c n", c=2)
        xc, rr, ii, aa, uu, gt = self.A[1], self.A[2], self.A[3], self.A[4], self.A[5], self.O
        xcb = self.B[0]
        cw = C[:, cp.sl(f"cw{l}")].rearrange("p (j c) -> p j c", c=2)
        smp = self.smp
        if not smp:
            S.add('act', 'activation', out=xbv[:, :, 0:3], in_=self.cvh[l][:], func=AF.Copy)

        def evac(bk, pair):
            v2 = bk[:].rearrange("p (c t) -> p c t", c=2)
            if pair[0] == 23:
                S.add('act', 'activation', out=xbv[:, :, 3:N + 3], in_=v2[:, :, :N], func=AF.Copy)
            else:
                S.add('act', 'activation', out=gt[:, 0:2, :N], in_=v2[:, :, :N], func=AF.Copy)
        self.project([23, 24, 25, 26], N, evac)
        if not smp:
            S.add('act', 'activation', out=self.cvh[l][:], in_=xbv[:, :, N:N + 3], func=AF.Copy)
        else:
            S.add('act', 'activation', out=self.XBs[:], in_=xbv[:, :, 3:N + 3], func=AF.Copy)
        for c in range(2):
            S.add('dve', 'tensor_scalar', out=xc[:, c, :N], in0=xbv[:, c, 3:N + 3], scalar1=cw[:, 3, c:c + 1], scalar2=C[:, cp.sl(f"cb{l}")][:, c:c + 1],
                  op0=ALU.mult, op1=ALU.add)
            for jj in range(3):
                prevj = self.CV[:, jj * 2 + c, :N] if smp else xbv[:, c, jj:jj + N]
                S.add('dve', 'scalar_tensor_tensor', out=xc[:, c, :N], in0=prevj, scalar=cw[:, jj, c:c + 1], in1=xc[:, c, :N],
                      op0=ALU.mult, op1=ALU.add)
        S.add('act', 'activation', out=xcb[:, 0:2, :N], in_=xc[:, 0:2, :N], func=AF.Copy)

        def lru_chain(c):
            bk = self.bank()
            for nn in range(2):
                pr_ = slice(nn * 64, nn * 64 + 64)
                S.add('pe', 'matmul', out=bk[pr_, 0:N], lhsT=self.SM[pr_, 768 + c * 64:768 + (c + 1) * 64], rhs=xcb[pr_, c, :N], start=True, stop=True)
                S.add('pe', 'matmul', out=bk[pr_, 256:256 + N], lhsT=self.SM[pr_, 896 + c * 64:896 + (c + 1) * 64], rhs=xcb[pr_, c, :N], start=True, stop=True)
            S.add('act', 'activation', out=rr[:, c, :N], in_=bk[:, 0:N], func=AF.Sigmoid, bias=C[:, cp.sl(f"ba{l}")][:, c:c + 1])
            S.add('act', 'activation', out=ii[:, c, :N], in_=bk[:, 256:256 + N], func=AF.Sigmoid, bias=C[:, cp.sl(f"bx{l}")][:, c:c + 1])
            S.add('act', 'activation', out=aa[:, c, :N], in_=rr[:, c, :N], func=AF.Exp, scale=self.c8[:, l, c:c + 1])
            S.add('act', 'activation', out=rr[:, c, :N], in_=rr[:, c, :N], func=AF.Exp, scale=self.c16[:, l, c:c + 1])
            S.add('dve', 'tensor_scalar', out=rr[:, c, :N], in0=rr[:, c, :N], scalar1=-1.0, scalar2=1.0, op0=ALU.mult, op1=ALU.add)
            S.add('dve', 'tensor_scalar', out=rr[:, c, :N], in0=rr[:, c, :N], scalar1=1e-12, scalar2=None, op0=ALU.max)
            S.add('act', 'activation', out=rr[:, c, :N], in_=rr[:, c, :N], func=AF.Sqrt)
            S.add('dve', 'tensor_tensor', out=uu[:, c, :N], in0=xc[:, c, :N], in1=ii[:, c, :N], op=ALU.mult)
            S.add('dve', 'tensor_tensor', out=uu[:, c, :N], in0=uu[:, c, :N], in1=rr[:, c, :N], op=ALU.mult)
            if smp:
                S.add('dve', 'tensor_tensor', out=ii[:, c, :N], in0=aa[:, c, :N], in1=self.H0[:, c, :N], op=ALU.mult)
                S.add('dve', 'tensor_tensor', out=ii[:, c, :N], in0=ii[:, c, :N], in1=uu[:, c, :N], op=ALU.add)
                S.add('act', 'activation', out=self.H0[:, c, :N], in_=ii[:, c, :N], func=AF.Copy)
            else:
                S.add('dve', 'tensor_tensor_scan', out=ii[:, c, :N], data0=aa[:, c, :N], data1=uu[:, c, :N], initial=self.hst[l][:, c, :], op0=ALU.mult, op1=ALU.add)
                S.add('act', 'activation', out=self.hst[l][:, c, :], in_=ii[:, c, N - 1:N], func=AF.Copy)
            S.add('act', 'activation', out=uu[:, c, :N], in_=gt[:, c, :N], func=AF.Square)
            S.add('dve', 'tensor_scalar', out=uu[:, c, :N], in0=uu[:, c, :N], scalar1=0.044715, scalar2=1.0, op0=ALU.mult, op1=ALU.add)
            S.add('dve', 'tensor_tensor', out=uu[:, c, :N], in0=uu[:, c, :N], in1=gt[:, c, :N], op=ALU.mult)
            S.add('act', 'activation', out=uu[:, c, :N], in_=uu[:, c, :N], func=AF.Sigmoid, scale=1.5957691216057308)
            S.add('dve', 'tensor_tensor', out=uu[:, c, :N], in0=uu[:, c, :N], in1=gt[:, c, :N], op=ALU.mult)
            S.add('dve', 'tensor_tensor', out=self.mix[:, 6 + c, :N], in0=uu[:, c, :N], in1=ii[:, c, :N], op=ALU.mult)
        self.interleave([lambda c=c: lru_chain(c) for c in range(2)])

    def alloc_b(self):
        self.sb_off = self.work_base
        sb = self.sb
        u = self.nm("b")
        self.xn2 = sb(f"xn2{u}", [128, KD, TPP], BF16)
        self.nsq = sb(f"nsq{u}", [128, KD, TT], BF16)
        self.nln = sb(f"nln{u}", [128, TT])
        self.nrs = sb(f"nrs{u}", [128, TT])
        self.w1g = [sb(f"w1g{i}{u}", [128, KD, FG], BF16) for i in range(2)]
        self.w2g = [sb(f"w2g{i}{u}", [128, FG // 128, D], BF16) for i in range(2)]
        self.hb = [sb(f"hb{i}{u}", [128, FG // 128, 512], BF16) for i in range(2)]
        self.xn2s = sb(f"xn2s{u}", [128, KD, 16], BF16)
        self.hbs = sb(f"hbs{u}", [128, FG // 128, 16], BF16)
        self.rls = sb(f"rls{u}", [128, FG // 128, 16])
        self.rl = [sb(f"rl{i}{u}", [128, 512]) for i in range(2)]

    def phase_b(self, l, nxt_loads=()):
        S, cp = self.S, self.cp
        g2 = self.C[:, cp.sl(f"g2{l}")]
        def load_group(g):
            s = g % 2
            S.add('pool', 'dma_start', dma=f'w1g{s}', out=self.w1g[s][:], in_=self.w1[l][:, g * FG:(g + 1) * FG].rearrange("(k p) c -> p k c", p=128))
            S.add('pool', 'dma_start', dma=f'w2g{s}', out=self.w2g[s][:], in_=self.w2[l][g * FG:(g + 1) * FG, :].rearrange("(k p) c -> p k c", p=128))
        nxt_loads = list(nxt_loads)
        load_group(0)
        load_group(1)
        if nxt_loads:
            nxt_loads.pop(0)()
        for tt in range(NTILE):
            self.rmsnorm_into(self.x[tt], g2, tt)
        nmt = TPP // 512
        do_s = self.do_smp and self.cur_ps == self.npass - 1
        if do_s:
            self.rmsnorm(self.xs, g2, self.xn2s, 16)

        def sample_group(g):
            s_ = g % 2
            bk = self.bank()
            for c in range(FG // 128):
                for k in range(KD):
                    S.add('pe', 'matmul', out=bk[:, c * 16:(c + 1) * 16], lhsT=self.w1g[s_][:, k, c * 128:(c + 1) * 128], rhs=self.xn2s[:, k, :],
                          start=(k == 0), stop=(k == KD - 1))
            S.add('act', 'activation', out=self.rls[:], in_=bk[:, 0:(FG // 128) * 16].rearrange("p (c t) -> p c t", t=16), func=AF.Relu)
            S.add('pool', 'tensor_tensor', out=self.hbs[:], in0=self.rls[:], in1=self.rls[:], op=ALU.mult)
            bk = self.bank()
            for c in range(KD):
                for k in range(FG // 128):
                    S.add('pe', 'matmul', out=bk[:, c * 16:(c + 1) * 16], lhsT=self.w2g[s_][:, k, c * 128:(c + 1) * 128], rhs=self.hbs[:, k, :],
                          start=(k == 0), stop=(k == FG // 128 - 1))
            S.add('dve', 'tensor_tensor', out=self.xs[:], in0=self.xs[:], in1=bk[:, 0:KD * 16].rearrange("p (c t) -> p c t", t=16), op=ALU.add)

        def stage1(u, g, mt):
            s_ = g % 2
            hb = self.hb[u % 2]
            for c in range(FG // 128):
                bk = self.bank()
                for k in range(KD):
                    S.add('pe', 'matmul', out=bk[:], lhsT=self.w1g[s_][:, k, c * 128:(c + 1) * 128], rhs=self.xn2[:, k, mt * 512:(mt + 1) * 512],
                          start=(k == 0), stop=(k == KD - 1))
                rl = self.rl[c % 2]
                S.add('act', 'activation', out=rl[:], in_=bk[:], func=AF.Relu)
                S.add('pool', 'tensor_tensor', out=hb[:, c, :], in0=rl[:], in1=rl[:], op=ALU.mult)

        def stage2(u, g, mt):
            s_ = g % 2
            hb = self.hb[u % 2]
            for c in range(KD):
                bk = self.bank()
                for k in range(FG // 128):
                    S.add('pe', 'matmul', out=bk[:], lhsT=self.w2g[s_][:, k, c * 128:(c + 1) * 128], rhs=hb[:, k, :],
                          start=(k == 0), stop=(k == FG // 128 - 1))
                for hh in range(512 // TT):
                    xt = self.x[mt * (512 // TT) + hh]
                    S.add('dve', 'tensor_tensor', out=xt[:, c, :], in0=xt[:, c, :], in1=bk[:, hh * TT:(hh + 1) * TT], op=ALU.add)

        units = [(g, mt) for g in range(NFG) for mt in range(nmt)]
        stage1(0, *units[0])
        for u, (g, mt) in enumerate(units):
            if u + 1 < len(units):
                stage1(u + 1, *units[u + 1])
            stage2(u, g, mt)
            if mt == nmt - 1:
                if do_s:
                    sample_group(g)
                if g + 2 < NFG:
                    load_group(g + 2)
                if nxt_loads:
                    nxt_loads.pop(0)()
        for t in nxt_loads:
            t()

    def rmsnorm_into(self, xt, g, tt):
        S = self.S
        N = TT
        S.add('act', 'activation', out=self.nsq[:], in_=xt[:], func=AF.Square)
        bk = self.bank()
        for k in range(KD):
            S.add('pe', 'matmul', out=bk[:, :N], lhsT=self.ones_bf[:], rhs=self.nsq[:, k, :], start=(k == 0), stop=(k == KD - 1))
        S.add('act', 'activation', out=self.nln[:], in_=bk[:, :N], func=AF.Ln, scale=1.0 / D, bias=self.epsc[NORM_EPS][:])
        S.add('act', 'activation', out=self.nrs[:], in_=self.nln[:], func=AF.Exp, scale=-0.5)
        for k in range(KD):
            S.add('dve', 'scalar_tensor_tensor', out=self.xn2[:, k, tt * TT:(tt + 1) * TT], in0=xt[:, k, :], scalar=g[:, k:k + 1], in1=self.nrs[:],
                  op0=ALU.mult, op1=ALU.mult)

    def store_y(self, ps):
        S, cp = self.S, self.cp
        g = self.C[:, cp.sl('fg')]
        for tt in range(NTILE):
            self.rmsnorm(self.x[tt], g, self.yfm, TT)
            for tb in range(TT // 128):
                ob = self.xtm[tb % 2]
                for kh in range(2):
                    bk = self.bank()
                    for kk in range(4):
                        k = kh * 4 + kk
                        S.add('pe', 'transpose', out=bk[:, kk * 128:(kk + 1) * 128],
                              in_=self.yfm[:, k, tb * 128:(tb + 1) * 128], identity=self.ident)
                    S.add('act', 'activation', out=ob[:, kh * 512:(kh + 1) * 512], in_=bk[:], func=AF.Copy)
                t0 = ps * TPP + tt * TT + tb * 128
                S.add('sp', 'dma_start', dma=f'ytm{tb % 2}', out=self.y_p[t0:t0 + 128, :], in_=ob[:])

    def store_states(self):
        S, nc = self.S, self.nc
        self.sb_off = self.work_base
        wk = self.sb("wkvT", [64, DEPTH, 3, 128])
        with nc.allow_non_contiguous_dma(reason="small strided state outputs"):
            S.add('sp', 'dma_start', dma='o_small', allow_slow_non_contiguous=True, out=self.o_pshift.rearrange("l (k p) -> p l k", p=128), in_=self.shl[:])
            for l in range(DEPTH):
                bk = self.bank()
                for j in range(3):
                    S.add('pe', 'transpose', out=bk[0:64, j * 128:(j + 1) * 128], in_=self.Tst[l][:, j, :], identity=self.ident)
                S.add('act', 'activation', out=wk[:, l, :, :], in_=bk[0:64, 0:384].rearrange("p (j f) -> p j f", j=3), func=AF.Copy)
                S.add('sp', 'dma_start', dma=f'o_wkv{l}', allow_slow_non_contiguous=True, out=self.o_pwkv[l].rearrange("h v k -> v h k"),
                      in_=wk[:, l, :, :].rearrange("p j (hh k) -> p (j hh) k", hh=2))
                for hh in range(2):
                    S.add('sp', 'dma_start', dma='o_small', allow_slow_non_contiguous=True, out=self.o_phgrn[l].rearrange("(j hh) k v -> hh k j v", hh=2)[hh],
                          in_=self.Sst[l][hh * 64:(hh + 1) * 64, :, :])
                S.add('sp', 'dma_start', dma='o_small', allow_slow_non_contiguous=True, out=self.o_plru[l].rearrange("(c p) -> p c", p=128), in_=self.hst[l][:, :, 0])
                for c in range(2):
                    S.add('sp', 'dma_start', dma='o_small', allow_slow_non_contiguous=True, out=self.o_pconv[l][:, c * 128:(c + 1) * 128].rearrange("j p -> p j"), in_=self.cvh[l][:, c, :])


def kernel(**inputs):
    inp = {k: np.asarray(v) for k, v in inputs.items()}
    import os
    b = Builder(nlayers=int(os.environ.get("K_NL", DEPTH)), npass=int(os.environ.get("K_NP", NPASS)),
                do_mlp=os.environ.get("K_MLP", "1") == "1", do_smp=os.environ.get("K_SMP", "1") == "1")
    nc = b.build()
    cpk = pack_consts(inp, b.cp)
    smat = pack_small_mats(inp)
    in_maps = []
    for c in range(NCORES):
        in_maps.append({
            "x_prompt": np.ascontiguousarray(inp['x_prompt'][c]),
            "x_sample": np.ascontiguousarray(inp['x_sample'][c * 16:(c + 1) * 16, 0]),
            "state_wkv": np.ascontiguousarray(inp['state_wkv'][:, c * 16:(c + 1) * 16]),
            "state_shift": np.ascontiguousarray(inp['state_shift'][:, c * 16:(c + 1) * 16]),
            "state_hgrn": np.ascontiguousarray(inp['state_hgrn'][:, c * 16:(c + 1) * 16]),
            "state_lru": np.ascontiguousarray(inp['state_lru'][:, c * 16:(c + 1) * 16]),
            "state_conv": np.ascontiguousarray(inp['state_conv'][:, c * 16:(c + 1) * 16]),
            "cpack": cpk, "smat": smat,
            "w_in": inp['w_in'][:b.nlayers], "w_out": inp['w_out'][:b.nlayers],
            "mlp_w1": inp['mlp_w1'][:(b.nlayers if b.do_mlp else 1)], "mlp_w2": inp['mlp_w2'][:(b.nlayers if b.do_mlp else 1)],
        })
    ncr = int(os.environ.get("K_CORES", NCORES))
    res = run_bass_kernel_spmd(nc, in_maps[:ncr], core_ids=list(range(ncr)))
    R = list(res.results) + [res.results[0]] * (NCORES - ncr)
    print("total ops", len(b.S.ops))
    y_prompt = np.stack([R[c]["y_prompt"] for c in range(NCORES)], 0)
    p_shift = np.stack([R[c]["p_shift"] for c in range(NCORES)], 1)
    p_wkv = np.stack([R[c]["p_wkv"] for c in range(NCORES)], 1)
    p_hgrn = np.stack([R[c]["p_hgrn"] for c in range(NCORES)], 1)
    p_lru = np.stack([R[c]["p_lru"] for c in range(NCORES)], 1)
    p_conv = np.stack([R[c]["p_conv"] for c in range(NCORES)], 1)
    cat = lambda n, ax: np.concatenate([R[c][n] for c in range(NCORES)], ax)
    y_sample = cat("y_sample", 0)[:, None, :]
    return (y_prompt, y_sample, p_wkv, p_shift, p_hgrn, p_lru, p_conv,
            cat("s_wkv", 1), cat("s_shift", 1), cat("s_hgrn", 1), cat("s_lru", 1), cat("s_conv", 1))
```
